# Optimizing a Trainium2 kernel written in Bass

```python
import jax, jax.numpy as jnp
from jax import lax
import numpy as np

D_MODEL = 1024
BATCH = 4
SEQ = 8192
DEPTH = 4

D_MIX = D_MODEL
D_CONF = D_MIX // 4
D_ATT = D_MIX // 4
D_SC = D_MIX // 4
D_POOL = D_MIX - D_CONF - D_ATT - D_SC
HEAD_DIM = 64
N_ATT_HEADS = D_ATT // HEAD_DIM
CONF_KERNEL = 31
SC_KERNEL = 3
POOL_WINDOWS = (2, 4, 8, 16)
N_POOL_GROUPS = len(POOL_WINDOWS)
POOL_GROUP_DIM = D_POOL // N_POOL_GROUPS
D_FF = 4 * D_MODEL
D_PLE = 256
Q_BLOCK = 128
EPS = 1e-6
SPLITS = (2 * D_CONF, D_ATT, D_ATT, D_ATT, N_ATT_HEADS, D_SC, D_SC, D_SC, D_POOL)
D_IN = sum(SPLITS)
SPLIT_IDX = tuple(int(s) for s in np.cumsum(SPLITS)[:-1])

kernel_name = "hybrid_parallel_groups_fox_conv_pool"


def rms_norm(x, g):
    x32 = x.astype(jnp.float32)
    y = x32 * lax.rsqrt(jnp.mean(x32 * x32, axis=-1, keepdims=True) + EPS)
    return (y * g.astype(jnp.float32)).astype(x.dtype)


def layer_norm(x, g, b):
    x32 = x.astype(jnp.float32)
    mu = jnp.mean(x32, axis=-1, keepdims=True)
    xc = x32 - mu
    y = xc * lax.rsqrt(jnp.mean(xc * xc, axis=-1, keepdims=True) + EPS)
    return (y * g.astype(jnp.float32) + b.astype(jnp.float32)).astype(x.dtype)


def causal_depthwise_conv(u, w):
    k = w.shape[0]
    return lax.conv_general_dilated(
        u, w[:, None, :].astype(u.dtype), window_strides=(1,), padding=((k - 1, 0),),
        dimension_numbers=("NWC", "WIO", "NWC"), feature_group_count=u.shape[-1])


def conformer_conv(ab, w_dw, ln_g, ln_b, w_pw):
    a, b = jnp.split(ab, 2, axis=-1)
    u = a * jax.nn.sigmoid(b)
    u = causal_depthwise_conv(u, w_dw)
    u = jax.nn.silu(layer_norm(u, ln_g, ln_b))
    return u @ w_pw


def forgetting_attention(q, k, v, f_logit):
    b, s, _ = q.shape
    q = q.reshape(b, s, N_ATT_HEADS, HEAD_DIM).transpose(0, 2, 1, 3)
    k = k.reshape(b, s, N_ATT_HEADS, HEAD_DIM).transpose(0, 2, 1, 3)
    v = v.reshape(b, s, N_ATT_HEADS, HEAD_DIM).transpose(0, 2, 1, 3)
    log_f = jax.nn.log_sigmoid(f_logit.astype(jnp.float32))
    c = jnp.cumsum(log_f, axis=1).transpose(0, 2, 1)
    nb = s // Q_BLOCK
    qb = q.reshape(b, N_ATT_HEADS, nb, Q_BLOCK, HEAD_DIM).transpose(2, 0, 1, 3, 4)
    cb = c.reshape(b, N_ATT_HEADS, nb, Q_BLOCK).transpose(2, 0, 1, 3)
    pos = jnp.arange(s, dtype=jnp.int32)
    posb = pos.reshape(nb, Q_BLOCK)
    k32 = k.astype(jnp.float32)
    scale = HEAD_DIM ** -0.5

    def block(args):
        qi, ci, pi = args
        logits = jnp.einsum("bhqd,bhkd->bhqk", qi.astype(jnp.float32), k32) * scale
        logits = logits + ci[..., None] - c[:, :, None, :]
        mask = pi[:, None] >= pos[None, :]
        logits = jnp.where(mask, logits, -jnp.inf)
        probs = jax.nn.softmax(logits, axis=-1)
        return jnp.einsum("bhqk,bhkd->bhqd", probs.astype(v.dtype), v)

    o = lax.map(block, (qb, cb, posb))
    return o.transpose(1, 0, 3, 2, 4).reshape(b, s, D_ATT)


def short_conv_mixer(h, bg, cg, w_sc):
    return bg * causal_depthwise_conv(cg * h, w_sc)


def multiscale_pool(v, w_pool, scale):
    b, s, _ = v.shape
    v32 = v.astype(jnp.float32)
    count = jnp.arange(1, s + 1, dtype=jnp.float32)[None, :, None]
    groups = jnp.split(v32, N_POOL_GROUPS, axis=-1)
    outs = []
    for g, w in zip(groups, POOL_WINDOWS):
        csum = jnp.cumsum(g, axis=1)
        lag = jnp.pad(csum, ((0, 0), (w, 0), (0, 0)))[:, :s]
        mean = (csum - lag) / jnp.minimum(count, w)
        outs.append(mean - g)
    d = jnp.stack(outs, axis=2).astype(v.dtype)
    d = jnp.einsum("bsgc,gcd->bsgd", d, w_pool).reshape(b, s, D_POOL)
    return d * scale


def setup_inputs(seed: int = 0) -> dict:
    key = jax.random.key(seed)
    ks = jax.random.split(key, 24)
    f32 = jnp.float32
    L = DEPTH

    def nrm(k, shape, fan_in):
        return jax.random.normal(k, shape, f32) * (fan_in ** -0.5)

    def gain(k, shape):
        return 1.0 + 0.05 * jax.random.normal(k, shape, f32)

    return {
        "x": jax.random.normal(ks[0], (BATCH, SEQ, D_MODEL), f32),
        "p": jax.random.normal(ks[1], (DEPTH, BATCH, SEQ, D_PLE), f32),
        "g_mix_pre": gain(ks[2], (L, D_MODEL)),
        "w_in": nrm(ks[3], (L, D_MODEL, D_IN), D_MODEL),
        "b_forget": 0.1 * jax.random.normal(ks[4], (L, N_ATT_HEADS), f32),
        "w_conf_dw": nrm(ks[5], (L, CONF_KERNEL, D_CONF), CONF_KERNEL),
        "conf_ln_g": gain(ks[6], (L, D_CONF)),
        "conf_ln_b": 0.02 * jax.random.normal(ks[7], (L, D_CONF), f32),
        "w_conf_pw": nrm(ks[8], (L, D_CONF, D_CONF), D_CONF),
        "w_sc": nrm(ks[9], (L, SC_KERNEL, D_SC), SC_KERNEL),
        "w_pool": nrm(ks[10], (L, N_POOL_GROUPS, POOL_GROUP_DIM, POOL_GROUP_DIM), POOL_GROUP_DIM),
        "pool_scale": gain(ks[11], (L, D_POOL)),
        "w_out": nrm(ks[12], (L, D_MIX, D_MODEL), D_MIX),
        "g_mix_post": gain(ks[13], (L, D_MODEL)),
        "g_mlp_pre": gain(ks[14], (L, D_MODEL)),
        "w_up": nrm(ks[15], (L, D_MODEL, D_FF), D_MODEL),
        "w_down": nrm(ks[16], (L, D_FF, D_MODEL), D_FF),
        "g_mlp_post": gain(ks[17], (L, D_MODEL)),
        "g_ple_pre": gain(ks[18], (L, D_MODEL)),
        "w_ple_gate": nrm(ks[19], (L, D_MODEL, D_MODEL), D_MODEL),
        "w_ple_proj": nrm(ks[20], (L, D_PLE, D_MODEL), D_PLE),
        "g_ple_post": gain(ks[21], (L, D_MODEL)),
    }


def reference(x, p, g_mix_pre, w_in, b_forget, w_conf_dw, conf_ln_g, conf_ln_b, w_conf_pw,
              w_sc, w_pool, pool_scale, w_out, g_mix_post, g_mlp_pre, w_up, w_down,
              g_mlp_post, g_ple_pre, w_ple_gate, w_ple_proj, g_ple_post):
    h = x
    for i in range(DEPTH):
        xn = rms_norm(h, g_mix_pre[i])
        z = xn @ w_in[i]
        conf_ab, q, k, v, f_logit, sc_h, sc_b, sc_c, pool_v = jnp.split(z, SPLIT_IDX, axis=-1)
        y_conf = conformer_conv(conf_ab, w_conf_dw[i], conf_ln_g[i], conf_ln_b[i], w_conf_pw[i])
        y_att = forgetting_attention(q, k, v, f_logit + b_forget[i])
        y_sc = short_conv_mixer(sc_h, sc_b, sc_c, w_sc[i])
        y_pool = multiscale_pool(pool_v, w_pool[i], pool_scale[i])
        mix = jnp.concatenate([y_conf, y_att, y_sc, y_pool], axis=-1) @ w_out[i]
        h = h + rms_norm(mix, g_mix_post[i])
        hn = rms_norm(h, g_mlp_pre[i])
        ff = jnp.square(jax.nn.relu(hn @ w_up[i])) @ w_down[i]
        h = h + rms_norm(ff, g_mlp_post[i])
        gate = jax.nn.sigmoid(rms_norm(h, g_ple_pre[i]) @ w_ple_gate[i])
        e = (p[i] @ w_ple_proj[i]) * gate
        h = h + rms_norm(e, g_ple_post[i])
    return h
```

```python
import contextlib
import numpy as np
import concourse.bass as bass
import concourse.mybir as mybir
from concourse.bass_utils import run_bass_kernel_spmd

F32 = mybir.dt.float32
BF16 = mybir.dt.bfloat16
ALU = mybir.AluOpType
AF = mybir.ActivationFunctionType
ENGS = ("sp", "act", "pe", "dve", "pool")
EPS = 1e-6
NEG = -30000.0


class Res:
    __slots__ = ("name", "w", "r", "excl")

    def __init__(self, name="", excl=False):
        self.name = name
        self.w = {}
        self.r = {}
        self.excl = excl


def PR(name):
    return Res(name, excl=True)


def _merge(d, s, v):
    if d.get(s, 0) < v:
        d[s] = v


class Prog:
    def __init__(self, nc, stack, n_dma_sems=32):
        self.nc = nc
        self.q = {e: [] for e in ENGS}
        self.sem_names = []
        self.sem_count = []
        self.seen = {e: {} for e in ENGS}
        self.pending_reads = {e: [] for e in ENGS}
        self.pending_writes = {e: [] for e in ENGS}
        self.eng_sem = {}
        for e in ("act", "pe", "dve", "pool"):
            self.eng_sem[e] = self._new_sem("c_" + e)
        self.dma_sems = {"sp": [self._new_sem("d%d" % i) for i in range(n_dma_sems)],
                         "pool": [self._new_sem("w%d" % i) for i in range(16)]}
        self.dma_rr = {"sp": 0, "pool": 0}
        self.n_ops = 0
        self.sems = [stack.enter_context(nc.semaphore(n)) for n in self.sem_names]

    def _new_sem(self, name):
        self.sem_names.append(name)
        self.sem_count.append(0)
        return len(self.sem_names) - 1

    def op(self, eng, fn, reads=(), writes=(), inc=True, dma=False):
        waits = {}
        for r in reads:
            for s, v in r.w.items():
                _merge(waits, s, v)
            if r.excl:
                for s, v in r.r.items():
                    if not (eng in self.eng_sem and s == self.eng_sem[eng]):
                        _merge(waits, s, v)
        for w in writes:
            for s, v in w.w.items():
                _merge(waits, s, v)
            for s, v in w.r.items():
                _merge(waits, s, v)
        tok = None
        if dma:
            pool_ = self.dma_sems[eng]
            s = pool_[self.dma_rr[eng] % len(pool_)]
            self.dma_rr[eng] += 1
            if self.sem_count[s] > 0:
                _merge(waits, s, self.sem_count[s])
            self.sem_count[s] += 16
            tok = (s, self.sem_count[s], 16)
        elif inc:
            s = self.eng_sem[eng]
            self.sem_count[s] += 1
            tok = (s, self.sem_count[s], 1)
        seen = self.seen[eng]
        wl = []
        for s, v in waits.items():
            if seen.get(s, 0) < v:
                if eng == "pe" and s == self.eng_sem["pe"]:
                    continue
                seen[s] = v
                wl.append((s, v))
        self.q[eng].append((fn, wl, tok))
        self.n_ops += 1
        if tok is None:
            self.pending_reads[eng].extend(reads)
            self.pending_writes[eng].extend(writes)
        else:
            s, v, _ = tok
            rl = list(reads)
            wr = list(writes)
            if not dma:
                rl += self.pending_reads[eng]
                wr += self.pending_writes[eng]
                self.pending_reads[eng] = []
                self.pending_writes[eng] = []
            for r in rl:
                _merge(r.r, s, v)
            for w in wr:
                _merge(w.w, s, v)
        return tok

    def barrier(self):
        for e in ENGS:
            wl = []
            for s, c in enumerate(self.sem_count):
                if c > 0 and self.seen[e].get(s, 0) < c:
                    self.seen[e][s] = c
                    wl.append((s, c))
            if wl:
                self.q[e].append((None, wl, None))

    def flush(self, final=False):
        nc = self.nc
        sems = self.sems
        if final:
            wl = [(s, c) for s, c in enumerate(self.sem_count) if c > 0]
            self.q["sp"].append((None, wl, None))
        with nc.Block() as block:
            def run(eng_name):
                ops = self.q[eng_name]

                def f(e):
                    for fn, wl, tok in ops:
                        for s, v in wl:
                            e.wait_ge(sems[s], v)
                        if fn is None:
                            continue
                        ins = fn(e)
                        if tok is not None:
                            ins.then_inc(sems[tok[0]], tok[2])
                return f
            block.sync(run("sp"))
            block.scalar(run("act"))
            block.tensor(run("pe"))
            block.vector(run("dve"))
            block.gpsimd(run("pool"))
        self.q = {e: [] for e in ENGS}


class Ring:
    def __init__(self, items):
        self.items = items
        self.i = 0

    def next(self):
        it = self.items[self.i % len(self.items)]
        self.i += 1
        return it


C_G = {"mix_pre": 0, "mix_post": 8, "mlp_pre": 16, "mlp_post": 24, "ple_pre": 32, "ple_post": 40}
C_LNG, C_LNB, C_PSC, C_DW, C_SC, C_BF = 48, 50, 52, 54, 116, 122
K_ID, K_MASK, K_PCOEF, K_PRATIO = 0, 128, 256, 288


def build(T, L, stop=None, dbg=False):
    NT = T // 512
    NB = T // 128
    nc = bass.Bass("TRN2", target_bir_lowering=False)

    def din(name, shape, dt=F32):
        return nc.dram_tensor(name, shape, dt, kind="ExternalInput").ap()

    def dscr(name, shape, dt):
        return nc.dram_tensor(name, shape, dt, kind="ExternalOutput" if dbg else "Internal").ap()

    xT = din("xT", [1024, T])
    pT = din("pT", [L * 256, T])
    w_in = din("w_in", [L * 1024, 2308])
    w_pw = din("w_pw", [L * 256, 256])
    w_pbd = din("w_pbd", [L * 256, 128])
    w_out = din("w_out", [L * 1024, 1024])
    w_up = din("w_up", [L * 1024, 4096])
    w_dn = din("w_dn", [L * 1024, 4096])
    w_gate = din("w_gate", [L * 1024, 1024])
    w_proj = din("w_proj", [L * 256, 1024])
    small = din("small", [128, L * 128])
    consts = din("consts", [128, 320])
    outT = nc.dram_tensor("outT", [1024, T], F32, kind="ExternalOutput").ap()

    hT = dscr("hT", [1024, T], F32)
    wup_bf = dscr("wup_bf", [L * 1024, 4096], BF16)
    wdn_bf = dscr("wdn_bf", [L * 1024, 4096], BF16)
    mixin = dscr("mixin", [1024, 32 + T], BF16)
    qT = dscr("qT", [256, T], BF16)
    kT = dscr("kT", [256, T], BF16)
    c3d = dscr("c3d", [4, 3, T], BF16)
    vaug = dscr("vaug", [NB, 128, 260], BF16)
    ymix = dscr("ymix", [1024, T], BF16)

    R_h = [Res("h%d" % j) for j in range(NT)]
    R_mixin = [Res("mi%d" % j) for j in range(NT + 1)]
    R_q = [Res("q%d" % j) for j in range(NT)]
    R_k = [Res("k%d" % j) for j in range(NT)]
    R_c3 = [Res("c3%d" % j) for j in range(NT)]
    R_v = [Res("v%d" % j) for j in range(NT)]
    R_ymA = [Res("ymA%d" % j) for j in range(NT)]
    R_ymB = [Res("ymB%d" % j) for j in range(NT)]
    R_wup = [[Res("wu%d_%d" % (l, g)) for g in range(8)] for l in range(L)]
    R_wdn = [[Res("wd%d_%d" % (l, g)) for g in range(8)] for l in range(L)]

    top = contextlib.ExitStack()
    with top:
        P = Prog(nc, top)

        uid = [0]

        def sbt(st, name, shape, dt):
            uid[0] += 1
            return st.enter_context(nc.sbuf_tensor("%s_u%d" % (name, uid[0]), shape, dt))

        def pst(st, name, shape=(128, 512), dt=F32):
            uid[0] += 1
            return st.enter_context(nc.psum_tensor("%s_u%d" % (name, uid[0]), list(shape), dt))

        small_sb = sbt(top, "small_sb", [128, L * 128], F32)
        consts_sb = sbt(top, "consts_sb", [128, 320], F32)
        ident_bf = sbt(top, "ident_bf", [128, 128], BF16)
        mask_bf = sbt(top, "mask_bf", [128, 128], BF16)
        ones_bf = sbt(top, "ones_bf", [128, 128], BF16)
        ones_f = sbt(top, "ones_f", [128, 128], F32)
        cneg = sbt(top, "cneg", [128, NB, 4], F32)
        R_small, R_consts, R_cst2, R_cneg = Res("small"), Res("consts"), Res("cst2"), Res("cneg")
        ident_f = consts_sb[:, K_ID:K_ID + 128]

        def scol(l, c, rows=slice(0, 128)):
            return small_sb[rows, l * 128 + c: l * 128 + c + 1]

        def dma(eng, out, in_, reads, writes):
            P.op(eng, lambda e: e.dma_start(out=out, in_=in_), reads=reads, writes=writes, dma=True)

        def mm(out, lhsT, rhs, start, stop, reads, writes):
            P.op("pe", lambda e: e.matmul(out, lhsT=lhsT, rhs=rhs, start=start, stop=stop),
                 reads=reads, writes=writes, inc=stop)

        def act(out, in_, func, reads, writes, bias=None, scale=None):
            kw = {}
            if bias is not None:
                kw["bias"] = bias
            if scale is not None:
                kw["scale"] = scale
            P.op("act", lambda e: e.activation(out, in_, func, **kw), reads=reads, writes=writes)

        def tt(eng, out, in0, in1, op, reads, writes):
            P.op(eng, lambda e: e.tensor_tensor(out, in0, in1, op), reads=reads, writes=writes)

        def stt(eng, out, in0, scalar, in1, op0, op1, reads, writes):
            P.op(eng, lambda e: e.scalar_tensor_tensor(out, in0, scalar, in1, op0, op1), reads=reads, writes=writes)

        def norm_scale(eng, out, in0, gcol, rstd_ap, reads, writes, ptmp_ring):
            if eng == "dve":
                stt("dve", out, in0, gcol, rstd_ap, ALU.mult, ALU.mult, reads, writes)
            else:
                tb, Rtb = ptmp_ring.next()
                tt("pool", tb[:], in0, rstd_ap, ALU.mult, reads, [Rtb])
                P.op("pool", lambda e: e.tensor_scalar_mul(out, tb[:], gcol), reads=[Rtb] + list(reads), writes=writes)

        def ts(eng, out, in0, s1, s2, op0, op1, reads, writes):
            if s2 is None:
                P.op(eng, lambda e: e.tensor_single_scalar(out, in0, s1, op0), reads=reads, writes=writes)
            else:
                P.op(eng, lambda e: e.tensor_scalar(out, in0, s1, s2, op0, op1), reads=reads, writes=writes)

        def cp(eng, out, in_, reads, writes):
            if eng == "act":
                P.op("act", lambda e: e.copy(out, in_), reads=reads, writes=writes)
            else:
                P.op(eng, lambda e: e.tensor_copy(out, in_), reads=reads, writes=writes)

        def memset(eng, ap, val, writes):
            P.op(eng, lambda e: e.memset(ap, val), writes=writes)

        def recip(out, in_, reads, writes):
            P.op("dve", lambda e: e.reciprocal(out, in_), reads=reads, writes=writes)

        def rms_stats(sq, R_sq, ps_stat, R_ps, rs, rstd, R_rs, R_rstd):
            for c in range(8):
                mm(ps_stat[:], ones_bf[:], sq[:, c, :], c == 0, c == 7, list(R_sq) + [R_cst2], [R_ps])
            act(rs[:], ps_stat[:], AF.Sqrt, [R_ps], [R_rs], bias=EPS, scale=1.0 / 1024.0)
            recip(rstd[:], rs[:], [R_rs], [R_rstd])

        with contextlib.ExitStack() as st:
            zt = sbt(st, "zt", [128, 8, 32], BF16)
            R_zt = Res("zt")
            dma("sp", small_sb[:], small, [], [R_small])
            dma("sp", consts_sb[:], consts, [], [R_consts])
            cp("dve", ident_bf[:], consts_sb[:, K_ID:K_ID + 128], [R_consts], [R_cst2])
            cp("dve", mask_bf[:], consts_sb[:, K_MASK:K_MASK + 128], [R_consts], [R_cst2])
            memset("pool", ones_bf[:], 1.0, [R_cst2])
            memset("pool", ones_f[:], 1.0, [R_cst2])
            memset("pool", zt[:], 0.0, [R_zt])
            dma("sp", mixin.rearrange("(c p) n -> p c n", p=128)[:, :, 0:32], zt[:], [R_zt], [R_mixin[0]])
            P.barrier()
            P.flush(final=(stop == "pro"))
        if stop == "pro":
            return nc

        def cast_mlp_weights(l):
            for g in range(8):
                for (src, dst, RR) in ((w_up, wup_bf, R_wup), (w_dn, wdn_bf, R_wdn)):
                    r0 = l * 1024 + g * 128
                    for hh in range(2):
                        dma("pool", dst[r0:r0 + 128, hh * 2048:(hh + 1) * 2048],
                            src[r0:r0 + 128, hh * 2048:(hh + 1) * 2048], [], [RR[l][g]])

        for l in range(L):
            h_src = xT if l == 0 else hT
            h_dst = outT if l == L - 1 else hT
            h_src_v = h_src.rearrange("(c p) n -> p c n", p=128)
            h_dst_v = h_dst.rearrange("(c p) n -> p c n", p=128)

            with contextlib.ExitStack() as st:
                win = sbt(st, "win", [128, 8, 2308], BF16)
                R_win = Res("win")
                ht = [sbt(st, "ht%d" % i, [128, 8, 512], F32) for i in range(2)]
                R_ht = [Res("ht0"), Res("ht1")]
                sq = sbt(st, "sq", [128, 8, 512], BF16)
                R_sq = Res("sq")
                xn = sbt(st, "xn", [128, 8, 512], BF16)
                R_xn = Res("xn")
                rs = sbt(st, "rs", [128, 512], F32)
                rstd = sbt(st, "rstd", [128, 512], F32)
                R_rs, R_rstd = Res("rs"), Res("rstd")
                tmpf = [sbt(st, "tmpf%d" % i, [128, 512], F32) for i in range(2)]
                tmp_ring = Ring([(tmpf[i], Res("tmpf%d" % i)) for i in range(2)])
                ptmp_ring = Ring([(sbt(st, "ptmp%d" % i, [128, 512], F32), Res("ptmp%d" % i)) for i in range(2)])
                mstage = [sbt(st, "mstage%d" % i, [128, 8, 512], BF16) for i in range(2)]
                R_mst = [Res("mst0"), Res("mst1")]
                qstage = [sbt(st, "qstage%d" % i, [128, 2, 512], BF16) for i in range(2)]
                R_qst = [Res("qst0"), Res("qst1")]
                kstage = [sbt(st, "kstage%d" % i, [128, 2, 512], BF16) for i in range(2)]
                R_kst = [Res("kst0"), Res("kst1")]
                vstage = [sbt(st, "vstage%d" % i, [128, 4, 260], BF16) for i in range(2)]
                R_vst = [Res("vst0"), Res("vst1")]
                xb = sbt(st, "xb", [4, 512], F32)
                ef = sbt(st, "ef", [4, 512], F32)
                lf = sbt(st, "lf", [4, 512], F32)
                ones4 = sbt(st, "ones4", [4, 512], F32)
                cc = [sbt(st, "cc%d" % i, [4, 512], F32) for i in range(2)]
                r1 = sbt(st, "r1", [4, 512], F32)
                r2 = sbt(st, "r2", [4, 512], F32)
                c3 = [sbt(st, "c3_%d" % i, [4, 3, 512], BF16) for i in range(2)]
                R_f = Res("fmisc")
                R_cc = [Res("cc0"), Res("cc1")]
                R_c3s = [Res("c3s0"), Res("c3s1")]
                ps_stat = pst(st, "ps_stat")
                R_pstat = PR("ps_stat")
                ps_ring = Ring([(pst(st, "psr%d" % i), PR("psr%d" % i)) for i in range(4)])
                ps_v = Ring([(pst(st, "psv%d" % i), PR("psv%d" % i)) for i in range(2)])
                ps_t = pst(st, "ps_t")
                R_pst = PR("ps_t")

                for k in range(8):
                    for hh in range(2):
                        dma("pool", win[:, k, hh * 1154:(hh + 1) * 1154],
                            w_in[l * 1024 + k * 128: l * 1024 + (k + 1) * 128, hh * 1154:(hh + 1) * 1154],
                            [], [R_win])
                cast_mlp_weights(l)
                memset("pool", ones4[:], 1.0, [R_f])
                for i in range(2):
                    memset("pool", vstage[i][:], 1.0, [R_vst[i]])

                def load_h(j):
                    dma("sp", ht[j % 2][:], h_src_v[:, :, j * 512:(j + 1) * 512], [R_h[j]], [R_ht[j % 2]])

                def proj(ps, R_ps, col0, M):
                    for k in range(8):
                        mm(ps[0:M, :], win[:, k, col0:col0 + M], xn[:, k, :], k == 0, k == 7, [R_win, R_xn], [R_ps])

                load_h(0)
                for j in range(NT):
                    b = j % 2
                    if j + 1 < NT:
                        load_h(j + 1)
                    h = ht[b]
                    cols = slice(j * 512, (j + 1) * 512)
                    act(sq[:], h[:], AF.Square, [R_ht[b]], [R_sq])
                    rms_stats(sq, [R_sq], ps_stat, R_pstat, rs, rstd, R_rs, R_rstd)
                    for c in range(8):
                        norm_scale("dve" if c % 2 == 0 else "pool", xn[:, c, :], h[:, c, :], scol(l, C_G["mix_pre"] + c),
                                   rstd[:], [R_ht[b], R_rstd, R_small], [R_xn], ptmp_ring)
                    for blk in range(2):
                        pa, Rpa = ps_ring.next()
                        pb, Rpb = ps_ring.next()
                        proj(pa, Rpa, (0 + blk) * 128, 128)
                        proj(pb, Rpb, (2 + blk) * 128, 128)
                        tb, Rtb = tmp_ring.next()
                        act(tb[:], pb[:], AF.Sigmoid, [Rpb], [Rtb])
                        tt("dve", mstage[b][:, 0 + blk, :], pa[:], tb[:], ALU.mult, [Rpa, Rtb], [R_mst[b]])
                    for blk in range(2):
                        pa, Rpa = ps_ring.next()
                        pb, Rpb = ps_ring.next()
                        proj(pa, Rpa, (8 + blk) * 128, 128)
                        proj(pb, Rpb, (12 + blk) * 128, 128)
                        tb, Rtb = tmp_ring.next()
                        cp("act", tb[:], pb[:], [Rpb], [Rtb])
                        tt("dve", mstage[b][:, 2 + blk, :], pa[:], tb[:], ALU.mult, [Rpa, Rtb], [R_mst[b]])
                    for blk in range(2):
                        pa, Rpa = ps_ring.next()
                        proj(pa, Rpa, (14 + blk) * 128, 128)
                        cp("dve", mstage[b][:, 4 + blk, :], pa[:], [Rpa], [R_mst[b]])
                        pb, Rpb = ps_ring.next()
                        proj(pb, Rpb, (10 + blk) * 128, 128)
                        cp("act", mstage[b][:, 6 + blk, :], pb[:], [Rpb], [R_mst[b]])
                    for blk in range(2):
                        pa, Rpa = ps_ring.next()
                        proj(pa, Rpa, (4 + blk) * 128, 128)
                        act(qstage[b][:, blk, :], pa[:], AF.Copy, [Rpa], [R_qst[b]], scale=0.125)
                        pb, Rpb = ps_ring.next()
                        proj(pb, Rpb, (6 + blk) * 128, 128)
                        cp("dve", kstage[b][:, blk, :], pb[:], [Rpb], [R_kst[b]])
                    pf, Rpf = ps_ring.next()
                    proj(pf, Rpf, 2048, 4)
                    ts("dve", xb[:], pf[0:4, :], scol(l, C_BF, slice(0, 4)), None, ALU.add, None, [Rpf, R_small], [R_f])
                    act(ef[:], xb[:], AF.Exp, [R_f], [R_f], scale=-1.0)
                    act(lf[:], ef[:], AF.Ln, [R_f], [R_f], bias=1.0, scale=1.0)
                    init = 0.0 if j == 0 else cc[1 - b][:, 511:512]
                    P.op("dve", (lambda o, d0, d1, ini: (lambda e: e.tensor_tensor_scan(o, d0, d1, ini, ALU.mult, ALU.subtract)))(
                        cc[b][:], ones4[:], lf[:], init), reads=[R_f, R_cc[1 - b]], writes=[R_cc[b]])
                    cp("dve", c3[b][:, 0, :], cc[b][:], [R_cc[b]], [R_c3s[b]])
                    tt("dve", r1[:], cc[b][:], c3[b][:, 0, :], ALU.subtract, [R_cc[b], R_c3s[b]], [R_f])
                    cp("dve", c3[b][:, 1, :], r1[:], [R_f], [R_c3s[b]])
                    tt("dve", r2[:], r1[:], c3[b][:, 1, :], ALU.subtract, [R_f, R_c3s[b]], [R_f])
                    cp("dve", c3[b][:, 2, :], r2[:], [R_f], [R_c3s[b]])
                    for s in range(4):
                        P.op("pe", (lambda o, a, bb: (lambda e: e.matmul(o, lhsT=a, rhs=bb, start=True, stop=True)))(
                            ps_t[:, s * 4:s * 4 + 4], cc[b][0:4, s * 128:(s + 1) * 128], consts_sb[0:4, K_ID:K_ID + 4]),
                            reads=[R_cc[b], R_consts], writes=[R_pst], inc=True)
                    ts("dve", cneg[:, j * 4:(j + 1) * 4, :], ps_t[:, 0:16].rearrange("p (s h) -> p s h", s=4), -1.0, None,
                       ALU.mult, None, [R_pst], [R_cneg])
                    for s in range(4):
                        pv, Rpv = ps_v.next()
                        for k in range(8):
                            mm(pv[:, 0:256], xn[:, k, s * 128:(s + 1) * 128], win[:, k, 2052:2308], k == 0, k == 7,
                               [R_win, R_xn], [Rpv])
                        dst = vstage[b][:, s, :].rearrange("p (h c) -> p h c", h=4)[:, :, 0:64]
                        src = pv[:, 0:256].rearrange("p (h c) -> p h c", h=4)
                        cp("dve" if s % 2 == 0 else "act", dst, src, [Rpv], [R_vst[b]])
                    dma("sp", mixin.rearrange("(c p) n -> p c n", p=128)[:, :, 32 + j * 512: 32 + (j + 1) * 512],
                        mstage[b][:], [R_mst[b]], [R_mixin[j + 1]])
                    dma("sp", qT.rearrange("(c p) n -> p c n", p=128)[:, :, cols], qstage[b][:], [R_qst[b]], [R_q[j]])
                    dma("sp", kT.rearrange("(c p) n -> p c n", p=128)[:, :, cols], kstage[b][:], [R_kst[b]], [R_k[j]])
                    dma("sp", c3d[:, :, cols], c3[b][:], [R_c3s[b]], [R_c3[j]])
                    dma("sp", vaug[j * 4:(j + 1) * 4].rearrange("s p c -> p s c"), vstage[b][:], [R_vst[b]], [R_v[j]])
                P.barrier()
                P.flush(final=(stop == "A1"))
            if stop == "A1":
                return nc

            with contextlib.ExitStack() as st:
                dconf = sbt(st, "dconf", [128, 2, 31, 128], BF16)
                dsc = sbt(st, "dsc", [128, 2, 3, 128], BF16)
                dpl = sbt(st, "dpl", [128, 2, 16, 128], BF16)
                R_dg = Res("diag")
                wpw = sbt(st, "wpw", [128, 2, 256], BF16)
                wpbd = sbt(st, "wpbd", [128, 2, 128], BF16)
                R_wA2 = Res("wA2")
                mt = [sbt(st, "mt%d" % i, [128, 8, 544], BF16) for i in range(2)]
                R_mt = [Res("mt0"), Res("mt1")]
                xc = sbt(st, "xc", [128, 2, 512], F32)
                sq2 = sbt(st, "sq2", [128, 2, 512], F32)
                R_xc, R_sq2 = Res("xc"), Res("sq2")
                mean = sbt(st, "mean", [128, 512], F32)
                msq = sbt(st, "msq", [128, 512], F32)
                var = sbt(st, "var", [128, 512], F32)
                sd = sbt(st, "sd", [128, 512], F32)
                rstd2 = sbt(st, "rstd2", [128, 512], F32)
                R_mean, R_msq, R_var, R_sd, R_rstd2 = Res("mean"), Res("msq"), Res("var"), Res("sd"), Res("rstd2")
                xm = [sbt(st, "xm%d" % i, [128, 512], F32) for i in range(2)]
                R_xm = [Res("xm0"), Res("xm1")]
                xnn = [sbt(st, "xnn%d" % i, [128, 512], F32) for i in range(2)]
                R_xnn = [Res("xnn0"), Res("xnn1")]
                sconf = sbt(st, "sconf", [128, 2, 512], BF16)
                R_sconf = Res("sconf")
                dpool = sbt(st, "dpool", [128, 2, 512], BF16)
                R_dpool = Res("dpool")
                t16 = sbt(st, "t16", [128, 2, 16], F32)
                R_t16 = Res("t16")
                ystage = [sbt(st, "ystage%d" % i, [128, 6, 512], BF16) for i in range(2)]
                R_yst = [Res("yst0"), Res("yst1")]
                psc = [pst(st, "psc%d" % i) for i in range(2)]
                R_psc = [PR("psc0"), PR("psc1")]
                ps_s1 = pst(st, "ps_s1")
                ps_s2 = pst(st, "ps_s2")
                R_s1, R_s2 = PR("s1"), PR("s2")
                ps_ring = Ring([(pst(st, "psq%d" % i), PR("psq%d" % i)) for i in range(3)])

                dma("pool", wpw[:], w_pw[l * 256:(l + 1) * 256, :].rearrange("(c p) n -> p c n", p=128), [], [R_wA2])
                dma("pool", wpbd[:], w_pbd[l * 256:(l + 1) * 256, :].rearrange("(c p) n -> p c n", p=128), [], [R_wA2])
                for blk in range(2):
                    for k in range(31):
                        P.op("pool", (lambda o, s: (lambda e: e.tensor_scalar_mul(o, ident_f, s)))(
                            dconf[:, blk, k, :], scol(l, C_DW + blk * 31 + k)), reads=[R_consts, R_small], writes=[R_dg])
                    for k in range(3):
                        P.op("pool", (lambda o, s: (lambda e: e.tensor_scalar_mul(o, ident_f, s)))(
                            dsc[:, blk, k, :], scol(l, C_SC + blk * 3 + k)), reads=[R_consts, R_small], writes=[R_dg])
                    for k in range(16):
                        P.op("pool", (lambda o, s: (lambda e: e.tensor_scalar_mul(o, ident_f, s)))(
                            dpl[:, blk, k, :], consts_sb[:, K_PCOEF + blk * 16 + k: K_PCOEF + blk * 16 + k + 1]),
                            reads=[R_consts], writes=[R_dg])

                def load_mt(j):
                    dma("sp", mt[j % 2][:], mixin.rearrange("(c p) n -> p c n", p=128)[:, :, j * 512: j * 512 + 544],
                        [R_mixin[j], R_mixin[j + 1]], [R_mt[j % 2]])

                load_mt(0)
                sect = stop.split(":")[1] if (stop and ":" in stop) else "all"

                def on(*names):
                    return sect == "all" or sect in names

                for j in range(NT):
                    b = j % 2
                    if j + 1 < NT:
                        load_mt(j + 1)
                    m_ = mt[b]
                    cols = slice(j * 512, (j + 1) * 512)
                    if sect == "conv1":
                        for blk in range(2):
                            for k in range(31):
                                mm(psc[blk][:], dconf[:, blk, k, :], m_[:, blk, 2 + k: 2 + k + 512], k == 0, k == 30,
                                   [R_dg, R_mt[b]], [R_psc[blk]])
                            cp("dve", xc[:, blk, :], psc[blk][:], [R_psc[blk]], [R_xc])
                    if sect == "conv2":
                        for blk in range(2):
                            for k in range(3):
                                mm(psc[blk][:], dconf[:, blk, k, :], m_[:, blk, 2 + k: 2 + k + 512], k == 0, k == 2,
                                   [R_dg, R_mt[b]], [R_psc[blk]])
                            act(sq2[:, blk, :], psc[blk][:], AF.Square, [R_psc[blk]], [R_sq2])
                    if sect == "pool1":
                        for blk in range(2):
                            pp, Rpp = ps_ring.next()
                            for k in range(16):
                                mm(pp[:], dpl[:, blk, k, :], m_[:, 4 + blk, 32 - k: 32 - k + 512], k == 0, k == 15,
                                   [R_dg, R_mt[b]], [Rpp])
                            cp("dve", dpool[:, blk, :], pp[:], [Rpp], [R_dpool])
                    if on("conv", "ln", "silu", "conf"):
                        for blk in range(2):
                            for k in range(31):
                                mm(psc[blk][:], dconf[:, blk, k, :], m_[:, blk, 2 + k: 2 + k + 512], k == 0, k == 30,
                                   [R_dg, R_mt[b]], [R_psc[blk]])
                            cp("dve", xc[:, blk, :], psc[blk][:], [R_psc[blk]], [R_xc])
                            act(sq2[:, blk, :], xc[:, blk, :], AF.Square, [R_xc], [R_sq2])
                    if on("ln", "silu", "conf"):
                        for blk in range(2):
                            mm(ps_s1[:], ones_f[:], xc[:, blk, :], blk == 0, blk == 1, [R_xc, R_cst2], [R_s1])
                        for blk in range(2):
                            mm(ps_s2[:], ones_f[:], sq2[:, blk, :], blk == 0, blk == 1, [R_sq2, R_cst2], [R_s2])
                        act(mean[:], ps_s1[:], AF.Copy, [R_s1], [R_mean], scale=1.0 / 256.0)
                        tt("pool", msq[:], mean[:], mean[:], ALU.mult, [R_mean], [R_msq])
                        stt("dve", var[:], ps_s2[:], 1.0 / 256.0, msq[:], ALU.mult, ALU.subtract, [R_s2, R_msq], [R_var])
                        act(sd[:], var[:], AF.Sqrt, [R_var], [R_sd], bias=EPS, scale=1.0)
                        recip(rstd2[:], sd[:], [R_sd], [R_rstd2])
                        for blk in range(2):
                            tt("dve", xm[blk][:], xc[:, blk, :], mean[:], ALU.subtract, [R_xc, R_mean], [R_xm[blk]])
                            tt("pool", xnn[blk][:], xm[blk][:], rstd2[:], ALU.mult, [R_xm[blk], R_rstd2], [R_xnn[blk]])
                    if on("silu", "conf"):
                        for blk in range(2):
                            act(sconf[:, blk, :], xnn[blk][:], AF.Silu, [R_xnn[blk], R_small], [R_sconf],
                                bias=scol(l, C_LNB + blk), scale=scol(l, C_LNG + blk))
                    if on("conf"):
                        for oc in range(2):
                            pp, Rpp = ps_ring.next()
                            for kb in range(2):
                                mm(pp[:], wpw[:, kb, oc * 128:(oc + 1) * 128], sconf[:, kb, :], kb == 0, kb == 1,
                                   [R_wA2, R_sconf], [Rpp])
                            cp("dve", ystage[b][:, oc, :], pp[:], [Rpp], [R_yst[b]])
                    if on("sc"):
                        for blk in range(2):
                            pp, Rpp = ps_ring.next()
                            for k in range(3):
                                mm(pp[:], dsc[:, blk, k, :], m_[:, 2 + blk, 30 + k: 30 + k + 512], k == 0, k == 2,
                                   [R_dg, R_mt[b]], [Rpp])
                            tt("dve", ystage[b][:, 2 + blk, :], pp[:], m_[:, 6 + blk, 32:544], ALU.mult, [Rpp, R_mt[b]], [R_yst[b]])
                    if on("pool"):
                        for blk in range(2):
                            pp, Rpp = ps_ring.next()
                            for k in range(16):
                                mm(pp[:], dpl[:, blk, k, :], m_[:, 4 + blk, 32 - k: 32 - k + 512], k == 0, k == 15,
                                   [R_dg, R_mt[b]], [Rpp])
                            cp("act", dpool[:, blk, :], pp[:], [Rpp], [R_dpool])
                            if j == 0:
                                tt("dve", t16[:, blk, :], pp[:, 0:16], m_[:, 4 + blk, 32:48], ALU.add, [Rpp, R_mt[b]], [R_t16])
                                tt("pool", t16[:, blk, :], t16[:, blk, :],
                                   consts_sb[:, K_PRATIO + blk * 16: K_PRATIO + blk * 16 + 16], ALU.mult, [R_t16, R_consts], [R_t16])
                                tt("dve", dpool[:, blk, 0:16], t16[:, blk, :], m_[:, 4 + blk, 32:48], ALU.subtract,
                                   [R_t16, R_mt[b]], [R_dpool])
                        for blk in range(2):
                            pp, Rpp = ps_ring.next()
                            mm(pp[:], wpbd[:, blk, :], dpool[:, blk, :], True, True, [R_wA2, R_dpool], [Rpp])
                            act(ystage[b][:, 4 + blk, :], pp[:], AF.Copy, [Rpp, R_small], [R_yst[b]], scale=scol(l, C_PSC + blk))
                    ym_v = ymix.rearrange("(c p) n -> p c n", p=128)
                    dma("sp", ym_v[:, 0:2, cols], ystage[b][:, 0:2, :], [R_yst[b]], [R_ymA[j]])
                    dma("sp", ym_v[:, 4:8, cols], ystage[b][:, 2:6, :], [R_yst[b]], [R_ymA[j]])
                P.barrier()
                P.flush(final=(stop is not None and stop.startswith("A2")))
            if stop is not None and stop.startswith("A2"):
                return nc

            with contextlib.ExitStack() as st:
                kaug = sbt(st, "kaug", [128, 4, T], BF16)
                R_kaug = Res("kaug")
                vsb = sbt(st, "vsb", [128, NB, 260], BF16)
                R_vsb = [Res("vsb%d" % j) for j in range(NT)]
                qa = [sbt(st, "qa%d" % i, [128, 4, 512], BF16) for i in range(2)]
                R_qa = [Res("qa0"), Res("qa1")]
                pts = Ring([(sbt(st, "pt%d" % i, [128, 512], BF16), Res("pt%d" % i)) for i in range(4)])
                rec = sbt(st, "rec", [128, 512], F32)
                R_rec = Res("rec")
                bcs = sbt(st, "bcs", [64, 512], F32)
                R_bcs = Res("bcs")
                yst = Ring([(sbt(st, "ysb%d" % i, [64, 512], BF16), Res("ysb%d" % i)) for i in range(2)])
                ps_s = Ring([(pst(st, "pss%d" % i), PR("pss%d" % i)) for i in range(3)])
                ps_o = Ring([(pst(st, "pso%d" % i), PR("pso%d" % i)) for i in range(2)])
                ps_bc = pst(st, "ps_bc")
                R_bc = PR("ps_bc")

                memset("pool", kaug[64:96, :, :], 0.0, [R_kaug])
                memset("pool", kaug[64:67, :, :], 1.0, [R_kaug])
                for i in range(2):
                    memset("pool", qa[i][64:96, :, :], 0.0, [R_qa[i]])
                for j in range(NT):
                    cols = slice(j * 512, (j + 1) * 512)
                    dma("sp", kaug[0:64, :, cols], kT.rearrange("(h r) n -> r h n", r=64)[:, :, cols], [R_k[j]], [R_kaug])
                    dma("sp", vsb[:, j * 4:(j + 1) * 4, :], vaug[j * 4:(j + 1) * 4].rearrange("s p c -> p s c"),
                        [R_v[j]], [R_vsb[j]])

                def load_q(j):
                    cols = slice(j * 512, (j + 1) * 512)
                    dma("sp", qa[j % 2][0:64, :, :], qT.rearrange("(h r) n -> r h n", r=64)[:, :, cols], [R_q[j]], [R_qa[j % 2]])
                    dma("sp", qa[j % 2][64:67, :, :], c3d.rearrange("h j n -> j h n")[:, :, cols], [R_c3[j]], [R_qa[j % 2]])

                load_q(0)
                for j in range(NT):
                    b = j % 2
                    if j + 1 < NT:
                        load_q(j + 1)
                    cols = slice(j * 512, (j + 1) * 512)
                    for h in range(4):
                        nblk = 4 * j + 4
                        po, Rpo = ps_o.next()

                        def qk(i):
                            dj = i - 4 * j
                            c0 = 128 * dj if dj >= 0 else 0
                            pS, RpS = ps_s.next()
                            mm(pS[:, c0:512], kaug[0:96, h, i * 128:(i + 1) * 128], qa[b][0:96, h, c0:512],
                               True, dj < 0, [R_kaug, R_qa[b]], [RpS])
                            if dj >= 0:
                                mm(pS[:, c0:c0 + 128], ident_bf[:], mask_bf[:], False, True, [R_cst2], [RpS])
                            return pS, RpS, c0

                        nxt = qk(0)
                        for i in range(nblk):
                            pS, RpS, c0 = nxt
                            if i + 1 < nblk:
                                nxt = qk(i + 1)
                            pt, Rpt = pts.next()
                            act(pt[:, c0:512], pS[:, c0:512], AF.Exp, [RpS, R_cneg], [Rpt], bias=cneg[:, i, h:h + 1], scale=1.0)
                            mm(po[0:65, c0:512], vsb[:, i, h * 65:(h + 1) * 65], pt[:, c0:512], i == 0, i == nblk - 1,
                               [R_vsb[i // 4], Rpt], [Rpo])
                        recip(rec[64:65, :], po[64:65, :], [Rpo], [R_rec])
                        mm(ps_bc[0:64, :], ones_f[64:65, 0:64], rec[64:65, :], True, True, [R_rec, R_cst2], [R_bc])
                        cp("act", bcs[:], ps_bc[0:64, :], [R_bc], [R_bcs])
                        ys, Rys = yst.next()
                        tt("dve", ys[:], po[0:64, :], bcs[:], ALU.mult, [Rpo, R_bcs], [Rys])
                        dma("sp", ymix[256 + 64 * h: 256 + 64 * h + 64, cols], ys[:], [Rys], [R_ymB[j]])
                P.barrier()
                P.flush(final=(stop == "B"))
            if stop == "B":
                return nc

            with contextlib.ExitStack() as st:
                wout = sbt(st, "wout", [128, 8, 1024], BF16)
                wgate = sbt(st, "wgate", [128, 8, 1024], BF16)
                wproj = sbt(st, "wproj", [128, 2, 1024], BF16)
                R_wC = Res("wC")
                wu = [sbt(st, "wu%d" % i, [128, 8, 512], BF16) for i in range(3)]
                wd = [sbt(st, "wd%d" % i, [128, 4, 1024], BF16) for i in range(3)]
                R_wu = [Res("wus%d" % i) for i in range(3)]
                R_wd = [Res("wds%d" % i) for i in range(3)]
                ymt = sbt(st, "ymt", [128, 8, 512], BF16)
                R_ymt = Res("ymt")
                ht = [sbt(st, "hc%d" % i, [128, 8, 512], F32) for i in range(2)]
                R_ht = [Res("hc0"), Res("hc1")]
                ptl = [sbt(st, "ptl%d" % i, [128, 2, 512], BF16) for i in range(2)]
                R_ptl = [Res("ptl0"), Res("ptl1")]
                m = sbt(st, "m", [128, 8, 512], F32)
                R_m = Res("m")
                sqag = sbt(st, "sqag", [128, 8, 512], BF16)
                R_ag = [Res("ag0"), Res("ag1")]
                hn = sbt(st, "hn", [128, 8, 512], BF16)
                R_hn = Res("hn")
                gate = sbt(st, "gate", [128, 8, 512], BF16)
                R_gate = Res("gate")
                rs = sbt(st, "rs_c", [128, 512], F32)
                rstd = sbt(st, "rstd_c", [128, 512], F32)
                R_rs, R_rstd = Res("rs"), Res("rstd")
                tmpf = Ring([(sbt(st, "tmpc%d" % i, [128, 512], F32), Res("tmpc%d" % i)) for i in range(2)])
                rl = Ring([(sbt(st, "rl%d" % i, [128, 512], BF16), Res("rl%d" % i)) for i in range(2)])
                ptmp_ring = Ring([(sbt(st, "ptmpc%d" % i, [128, 512], F32), Res("ptmpc%d" % i)) for i in range(2)])
                ps_stat = pst(st, "ps_stat_c")
                R_pstat = PR("ps_stat_c")
                ps_ring = Ring([(pst(st, "psm%d" % i), PR("psm%d" % i)) for i in range(4)])
                ps_dn = Ring([(pst(st, "psd%d" % i), PR("psd%d" % i)) for i in range(2)])

                for k in range(8):
                    dma("pool", wout[:, k, :], w_out[l * 1024 + k * 128: l * 1024 + (k + 1) * 128, :], [], [R_wC])
                    dma("pool", wgate[:, k, :], w_gate[l * 1024 + k * 128: l * 1024 + (k + 1) * 128, :], [], [R_wC])
                for k in range(2):
                    dma("pool", wproj[:, k, :], w_proj[l * 256 + k * 128: l * 256 + (k + 1) * 128, :], [], [R_wC])

                def load_tile(j):
                    cols = slice(j * 512, (j + 1) * 512)
                    dma("sp", ht[j % 2][:], h_src_v[:, :, cols], [R_h[j]], [R_ht[j % 2]])
                    dma("pool", ptl[j % 2][:], pT[l * 256:(l + 1) * 256, :].rearrange("(c p) n -> p c n", p=128)[:, :, cols],
                        [], [R_ptl[j % 2]])

                def load_ym(j):
                    cols = slice(j * 512, (j + 1) * 512)
                    dma("sp", ymt[:], ymix.rearrange("(c p) n -> p c n", p=128)[:, :, cols], [R_ymA[j], R_ymB[j]], [R_ymt])

                gran_loaded = [0]

                def load_gran(n):
                    if n >= NT * 8:
                        return
                    g = n % 8
                    s = n % 3
                    r0 = l * 1024 + g * 128
                    dma("sp", wu[s][:], wup_bf[r0:r0 + 128, :].rearrange("p (k c) -> p k c", k=8), [R_wup[l][g]], [R_wu[s]])
                    dma("sp", wd[s][:], wdn_bf[r0:r0 + 128, :].rearrange("p (k c) -> p k c", k=4), [R_wdn[l][g]], [R_wd[s]])

                def post_norm_residual(h, R_h_, gname):
                    act(sqag[:], m[:], AF.Square, [R_m], R_ag)
                    rms_stats(sqag, R_ag, ps_stat, R_pstat, rs, rstd, R_rs, R_rstd)
                    for c in range(8):
                        tb, Rtb = tmpf.next()
                        stt("dve", tb[:], m[:, c, :], scol(l, C_G[gname] + c), rstd[:], ALU.mult, ALU.mult,
                            [R_m, R_rstd, R_small], [Rtb])
                        tt("pool", h[:, c, :], h[:, c, :], tb[:], ALU.add, [R_h_, Rtb], [R_h_])

                def pre_norm(h, R_h_, gname):
                    act(sqag[:], h[:], AF.Square, [R_h_], R_ag)
                    rms_stats(sqag, R_ag, ps_stat, R_pstat, rs, rstd, R_rs, R_rstd)
                    for c in range(8):
                        norm_scale("dve" if c % 2 == 0 else "pool", hn[:, c, :], h[:, c, :], scol(l, C_G[gname] + c), rstd[:],
                                   [R_h_, R_rstd, R_small], [R_hn], ptmp_ring)

                load_tile(0)
                load_ym(0)
                load_gran(0)
                load_gran(1)
                for j in range(NT):
                    b = j % 2
                    cols = slice(j * 512, (j + 1) * 512)
                    if j + 1 < NT:
                        load_tile(j + 1)
                    h = ht[b]
                    Rh = R_ht[b]
                    for oc in range(8):
                        pp, Rpp = ps_ring.next()
                        for k in range(8):
                            mm(pp[:], wout[:, k, oc * 128:(oc + 1) * 128], ymt[:, k, :], k == 0, k == 7, [R_wC, R_ymt], [Rpp])
                        cp("dve" if oc % 2 == 0 else "act", m[:, oc, :], pp[:], [Rpp], [R_m])
                    if j + 1 < NT:
                        load_ym(j + 1)
                    post_norm_residual(h, Rh, "mix_post")
                    pre_norm(h, Rh, "mlp_pre")

                    def up(g):
                        n = j * 8 + g
                        s = n % 3
                        for c4 in range(4):
                            pp, Rpp = ps_ring.next()
                            for k in range(8):
                                mm(pp[:], wu[s][:, k, c4 * 128:(c4 + 1) * 128], hn[:, k, :], k == 0, k == 7, [R_wu[s], R_hn], [Rpp])
                            rr, Rrr = rl.next()
                            act(rr[:], pp[:], AF.Relu, [Rpp], [Rrr])
                            tt("pool", sqag[:, (g % 2) * 4 + c4, :], rr[:], rr[:], ALU.mult, [Rrr], [R_ag[g % 2]])

                    def down(g):
                        n = j * 8 + g
                        s = n % 3
                        for oc in range(8):
                            pd, Rpd = ps_dn.next()
                            for k4 in range(4):
                                mm(pd[:], wd[s][:, k4, oc * 128:(oc + 1) * 128], sqag[:, (g % 2) * 4 + k4, :], k4 == 0, k4 == 3,
                                   [R_wd[s], R_ag[g % 2]], [Rpd])
                            if g == 0:
                                cp("dve", m[:, oc, :], pd[:], [Rpd], [R_m])
                            else:
                                tt("dve", m[:, oc, :], pd[:], m[:, oc, :], ALU.add, [Rpd, R_m], [R_m])

                    for g in range(9):
                        if g < 8:
                            up(g)
                        if g > 0:
                            down(g - 1)
                        if g < 8:
                            load_gran(j * 8 + g + 2)
                    post_norm_residual(h, Rh, "mlp_post")
                    pre_norm(h, Rh, "ple_pre")
                    for oc in range(8):
                        pp, Rpp = ps_ring.next()
                        for k in range(8):
                            mm(pp[:], wgate[:, k, oc * 128:(oc + 1) * 128], hn[:, k, :], k == 0, k == 7, [R_wC, R_hn], [Rpp])
                        act(gate[:, oc, :], pp[:], AF.Sigmoid, [Rpp], [R_gate])
                    for oc in range(8):
                        pp, Rpp = ps_ring.next()
                        for k in range(2):
                            mm(pp[:], wproj[:, k, oc * 128:(oc + 1) * 128], ptl[b][:, k, :], k == 0, k == 1, [R_wC, R_ptl[b]], [Rpp])
                        tt("dve", m[:, oc, :], pp[:], gate[:, oc, :], ALU.mult, [Rpp, R_gate], [R_m])
                    post_norm_residual(h, Rh, "ple_post")
                    dma("sp", h_dst_v[:, :, cols], h[:], [Rh], [R_h[j]])
                P.barrier()
                P.flush(final=(l == L - 1))
    return nc


POOL_WINDOWS = (2, 4, 8, 16)


def _chunkcols(v):
    return np.ascontiguousarray(v.reshape(-1, 128).T)


def prep_shared(inp, L):
    f32 = np.float32
    w_in = np.asarray(inp["w_in"], f32)[:L]
    perm = np.concatenate([np.arange(0, 1024), np.arange(1284, 2308), np.arange(1280, 1284), np.arange(1024, 1280)])
    w_in_r = np.ascontiguousarray(w_in[:, :, perm]).reshape(L * 1024, 2308)
    w_pw = np.ascontiguousarray(np.asarray(inp["w_conf_pw"], f32)[:L]).reshape(L * 256, 256)
    wp = np.asarray(inp["w_pool"], f32)[:L]
    w_pbd = np.zeros((L, 2, 128, 128), f32)
    for blk in range(2):
        for gg in range(2):
            w_pbd[:, blk, gg * 64:(gg + 1) * 64, gg * 64:(gg + 1) * 64] = wp[:, blk * 2 + gg]
    w_pbd = w_pbd.reshape(L * 256, 128)
    w_out = np.ascontiguousarray(np.asarray(inp["w_out"], f32)[:L]).reshape(L * 1024, 1024)
    wu = np.asarray(inp["w_up"], f32)[:L]
    wu = wu.reshape(L, 8, 128, 8, 512).transpose(0, 3, 2, 1, 4)
    w_up = np.ascontiguousarray(wu).reshape(L * 1024, 4096)
    wdn = np.asarray(inp["w_down"], f32)[:L]
    wdn = wdn.reshape(L, 8, 4, 128, 1024).transpose(0, 1, 3, 2, 4)
    w_dn = np.ascontiguousarray(wdn).reshape(L * 1024, 4096)
    w_gate = np.ascontiguousarray(np.asarray(inp["w_ple_gate"], f32)[:L]).reshape(L * 1024, 1024)
    w_proj = np.ascontiguousarray(np.asarray(inp["w_ple_proj"], f32)[:L]).reshape(L * 256, 1024)
    small = np.zeros((128, L * 128), f32)
    for l in range(L):
        o = l * 128
        for name, key in (("mix_pre", "g_mix_pre"), ("mix_post", "g_mix_post"), ("mlp_pre", "g_mlp_pre"),
                          ("mlp_post", "g_mlp_post"), ("ple_pre", "g_ple_pre"), ("ple_post", "g_ple_post")):
            small[:, o + C_G[name]: o + C_G[name] + 8] = _chunkcols(np.asarray(inp[key], f32)[l])
        small[:, o + C_LNG: o + C_LNG + 2] = _chunkcols(np.asarray(inp["conf_ln_g"], f32)[l])
        small[:, o + C_LNB: o + C_LNB + 2] = _chunkcols(np.asarray(inp["conf_ln_b"], f32)[l])
        small[:, o + C_PSC: o + C_PSC + 2] = _chunkcols(np.asarray(inp["pool_scale"], f32)[l])
        dw = np.asarray(inp["w_conf_dw"], f32)[l]
        for blk in range(2):
            small[:, o + C_DW + blk * 31: o + C_DW + blk * 31 + 31] = dw[:, blk * 128:(blk + 1) * 128].T
        sc = np.asarray(inp["w_sc"], f32)[l]
        for blk in range(2):
            small[:, o + C_SC + blk * 3: o + C_SC + blk * 3 + 3] = sc[:, blk * 128:(blk + 1) * 128].T
        small[0:4, o + C_BF] = np.asarray(inp["b_forget"], f32)[l]
    consts = np.zeros((128, 320), f32)
    consts[:, K_ID:K_ID + 128] = np.eye(128, dtype=f32)
    kk, qq = np.meshgrid(np.arange(128), np.arange(128), indexing="ij")
    consts[:, K_MASK:K_MASK + 128] = np.where(kk > qq, NEG, 0.0)
    for blk in range(2):
        for p in range(128):
            w = POOL_WINDOWS[(blk * 128 + p) // 64]
            for t in range(16):
                consts[p, K_PCOEF + blk * 16 + t] = (1.0 / w if t < w else 0.0) - (1.0 if t == 0 else 0.0)
                consts[p, K_PRATIO + blk * 16 + t] = w / min(t + 1, w)
    return dict(w_in=w_in_r, w_pw=w_pw, w_pbd=w_pbd, w_out=w_out, w_up=w_up, w_dn=w_dn, w_gate=w_gate,
                w_proj=w_proj, small=small, consts=consts)


_NC_CACHE = {}


def run(inp, T, L, n_seq, stop=None, dbg=False):
    f32 = np.float32
    shared = prep_shared(inp, L)
    x = np.asarray(inp["x"], f32)
    p = np.asarray(inp["p"], f32)
    in_maps = []
    for core in range(8):
        bi = core % n_seq
        m = dict(shared)
        m["xT"] = np.ascontiguousarray(x[bi].T)
        m["pT"] = np.ascontiguousarray(p[:L, bi].transpose(0, 2, 1)).reshape(L * 256, T)
        in_maps.append(m)
    key = (T, L, stop, dbg)
    if key not in _NC_CACHE:
        _NC_CACHE[key] = build(T, L, stop, dbg)
    nc = _NC_CACHE[key]
    res = run_bass_kernel_spmd(nc, in_maps, core_ids=list(range(8)))
    if dbg:
        return res.results[0]
    out = np.stack([np.ascontiguousarray(res.results[bi]["outT"].T) for bi in range(n_seq)], axis=0)
    return out.astype(f32)


def kernel(**inputs):
    return run(inputs, 8192, 4, 4)
```

```python
import contextlib
import numpy as np
import concourse.bass as bass
import concourse.mybir as mybir
from concourse.bass_utils import run_bass_kernel_spmd

F32 = mybir.dt.float32
BF16 = mybir.dt.bfloat16
ALU = mybir.AluOpType
AF = mybir.ActivationFunctionType
ENGS = ("sp", "act", "pe", "dve", "pool")
EPS = 1e-6
NEG = -30000.0


class Res:
    __slots__ = ("name", "w", "r", "excl")

    def __init__(self, name="", excl=False):
        self.name = name
        self.w = {}
        self.r = {}
        self.excl = excl


def PR(name):
    return Res(name, excl=True)


def _merge(d, s, v):
    if d.get(s, 0) < v:
        d[s] = v


class Prog:
    def __init__(self, nc, stack, n_dma_sems=32):
        self.nc = nc
        self.q = {e: [] for e in ENGS}
        self.sem_names = []
        self.sem_count = []
        self.seen = {e: {} for e in ENGS}
        self.pending_reads = {e: [] for e in ENGS}
        self.pending_writes = {e: [] for e in ENGS}
        self.eng_sem = {}
        for e in ("act", "pe", "dve", "pool"):
            self.eng_sem[e] = self._new_sem("c_" + e)
        self.dma_sems = {"sp": [self._new_sem("d%d" % i) for i in range(n_dma_sems)],
                         "pool": [self._new_sem("w%d" % i) for i in range(16)]}
        self.dma_rr = {"sp": 0, "pool": 0}
        self.n_ops = 0
        self.sems = [stack.enter_context(nc.semaphore(n)) for n in self.sem_names]

    def _new_sem(self, name):
        self.sem_names.append(name)
        self.sem_count.append(0)
        return len(self.sem_names) - 1

    def op(self, eng, fn, reads=(), writes=(), inc=True, dma=False):
        waits = {}
        for r in reads:
            for s, v in r.w.items():
                _merge(waits, s, v)
            if r.excl:
                for s, v in r.r.items():
                    if not (eng in self.eng_sem and s == self.eng_sem[eng]):
                        _merge(waits, s, v)
        for w in writes:
            for s, v in w.w.items():
                _merge(waits, s, v)
            for s, v in w.r.items():
                _merge(waits, s, v)
        tok = None
        if dma:
            pool_ = self.dma_sems[eng]
            s = pool_[self.dma_rr[eng] % len(pool_)]
            self.dma_rr[eng] += 1
            if self.sem_count[s] > 0:
                _merge(waits, s, self.sem_count[s])
            self.sem_count[s] += 16
            tok = (s, self.sem_count[s], 16)
        elif inc:
            s = self.eng_sem[eng]
            self.sem_count[s] += 1
            tok = (s, self.sem_count[s], 1)
        seen = self.seen[eng]
        wl = []
        for s, v in waits.items():
            if seen.get(s, 0) < v:
                if eng == "pe" and s == self.eng_sem["pe"]:
                    continue
                seen[s] = v
                wl.append((s, v))
        self.q[eng].append((fn, wl, tok))
        self.n_ops += 1
        if tok is None:
            self.pending_reads[eng].extend(reads)
            self.pending_writes[eng].extend(writes)
        else:
            s, v, _ = tok
            rl = list(reads)
            wr = list(writes)
            if not dma:
                rl += self.pending_reads[eng]
                wr += self.pending_writes[eng]
                self.pending_reads[eng] = []
                self.pending_writes[eng] = []
            for r in rl:
                _merge(r.r, s, v)
            for w in wr:
                _merge(w.w, s, v)
        return tok

    def barrier(self):
        for e in ENGS:
            wl = []
            for s, c in enumerate(self.sem_count):
                if c > 0 and self.seen[e].get(s, 0) < c:
                    self.seen[e][s] = c
                    wl.append((s, c))
            if wl:
                self.q[e].append((None, wl, None))

    def flush(self, final=False):
        nc = self.nc
        sems = self.sems
        if final:
            wl = [(s, c) for s, c in enumerate(self.sem_count) if c > 0]
            self.q["sp"].append((None, wl, None))
        with nc.Block() as block:
            def run(eng_name):
                ops = self.q[eng_name]

                def f(e):
                    for fn, wl, tok in ops:
                        for s, v in wl:
                            e.wait_ge(sems[s], v)
                        if fn is None:
                            continue
                        ins = fn(e)
                        if tok is not None:
                            ins.then_inc(sems[tok[0]], tok[2])
                return f
            block.sync(run("sp"))
            block.scalar(run("act"))
            block.tensor(run("pe"))
            block.vector(run("dve"))
            block.gpsimd(run("pool"))
        self.q = {e: [] for e in ENGS}


class Ring:
    def __init__(self, items):
        self.items = items
        self.i = 0

    def next(self):
        it = self.items[self.i % len(self.items)]
        self.i += 1
        return it


C_G = {"mix_pre": 0, "mix_post": 8, "mlp_pre": 16, "mlp_post": 24, "ple_pre": 32, "ple_post": 40}
C_LNG, C_LNB, C_PSC, C_DW, C_SC, C_BF = 48, 50, 52, 54, 116, 122
K_ID, K_MASK, K_PCOEF, K_PRATIO = 0, 128, 256, 288


def build(T, L, stop=None, dbg=False):
    NT = T // 512
    NB = T // 128
    nc = bass.Bass("TRN2", target_bir_lowering=False)

    def din(name, shape, dt=F32):
        return nc.dram_tensor(name, shape, dt, kind="ExternalInput").ap()

    def dscr(name, shape, dt):
        return nc.dram_tensor(name, shape, dt, kind="ExternalOutput" if dbg else "Internal").ap()

    xT = din("xT", [1024, T])
    pT = din("pT", [L * 256, T])
    w_in = din("w_in", [L * 1024, 2308])
    w_pw = din("w_pw", [L * 256, 256])
    w_pbd = din("w_pbd", [L * 256, 128])
    w_out = din("w_out", [L * 1024, 1024])
    w_up = din("w_up", [L * 1024, 4096])
    w_dn = din("w_dn", [L * 1024, 4096])
    w_gate = din("w_gate", [L * 1024, 1024])
    w_proj = din("w_proj", [L * 256, 1024])
    small = din("small", [128, L * 128])
    consts = din("consts", [128, 320])
    outT = nc.dram_tensor("outT", [1024, T], F32, kind="ExternalOutput").ap()

    hT = dscr("hT", [1024, T], F32)
    wup_bf = dscr("wup_bf", [L * 1024, 4096], BF16)
    wdn_bf = dscr("wdn_bf", [L * 1024, 4096], BF16)
    mixin = dscr("mixin", [1024, 32 + T], BF16)
    qT = dscr("qT", [256, T], BF16)
    kT = dscr("kT", [256, T], BF16)
    c3d = dscr("c3d", [4, 3, T], BF16)
    vaug = dscr("vaug", [NB, 128, 512], BF16)
    ymix = dscr("ymix", [1024, T], BF16)

    R_h = [Res("h%d" % j) for j in range(NT)]
    R_mixin = [Res("mi%d" % j) for j in range(NT + 1)]
    R_q = [Res("q%d" % j) for j in range(NT)]
    R_k = [Res("k%d" % j) for j in range(NT)]
    R_c3 = [Res("c3%d" % j) for j in range(NT)]
    R_v = [Res("v%d" % j) for j in range(NT)]
    R_ymA = [Res("ymA%d" % j) for j in range(NT)]
    R_ymB = [Res("ymB%d" % j) for j in range(NT)]
    R_wup = [[Res("wu%d_%d" % (l, g)) for g in range(8)] for l in range(L)]
    R_wdn = [[Res("wd%d_%d" % (l, g)) for g in range(8)] for l in range(L)]

    top = contextlib.ExitStack()
    with top:
        P = Prog(nc, top)

        uid = [0]

        def sbt(st, name, shape, dt):
            uid[0] += 1
            return st.enter_context(nc.sbuf_tensor("%s_u%d" % (name, uid[0]), shape, dt))

        def pst(st, name, shape=(128, 512), dt=F32):
            uid[0] += 1
            return st.enter_context(nc.psum_tensor("%s_u%d" % (name, uid[0]), list(shape), dt))

        small_sb = sbt(top, "small_sb", [128, L * 128], F32)
        consts_sb = sbt(top, "consts_sb", [128, 320], F32)
        ident_bf = sbt(top, "ident_bf", [128, 128], BF16)
        mask_bf = sbt(top, "mask_bf", [128, 128], BF16)
        ones_bf = sbt(top, "ones_bf", [128, 128], BF16)
        ones_f = sbt(top, "ones_f", [128, 128], F32)
        cneg = sbt(top, "cneg", [128, NB, 4], F32)
        R_small, R_consts, R_cst2, R_cneg = Res("small"), Res("consts"), Res("cst2"), Res("cneg")
        ident_f = consts_sb[:, K_ID:K_ID + 128]

        def scol(l, c, rows=slice(0, 128)):
            return small_sb[rows, l * 128 + c: l * 128 + c + 1]

        def dma(eng, out, in_, reads, writes):
            P.op(eng, lambda e: e.dma_start(out=out, in_=in_), reads=reads, writes=writes, dma=True)

        def mm(out, lhsT, rhs, start, stop, reads, writes):
            P.op("pe", lambda e: e.matmul(out, lhsT=lhsT, rhs=rhs, start=start, stop=stop),
                 reads=reads, writes=writes, inc=stop)

        def act(out, in_, func, reads, writes, bias=None, scale=None):
            kw = {}
            if bias is not None:
                kw["bias"] = bias
            if scale is not None:
                kw["scale"] = scale
            P.op("act", lambda e: e.activation(out, in_, func, **kw), reads=reads, writes=writes)

        def tt(eng, out, in0, in1, op, reads, writes):
            P.op(eng, lambda e: e.tensor_tensor(out, in0, in1, op), reads=reads, writes=writes)

        def stt(eng, out, in0, scalar, in1, op0, op1, reads, writes):
            P.op(eng, lambda e: e.scalar_tensor_tensor(out, in0, scalar, in1, op0, op1), reads=reads, writes=writes)

        def norm_scale(eng, out, in0, gcol, rstd_ap, reads, writes, ptmp_ring):
            if eng == "dve":
                stt("dve", out, in0, gcol, rstd_ap, ALU.mult, ALU.mult, reads, writes)
            else:
                tb, Rtb = ptmp_ring.next()
                tt("pool", tb[:], in0, rstd_ap, ALU.mult, reads, [Rtb])
                P.op("pool", lambda e: e.tensor_scalar_mul(out, tb[:], gcol), reads=[Rtb] + list(reads), writes=writes)

        def ts(eng, out, in0, s1, s2, op0, op1, reads, writes):
            if s2 is None:
                P.op(eng, lambda e: e.tensor_single_scalar(out, in0, s1, op0), reads=reads, writes=writes)
            else:
                P.op(eng, lambda e: e.tensor_scalar(out, in0, s1, s2, op0, op1), reads=reads, writes=writes)

        def cp(eng, out, in_, reads, writes):
            if eng == "act":
                P.op("act", lambda e: e.copy(out, in_), reads=reads, writes=writes)
            else:
                P.op(eng, lambda e: e.tensor_copy(out, in_), reads=reads, writes=writes)

        def memset(eng, ap, val, writes):
            P.op(eng, lambda e: e.memset(ap, val), writes=writes)

        def recip(out, in_, reads, writes):
            P.op("dve", lambda e: e.reciprocal(out, in_), reads=reads, writes=writes)

        def rms_stats(sq, R_sq, ps_stat, R_ps, rs, rstd, R_rs, R_rstd):
            for c in range(8):
                mm(ps_stat[:], ones_bf[:], sq[:, c, :], c == 0, c == 7, list(R_sq) + [R_cst2], [R_ps])
            act(rs[:], ps_stat[:], AF.Sqrt, [R_ps], [R_rs], bias=EPS, scale=1.0 / 1024.0)
            recip(rstd[:], rs[:], [R_rs], [R_rstd])

        with contextlib.ExitStack() as st:
            zt = sbt(st, "zt", [128, 8, 32], BF16)
            R_zt = Res("zt")
            dma("sp", small_sb[:], small, [], [R_small])
            dma("sp", consts_sb[:], consts, [], [R_consts])
            cp("dve", ident_bf[:], consts_sb[:, K_ID:K_ID + 128], [R_consts], [R_cst2])
            cp("dve", mask_bf[:], consts_sb[:, K_MASK:K_MASK + 128], [R_consts], [R_cst2])
            memset("pool", ones_bf[:], 1.0, [R_cst2])
            memset("pool", ones_f[:], 1.0, [R_cst2])
            memset("pool", zt[:], 0.0, [R_zt])
            dma("sp", mixin.rearrange("(c p) n -> p c n", p=128)[:, :, 0:32], zt[:], [R_zt], [R_mixin[0]])
            P.barrier()
            P.flush(final=(stop == "pro"))
        if stop == "pro":
            return nc

        def cast_mlp_weights(l):
            for g in range(8):
                for (src, dst, RR) in ((w_up, wup_bf, R_wup), (w_dn, wdn_bf, R_wdn)):
                    r0 = l * 1024 + g * 128
                    for hh in range(2):
                        dma("pool", dst[r0:r0 + 128, hh * 2048:(hh + 1) * 2048],
                            src[r0:r0 + 128, hh * 2048:(hh + 1) * 2048], [], [RR[l][g]])

        for l in range(L):
            h_src = xT if l == 0 else hT
            h_dst = outT if l == L - 1 else hT
            h_src_v = h_src.rearrange("(c p) n -> p c n", p=128)
            h_dst_v = h_dst.rearrange("(c p) n -> p c n", p=128)

            with contextlib.ExitStack() as st:
                win = sbt(st, "win", [128, 8, 2308], BF16)
                R_win = Res("win")
                ht = [sbt(st, "ht%d" % i, [128, 8, 512], F32) for i in range(2)]
                R_ht = [Res("ht0"), Res("ht1")]
                sq = sbt(st, "sq", [128, 8, 512], BF16)
                R_sq = Res("sq")
                xn2 = [sbt(st, "xn%d" % i, [128, 8, 512], BF16) for i in range(2)]
                R_xn2 = [Res("xn0"), Res("xn1")]
                rs = sbt(st, "rs", [128, 512], F32)
                rstd = sbt(st, "rstd", [128, 512], F32)
                R_rs, R_rstd = Res("rs"), Res("rstd")
                tmpf = [sbt(st, "tmpf%d" % i, [128, 512], F32) for i in range(2)]
                tmp_ring = Ring([(tmpf[i], Res("tmpf%d" % i)) for i in range(2)])
                ptmp_ring = Ring([(sbt(st, "ptmp%d" % i, [128, 512], F32), Res("ptmp%d" % i)) for i in range(2)])
                mstage = [sbt(st, "mstage%d" % i, [128, 8, 512], BF16) for i in range(2)]
                R_mst = [Res("mst0"), Res("mst1")]
                qstage = [sbt(st, "qstage%d" % i, [128, 2, 512], BF16) for i in range(2)]
                R_qst = [Res("qst0"), Res("qst1")]
                kstage = [sbt(st, "kstage%d" % i, [128, 2, 512], BF16) for i in range(2)]
                R_kst = [Res("kst0"), Res("kst1")]
                vstage = [sbt(st, "vstage%d" % i, [128, 4, 512], BF16) for i in range(2)]
                R_vst = [Res("vst0"), Res("vst1")]
                xb = sbt(st, "xb", [4, 512], F32)
                ef = sbt(st, "ef", [4, 512], F32)
                lf = sbt(st, "lf", [4, 512], F32)
                ones4 = sbt(st, "ones4", [4, 512], F32)
                cc = [sbt(st, "cc%d" % i, [4, 512], F32) for i in range(2)]
                r1 = sbt(st, "r1", [4, 512], F32)
                r2 = sbt(st, "r2", [4, 512], F32)
                c3 = [sbt(st, "c3_%d" % i, [4, 3, 512], BF16) for i in range(2)]
                R_f = Res("fmisc")
                R_cc = [Res("cc0"), Res("cc1")]
                R_c3s = [Res("c3s0"), Res("c3s1")]
                ps_stat = pst(st, "ps_stat")
                R_pstat = PR("ps_stat")
                ps_ring = Ring([(pst(st, "psr%d" % i), PR("psr%d" % i)) for i in range(4)])
                ps_v = Ring([(pst(st, "psv%d" % i), PR("psv%d" % i)) for i in range(2)])
                ps_t = pst(st, "ps_t")
                R_pst = PR("ps_t")

                for k in range(8):
                    for hh in range(2):
                        dma("pool", win[:, k, hh * 1154:(hh + 1) * 1154],
                            w_in[l * 1024 + k * 128: l * 1024 + (k + 1) * 128, hh * 1154:(hh + 1) * 1154],
                            [], [R_win])
                cast_mlp_weights(l)
                memset("pool", ones4[:], 1.0, [R_f])
                for i in range(2):
                    memset("pool", vstage[i][:], 1.0, [R_vst[i]])

                def load_h(j):
                    dma("sp", ht[j % 2][:], h_src_v[:, :, j * 512:(j + 1) * 512], [R_h[j]], [R_ht[j % 2]])

                cur = {}

                def proj(ps, R_ps, col0, M):
                    xn, R_xn = cur["xn"], cur["R_xn"]
                    for k in range(8):
                        mm(ps[0:M, :], win[:, k, col0:col0 + M], xn[:, k, :], k == 0, k == 7, [R_win, R_xn], [R_ps])

                load_h(0)
                for j in range(NT):
                    b = j % 2
                    if j + 1 < NT:
                        load_h(j + 1)
                    h = ht[b]
                    cols = slice(j * 512, (j + 1) * 512)
                    xnb, R_xnb = xn2[b], R_xn2[b]
                    xn, R_xn = xnb, R_xnb
                    cur["xn"], cur["R_xn"] = xnb, R_xnb
                    act(sq[:], h[:], AF.Square, [R_ht[b]], [R_sq])
                    rms_stats(sq, [R_sq], ps_stat, R_pstat, rs, rstd, R_rs, R_rstd)
                    for c in range(8):
                        norm_scale("dve", xnb[:, c, :], h[:, c, :], scol(l, C_G["mix_pre"] + c),
                                   rstd[:], [R_ht[b], R_rstd, R_small], [R_xnb], ptmp_ring)
                    for blk in range(2):
                        pa, Rpa = ps_ring.next()
                        pb, Rpb = ps_ring.next()
                        proj(pa, Rpa, (0 + blk) * 128, 128)
                        proj(pb, Rpb, (2 + blk) * 128, 128)
                        tb, Rtb = tmp_ring.next()
                        act(tb[:], pb[:], AF.Sigmoid, [Rpb], [Rtb])
                        tt("dve", mstage[b][:, 0 + blk, :], pa[:], tb[:], ALU.mult, [Rpa, Rtb], [R_mst[b]])
                    for blk in range(2):
                        pa, Rpa = ps_ring.next()
                        pb, Rpb = ps_ring.next()
                        proj(pa, Rpa, (8 + blk) * 128, 128)
                        proj(pb, Rpb, (12 + blk) * 128, 128)
                        tb, Rtb = tmp_ring.next()
                        cp("act", tb[:], pb[:], [Rpb], [Rtb])
                        tt("dve", mstage[b][:, 2 + blk, :], pa[:], tb[:], ALU.mult, [Rpa, Rtb], [R_mst[b]])
                    for blk in range(2):
                        pa, Rpa = ps_ring.next()
                        proj(pa, Rpa, (14 + blk) * 128, 128)
                        cp("dve", mstage[b][:, 4 + blk, :], pa[:], [Rpa], [R_mst[b]])
                        pb, Rpb = ps_ring.next()
                        proj(pb, Rpb, (10 + blk) * 128, 128)
                        cp("act", mstage[b][:, 6 + blk, :], pb[:], [Rpb], [R_mst[b]])
                    for blk in range(2):
                        pa, Rpa = ps_ring.next()
                        proj(pa, Rpa, (4 + blk) * 128, 128)
                        act(qstage[b][:, blk, :], pa[:], AF.Copy, [Rpa], [R_qst[b]], scale=0.125)
                        pb, Rpb = ps_ring.next()
                        proj(pb, Rpb, (6 + blk) * 128, 128)
                        cp("dve", kstage[b][:, blk, :], pb[:], [Rpb], [R_kst[b]])
                    pf, Rpf = ps_ring.next()
                    proj(pf, Rpf, 2048, 4)
                    ts("dve", xb[:], pf[0:4, :], scol(l, C_BF, slice(0, 4)), None, ALU.add, None, [Rpf, R_small], [R_f])
                    act(ef[:], xb[:], AF.Exp, [R_f], [R_f], scale=-1.0)
                    act(lf[:], ef[:], AF.Ln, [R_f], [R_f], bias=1.0, scale=1.0)
                    init = 0.0 if j == 0 else cc[1 - b][:, 511:512]
                    P.op("dve", (lambda o, d0, d1, ini: (lambda e: e.tensor_tensor_scan(o, d0, d1, ini, ALU.mult, ALU.subtract)))(
                        cc[b][:], ones4[:], lf[:], init), reads=[R_f, R_cc[1 - b]], writes=[R_cc[b]])
                    cp("dve", c3[b][:, 0, :], cc[b][:], [R_cc[b]], [R_c3s[b]])
                    tt("dve", r1[:], cc[b][:], c3[b][:, 0, :], ALU.subtract, [R_cc[b], R_c3s[b]], [R_f])
                    cp("dve", c3[b][:, 1, :], r1[:], [R_f], [R_c3s[b]])
                    tt("dve", r2[:], r1[:], c3[b][:, 1, :], ALU.subtract, [R_f, R_c3s[b]], [R_f])
                    cp("dve", c3[b][:, 2, :], r2[:], [R_f], [R_c3s[b]])
                    for s in range(4):
                        P.op("pe", (lambda o, a, bb: (lambda e: e.matmul(o, lhsT=a, rhs=bb, start=True, stop=True)))(
                            ps_t[:, s * 4:s * 4 + 4], cc[b][0:4, s * 128:(s + 1) * 128], consts_sb[0:4, K_ID:K_ID + 4]),
                            reads=[R_cc[b], R_consts], writes=[R_pst], inc=True)
                    ts("dve", cneg[:, j * 4:(j + 1) * 4, :], ps_t[:, 0:16].rearrange("p (s h) -> p s h", s=4), -1.0, None,
                       ALU.mult, None, [R_pst], [R_cneg])
                    for s in range(4):
                        pv, Rpv = ps_v.next()
                        for k in range(8):
                            mm(pv[:, 0:256], xn[:, k, s * 128:(s + 1) * 128], win[:, k, 2052:2308], k == 0, k == 7,
                               [R_win, R_xn], [Rpv])
                        dst = vstage[b][:, s, :].rearrange("p (h c) -> p h c", h=4)[:, :, 0:64]
                        src = pv[:, 0:256].rearrange("p (h c) -> p h c", h=4)
                        cp("dve" if s % 2 == 0 else "act", dst, src, [Rpv], [R_vst[b]])
                    dma("sp", mixin.rearrange("(c p) n -> p c n", p=128)[:, :, 32 + j * 512: 32 + (j + 1) * 512],
                        mstage[b][:], [R_mst[b]], [R_mixin[j + 1]])
                    dma("sp", qT.rearrange("(c p) n -> p c n", p=128)[:, :, cols], qstage[b][:], [R_qst[b]], [R_q[j]])
                    dma("sp", kT.rearrange("(c p) n -> p c n", p=128)[:, :, cols], kstage[b][:], [R_kst[b]], [R_k[j]])
                    dma("sp", c3d[:, :, cols], c3[b][:], [R_c3s[b]], [R_c3[j]])
                    dma("sp", vaug[j * 4:(j + 1) * 4].rearrange("s p c -> p s c"), vstage[b][:], [R_vst[b]], [R_v[j]])
                P.barrier()
                P.flush(final=(stop == "A1"))
            if stop == "A1":
                return nc

            with contextlib.ExitStack() as st:
                dconf = sbt(st, "dconf", [128, 2, 31, 128], BF16)
                dsc = sbt(st, "dsc", [128, 2, 3, 128], BF16)
                dpl = sbt(st, "dpl", [128, 2, 16, 128], BF16)
                R_dg = Res("diag")
                wpw = sbt(st, "wpw", [128, 2, 256], BF16)
                wpbd = sbt(st, "wpbd", [128, 2, 128], BF16)
                R_wA2 = Res("wA2")
                mt = [sbt(st, "mt%d" % i, [128, 8, 544], BF16) for i in range(2)]
                R_mt = [Res("mt0"), Res("mt1")]
                xc = sbt(st, "xc", [128, 2, 512], F32)
                sq2 = sbt(st, "sq2", [128, 2, 512], F32)
                R_xc, R_sq2 = Res("xc"), Res("sq2")
                mean = sbt(st, "mean", [128, 512], F32)
                msq = sbt(st, "msq", [128, 512], F32)
                var = sbt(st, "var", [128, 512], F32)
                sd = sbt(st, "sd", [128, 512], F32)
                rstd2 = sbt(st, "rstd2", [128, 512], F32)
                R_mean, R_msq, R_var, R_sd, R_rstd2 = Res("mean"), Res("msq"), Res("var"), Res("sd"), Res("rstd2")
                xm = [sbt(st, "xm%d" % i, [128, 512], F32) for i in range(2)]
                R_xm = [Res("xm0"), Res("xm1")]
                xnn = [sbt(st, "xnn%d" % i, [128, 512], F32) for i in range(2)]
                R_xnn = [Res("xnn0"), Res("xnn1")]
                sconf = sbt(st, "sconf", [128, 2, 512], BF16)
                R_sconf = Res("sconf")
                dpool = sbt(st, "dpool", [128, 2, 512], BF16)
                R_dpool = Res("dpool")
                t16 = sbt(st, "t16", [128, 2, 16], F32)
                R_t16 = Res("t16")
                ystage = [sbt(st, "ystage%d" % i, [128, 6, 512], BF16) for i in range(2)]
                R_yst = [Res("yst0"), Res("yst1")]
                psc = [pst(st, "psc%d" % i) for i in range(2)]
                R_psc = [PR("psc0"), PR("psc1")]
                ps_s1 = pst(st, "ps_s1")
                ps_s2 = pst(st, "ps_s2")
                R_s1, R_s2 = PR("s1"), PR("s2")
                ps_ring = Ring([(pst(st, "psq%d" % i), PR("psq%d" % i)) for i in range(3)])

                dma("pool", wpw[:], w_pw[l * 256:(l + 1) * 256, :].rearrange("(c p) n -> p c n", p=128), [], [R_wA2])
                dma("pool", wpbd[:], w_pbd[l * 256:(l + 1) * 256, :].rearrange("(c p) n -> p c n", p=128), [], [R_wA2])
                for blk in range(2):
                    for k in range(31):
                        P.op("pool", (lambda o, s: (lambda e: e.tensor_scalar_mul(o, ident_f, s)))(
                            dconf[:, blk, k, :], scol(l, C_DW + blk * 31 + k)), reads=[R_consts, R_small], writes=[R_dg])
                    for k in range(3):
                        P.op("pool", (lambda o, s: (lambda e: e.tensor_scalar_mul(o, ident_f, s)))(
                            dsc[:, blk, k, :], scol(l, C_SC + blk * 3 + k)), reads=[R_consts, R_small], writes=[R_dg])
                    for k in range(16):
                        P.op("pool", (lambda o, s: (lambda e: e.tensor_scalar_mul(o, ident_f, s)))(
                            dpl[:, blk, k, :], consts_sb[:, K_PCOEF + blk * 16 + k: K_PCOEF + blk * 16 + k + 1]),
                            reads=[R_consts], writes=[R_dg])

                def load_mt(j):
                    dma("sp", mt[j % 2][:], mixin.rearrange("(c p) n -> p c n", p=128)[:, :, j * 512: j * 512 + 544],
                        [R_mixin[j], R_mixin[j + 1]], [R_mt[j % 2]])

                load_mt(0)
                sect = stop.split(":")[1] if (stop and ":" in stop) else "all"

                def on(*names):
                    return sect == "all" or sect in names

                for j in range(NT):
                    b = j % 2
                    if j + 1 < NT:
                        load_mt(j + 1)
                    m_ = mt[b]
                    cols = slice(j * 512, (j + 1) * 512)
                    if sect == "conv1":
                        for blk in range(2):
                            for k in range(31):
                                mm(psc[blk][:], dconf[:, blk, k, :], m_[:, blk, 2 + k: 2 + k + 512], k == 0, k == 30,
                                   [R_dg, R_mt[b]], [R_psc[blk]])
                            cp("dve", xc[:, blk, :], psc[blk][:], [R_psc[blk]], [R_xc])
                    if sect == "conv2":
                        for blk in range(2):
                            for k in range(3):
                                mm(psc[blk][:], dconf[:, blk, k, :], m_[:, blk, 2 + k: 2 + k + 512], k == 0, k == 2,
                                   [R_dg, R_mt[b]], [R_psc[blk]])
                            act(sq2[:, blk, :], psc[blk][:], AF.Square, [R_psc[blk]], [R_sq2])
                    if sect == "pool1":
                        for blk in range(2):
                            pp, Rpp = ps_ring.next()
                            for k in range(16):
                                mm(pp[:], dpl[:, blk, k, :], m_[:, 4 + blk, 32 - k: 32 - k + 512], k == 0, k == 15,
                                   [R_dg, R_mt[b]], [Rpp])
                            cp("dve", dpool[:, blk, :], pp[:], [Rpp], [R_dpool])
                    if on("conv", "ln", "silu", "conf"):
                        for blk in range(2):
                            for k in range(31):
                                mm(psc[blk][:], dconf[:, blk, k, :], m_[:, blk, 2 + k: 2 + k + 512], k == 0, k == 30,
                                   [R_dg, R_mt[b]], [R_psc[blk]])
                            cp("dve", xc[:, blk, :], psc[blk][:], [R_psc[blk]], [R_xc])
                            act(sq2[:, blk, :], xc[:, blk, :], AF.Square, [R_xc], [R_sq2])
                    if on("sc"):
                        for blk in range(2):
                            pp, Rpp = ps_ring.next()
                            for k in range(3):
                                mm(pp[:], dsc[:, blk, k, :], m_[:, 2 + blk, 30 + k: 30 + k + 512], k == 0, k == 2,
                                   [R_dg, R_mt[b]], [Rpp])
                            tt("dve", ystage[b][:, 2 + blk, :], pp[:], m_[:, 6 + blk, 32:544], ALU.mult, [Rpp, R_mt[b]], [R_yst[b]])
                    if on("pool"):
                        for blk in range(2):
                            pp, Rpp = ps_ring.next()
                            for k in range(16):
                                mm(pp[:], dpl[:, blk, k, :], m_[:, 4 + blk, 32 - k: 32 - k + 512], k == 0, k == 15,
                                   [R_dg, R_mt[b]], [Rpp])
                            cp("act", dpool[:, blk, :], pp[:], [Rpp], [R_dpool])
                            if j == 0:
                                tt("dve", t16[:, blk, :], pp[:, 0:16], m_[:, 4 + blk, 32:48], ALU.add, [Rpp, R_mt[b]], [R_t16])
                                tt("pool", t16[:, blk, :], t16[:, blk, :],
                                   consts_sb[:, K_PRATIO + blk * 16: K_PRATIO + blk * 16 + 16], ALU.mult, [R_t16, R_consts], [R_t16])
                                tt("dve", dpool[:, blk, 0:16], t16[:, blk, :], m_[:, 4 + blk, 32:48], ALU.subtract,
                                   [R_t16, R_mt[b]], [R_dpool])
                        for blk in range(2):
                            pp, Rpp = ps_ring.next()
                            mm(pp[:], wpbd[:, blk, :], dpool[:, blk, :], True, True, [R_wA2, R_dpool], [Rpp])
                            act(ystage[b][:, 4 + blk, :], pp[:], AF.Copy, [Rpp, R_small], [R_yst[b]], scale=scol(l, C_PSC + blk))
                    if on("ln", "silu", "conf"):
                        for blk in range(2):
                            mm(ps_s1[:], ones_f[:], xc[:, blk, :], blk == 0, blk == 1, [R_xc, R_cst2], [R_s1])
                        for blk in range(2):
                            mm(ps_s2[:], ones_f[:], sq2[:, blk, :], blk == 0, blk == 1, [R_sq2, R_cst2], [R_s2])
                        act(mean[:], ps_s1[:], AF.Copy, [R_s1], [R_mean], scale=1.0 / 256.0)
                        tt("pool", msq[:], mean[:], mean[:], ALU.mult, [R_mean], [R_msq])
                        stt("dve", var[:], ps_s2[:], 1.0 / 256.0, msq[:], ALU.mult, ALU.subtract, [R_s2, R_msq], [R_var])
                        act(sd[:], var[:], AF.Sqrt, [R_var], [R_sd], bias=EPS, scale=1.0)
                        recip(rstd2[:], sd[:], [R_sd], [R_rstd2])
                        for blk in range(2):
                            tt("dve", xm[blk][:], xc[:, blk, :], mean[:], ALU.subtract, [R_xc, R_mean], [R_xm[blk]])
                            tt("pool", xnn[blk][:], xm[blk][:], rstd2[:], ALU.mult, [R_xm[blk], R_rstd2], [R_xnn[blk]])
                    if on("silu", "conf"):
                        for blk in range(2):
                            act(sconf[:, blk, :], xnn[blk][:], AF.Silu, [R_xnn[blk], R_small], [R_sconf],
                                bias=scol(l, C_LNB + blk), scale=scol(l, C_LNG + blk))
                    if on("conf"):
                        for oc in range(2):
                            pp, Rpp = ps_ring.next()
                            for kb in range(2):
                                mm(pp[:], wpw[:, kb, oc * 128:(oc + 1) * 128], sconf[:, kb, :], kb == 0, kb == 1,
                                   [R_wA2, R_sconf], [Rpp])
                            cp("dve", ystage[b][:, oc, :], pp[:], [Rpp], [R_yst[b]])
                    ym_v = ymix.rearrange("(c p) n -> p c n", p=128)
                    dma("sp", ym_v[:, 0:2, cols], ystage[b][:, 0:2, :], [R_yst[b]], [R_ymA[j]])
                    dma("sp", ym_v[:, 4:8, cols], ystage[b][:, 2:6, :], [R_yst[b]], [R_ymA[j]])
                P.barrier()
                P.flush(final=(stop is not None and stop.startswith("A2")))
            if stop is not None and stop.startswith("A2"):
                return nc

            with contextlib.ExitStack() as st:
                kaug = sbt(st, "kaug", [128, 4, T], BF16)
                R_kaug = Res("kaug")
                vsb = sbt(st, "vsb", [128, NB, 512], BF16)
                R_vsb = [Res("vsb%d" % j) for j in range(NT)]
                qa = [sbt(st, "qa%d" % i, [128, 4, 512], BF16) for i in range(2)]
                R_qa = [Res("qa0"), Res("qa1")]
                pts = Ring([(sbt(st, "pt%d" % i, [128, 512], BF16), Res("pt%d" % i)) for i in range(4)])
                rec = sbt(st, "rec", [128, 512], F32)
                R_rec = Res("rec")
                bcs = sbt(st, "bcs", [64, 512], F32)
                R_bcs = Res("bcs")
                yst = Ring([(sbt(st, "ysb%d" % i, [64, 512], BF16), Res("ysb%d" % i)) for i in range(2)])
                ps_s = Ring([(pst(st, "pss%d" % i), PR("pss%d" % i)) for i in range(3)])
                ps_o = Ring([(pst(st, "pso%d" % i), PR("pso%d" % i)) for i in range(2)])
                ps_bc = pst(st, "ps_bc")
                R_bc = PR("ps_bc")

                memset("pool", kaug[64:96, :, :], 0.0, [R_kaug])
                memset("pool", kaug[64:67, :, :], 1.0, [R_kaug])
                for i in range(2):
                    memset("pool", qa[i][64:96, :, :], 0.0, [R_qa[i]])
                for j in range(NT):
                    cols = slice(j * 512, (j + 1) * 512)
                    dma("sp", kaug[0:64, :, cols], kT.rearrange("(h r) n -> r h n", r=64)[:, :, cols], [R_k[j]], [R_kaug])
                    dma("sp", vsb[:, j * 4:(j + 1) * 4, :], vaug[j * 4:(j + 1) * 4].rearrange("s p c -> p s c"),
                        [R_v[j]], [R_vsb[j]])

                def load_q(j):
                    cols = slice(j * 512, (j + 1) * 512)
                    dma("sp", qa[j % 2][0:64, :, :], qT.rearrange("(h r) n -> r h n", r=64)[:, :, cols], [R_q[j]], [R_qa[j % 2]])
                    dma("sp", qa[j % 2][64:67, :, :], c3d.rearrange("h j n -> j h n")[:, :, cols], [R_c3[j]], [R_qa[j % 2]])

                load_q(0)
                for j in range(NT):
                    b = j % 2
                    if j + 1 < NT:
                        load_q(j + 1)
                    cols = slice(j * 512, (j + 1) * 512)
                    for h in range(4):
                        nblk = 4 * j + 4
                        po, Rpo = ps_o.next()

                        def qk(i):
                            dj = i - 4 * j
                            c0 = 128 * dj if dj >= 0 else 0
                            pS, RpS = ps_s.next()
                            mm(pS[:, c0:512], kaug[0:96, h, i * 128:(i + 1) * 128], qa[b][0:96, h, c0:512],
                               True, dj < 0, [R_kaug, R_qa[b]], [RpS])
                            if dj >= 0:
                                mm(pS[:, c0:c0 + 128], ident_bf[:], mask_bf[:], False, True, [R_cst2], [RpS])
                            return pS, RpS, c0

                        nxt = qk(0)
                        for i in range(nblk):
                            pS, RpS, c0 = nxt
                            if i + 1 < nblk:
                                nxt = qk(i + 1)
                            pt, Rpt = pts.next()
                            act(pt[:, c0:512], pS[:, c0:512], AF.Exp, [RpS, R_cneg], [Rpt], bias=cneg[:, i, h:h + 1], scale=1.0)
                            mm(po[:, c0:512], vsb[:, i, h * 128:(h + 1) * 128], pt[:, c0:512],
                               i == 0, i == nblk - 1, [R_vsb[i // 4], Rpt], [Rpo])
                        recip(rec[64:128, :], po[64:128, :], [Rpo], [R_rec])
                        ys, Rys = yst.next()
                        tt("dve", ys[:], po[0:64, :], rec[64:128, :], ALU.mult, [Rpo, R_rec], [Rys])
                        dma("sp", ymix[256 + 64 * h: 256 + 64 * h + 64, cols], ys[:], [Rys], [R_ymB[j]])
                P.barrier()
                P.flush(final=(stop == "B"))
            if stop == "B":
                return nc

            with contextlib.ExitStack() as st:
                wout = sbt(st, "wout", [128, 8, 1024], BF16)
                wgate = sbt(st, "wgate", [128, 8, 1024], BF16)
                wproj = sbt(st, "wproj", [128, 2, 1024], BF16)
                R_wC = Res("wC")
                wu = [sbt(st, "wu%d" % i, [128, 8, 512], BF16) for i in range(3)]
                wd = [sbt(st, "wd%d" % i, [128, 4, 1024], BF16) for i in range(3)]
                R_wu = [Res("wus%d" % i) for i in range(3)]
                R_wd = [Res("wds%d" % i) for i in range(3)]
                ymt = sbt(st, "ymt", [128, 8, 512], BF16)
                R_ymt = Res("ymt")
                ht = [sbt(st, "hc%d" % i, [128, 8, 512], F32) for i in range(2)]
                R_ht = [Res("hc0"), Res("hc1")]
                ptl = [sbt(st, "ptl%d" % i, [128, 2, 512], BF16) for i in range(2)]
                R_ptl = [Res("ptl0"), Res("ptl1")]
                m = sbt(st, "m", [128, 8, 512], F32)
                R_m = Res("m")
                sqag = sbt(st, "sqag", [128, 8, 512], BF16)
                R_ag = [Res("ag0"), Res("ag1")]
                hn = sbt(st, "hn", [128, 8, 512], BF16)
                R_hn = Res("hn")
                gate = sbt(st, "gate", [128, 8, 512], BF16)
                R_gate = Res("gate")
                rs = sbt(st, "rs_c", [128, 512], F32)
                rstd = sbt(st, "rstd_c", [128, 512], F32)
                R_rs, R_rstd = Res("rs"), Res("rstd")
                tmpf = Ring([(sbt(st, "tmpc%d" % i, [128, 512], F32), Res("tmpc%d" % i)) for i in range(4)])
                rl = Ring([(sbt(st, "rl%d" % i, [128, 512], BF16), Res("rl%d" % i)) for i in range(4)])
                ptmp_ring = Ring([(sbt(st, "ptmpc%d" % i, [128, 512], F32), Res("ptmpc%d" % i)) for i in range(2)])
                ps_stat = pst(st, "ps_stat_c")
                R_pstat = PR("ps_stat_c")
                ps_ring = Ring([(pst(st, "psm%d" % i), PR("psm%d" % i)) for i in range(4)])
                ps_dn = Ring([(pst(st, "psd%d" % i), PR("psd%d" % i)) for i in range(2)])

                for k in range(8):
                    dma("pool", wout[:, k, :], w_out[l * 1024 + k * 128: l * 1024 + (k + 1) * 128, :], [], [R_wC])
                    dma("pool", wgate[:, k, :], w_gate[l * 1024 + k * 128: l * 1024 + (k + 1) * 128, :], [], [R_wC])
                for k in range(2):
                    dma("pool", wproj[:, k, :], w_proj[l * 256 + k * 128: l * 256 + (k + 1) * 128, :], [], [R_wC])

                def load_tile(j):
                    cols = slice(j * 512, (j + 1) * 512)
                    dma("sp", ht[j % 2][:], h_src_v[:, :, cols], [R_h[j]], [R_ht[j % 2]])
                    dma("pool", ptl[j % 2][:], pT[l * 256:(l + 1) * 256, :].rearrange("(c p) n -> p c n", p=128)[:, :, cols],
                        [], [R_ptl[j % 2]])

                def load_ym(j):
                    cols = slice(j * 512, (j + 1) * 512)
                    dma("sp", ymt[:], ymix.rearrange("(c p) n -> p c n", p=128)[:, :, cols], [R_ymA[j], R_ymB[j]], [R_ymt])

                gran_loaded = [0]

                def load_gran(n):
                    if n >= NT * 8:
                        return
                    g = n % 8
                    s = n % 3
                    r0 = l * 1024 + g * 128
                    dma("sp", wu[s][:], wup_bf[r0:r0 + 128, :].rearrange("p (k c) -> p k c", k=8), [R_wup[l][g]], [R_wu[s]])
                    dma("sp", wd[s][:], wdn_bf[r0:r0 + 128, :].rearrange("p (k c) -> p k c", k=4), [R_wdn[l][g]], [R_wd[s]])

                def post_norm_residual(h, R_h_, gname):
                    act(sqag[:], m[:], AF.Square, [R_m], R_ag)
                    rms_stats(sqag, R_ag, ps_stat, R_pstat, rs, rstd, R_rs, R_rstd)
                    for c in range(8):
                        tb, Rtb = tmpf.next()
                        stt("dve", tb[:], m[:, c, :], scol(l, C_G[gname] + c), rstd[:], ALU.mult, ALU.mult,
                            [R_m, R_rstd, R_small], [Rtb])
                        tt("pool" if c % 2 == 0 else "dve", h[:, c, :], h[:, c, :], tb[:], ALU.add, [R_h_, Rtb], [R_h_])

                def pre_norm(h, R_h_, gname):
                    act(sqag[:], h[:], AF.Square, [R_h_], R_ag)
                    rms_stats(sqag, R_ag, ps_stat, R_pstat, rs, rstd, R_rs, R_rstd)
                    for c in range(8):
                        norm_scale("dve", hn[:, c, :], h[:, c, :], scol(l, C_G[gname] + c), rstd[:],
                                   [R_h_, R_rstd, R_small], [R_hn], ptmp_ring)

                load_tile(0)
                load_ym(0)
                load_gran(0)
                load_gran(1)
                for j in range(NT):
                    b = j % 2
                    cols = slice(j * 512, (j + 1) * 512)
                    if j + 1 < NT:
                        load_tile(j + 1)
                    h = ht[b]
                    Rh = R_ht[b]
                    for oc in range(8):
                        pp, Rpp = ps_ring.next()
                        for k in range(8):
                            mm(pp[:], wout[:, k, oc * 128:(oc + 1) * 128], ymt[:, k, :], k == 0, k == 7, [R_wC, R_ymt], [Rpp])
                        cp("dve" if oc % 2 == 0 else "act", m[:, oc, :], pp[:], [Rpp], [R_m])
                    if j + 1 < NT:
                        load_ym(j + 1)
                    post_norm_residual(h, Rh, "mix_post")
                    pre_norm(h, Rh, "mlp_pre")

                    def up(g):
                        n = j * 8 + g
                        s = n % 3
                        for c4 in range(4):
                            pp, Rpp = ps_ring.next()
                            for k in range(8):
                                mm(pp[:], wu[s][:, k, c4 * 128:(c4 + 1) * 128], hn[:, k, :], k == 0, k == 7, [R_wu[s], R_hn], [Rpp])
                            rr, Rrr = rl.next()
                            act(rr[:], pp[:], AF.Relu, [Rpp], [Rrr])
                            tt("pool" if c4 % 2 == 0 else "dve", sqag[:, (g % 2) * 4 + c4, :], rr[:], rr[:], ALU.mult, [Rrr], [R_ag[g % 2]])

                    def down(g):
                        n = j * 8 + g
                        s = n % 3
                        for oc in range(8):
                            pd, Rpd = ps_dn.next()
                            for k4 in range(4):
                                mm(pd[:], wd[s][:, k4, oc * 128:(oc + 1) * 128], sqag[:, (g % 2) * 4 + k4, :], k4 == 0, k4 == 3,
                                   [R_wd[s], R_ag[g % 2]], [Rpd])
                            if g == 0:
                                cp("dve", m[:, oc, :], pd[:], [Rpd], [R_m])
                            else:
                                tt("dve", m[:, oc, :], pd[:], m[:, oc, :], ALU.add, [Rpd, R_m], [R_m])

                    for g in range(9):
                        if g < 8:
                            up(g)
                        if g > 0:
                            down(g - 1)
                        if g < 8:
                            load_gran(j * 8 + g + 2)
                    post_norm_residual(h, Rh, "mlp_post")
                    pre_norm(h, Rh, "ple_pre")
                    for oc in range(8):
                        pp, Rpp = ps_ring.next()
                        for k in range(8):
                            mm(pp[:], wgate[:, k, oc * 128:(oc + 1) * 128], hn[:, k, :], k == 0, k == 7, [R_wC, R_hn], [Rpp])
                        act(gate[:, oc, :], pp[:], AF.Sigmoid, [Rpp], [R_gate])
                    for oc in range(8):
                        pp, Rpp = ps_ring.next()
                        for k in range(2):
                            mm(pp[:], wproj[:, k, oc * 128:(oc + 1) * 128], ptl[b][:, k, :], k == 0, k == 1, [R_wC, R_ptl[b]], [Rpp])
                        tt("dve", m[:, oc, :], pp[:], gate[:, oc, :], ALU.mult, [Rpp, R_gate], [R_m])
                    post_norm_residual(h, Rh, "ple_post")
                    dma("sp", h_dst_v[:, :, cols], h[:], [Rh], [R_h[j]])
                P.barrier()
                P.flush(final=(l == L - 1))
    return nc


POOL_WINDOWS = (2, 4, 8, 16)


def _chunkcols(v):
    return np.ascontiguousarray(v.reshape(-1, 128).T)


def prep_shared(inp, L):
    f32 = np.float32
    w_in = np.asarray(inp["w_in"], f32)[:L]
    perm = np.concatenate([np.arange(0, 1024), np.arange(1284, 2308), np.arange(1280, 1284), np.arange(1024, 1280)])
    w_in_r = np.ascontiguousarray(w_in[:, :, perm]).reshape(L * 1024, 2308)
    w_pw = np.ascontiguousarray(np.asarray(inp["w_conf_pw"], f32)[:L]).reshape(L * 256, 256)
    wp = np.asarray(inp["w_pool"], f32)[:L]
    w_pbd = np.zeros((L, 2, 128, 128), f32)
    for blk in range(2):
        for gg in range(2):
            w_pbd[:, blk, gg * 64:(gg + 1) * 64, gg * 64:(gg + 1) * 64] = wp[:, blk * 2 + gg]
    w_pbd = w_pbd.reshape(L * 256, 128)
    w_out = np.ascontiguousarray(np.asarray(inp["w_out"], f32)[:L]).reshape(L * 1024, 1024)
    wu = np.asarray(inp["w_up"], f32)[:L]
    wu = wu.reshape(L, 8, 128, 8, 512).transpose(0, 3, 2, 1, 4)
    w_up = np.ascontiguousarray(wu).reshape(L * 1024, 4096)
    wdn = np.asarray(inp["w_down"], f32)[:L]
    wdn = wdn.reshape(L, 8, 4, 128, 1024).transpose(0, 1, 3, 2, 4)
    w_dn = np.ascontiguousarray(wdn).reshape(L * 1024, 4096)
    w_gate = np.ascontiguousarray(np.asarray(inp["w_ple_gate"], f32)[:L]).reshape(L * 1024, 1024)
    w_proj = np.ascontiguousarray(np.asarray(inp["w_ple_proj"], f32)[:L]).reshape(L * 256, 1024)
    small = np.zeros((128, L * 128), f32)
    for l in range(L):
        o = l * 128
        for name, key in (("mix_pre", "g_mix_pre"), ("mix_post", "g_mix_post"), ("mlp_pre", "g_mlp_pre"),
                          ("mlp_post", "g_mlp_post"), ("ple_pre", "g_ple_pre"), ("ple_post", "g_ple_post")):
            small[:, o + C_G[name]: o + C_G[name] + 8] = _chunkcols(np.asarray(inp[key], f32)[l])
        small[:, o + C_LNG: o + C_LNG + 2] = _chunkcols(np.asarray(inp["conf_ln_g"], f32)[l])
        small[:, o + C_LNB: o + C_LNB + 2] = _chunkcols(np.asarray(inp["conf_ln_b"], f32)[l])
        small[:, o + C_PSC: o + C_PSC + 2] = _chunkcols(np.asarray(inp["pool_scale"], f32)[l])
        dw = np.asarray(inp["w_conf_dw"], f32)[l]
        for blk in range(2):
            small[:, o + C_DW + blk * 31: o + C_DW + blk * 31 + 31] = dw[:, blk * 128:(blk + 1) * 128].T
        sc = np.asarray(inp["w_sc"], f32)[l]
        for blk in range(2):
            small[:, o + C_SC + blk * 3: o + C_SC + blk * 3 + 3] = sc[:, blk * 128:(blk + 1) * 128].T
        small[0:4, o + C_BF] = np.asarray(inp["b_forget"], f32)[l]
    consts = np.zeros((128, 320), f32)
    consts[:, K_ID:K_ID + 128] = np.eye(128, dtype=f32)
    kk, qq = np.meshgrid(np.arange(128), np.arange(128), indexing="ij")
    consts[:, K_MASK:K_MASK + 128] = np.where(kk > qq, NEG, 0.0)
    for blk in range(2):
        for p in range(128):
            w = POOL_WINDOWS[(blk * 128 + p) // 64]
            for t in range(16):
                consts[p, K_PCOEF + blk * 16 + t] = (1.0 / w if t < w else 0.0) - (1.0 if t == 0 else 0.0)
                consts[p, K_PRATIO + blk * 16 + t] = w / min(t + 1, w)
    return dict(w_in=w_in_r, w_pw=w_pw, w_pbd=w_pbd, w_out=w_out, w_up=w_up, w_dn=w_dn, w_gate=w_gate,
                w_proj=w_proj, small=small, consts=consts)


_NC_CACHE = {}


def run(inp, T, L, n_seq, stop=None, dbg=False):
    f32 = np.float32
    shared = prep_shared(inp, L)
    x = np.asarray(inp["x"], f32)
    p = np.asarray(inp["p"], f32)
    in_maps = []
    for core in range(8):
        bi = core % n_seq
        m = dict(shared)
        m["xT"] = np.ascontiguousarray(x[bi].T)
        m["pT"] = np.ascontiguousarray(p[:L, bi].transpose(0, 2, 1)).reshape(L * 256, T)
        in_maps.append(m)
    key = (T, L, stop, dbg)
    if key not in _NC_CACHE:
        _NC_CACHE[key] = build(T, L, stop, dbg)
    nc = _NC_CACHE[key]
    res = run_bass_kernel_spmd(nc, in_maps, core_ids=list(range(8)))
    if dbg:
        return res.results[0]
    out = np.stack([np.ascontiguousarray(res.results[bi]["outT"].T) for bi in range(n_seq)], axis=0)
    return out.astype(f32)


def kernel(**inputs):
    return run(inputs, 8192, 4, 4)
```

```python
import contextlib
import numpy as np
import concourse.bass as bass
import concourse.mybir as mybir
from concourse.bass_utils import run_bass_kernel_spmd

F32 = mybir.dt.float32
BF16 = mybir.dt.bfloat16
ALU = mybir.AluOpType
AF = mybir.ActivationFunctionType
ENGS = ("sp", "act", "pe", "dve", "pool")
EPS = 1e-6
NEG = -30000.0


class Res:
    __slots__ = ("name", "w", "r", "excl")

    def __init__(self, name="", excl=False):
        self.name = name
        self.w = {}
        self.r = {}
        self.excl = excl


def PR(name):
    return Res(name, excl=True)


def _merge(d, s, v):
    if d.get(s, 0) < v:
        d[s] = v


class Prog:
    def __init__(self, nc, stack, n_dma_sems=32):
        self.nc = nc
        self.q = {e: [] for e in ENGS}
        self.sem_names = []
        self.sem_count = []
        self.seen = {e: {} for e in ENGS}
        self.pending_reads = {e: [] for e in ENGS}
        self.pending_writes = {e: [] for e in ENGS}
        self.eng_sem = {}
        for e in ("act", "pe", "dve", "pool"):
            self.eng_sem[e] = self._new_sem("c_" + e)
        self.dma_sems = {"sp": [self._new_sem("d%d" % i) for i in range(n_dma_sems)],
                         "pool": [self._new_sem("w%d" % i) for i in range(16)]}
        self.dma_rr = {"sp": 0, "pool": 0}
        self.n_ops = 0
        self.sems = [stack.enter_context(nc.semaphore(n)) for n in self.sem_names]

    def _new_sem(self, name):
        self.sem_names.append(name)
        self.sem_count.append(0)
        return len(self.sem_names) - 1

    def op(self, eng, fn, reads=(), writes=(), inc=True, dma=False):
        waits = {}
        for r in reads:
            for s, v in r.w.items():
                _merge(waits, s, v)
            if r.excl:
                for s, v in r.r.items():
                    if not (eng in self.eng_sem and s == self.eng_sem[eng]):
                        _merge(waits, s, v)
        for w in writes:
            for s, v in w.w.items():
                _merge(waits, s, v)
            for s, v in w.r.items():
                _merge(waits, s, v)
        tok = None
        if dma:
            pool_ = self.dma_sems[eng]
            s = pool_[self.dma_rr[eng] % len(pool_)]
            self.dma_rr[eng] += 1
            if self.sem_count[s] > 0:
                _merge(waits, s, self.sem_count[s])
            self.sem_count[s] += 16
            tok = (s, self.sem_count[s], 16)
        elif inc:
            s = self.eng_sem[eng]
            self.sem_count[s] += 1
            tok = (s, self.sem_count[s], 1)
        seen = self.seen[eng]
        wl = []
        for s, v in waits.items():
            if seen.get(s, 0) < v:
                if eng == "pe" and s == self.eng_sem["pe"]:
                    continue
                seen[s] = v
                wl.append((s, v))
        self.q[eng].append((fn, wl, tok))
        self.n_ops += 1
        if tok is None:
            self.pending_reads[eng].extend(reads)
            self.pending_writes[eng].extend(writes)
        else:
            s, v, _ = tok
            rl = list(reads)
            wr = list(writes)
            if not dma:
                rl += self.pending_reads[eng]
                wr += self.pending_writes[eng]
                self.pending_reads[eng] = []
                self.pending_writes[eng] = []
            for r in rl:
                _merge(r.r, s, v)
            for w in wr:
                _merge(w.w, s, v)
        return tok

    def barrier(self):
        for e in ENGS:
            wl = []
            for s, c in enumerate(self.sem_count):
                if c > 0 and self.seen[e].get(s, 0) < c:
                    self.seen[e][s] = c
                    wl.append((s, c))
            if wl:
                self.q[e].append((None, wl, None))

    def flush(self, final=False):
        nc = self.nc
        sems = self.sems
        if final:
            wl = [(s, c) for s, c in enumerate(self.sem_count) if c > 0]
            self.q["sp"].append((None, wl, None))
        with nc.Block() as block:
            def run(eng_name):
                ops = self.q[eng_name]

                def f(e):
                    for fn, wl, tok in ops:
                        for s, v in wl:
                            e.wait_ge(sems[s], v)
                        if fn is None:
                            continue
                        ins = fn(e)
                        if tok is not None:
                            ins.then_inc(sems[tok[0]], tok[2])
                return f
            block.sync(run("sp"))
            block.scalar(run("act"))
            block.tensor(run("pe"))
            block.vector(run("dve"))
            block.gpsimd(run("pool"))
        self.q = {e: [] for e in ENGS}


class Ring:
    def __init__(self, items):
        self.items = items
        self.i = 0

    def next(self):
        it = self.items[self.i % len(self.items)]
        self.i += 1
        return it


C_G = {"mix_pre": 0, "mix_post": 8, "mlp_pre": 16, "mlp_post": 24, "ple_pre": 32, "ple_post": 40}
C_LNG, C_LNB, C_PSC, C_DW, C_SC, C_BF = 48, 50, 52, 54, 116, 122
K_ID, K_MASK, K_PCOEF, K_PRATIO = 0, 128, 256, 288


def build(T, L, stop=None, dbg=False):
    NT = T // 512
    NB = T // 128
    nc = bass.Bass("TRN2", target_bir_lowering=False)

    def din(name, shape, dt=F32):
        return nc.dram_tensor(name, shape, dt, kind="ExternalInput").ap()

    def dscr(name, shape, dt):
        return nc.dram_tensor(name, shape, dt, kind="ExternalOutput" if dbg else "Internal").ap()

    xT = din("xT", [1024, T])
    pT = din("pT", [L * 256, T])
    w_in = din("w_in", [L * 1024, 2308])
    w_pw = din("w_pw", [L * 256, 256])
    w_pbd = din("w_pbd", [L * 256, 128])
    w_out = din("w_out", [L * 1024, 1024])
    w_up = din("w_up", [L * 1024, 4096])
    w_dn = din("w_dn", [L * 1024, 4096])
    w_gate = din("w_gate", [L * 1024, 1024])
    w_proj = din("w_proj", [L * 256, 1024])
    small = din("small", [128, L * 128])
    consts = din("consts", [128, 320])
    outT = nc.dram_tensor("outT", [1024, T], F32, kind="ExternalOutput").ap()

    hT = dscr("hT", [1024, T], F32)
    wup_bf = dscr("wup_bf", [L * 1024, 4096], BF16)
    wdn_bf = dscr("wdn_bf", [L * 1024, 4096], BF16)
    mixin = dscr("mixin", [1024, 32 + T], BF16)
    qT = dscr("qT", [256, T], BF16)
    kT = dscr("kT", [256, T], BF16)
    c3d = dscr("c3d", [4, 3, T], BF16)
    vaug = dscr("vaug", [NB, 128, 512], BF16)
    ymix = dscr("ymix", [1024, T], BF16)

    R_h = [Res("h%d" % j) for j in range(NT)]
    R_mixin = [Res("mi%d" % j) for j in range(NT + 1)]
    R_q = [Res("q%d" % j) for j in range(NT)]
    R_k = [Res("k%d" % j) for j in range(NT)]
    R_c3 = [Res("c3%d" % j) for j in range(NT)]
    R_v = [Res("v%d" % j) for j in range(NT)]
    R_ymA = [Res("ymA%d" % j) for j in range(NT)]
    R_ymB = [Res("ymB%d" % j) for j in range(NT)]
    R_wup = [[Res("wu%d_%d" % (l, g)) for g in range(8)] for l in range(L)]
    R_wdn = [[Res("wd%d_%d" % (l, g)) for g in range(8)] for l in range(L)]

    top = contextlib.ExitStack()
    with top:
        P = Prog(nc, top)

        uid = [0]

        def sbt(st, name, shape, dt):
            uid[0] += 1
            return st.enter_context(nc.sbuf_tensor("%s_u%d" % (name, uid[0]), shape, dt))

        def pst(st, name, shape=(128, 512), dt=F32):
            uid[0] += 1
            return st.enter_context(nc.psum_tensor("%s_u%d" % (name, uid[0]), list(shape), dt))

        small_sb = sbt(top, "small_sb", [128, L * 128], F32)
        consts_sb = sbt(top, "consts_sb", [128, 320], F32)
        ident_bf = sbt(top, "ident_bf", [128, 128], BF16)
        mask_bf = sbt(top, "mask_bf", [128, 128], BF16)
        ones_bf = sbt(top, "ones_bf", [128, 128], BF16)
        ones_f = sbt(top, "ones_f", [128, 128], F32)
        cneg = sbt(top, "cneg", [128, NB, 4], F32)
        R_small, R_consts, R_cst2, R_cneg = Res("small"), Res("consts"), Res("cst2"), Res("cneg")
        ident_f = consts_sb[:, K_ID:K_ID + 128]

        def scol(l, c, rows=slice(0, 128)):
            return small_sb[rows, l * 128 + c: l * 128 + c + 1]

        def dma(eng, out, in_, reads, writes):
            P.op(eng, lambda e: e.dma_start(out=out, in_=in_), reads=reads, writes=writes, dma=True)

        def mm(out, lhsT, rhs, start, stop, reads, writes):
            P.op("pe", lambda e: e.matmul(out, lhsT=lhsT, rhs=rhs, start=start, stop=stop),
                 reads=reads, writes=writes, inc=stop)

        def act(out, in_, func, reads, writes, bias=None, scale=None):
            kw = {}
            if bias is not None:
                kw["bias"] = bias
            if scale is not None:
                kw["scale"] = scale
            P.op("act", lambda e: e.activation(out, in_, func, **kw), reads=reads, writes=writes)

        def tt(eng, out, in0, in1, op, reads, writes):
            P.op(eng, lambda e: e.tensor_tensor(out, in0, in1, op), reads=reads, writes=writes)

        def stt(eng, out, in0, scalar, in1, op0, op1, reads, writes):
            P.op(eng, lambda e: e.scalar_tensor_tensor(out, in0, scalar, in1, op0, op1), reads=reads, writes=writes)

        def norm_scale(eng, out, in0, gcol, rstd_ap, reads, writes, ptmp_ring):
            if eng == "dve":
                stt("dve", out, in0, gcol, rstd_ap, ALU.mult, ALU.mult, reads, writes)
            else:
                tb, Rtb = ptmp_ring.next()
                tt("pool", tb[:], in0, rstd_ap, ALU.mult, reads, [Rtb])
                P.op("pool", lambda e: e.tensor_scalar_mul(out, tb[:], gcol), reads=[Rtb] + list(reads), writes=writes)

        def ts(eng, out, in0, s1, s2, op0, op1, reads, writes):
            if s2 is None:
                P.op(eng, lambda e: e.tensor_single_scalar(out, in0, s1, op0), reads=reads, writes=writes)
            else:
                P.op(eng, lambda e: e.tensor_scalar(out, in0, s1, s2, op0, op1), reads=reads, writes=writes)

        def cp(eng, out, in_, reads, writes):
            if eng == "act":
                P.op("act", lambda e: e.copy(out, in_), reads=reads, writes=writes)
            else:
                P.op(eng, lambda e: e.tensor_copy(out, in_), reads=reads, writes=writes)

        def memset(eng, ap, val, writes):
            P.op(eng, lambda e: e.memset(ap, val), writes=writes)

        def recip(out, in_, reads, writes):
            P.op("dve", lambda e: e.reciprocal(out, in_), reads=reads, writes=writes)

        def rms_stats(sq, R_sq, ps_stat, R_ps, rs, rstd, R_rs, R_rstd):
            for c in range(8):
                mm(ps_stat[:], ones_bf[:], sq[:, c, :], c == 0, c == 7, list(R_sq) + [R_cst2], [R_ps])
            act(rs[:], ps_stat[:], AF.Sqrt, [R_ps], [R_rs], bias=EPS, scale=1.0 / 1024.0)
            recip(rstd[:], rs[:], [R_rs], [R_rstd])

        with contextlib.ExitStack() as st:
            zt = sbt(st, "zt", [128, 8, 32], BF16)
            R_zt = Res("zt")
            dma("sp", small_sb[:], small, [], [R_small])
            dma("sp", consts_sb[:], consts, [], [R_consts])
            cp("dve", ident_bf[:], consts_sb[:, K_ID:K_ID + 128], [R_consts], [R_cst2])
            cp("dve", mask_bf[:], consts_sb[:, K_MASK:K_MASK + 128], [R_consts], [R_cst2])
            memset("pool", ones_bf[:], 1.0, [R_cst2])
            memset("pool", ones_f[:], 1.0, [R_cst2])
            memset("pool", zt[:], 0.0, [R_zt])
            dma("sp", mixin.rearrange("(c p) n -> p c n", p=128)[:, :, 0:32], zt[:], [R_zt], [R_mixin[0]])
            P.barrier()
            P.flush(final=(stop == "pro"))
        if stop == "pro":
            return nc

        def cast_mlp_weights(l):
            for g in range(8):
                for (src, dst, RR) in ((w_up, wup_bf, R_wup), (w_dn, wdn_bf, R_wdn)):
                    r0 = l * 1024 + g * 128
                    for hh in range(2):
                        dma("pool", dst[r0:r0 + 128, hh * 2048:(hh + 1) * 2048],
                            src[r0:r0 + 128, hh * 2048:(hh + 1) * 2048], [], [RR[l][g]])

        for l in range(L):
            h_src = xT if l == 0 else hT
            h_dst = outT if l == L - 1 else hT
            h_src_v = h_src.rearrange("(c p) n -> p c n", p=128)
            h_dst_v = h_dst.rearrange("(c p) n -> p c n", p=128)

            with contextlib.ExitStack() as st:
                win = sbt(st, "win", [128, 8, 2308], BF16)
                R_win = Res("win")
                ht = [sbt(st, "ht%d" % i, [128, 8, 512], F32) for i in range(2)]
                R_ht = [Res("ht0"), Res("ht1")]
                sq = sbt(st, "sq", [128, 8, 512], BF16)
                R_sq = Res("sq")
                xn2 = [sbt(st, "xn%d" % i, [128, 8, 512], BF16) for i in range(2)]
                R_xn2 = [Res("xn0"), Res("xn1")]
                rs = sbt(st, "rs", [128, 512], F32)
                rstd = sbt(st, "rstd", [128, 512], F32)
                R_rs, R_rstd = Res("rs"), Res("rstd")
                tmpf = [sbt(st, "tmpf%d" % i, [128, 512], F32) for i in range(2)]
                tmp_ring = Ring([(tmpf[i], Res("tmpf%d" % i)) for i in range(2)])
                ptmp_ring = Ring([(sbt(st, "ptmp%d" % i, [128, 512], F32), Res("ptmp%d" % i)) for i in range(2)])
                mstage = [sbt(st, "mstage%d" % i, [128, 8, 512], BF16) for i in range(2)]
                R_mst = [Res("mst0"), Res("mst1")]
                qstage = [sbt(st, "qstage%d" % i, [128, 2, 512], BF16) for i in range(2)]
                R_qst = [Res("qst0"), Res("qst1")]
                kstage = [sbt(st, "kstage%d" % i, [128, 2, 512], BF16) for i in range(2)]
                R_kst = [Res("kst0"), Res("kst1")]
                vstage = [sbt(st, "vstage%d" % i, [128, 4, 512], BF16) for i in range(2)]
                R_vst = [Res("vst0"), Res("vst1")]
                xb = sbt(st, "xb", [4, 512], F32)
                ef = sbt(st, "ef", [4, 512], F32)
                lf = sbt(st, "lf", [4, 512], F32)
                ones4 = sbt(st, "ones4", [4, 512], F32)
                cc = [sbt(st, "cc%d" % i, [4, 512], F32) for i in range(2)]
                r1 = sbt(st, "r1", [4, 512], F32)
                r2 = sbt(st, "r2", [4, 512], F32)
                c3 = [sbt(st, "c3_%d" % i, [4, 3, 512], BF16) for i in range(2)]
                R_f = Res("fmisc")
                R_cc = [Res("cc0"), Res("cc1")]
                R_c3s = [Res("c3s0"), Res("c3s1")]
                ps_stat = pst(st, "ps_stat")
                R_pstat = PR("ps_stat")
                ps_ring = Ring([(pst(st, "psr%d" % i), PR("psr%d" % i)) for i in range(4)])
                ps_v = Ring([(pst(st, "psv%d" % i), PR("psv%d" % i)) for i in range(2)])
                ps_t = pst(st, "ps_t")
                R_pst = PR("ps_t")

                for k in range(8):
                    for hh in range(2):
                        dma("pool", win[:, k, hh * 1154:(hh + 1) * 1154],
                            w_in[l * 1024 + k * 128: l * 1024 + (k + 1) * 128, hh * 1154:(hh + 1) * 1154],
                            [], [R_win])
                cast_mlp_weights(l)
                memset("pool", ones4[:], 1.0, [R_f])
                for i in range(2):
                    memset("pool", vstage[i][:], 1.0, [R_vst[i]])

                def load_h(j):
                    dma("sp", ht[j % 2][:], h_src_v[:, :, j * 512:(j + 1) * 512], [R_h[j]], [R_ht[j % 2]])

                cur = {}

                def proj(ps, R_ps, col0, M):
                    xn, R_xn = cur["xn"], cur["R_xn"]
                    for k in range(8):
                        mm(ps[0:M, :], win[:, k, col0:col0 + M], xn[:, k, :], k == 0, k == 7, [R_win, R_xn], [R_ps])

                load_h(0)
                for j in range(NT):
                    b = j % 2
                    if j + 1 < NT:
                        load_h(j + 1)
                    h = ht[b]
                    cols = slice(j * 512, (j + 1) * 512)
                    xnb, R_xnb = xn2[b], R_xn2[b]
                    xn, R_xn = xnb, R_xnb
                    cur["xn"], cur["R_xn"] = xnb, R_xnb
                    act(sq[:], h[:], AF.Square, [R_ht[b]], [R_sq])
                    rms_stats(sq, [R_sq], ps_stat, R_pstat, rs, rstd, R_rs, R_rstd)
                    for c in range(8):
                        norm_scale("dve", xnb[:, c, :], h[:, c, :], scol(l, C_G["mix_pre"] + c),
                                   rstd[:], [R_ht[b], R_rstd, R_small], [R_xnb], ptmp_ring)
                    for blk in range(2):
                        pa, Rpa = ps_ring.next()
                        pb, Rpb = ps_ring.next()
                        proj(pa, Rpa, (0 + blk) * 128, 128)
                        proj(pb, Rpb, (2 + blk) * 128, 128)
                        tb, Rtb = tmp_ring.next()
                        act(tb[:], pb[:], AF.Sigmoid, [Rpb], [Rtb])
                        tt("dve", mstage[b][:, 0 + blk, :], pa[:], tb[:], ALU.mult, [Rpa, Rtb], [R_mst[b]])
                    for blk in range(2):
                        pa, Rpa = ps_ring.next()
                        pb, Rpb = ps_ring.next()
                        proj(pa, Rpa, (8 + blk) * 128, 128)
                        proj(pb, Rpb, (12 + blk) * 128, 128)
                        tb, Rtb = tmp_ring.next()
                        cp("act", tb[:], pb[:], [Rpb], [Rtb])
                        tt("dve", mstage[b][:, 2 + blk, :], pa[:], tb[:], ALU.mult, [Rpa, Rtb], [R_mst[b]])
                    for blk in range(2):
                        pa, Rpa = ps_ring.next()
                        proj(pa, Rpa, (14 + blk) * 128, 128)
                        cp("dve", mstage[b][:, 4 + blk, :], pa[:], [Rpa], [R_mst[b]])
                        pb, Rpb = ps_ring.next()
                        proj(pb, Rpb, (10 + blk) * 128, 128)
                        cp("act", mstage[b][:, 6 + blk, :], pb[:], [Rpb], [R_mst[b]])
                    for blk in range(2):
                        pa, Rpa = ps_ring.next()
                        proj(pa, Rpa, (4 + blk) * 128, 128)
                        act(qstage[b][:, blk, :], pa[:], AF.Copy, [Rpa], [R_qst[b]], scale=0.125)
                        pb, Rpb = ps_ring.next()
                        proj(pb, Rpb, (6 + blk) * 128, 128)
                        cp("dve", kstage[b][:, blk, :], pb[:], [Rpb], [R_kst[b]])
                    pf, Rpf = ps_ring.next()
                    proj(pf, Rpf, 2048, 4)
                    ts("dve", xb[:], pf[0:4, :], scol(l, C_BF, slice(0, 4)), None, ALU.add, None, [Rpf, R_small], [R_f])
                    act(ef[:], xb[:], AF.Exp, [R_f], [R_f], scale=-1.0)
                    act(lf[:], ef[:], AF.Ln, [R_f], [R_f], bias=1.0, scale=1.0)
                    init = 0.0 if j == 0 else cc[1 - b][:, 511:512]
                    P.op("dve", (lambda o, d0, d1, ini: (lambda e: e.tensor_tensor_scan(o, d0, d1, ini, ALU.mult, ALU.subtract)))(
                        cc[b][:], ones4[:], lf[:], init), reads=[R_f, R_cc[1 - b]], writes=[R_cc[b]])
                    cp("dve", c3[b][:, 0, :], cc[b][:], [R_cc[b]], [R_c3s[b]])
                    tt("dve", r1[:], cc[b][:], c3[b][:, 0, :], ALU.subtract, [R_cc[b], R_c3s[b]], [R_f])
                    cp("dve", c3[b][:, 1, :], r1[:], [R_f], [R_c3s[b]])
                    tt("dve", r2[:], r1[:], c3[b][:, 1, :], ALU.subtract, [R_f, R_c3s[b]], [R_f])
                    cp("dve", c3[b][:, 2, :], r2[:], [R_f], [R_c3s[b]])
                    for s in range(4):
                        P.op("pe", (lambda o, a, bb: (lambda e: e.matmul(o, lhsT=a, rhs=bb, start=True, stop=True)))(
                            ps_t[:, s * 4:s * 4 + 4], cc[b][0:4, s * 128:(s + 1) * 128], consts_sb[0:4, K_ID:K_ID + 4]),
                            reads=[R_cc[b], R_consts], writes=[R_pst], inc=True)
                    ts("dve", cneg[:, j * 4:(j + 1) * 4, :], ps_t[:, 0:16].rearrange("p (s h) -> p s h", s=4), -1.0, None,
                       ALU.mult, None, [R_pst], [R_cneg])
                    for s in range(4):
                        pv, Rpv = ps_v.next()
                        for k in range(8):
                            mm(pv[:, 0:256], xn[:, k, s * 128:(s + 1) * 128], win[:, k, 2052:2308], k == 0, k == 7,
                               [R_win, R_xn], [Rpv])
                        dst = vstage[b][:, s, :].rearrange("p (h c) -> p h c", h=4)[:, :, 0:64]
                        src = pv[:, 0:256].rearrange("p (h c) -> p h c", h=4)
                        cp("dve" if s % 2 == 0 else "act", dst, src, [Rpv], [R_vst[b]])
                    dma("sp", mixin.rearrange("(c p) n -> p c n", p=128)[:, :, 32 + j * 512: 32 + (j + 1) * 512],
                        mstage[b][:], [R_mst[b]], [R_mixin[j + 1]])
                    dma("sp", qT.rearrange("(c p) n -> p c n", p=128)[:, :, cols], qstage[b][:], [R_qst[b]], [R_q[j]])
                    dma("sp", kT.rearrange("(c p) n -> p c n", p=128)[:, :, cols], kstage[b][:], [R_kst[b]], [R_k[j]])
                    dma("sp", c3d[:, :, cols], c3[b][:], [R_c3s[b]], [R_c3[j]])
                    dma("sp", vaug[j * 4:(j + 1) * 4].rearrange("s p c -> p s c"), vstage[b][:], [R_vst[b]], [R_v[j]])
                P.barrier()
                P.flush(final=(stop == "A1"))
            if stop == "A1":
                return nc

            with contextlib.ExitStack() as st:
                dconf = sbt(st, "dconf", [128, 2, 31, 128], BF16)
                dsc = sbt(st, "dsc", [128, 2, 3, 128], BF16)
                dpl = sbt(st, "dpl", [128, 2, 16, 128], BF16)
                R_dg = Res("diag")
                wpw = sbt(st, "wpw", [128, 2, 256], BF16)
                wpbd = sbt(st, "wpbd", [128, 2, 128], BF16)
                R_wA2 = Res("wA2")
                mt = [sbt(st, "mt%d" % i, [128, 8, 544], BF16) for i in range(2)]
                R_mt = [Res("mt0"), Res("mt1")]
                xc = sbt(st, "xc", [128, 2, 512], F32)
                sq2 = sbt(st, "sq2", [128, 2, 512], F32)
                R_xc, R_sq2 = Res("xc"), Res("sq2")
                mean = sbt(st, "mean", [128, 512], F32)
                msq = sbt(st, "msq", [128, 512], F32)
                var = sbt(st, "var", [128, 512], F32)
                sd = sbt(st, "sd", [128, 512], F32)
                rstd2 = sbt(st, "rstd2", [128, 512], F32)
                R_mean, R_msq, R_var, R_sd, R_rstd2 = Res("mean"), Res("msq"), Res("var"), Res("sd"), Res("rstd2")
                xm = [sbt(st, "xm%d" % i, [128, 512], F32) for i in range(2)]
                R_xm = [Res("xm0"), Res("xm1")]
                xnn = [sbt(st, "xnn%d" % i, [128, 512], F32) for i in range(2)]
                R_xnn = [Res("xnn0"), Res("xnn1")]
                sconf = sbt(st, "sconf", [128, 2, 512], BF16)
                R_sconf = Res("sconf")
                dpool = sbt(st, "dpool", [128, 2, 512], BF16)
                R_dpool = Res("dpool")
                t16 = sbt(st, "t16", [128, 2, 16], F32)
                R_t16 = Res("t16")
                ystage = [sbt(st, "ystage%d" % i, [128, 6, 512], BF16) for i in range(2)]
                R_yst = [Res("yst0"), Res("yst1")]
                psc = [pst(st, "psc%d" % i) for i in range(2)]
                R_psc = [PR("psc0"), PR("psc1")]
                ps_s1 = pst(st, "ps_s1")
                ps_s2 = pst(st, "ps_s2")
                R_s1, R_s2 = PR("s1"), PR("s2")
                ps_ring = Ring([(pst(st, "psq%d" % i), PR("psq%d" % i)) for i in range(3)])

                dma("pool", wpw[:], w_pw[l * 256:(l + 1) * 256, :].rearrange("(c p) n -> p c n", p=128), [], [R_wA2])
                dma("pool", wpbd[:], w_pbd[l * 256:(l + 1) * 256, :].rearrange("(c p) n -> p c n", p=128), [], [R_wA2])
                for blk in range(2):
                    for k in range(31):
                        P.op("pool", (lambda o, s: (lambda e: e.tensor_scalar_mul(o, ident_f, s)))(
                            dconf[:, blk, k, :], scol(l, C_DW + blk * 31 + k)), reads=[R_consts, R_small], writes=[R_dg])
                    for k in range(3):
                        P.op("pool", (lambda o, s: (lambda e: e.tensor_scalar_mul(o, ident_f, s)))(
                            dsc[:, blk, k, :], scol(l, C_SC + blk * 3 + k)), reads=[R_consts, R_small], writes=[R_dg])
                    for k in range(16):
                        P.op("pool", (lambda o, s: (lambda e: e.tensor_scalar_mul(o, ident_f, s)))(
                            dpl[:, blk, k, :], consts_sb[:, K_PCOEF + blk * 16 + k: K_PCOEF + blk * 16 + k + 1]),
                            reads=[R_consts], writes=[R_dg])

                def load_mt(j):
                    dma("sp", mt[j % 2][:], mixin.rearrange("(c p) n -> p c n", p=128)[:, :, j * 512: j * 512 + 544],
                        [R_mixin[j], R_mixin[j + 1]], [R_mt[j % 2]])

                load_mt(0)
                sect = stop.split(":")[1] if (stop and ":" in stop) else "all"

                def on(*names):
                    return sect == "all" or sect in names

                for j in range(NT):
                    b = j % 2
                    if j + 1 < NT:
                        load_mt(j + 1)
                    m_ = mt[b]
                    cols = slice(j * 512, (j + 1) * 512)
                    if sect == "conv1":
                        for blk in range(2):
                            for k in range(31):
                                mm(psc[blk][:], dconf[:, blk, k, :], m_[:, blk, 2 + k: 2 + k + 512], k == 0, k == 30,
                                   [R_dg, R_mt[b]], [R_psc[blk]])
                            cp("dve", xc[:, blk, :], psc[blk][:], [R_psc[blk]], [R_xc])
                    if sect == "conv2":
                        for blk in range(2):
                            for k in range(3):
                                mm(psc[blk][:], dconf[:, blk, k, :], m_[:, blk, 2 + k: 2 + k + 512], k == 0, k == 2,
                                   [R_dg, R_mt[b]], [R_psc[blk]])
                            act(sq2[:, blk, :], psc[blk][:], AF.Square, [R_psc[blk]], [R_sq2])
                    if sect == "pool1":
                        for blk in range(2):
                            pp, Rpp = ps_ring.next()
                            for k in range(16):
                                mm(pp[:], dpl[:, blk, k, :], m_[:, 4 + blk, 32 - k: 32 - k + 512], k == 0, k == 15,
                                   [R_dg, R_mt[b]], [Rpp])
                            cp("dve", dpool[:, blk, :], pp[:], [Rpp], [R_dpool])
                    if on("conv", "ln", "silu", "conf"):
                        for blk in range(2):
                            for k in range(31):
                                mm(psc[blk][:], dconf[:, blk, k, :], m_[:, blk, 2 + k: 2 + k + 512], k == 0, k == 30,
                                   [R_dg, R_mt[b]], [R_psc[blk]])
                            cp("dve", xc[:, blk, :], psc[blk][:], [R_psc[blk]], [R_xc])
                            act(sq2[:, blk, :], xc[:, blk, :], AF.Square, [R_xc], [R_sq2])
                    if on("sc"):
                        for blk in range(2):
                            pp, Rpp = ps_ring.next()
                            for k in range(3):
                                mm(pp[:], dsc[:, blk, k, :], m_[:, 2 + blk, 30 + k: 30 + k + 512], k == 0, k == 2,
                                   [R_dg, R_mt[b]], [Rpp])
                            tt("dve", ystage[b][:, 2 + blk, :], pp[:], m_[:, 6 + blk, 32:544], ALU.mult, [Rpp, R_mt[b]], [R_yst[b]])
                    if on("pool"):
                        for blk in range(2):
                            pp, Rpp = ps_ring.next()
                            for k in range(16):
                                mm(pp[:], dpl[:, blk, k, :], m_[:, 4 + blk, 32 - k: 32 - k + 512], k == 0, k == 15,
                                   [R_dg, R_mt[b]], [Rpp])
                            cp("act", dpool[:, blk, :], pp[:], [Rpp], [R_dpool])
                            if j == 0:
                                tt("dve", t16[:, blk, :], pp[:, 0:16], m_[:, 4 + blk, 32:48], ALU.add, [Rpp, R_mt[b]], [R_t16])
                                tt("pool", t16[:, blk, :], t16[:, blk, :],
                                   consts_sb[:, K_PRATIO + blk * 16: K_PRATIO + blk * 16 + 16], ALU.mult, [R_t16, R_consts], [R_t16])
                                tt("dve", dpool[:, blk, 0:16], t16[:, blk, :], m_[:, 4 + blk, 32:48], ALU.subtract,
                                   [R_t16, R_mt[b]], [R_dpool])
                        for blk in range(2):
                            pp, Rpp = ps_ring.next()
                            mm(pp[:], wpbd[:, blk, :], dpool[:, blk, :], True, True, [R_wA2, R_dpool], [Rpp])
                            act(ystage[b][:, 4 + blk, :], pp[:], AF.Copy, [Rpp, R_small], [R_yst[b]], scale=scol(l, C_PSC + blk))
                    if on("ln", "silu", "conf"):
                        for blk in range(2):
                            mm(ps_s1[:], ones_f[:], xc[:, blk, :], blk == 0, blk == 1, [R_xc, R_cst2], [R_s1])
                        for blk in range(2):
                            mm(ps_s2[:], ones_f[:], sq2[:, blk, :], blk == 0, blk == 1, [R_sq2, R_cst2], [R_s2])
                        act(mean[:], ps_s1[:], AF.Copy, [R_s1], [R_mean], scale=1.0 / 256.0)
                        tt("pool", msq[:], mean[:], mean[:], ALU.mult, [R_mean], [R_msq])
                        stt("dve", var[:], ps_s2[:], 1.0 / 256.0, msq[:], ALU.mult, ALU.subtract, [R_s2, R_msq], [R_var])
                        act(sd[:], var[:], AF.Sqrt, [R_var], [R_sd], bias=EPS, scale=1.0)
                        recip(rstd2[:], sd[:], [R_sd], [R_rstd2])
                        for blk in range(2):
                            tt("dve", xm[blk][:], xc[:, blk, :], mean[:], ALU.subtract, [R_xc, R_mean], [R_xm[blk]])
                            tt("pool", xnn[blk][:], xm[blk][:], rstd2[:], ALU.mult, [R_xm[blk], R_rstd2], [R_xnn[blk]])
                    if on("silu", "conf"):
                        for blk in range(2):
                            act(sconf[:, blk, :], xnn[blk][:], AF.Silu, [R_xnn[blk], R_small], [R_sconf],
                                bias=scol(l, C_LNB + blk), scale=scol(l, C_LNG + blk))
                    if on("conf"):
                        for oc in range(2):
                            pp, Rpp = ps_ring.next()
                            for kb in range(2):
                                mm(pp[:], wpw[:, kb, oc * 128:(oc + 1) * 128], sconf[:, kb, :], kb == 0, kb == 1,
                                   [R_wA2, R_sconf], [Rpp])
                            cp("dve", ystage[b][:, oc, :], pp[:], [Rpp], [R_yst[b]])
                    ym_v = ymix.rearrange("(c p) n -> p c n", p=128)
                    dma("sp", ym_v[:, 0:2, cols], ystage[b][:, 0:2, :], [R_yst[b]], [R_ymA[j]])
                    dma("sp", ym_v[:, 4:8, cols], ystage[b][:, 2:6, :], [R_yst[b]], [R_ymA[j]])
                P.barrier()
                P.flush(final=(stop is not None and stop.startswith("A2")))
            if stop is not None and stop.startswith("A2"):
                return nc

            with contextlib.ExitStack() as st:
                kaug = sbt(st, "kaug", [128, 4, T], BF16)
                R_kaug = Res("kaug")
                vsb = sbt(st, "vsb", [128, NB, 512], BF16)
                R_vsb = [Res("vsb%d" % j) for j in range(NT)]
                qa = [sbt(st, "qa%d" % i, [128, 4, 512], BF16) for i in range(2)]
                R_qa = [Res("qa0"), Res("qa1")]
                pts = Ring([(sbt(st, "pt%d" % i, [128, 512], BF16), Res("pt%d" % i)) for i in range(4)])
                rec = sbt(st, "rec", [128, 512], F32)
                R_rec = Res("rec")
                bcs = sbt(st, "bcs", [64, 512], F32)
                R_bcs = Res("bcs")
                yst = Ring([(sbt(st, "ysb%d" % i, [64, 512], BF16), Res("ysb%d" % i)) for i in range(2)])
                ps_s = Ring([(pst(st, "pss%d" % i), PR("pss%d" % i)) for i in range(3)])
                ps_o = Ring([(pst(st, "pso%d" % i), PR("pso%d" % i)) for i in range(2)])
                ps_bc = pst(st, "ps_bc")
                R_bc = PR("ps_bc")

                memset("pool", kaug[64:96, :, :], 0.0, [R_kaug])
                memset("pool", kaug[64:67, :, :], 1.0, [R_kaug])
                for i in range(2):
                    memset("pool", qa[i][64:96, :, :], 0.0, [R_qa[i]])
                for j in range(NT):
                    cols = slice(j * 512, (j + 1) * 512)
                    dma("sp", kaug[0:64, :, cols], kT.rearrange("(h r) n -> r h n", r=64)[:, :, cols], [R_k[j]], [R_kaug])
                    dma("sp", vsb[:, j * 4:(j + 1) * 4, :], vaug[j * 4:(j + 1) * 4].rearrange("s p c -> p s c"),
                        [R_v[j]], [R_vsb[j]])

                def load_q(j):
                    cols = slice(j * 512, (j + 1) * 512)
                    dma("sp", qa[j % 2][0:64, :, :], qT.rearrange("(h r) n -> r h n", r=64)[:, :, cols], [R_q[j]], [R_qa[j % 2]])
                    dma("sp", qa[j % 2][64:67, :, :], c3d.rearrange("h j n -> j h n")[:, :, cols], [R_c3[j]], [R_qa[j % 2]])

                load_q(0)
                for j in range(NT):
                    b = j % 2
                    if j + 1 < NT:
                        load_q(j + 1)
                    cols = slice(j * 512, (j + 1) * 512)
                    for h in range(4):
                        nblk = 4 * j + 4
                        po, Rpo = ps_o.next()

                        def qk(i):
                            dj = i - 4 * j
                            c0 = 128 * dj if dj >= 0 else 0
                            pS, RpS = ps_s.next()
                            mm(pS[:, c0:512], kaug[0:96, h, i * 128:(i + 1) * 128], qa[b][0:96, h, c0:512],
                               True, dj < 0, [R_kaug, R_qa[b]], [RpS])
                            if dj >= 0:
                                mm(pS[:, c0:c0 + 128], ident_bf[:], mask_bf[:], False, True, [R_cst2], [RpS])
                            return pS, RpS, c0

                        nxt = qk(0)
                        for i in range(nblk):
                            pS, RpS, c0 = nxt
                            if i + 1 < nblk:
                                nxt = qk(i + 1)
                            pt, Rpt = pts.next()
                            act(pt[:, c0:512], pS[:, c0:512], AF.Exp, [RpS, R_cneg], [Rpt], bias=cneg[:, i, h:h + 1], scale=1.0)
                            mm(po[:, c0:512], vsb[:, i, h * 128:(h + 1) * 128], pt[:, c0:512],
                               i == 0, i == nblk - 1, [R_vsb[i // 4], Rpt], [Rpo])
                        recip(rec[64:128, :], po[64:128, :], [Rpo], [R_rec])
                        ys, Rys = yst.next()
                        tt("dve", ys[:], po[0:64, :], rec[64:128, :], ALU.mult, [Rpo, R_rec], [Rys])
                        dma("sp", ymix[256 + 64 * h: 256 + 64 * h + 64, cols], ys[:], [Rys], [R_ymB[j]])
                P.barrier()
                P.flush(final=(stop == "B"))
            if stop == "B":
                return nc

            with contextlib.ExitStack() as st:
                wout = sbt(st, "wout", [128, 8, 1024], BF16)
                wgate = sbt(st, "wgate", [128, 8, 1024], BF16)
                wproj = sbt(st, "wproj", [128, 2, 1024], BF16)
                R_wC = Res("wC")
                wu = [sbt(st, "wu%d" % i, [128, 8, 512], BF16) for i in range(2)]
                wd = [sbt(st, "wd%d" % i, [128, 4, 1024], BF16) for i in range(2)]
                R_wu = [Res("wus%d" % i) for i in range(2)]
                R_wd = [Res("wds%d" % i) for i in range(2)]
                ymt = sbt(st, "ymt", [128, 8, 512], BF16)
                R_ymt = Res("ymt")
                ht = [sbt(st, "hc%d" % i, [128, 8, 512], F32) for i in range(2)]
                R_ht = [Res("hc0"), Res("hc1")]
                ptl = [sbt(st, "ptl%d" % i, [128, 2, 512], BF16) for i in range(2)]
                R_ptl = [Res("ptl0"), Res("ptl1")]
                mb = [sbt(st, "mb%d" % i, [128, 8, 512], F32) for i in range(2)]
                R_mb = [Res("mb0"), Res("mb1")]
                hnD = [sbt(st, "hnD%d" % i, [128, 8, 512], BF16) for i in range(2)]
                R_hnD = [Res("hnD0"), Res("hnD1")]
                hnE = sbt(st, "hnE", [128, 8, 512], BF16)
                R_hnE = Res("hnE")
                sqg = sbt(st, "sqg", [128, 8, 512], BF16)
                R_sqg = Res("sqg")
                ag = sbt(st, "ag", [128, 8, 512], BF16)
                R_ag = [Res("ag0"), Res("ag1")]
                rs = sbt(st, "rs_c", [128, 512], F32)
                rstd = sbt(st, "rstd_c", [128, 512], F32)
                R_rs, R_rstd = Res("rs"), Res("rstd")
                tmpf = Ring([(sbt(st, "tmpc%d" % i, [128, 512], F32), Res("tmpc%d" % i)) for i in range(2)])
                rl = Ring([(sbt(st, "rl%d" % i, [128, 512], BF16), Res("rl%d" % i)) for i in range(2)])
                ps_stat = pst(st, "ps_stat_c")
                R_pstat = PR("ps_stat_c")
                ps_ring = Ring([(pst(st, "psm%d" % i), PR("psm%d" % i)) for i in range(4)])
                ps_dn = Ring([(pst(st, "psd%d" % i), PR("psd%d" % i)) for i in range(2)])

                for k in range(8):
                    dma("pool", wout[:, k, :], w_out[l * 1024 + k * 128: l * 1024 + (k + 1) * 128, :], [], [R_wC])
                for k in range(8):
                    dma("pool", wgate[:, k, :], w_gate[l * 1024 + k * 128: l * 1024 + (k + 1) * 128, :], [], [R_wC])
                for k in range(2):
                    dma("pool", wproj[:, k, :], w_proj[l * 256 + k * 128: l * 256 + (k + 1) * 128, :], [], [R_wC])

                def load_wu(n):
                    if n >= NT * 8:
                        return
                    g = n % 8
                    r0 = l * 1024 + g * 128
                    dma("sp", wu[n % 2][:], wup_bf[r0:r0 + 128, :].rearrange("p (k c) -> p k c", k=8), [R_wup[l][g]], [R_wu[n % 2]])

                def load_wd(n):
                    if n >= NT * 8:
                        return
                    g = n % 8
                    r0 = l * 1024 + g * 128
                    dma("sp", wd[n % 2][:], wdn_bf[r0:r0 + 128, :].rearrange("p (k c) -> p k c", k=4), [R_wdn[l][g]], [R_wd[n % 2]])

                def gen_post(m, R_m, h, R_h_, gname):
                    act(sqg[:], m[:], AF.Square, [R_m], [R_sqg])
                    yield
                    rms_stats(sqg, [R_sqg], ps_stat, R_pstat, rs, rstd, R_rs, R_rstd)
                    yield
                    for c in range(8):
                        tb, Rtb = tmpf.next()
                        stt("dve", tb[:], m[:, c, :], scol(l, C_G[gname] + c), rstd[:], ALU.mult, ALU.mult,
                            [R_m, R_rstd, R_small], [Rtb])
                        tt("pool", h[:, c, :], h[:, c, :], tb[:], ALU.add, [R_h_, Rtb], [R_h_])
                        if c == 3:
                            yield
                    yield

                def gen_pre(h, R_h_, hn, R_hn, gname):
                    act(sqg[:], h[:], AF.Square, [R_h_], [R_sqg])
                    yield
                    rms_stats(sqg, [R_sqg], ps_stat, R_pstat, rs, rstd, R_rs, R_rstd)
                    yield
                    for c in range(8):
                        stt("dve", hn[:, c, :], h[:, c, :], scol(l, C_G[gname] + c), rstd[:], ALU.mult, ALU.mult,
                            [R_h_, R_rstd, R_small], [R_hn])
                    yield

                def H1(j):
                    p = j % 2
                    cols = slice(j * 512, (j + 1) * 512)
                    dma("sp", ht[p][:], h_src_v[:, :, cols], [R_h[j]], [R_ht[p]])
                    dma("sp", ymt[:], ymix.rearrange("(c p) n -> p c n", p=128)[:, :, cols], [R_ymA[j], R_ymB[j]], [R_ymt])
                    dma("pool", ptl[p][:], pT[l * 256:(l + 1) * 256, :].rearrange("(c p) n -> p c n", p=128)[:, :, cols],
                        [], [R_ptl[p]])
                    yield
                    for oc in range(8):
                        pp, Rpp = ps_ring.next()
                        for k in range(8):
                            mm(pp[:], wout[:, k, oc * 128:(oc + 1) * 128], ymt[:, k, :], k == 0, k == 7, [R_wC, R_ymt], [Rpp])
                        cp("dve" if oc % 2 == 0 else "act", mb[p][:, oc, :], pp[:], [Rpp], [R_mb[p]])
                        if oc % 4 == 3:
                            yield
                    yield from gen_post(mb[p], R_mb[p], ht[p], R_ht[p], "mix_post")
                    yield from gen_pre(ht[p], R_ht[p], hnD[p], R_hnD[p], "mlp_pre")

                def H3(j):
                    p = j % 2
                    cols = slice(j * 512, (j + 1) * 512)
                    yield from gen_post(mb[p], R_mb[p], ht[p], R_ht[p], "mlp_post")
                    yield from gen_pre(ht[p], R_ht[p], hnE, R_hnE, "ple_pre")
                    for oc in range(8):
                        pp, Rpp = ps_ring.next()
                        for k in range(8):
                            mm(pp[:], wgate[:, k, oc * 128:(oc + 1) * 128], hnE[:, k, :], k == 0, k == 7, [R_wC, R_hnE], [Rpp])
                        act(sqg[:, oc, :], pp[:], AF.Sigmoid, [Rpp], [R_sqg])
                        if oc % 4 == 3:
                            yield
                    for oc in range(8):
                        pp, Rpp = ps_ring.next()
                        for k in range(2):
                            mm(pp[:], wproj[:, k, oc * 128:(oc + 1) * 128], ptl[p][:, k, :], k == 0, k == 1, [R_wC, R_ptl[p]], [Rpp])
                        tt("dve", mb[p][:, oc, :], pp[:], sqg[:, oc, :], ALU.mult, [Rpp, R_sqg], [R_mb[p]])
                        if oc % 4 == 3:
                            yield
                    yield from gen_post(mb[p], R_mb[p], ht[p], R_ht[p], "ple_post")
                    dma("sp", h_dst_v[:, :, cols], ht[p][:], [R_ht[p]], [R_h[j]])
                    yield

                def step(side, n):
                    for _ in range(n):
                        try:
                            next(side)
                        except StopIteration:
                            return

                def up(j, g):
                    n = j * 8 + g
                    p = j % 2
                    for c4 in range(4):
                        pp, Rpp = ps_ring.next()
                        for k in range(8):
                            mm(pp[:], wu[n % 2][:, k, c4 * 128:(c4 + 1) * 128], hnD[p][:, k, :], k == 0, k == 7,
                               [R_wu[n % 2], R_hnD[p]], [Rpp])
                        rr, Rrr = rl.next()
                        act(rr[:], pp[:], AF.Relu, [Rpp], [Rrr])
                        tt("pool" if c4 % 2 == 0 else "dve", ag[:, (g % 2) * 4 + c4, :], rr[:], rr[:], ALU.mult, [Rrr], [R_ag[g % 2]])

                def down(j, g):
                    n = j * 8 + g
                    p = j % 2
                    for oc in range(8):
                        pd, Rpd = ps_dn.next()
                        for k4 in range(4):
                            mm(pd[:], wd[n % 2][:, k4, oc * 128:(oc + 1) * 128], ag[:, (g % 2) * 4 + k4, :], k4 == 0, k4 == 3,
                               [R_wd[n % 2], R_ag[g % 2]], [Rpd])
                        if g == 0:
                            cp("dve", mb[p][:, oc, :], pd[:], [Rpd], [R_mb[p]])
                        else:
                            tt("dve", mb[p][:, oc, :], pd[:], mb[p][:, oc, :], ALU.add, [Rpd, R_mb[p]], [R_mb[p]])

                def chain(*gens):
                    for gq in gens:
                        if gq is not None:
                            yield from gq

                load_wu(0)
                load_wu(1)
                load_wd(0)
                for _ in H1(0):
                    pass
                for j in range(NT):
                    side = chain(H3(j - 1) if j > 0 else None, H1(j + 1) if j + 1 < NT else None)
                    for g in range(9):
                        n = j * 8 + g
                        if g < 8:
                            up(j, g)
                            step(side, 1)
                        if g > 0:
                            down(j, g - 1)
                            step(side, 2)
                        if g < 8:
                            load_wu(n + 2)
                            load_wd(n + 1)
                    for _ in side:
                        pass
                for _ in H3(NT - 1):
                    pass
                P.barrier()
                P.flush(final=(l == L - 1))
    return nc


POOL_WINDOWS = (2, 4, 8, 16)


def _chunkcols(v):
    return np.ascontiguousarray(v.reshape(-1, 128).T)


def prep_shared(inp, L):
    f32 = np.float32
    w_in = np.asarray(inp["w_in"], f32)[:L]
    perm = np.concatenate([np.arange(0, 1024), np.arange(1284, 2308), np.arange(1280, 1284), np.arange(1024, 1280)])
    w_in_r = np.ascontiguousarray(w_in[:, :, perm]).reshape(L * 1024, 2308)
    w_pw = np.ascontiguousarray(np.asarray(inp["w_conf_pw"], f32)[:L]).reshape(L * 256, 256)
    wp = np.asarray(inp["w_pool"], f32)[:L]
    w_pbd = np.zeros((L, 2, 128, 128), f32)
    for blk in range(2):
        for gg in range(2):
            w_pbd[:, blk, gg * 64:(gg + 1) * 64, gg * 64:(gg + 1) * 64] = wp[:, blk * 2 + gg]
    w_pbd = w_pbd.reshape(L * 256, 128)
    w_out = np.ascontiguousarray(np.asarray(inp["w_out"], f32)[:L]).reshape(L * 1024, 1024)
    wu = np.asarray(inp["w_up"], f32)[:L]
    wu = wu.reshape(L, 8, 128, 8, 512).transpose(0, 3, 2, 1, 4)
    w_up = np.ascontiguousarray(wu).reshape(L * 1024, 4096)
    wdn = np.asarray(inp["w_down"], f32)[:L]
    wdn = wdn.reshape(L, 8, 4, 128, 1024).transpose(0, 1, 3, 2, 4)
    w_dn = np.ascontiguousarray(wdn).reshape(L * 1024, 4096)
    w_gate = np.ascontiguousarray(np.asarray(inp["w_ple_gate"], f32)[:L]).reshape(L * 1024, 1024)
    w_proj = np.ascontiguousarray(np.asarray(inp["w_ple_proj"], f32)[:L]).reshape(L * 256, 1024)
    small = np.zeros((128, L * 128), f32)
    for l in range(L):
        o = l * 128
        for name, key in (("mix_pre", "g_mix_pre"), ("mix_post", "g_mix_post"), ("mlp_pre", "g_mlp_pre"),
                          ("mlp_post", "g_mlp_post"), ("ple_pre", "g_ple_pre"), ("ple_post", "g_ple_post")):
            small[:, o + C_G[name]: o + C_G[name] + 8] = _chunkcols(np.asarray(inp[key], f32)[l])
        small[:, o + C_LNG: o + C_LNG + 2] = _chunkcols(np.asarray(inp["conf_ln_g"], f32)[l])
        small[:, o + C_LNB: o + C_LNB + 2] = _chunkcols(np.asarray(inp["conf_ln_b"], f32)[l])
        small[:, o + C_PSC: o + C_PSC + 2] = _chunkcols(np.asarray(inp["pool_scale"], f32)[l])
        dw = np.asarray(inp["w_conf_dw"], f32)[l]
        for blk in range(2):
            small[:, o + C_DW + blk * 31: o + C_DW + blk * 31 + 31] = dw[:, blk * 128:(blk + 1) * 128].T
        sc = np.asarray(inp["w_sc"], f32)[l]
        for blk in range(2):
            small[:, o + C_SC + blk * 3: o + C_SC + blk * 3 + 3] = sc[:, blk * 128:(blk + 1) * 128].T
        small[0:4, o + C_BF] = np.asarray(inp["b_forget"], f32)[l]
    consts = np.zeros((128, 320), f32)
    consts[:, K_ID:K_ID + 128] = np.eye(128, dtype=f32)
    kk, qq = np.meshgrid(np.arange(128), np.arange(128), indexing="ij")
    consts[:, K_MASK:K_MASK + 128] = np.where(kk > qq, NEG, 0.0)
    for blk in range(2):
        for p in range(128):
            w = POOL_WINDOWS[(blk * 128 + p) // 64]
            for t in range(16):
                consts[p, K_PCOEF + blk * 16 + t] = (1.0 / w if t < w else 0.0) - (1.0 if t == 0 else 0.0)
                consts[p, K_PRATIO + blk * 16 + t] = w / min(t + 1, w)
    return dict(w_in=w_in_r, w_pw=w_pw, w_pbd=w_pbd, w_out=w_out, w_up=w_up, w_dn=w_dn, w_gate=w_gate,
                w_proj=w_proj, small=small, consts=consts)


_NC_CACHE = {}


def run(inp, T, L, n_seq, stop=None, dbg=False):
    f32 = np.float32
    shared = prep_shared(inp, L)
    x = np.asarray(inp["x"], f32)
    p = np.asarray(inp["p"], f32)
    in_maps = []
    for core in range(8):
        bi = core % n_seq
        m = dict(shared)
        m["xT"] = np.ascontiguousarray(x[bi].T)
        m["pT"] = np.ascontiguousarray(p[:L, bi].transpose(0, 2, 1)).reshape(L * 256, T)
        in_maps.append(m)
    key = (T, L, stop, dbg)
    if key not in _NC_CACHE:
        _NC_CACHE[key] = build(T, L, stop, dbg)
    nc = _NC_CACHE[key]
    res = run_bass_kernel_spmd(nc, in_maps, core_ids=list(range(8)))
    if dbg:
        return res.results[0]
    out = np.stack([np.ascontiguousarray(res.results[bi]["outT"].T) for bi in range(n_seq)], axis=0)
    return out.astype(f32)


def kernel(**inputs):
    return run(inputs, 8192, 4, 4)
```

```python
import contextlib
import numpy as np
import concourse.bass as bass
import concourse.mybir as mybir
from concourse.bass_utils import run_bass_kernel_spmd

F32 = mybir.dt.float32
BF16 = mybir.dt.bfloat16
ALU = mybir.AluOpType
AF = mybir.ActivationFunctionType
ENGS = ("sp", "act", "pe", "dve", "pool")
EPS = 1e-6
NEG = -30000.0


class Res:
    __slots__ = ("name", "w", "r", "excl")

    def __init__(self, name="", excl=False):
        self.name = name
        self.w = {}
        self.r = {}
        self.excl = excl


def PR(name):
    return Res(name, excl=True)


def _merge(d, s, v):
    if d.get(s, 0) < v:
        d[s] = v


class Prog:
    def __init__(self, nc, stack, n_dma_sems=32):
        self.nc = nc
        self.q = {e: [] for e in ENGS}
        self.sem_names = []
        self.sem_count = []
        self.seen = {e: {} for e in ENGS}
        self.pending_reads = {e: [] for e in ENGS}
        self.pending_writes = {e: [] for e in ENGS}
        self.eng_sem = {}
        for e in ("act", "pe", "dve", "pool"):
            self.eng_sem[e] = self._new_sem("c_" + e)
        self.dma_sems = {"sp": [self._new_sem("d%d" % i) for i in range(n_dma_sems)],
                         "pool": [self._new_sem("w%d" % i) for i in range(16)]}
        self.dma_rr = {"sp": 0, "pool": 0}
        self.n_ops = 0
        self.sems = [stack.enter_context(nc.semaphore(n)) for n in self.sem_names]

    def _new_sem(self, name):
        self.sem_names.append(name)
        self.sem_count.append(0)
        return len(self.sem_names) - 1

    def op(self, eng, fn, reads=(), writes=(), inc=True, dma=False):
        waits = {}
        for r in reads:
            for s, v in r.w.items():
                _merge(waits, s, v)
            if r.excl:
                for s, v in r.r.items():
                    if not (eng in self.eng_sem and s == self.eng_sem[eng]):
                        _merge(waits, s, v)
        for w in writes:
            for s, v in w.w.items():
                _merge(waits, s, v)
            for s, v in w.r.items():
                _merge(waits, s, v)
        tok = None
        if dma:
            pool_ = self.dma_sems[eng]
            s = pool_[self.dma_rr[eng] % len(pool_)]
            self.dma_rr[eng] += 1
            if self.sem_count[s] > 0:
                _merge(waits, s, self.sem_count[s])
            self.sem_count[s] += 16
            tok = (s, self.sem_count[s], 16)
        elif inc:
            s = self.eng_sem[eng]
            self.sem_count[s] += 1
            tok = (s, self.sem_count[s], 1)
        seen = self.seen[eng]
        wl = []
        for s, v in waits.items():
            if seen.get(s, 0) < v:
                if eng == "pe" and s == self.eng_sem["pe"]:
                    continue
                seen[s] = v
                wl.append((s, v))
        self.q[eng].append((fn, wl, tok))
        self.n_ops += 1
        if tok is None:
            self.pending_reads[eng].extend(reads)
            self.pending_writes[eng].extend(writes)
        else:
            s, v, _ = tok
            rl = list(reads)
            wr = list(writes)
            if not dma:
                rl += self.pending_reads[eng]
                wr += self.pending_writes[eng]
                self.pending_reads[eng] = []
                self.pending_writes[eng] = []
            for r in rl:
                _merge(r.r, s, v)
            for w in wr:
                _merge(w.w, s, v)
        return tok

    def barrier(self):
        for e in ENGS:
            wl = []
            for s, c in enumerate(self.sem_count):
                if c > 0 and self.seen[e].get(s, 0) < c:
                    self.seen[e][s] = c
                    wl.append((s, c))
            if wl:
                self.q[e].append((None, wl, None))

    def flush(self, final=False):
        nc = self.nc
        sems = self.sems
        if final:
            wl = [(s, c) for s, c in enumerate(self.sem_count) if c > 0]
            self.q["sp"].append((None, wl, None))
        with nc.Block() as block:
            def run(eng_name):
                ops = self.q[eng_name]

                def f(e):
                    for fn, wl, tok in ops:
                        for s, v in wl:
                            e.wait_ge(sems[s], v)
                        if fn is None:
                            continue
                        ins = fn(e)
                        if tok is not None:
                            ins.then_inc(sems[tok[0]], tok[2])
                return f
            block.sync(run("sp"))
            block.scalar(run("act"))
            block.tensor(run("pe"))
            block.vector(run("dve"))
            block.gpsimd(run("pool"))
        self.q = {e: [] for e in ENGS}


class Ring:
    def __init__(self, items):
        self.items = items
        self.i = 0

    def next(self):
        it = self.items[self.i % len(self.items)]
        self.i += 1
        return it


C_G = {"mix_pre": 0, "mix_post": 8, "mlp_pre": 16, "mlp_post": 24, "ple_pre": 32, "ple_post": 40}
C_LNG, C_LNB, C_PSC, C_DW, C_SC, C_BF = 48, 50, 52, 54, 116, 122
K_ID, K_MASK, K_PCOEF, K_PRATIO = 0, 128, 256, 288


def build(T, L, stop=None, dbg=False):
    NT = T // 512
    NB = T // 128
    nc = bass.Bass("TRN2", target_bir_lowering=False)

    def din(name, shape, dt=F32):
        return nc.dram_tensor(name, shape, dt, kind="ExternalInput").ap()

    def dscr(name, shape, dt):
        return nc.dram_tensor(name, shape, dt, kind="ExternalOutput" if dbg else "Internal").ap()

    xT = din("xT", [1024, T])
    pT = din("pT", [L * 256, T])
    w_in = din("w_in", [L * 1024, 2308])
    w_pw = din("w_pw", [L * 256, 256])
    w_pbd = din("w_pbd", [L * 256, 128])
    w_out = din("w_out", [L * 1024, 1024])
    w_up = din("w_up", [L * 1024, 4096])
    w_dn = din("w_dn", [L * 1024, 4096])
    w_gate = din("w_gate", [L * 1024, 1024])
    w_proj = din("w_proj", [L * 256, 1024])
    small = din("small", [128, L * 128])
    consts = din("consts", [128, 320])
    outT = nc.dram_tensor("outT", [1024, T], F32, kind="ExternalOutput").ap()

    hT = dscr("hT", [1024, T], F32)
    wup_bf = dscr("wup_bf", [L * 1024, 4096], BF16)
    wdn_bf = dscr("wdn_bf", [L * 1024, 4096], BF16)
    mixin = dscr("mixin", [1024, 32 + T], BF16)
    qT = dscr("qT", [256, T], BF16)
    kT = dscr("kT", [256, T], BF16)
    c3d = dscr("c3d", [4, 3, T], BF16)
    vaug = dscr("vaug", [NB, 128, 512], BF16)
    ymix = dscr("ymix", [1024, T], BF16)

    R_h = [Res("h%d" % j) for j in range(NT)]
    R_mixin = [Res("mi%d" % j) for j in range(NT + 1)]
    R_q = [Res("q%d" % j) for j in range(NT)]
    R_k = [Res("k%d" % j) for j in range(NT)]
    R_c3 = [Res("c3%d" % j) for j in range(NT)]
    R_v = [Res("v%d" % j) for j in range(NT)]
    R_ymA = [Res("ymA%d" % j) for j in range(NT)]
    R_ymB = [Res("ymB%d" % j) for j in range(NT)]
    R_wup = [[Res("wu%d_%d" % (l, g)) for g in range(8)] for l in range(L)]
    R_wdn = [[Res("wd%d_%d" % (l, g)) for g in range(8)] for l in range(L)]

    top = contextlib.ExitStack()
    with top:
        P = Prog(nc, top)

        uid = [0]

        def sbt(st, name, shape, dt):
            uid[0] += 1
            return st.enter_context(nc.sbuf_tensor("%s_u%d" % (name, uid[0]), shape, dt))

        def pst(st, name, shape=(128, 512), dt=F32):
            uid[0] += 1
            return st.enter_context(nc.psum_tensor("%s_u%d" % (name, uid[0]), list(shape), dt))

        small_sb = sbt(top, "small_sb", [128, L * 128], F32)
        consts_sb = sbt(top, "consts_sb", [128, 320], F32)
        ident_bf = sbt(top, "ident_bf", [128, 128], BF16)
        mask_bf = sbt(top, "mask_bf", [128, 128], BF16)
        ones_bf = sbt(top, "ones_bf", [128, 128], BF16)
        ones_f = sbt(top, "ones_f", [128, 128], F32)
        cneg = sbt(top, "cneg", [128, NB, 4], F32)
        R_small, R_consts, R_cst2, R_cneg = Res("small"), Res("consts"), Res("cst2"), Res("cneg")
        ident_f = consts_sb[:, K_ID:K_ID + 128]

        def scol(l, c, rows=slice(0, 128)):
            return small_sb[rows, l * 128 + c: l * 128 + c + 1]

        def dma(eng, out, in_, reads, writes):
            P.op(eng, lambda e: e.dma_start(out=out, in_=in_), reads=reads, writes=writes, dma=True)

        def mm(out, lhsT, rhs, start, stop, reads, writes):
            P.op("pe", lambda e: e.matmul(out, lhsT=lhsT, rhs=rhs, start=start, stop=stop),
                 reads=reads, writes=writes, inc=stop)

        def act(out, in_, func, reads, writes, bias=None, scale=None):
            kw = {}
            if bias is not None:
                kw["bias"] = bias
            if scale is not None:
                kw["scale"] = scale
            P.op("act", lambda e: e.activation(out, in_, func, **kw), reads=reads, writes=writes)

        def tt(eng, out, in0, in1, op, reads, writes):
            P.op(eng, lambda e: e.tensor_tensor(out, in0, in1, op), reads=reads, writes=writes)

        def stt(eng, out, in0, scalar, in1, op0, op1, reads, writes):
            P.op(eng, lambda e: e.scalar_tensor_tensor(out, in0, scalar, in1, op0, op1), reads=reads, writes=writes)

        def norm_scale(eng, out, in0, gcol, rstd_ap, reads, writes, ptmp_ring):
            if eng == "dve":
                stt("dve", out, in0, gcol, rstd_ap, ALU.mult, ALU.mult, reads, writes)
            else:
                tb, Rtb = ptmp_ring.next()
                tt("pool", tb[:], in0, rstd_ap, ALU.mult, reads, [Rtb])
                P.op("pool", lambda e: e.tensor_scalar_mul(out, tb[:], gcol), reads=[Rtb] + list(reads), writes=writes)

        def ts(eng, out, in0, s1, s2, op0, op1, reads, writes):
            if s2 is None:
                P.op(eng, lambda e: e.tensor_single_scalar(out, in0, s1, op0), reads=reads, writes=writes)
            else:
                P.op(eng, lambda e: e.tensor_scalar(out, in0, s1, s2, op0, op1), reads=reads, writes=writes)

        def cp(eng, out, in_, reads, writes):
            if eng == "act":
                P.op("act", lambda e: e.copy(out, in_), reads=reads, writes=writes)
            else:
                P.op(eng, lambda e: e.tensor_copy(out, in_), reads=reads, writes=writes)

        def memset(eng, ap, val, writes):
            P.op(eng, lambda e: e.memset(ap, val), writes=writes)

        def recip(out, in_, reads, writes):
            P.op("dve", lambda e: e.reciprocal(out, in_), reads=reads, writes=writes)

        def rms_stats(sq, R_sq, ps_stat, R_ps, rs, rstd, R_rs, R_rstd):
            for c in range(8):
                mm(ps_stat[:], ones_bf[:], sq[:, c, :], c == 0, c == 7, list(R_sq) + [R_cst2], [R_ps])
            act(rs[:], ps_stat[:], AF.Sqrt, [R_ps], [R_rs], bias=EPS, scale=1.0 / 1024.0)
            recip(rstd[:], rs[:], [R_rs], [R_rstd])

        with contextlib.ExitStack() as st:
            zt = sbt(st, "zt", [128, 8, 32], BF16)
            R_zt = Res("zt")
            dma("sp", small_sb[:], small, [], [R_small])
            dma("sp", consts_sb[:], consts, [], [R_consts])
            cp("dve", ident_bf[:], consts_sb[:, K_ID:K_ID + 128], [R_consts], [R_cst2])
            cp("dve", mask_bf[:], consts_sb[:, K_MASK:K_MASK + 128], [R_consts], [R_cst2])
            memset("pool", ones_bf[:], 1.0, [R_cst2])
            memset("pool", ones_f[:], 1.0, [R_cst2])
            memset("pool", zt[:], 0.0, [R_zt])
            dma("sp", mixin.rearrange("(c p) n -> p c n", p=128)[:, :, 0:32], zt[:], [R_zt], [R_mixin[0]])
            P.barrier()
            P.flush(final=(stop == "pro"))
        if stop == "pro":
            return nc

        def cast_mlp_weights(l):
            for g in range(8):
                for (src, dst, RR) in ((w_up, wup_bf, R_wup), (w_dn, wdn_bf, R_wdn)):
                    r0 = l * 1024 + g * 128
                    for hh in range(2):
                        dma("pool", dst[r0:r0 + 128, hh * 2048:(hh + 1) * 2048],
                            src[r0:r0 + 128, hh * 2048:(hh + 1) * 2048], [], [RR[l][g]])

        for l in range(L):
            h_src = xT if l == 0 else hT
            h_dst = outT if l == L - 1 else hT
            h_src_v = h_src.rearrange("(c p) n -> p c n", p=128)
            h_dst_v = h_dst.rearrange("(c p) n -> p c n", p=128)

            with contextlib.ExitStack() as st:
                win = sbt(st, "win", [128, 8, 2308], BF16)
                R_win = Res("win")
                ht = [sbt(st, "ht%d" % i, [128, 8, 512], F32) for i in range(2)]
                R_ht = [Res("ht0"), Res("ht1")]
                sq = sbt(st, "sq", [128, 8, 512], BF16)
                R_sq = Res("sq")
                xn2 = [sbt(st, "xn%d" % i, [128, 8, 512], BF16) for i in range(2)]
                R_xn2 = [Res("xn0"), Res("xn1")]
                rs = sbt(st, "rs", [128, 512], F32)
                rstd = sbt(st, "rstd", [128, 512], F32)
                R_rs, R_rstd = Res("rs"), Res("rstd")
                tmpf = [sbt(st, "tmpf%d" % i, [128, 512], F32) for i in range(2)]
                tmp_ring = Ring([(tmpf[i], Res("tmpf%d" % i)) for i in range(2)])
                ptmp_ring = Ring([(sbt(st, "ptmp%d" % i, [128, 512], F32), Res("ptmp%d" % i)) for i in range(2)])
                mstage = [sbt(st, "mstage%d" % i, [128, 8, 512], BF16) for i in range(2)]
                R_mst = [Res("mst0"), Res("mst1")]
                qstage = [sbt(st, "qstage%d" % i, [128, 2, 512], BF16) for i in range(2)]
                R_qst = [Res("qst0"), Res("qst1")]
                kstage = [sbt(st, "kstage%d" % i, [128, 2, 512], BF16) for i in range(2)]
                R_kst = [Res("kst0"), Res("kst1")]
                vstage = [sbt(st, "vstage%d" % i, [128, 4, 512], BF16) for i in range(2)]
                R_vst = [Res("vst0"), Res("vst1")]
                xb = sbt(st, "xb", [4, 512], F32)
                ef = sbt(st, "ef", [4, 512], F32)
                lf = sbt(st, "lf", [4, 512], F32)
                ones4 = sbt(st, "ones4", [4, 512], F32)
                cc = [sbt(st, "cc%d" % i, [4, 512], F32) for i in range(2)]
                r1 = sbt(st, "r1", [4, 512], F32)
                r2 = sbt(st, "r2", [4, 512], F32)
                c3 = [sbt(st, "c3_%d" % i, [4, 3, 512], BF16) for i in range(2)]
                R_f = Res("fmisc")
                R_cc = [Res("cc0"), Res("cc1")]
                R_c3s = [Res("c3s0"), Res("c3s1")]
                ps_stat = pst(st, "ps_stat")
                R_pstat = PR("ps_stat")
                ps_ring = Ring([(pst(st, "psr%d" % i), PR("psr%d" % i)) for i in range(4)])
                ps_v = Ring([(pst(st, "psv%d" % i), PR("psv%d" % i)) for i in range(2)])
                ps_t = pst(st, "ps_t")
                R_pst = PR("ps_t")

                for k in range(8):
                    for hh in range(2):
                        dma("pool", win[:, k, hh * 1154:(hh + 1) * 1154],
                            w_in[l * 1024 + k * 128: l * 1024 + (k + 1) * 128, hh * 1154:(hh + 1) * 1154],
                            [], [R_win])
                cast_mlp_weights(l)
                memset("pool", ones4[:], 1.0, [R_f])
                for i in range(2):
                    memset("pool", vstage[i][:], 1.0, [R_vst[i]])

                def load_h(j):
                    dma("sp", ht[j % 2][:], h_src_v[:, :, j * 512:(j + 1) * 512], [R_h[j]], [R_ht[j % 2]])

                cur = {}

                def proj(ps, R_ps, col0, M):
                    xn, R_xn = cur["xn"], cur["R_xn"]
                    for k in range(8):
                        mm(ps[0:M, :], win[:, k, col0:col0 + M], xn[:, k, :], k == 0, k == 7, [R_win, R_xn], [R_ps])

                load_h(0)
                for j in range(NT):
                    b = j % 2
                    if j + 1 < NT:
                        load_h(j + 1)
                    h = ht[b]
                    cols = slice(j * 512, (j + 1) * 512)
                    xnb, R_xnb = xn2[b], R_xn2[b]
                    xn, R_xn = xnb, R_xnb
                    cur["xn"], cur["R_xn"] = xnb, R_xnb
                    act(sq[:], h[:], AF.Square, [R_ht[b]], [R_sq])
                    rms_stats(sq, [R_sq], ps_stat, R_pstat, rs, rstd, R_rs, R_rstd)
                    for c in range(8):
                        norm_scale("dve", xnb[:, c, :], h[:, c, :], scol(l, C_G["mix_pre"] + c),
                                   rstd[:], [R_ht[b], R_rstd, R_small], [R_xnb], ptmp_ring)
                    for blk in range(2):
                        pa, Rpa = ps_ring.next()
                        pb, Rpb = ps_ring.next()
                        proj(pa, Rpa, (0 + blk) * 128, 128)
                        proj(pb, Rpb, (2 + blk) * 128, 128)
                        tb, Rtb = tmp_ring.next()
                        act(tb[:], pb[:], AF.Sigmoid, [Rpb], [Rtb])
                        tt("dve", mstage[b][:, 0 + blk, :], pa[:], tb[:], ALU.mult, [Rpa, Rtb], [R_mst[b]])
                    for blk in range(2):
                        pa, Rpa = ps_ring.next()
                        pb, Rpb = ps_ring.next()
                        proj(pa, Rpa, (8 + blk) * 128, 128)
                        proj(pb, Rpb, (12 + blk) * 128, 128)
                        tb, Rtb = tmp_ring.next()
                        cp("act", tb[:], pb[:], [Rpb], [Rtb])
                        tt("dve", mstage[b][:, 2 + blk, :], pa[:], tb[:], ALU.mult, [Rpa, Rtb], [R_mst[b]])
                    for blk in range(2):
                        pa, Rpa = ps_ring.next()
                        proj(pa, Rpa, (14 + blk) * 128, 128)
                        cp("dve", mstage[b][:, 4 + blk, :], pa[:], [Rpa], [R_mst[b]])
                        pb, Rpb = ps_ring.next()
                        proj(pb, Rpb, (10 + blk) * 128, 128)
                        cp("act", mstage[b][:, 6 + blk, :], pb[:], [Rpb], [R_mst[b]])
                    for blk in range(2):
                        pa, Rpa = ps_ring.next()
                        proj(pa, Rpa, (4 + blk) * 128, 128)
                        act(qstage[b][:, blk, :], pa[:], AF.Copy, [Rpa], [R_qst[b]], scale=0.125)
                        pb, Rpb = ps_ring.next()
                        proj(pb, Rpb, (6 + blk) * 128, 128)
                        cp("dve", kstage[b][:, blk, :], pb[:], [Rpb], [R_kst[b]])
                    pf, Rpf = ps_ring.next()
                    proj(pf, Rpf, 2048, 4)
                    ts("dve", xb[:], pf[0:4, :], scol(l, C_BF, slice(0, 4)), None, ALU.add, None, [Rpf, R_small], [R_f])
                    act(ef[:], xb[:], AF.Exp, [R_f], [R_f], scale=-1.0)
                    act(lf[:], ef[:], AF.Ln, [R_f], [R_f], bias=1.0, scale=1.0)
                    init = 0.0 if j == 0 else cc[1 - b][:, 511:512]
                    P.op("dve", (lambda o, d0, d1, ini: (lambda e: e.tensor_tensor_scan(o, d0, d1, ini, ALU.mult, ALU.subtract)))(
                        cc[b][:], ones4[:], lf[:], init), reads=[R_f, R_cc[1 - b]], writes=[R_cc[b]])
                    cp("dve", c3[b][:, 0, :], cc[b][:], [R_cc[b]], [R_c3s[b]])
                    tt("dve", r1[:], cc[b][:], c3[b][:, 0, :], ALU.subtract, [R_cc[b], R_c3s[b]], [R_f])
                    cp("dve", c3[b][:, 1, :], r1[:], [R_f], [R_c3s[b]])
                    tt("dve", r2[:], r1[:], c3[b][:, 1, :], ALU.subtract, [R_f, R_c3s[b]], [R_f])
                    cp("dve", c3[b][:, 2, :], r2[:], [R_f], [R_c3s[b]])
                    for s in range(4):
                        P.op("pe", (lambda o, a, bb: (lambda e: e.matmul(o, lhsT=a, rhs=bb, start=True, stop=True)))(
                            ps_t[:, s * 4:s * 4 + 4], cc[b][0:4, s * 128:(s + 1) * 128], consts_sb[0:4, K_ID:K_ID + 4]),
                            reads=[R_cc[b], R_consts], writes=[R_pst], inc=True)
                    ts("dve", cneg[:, j * 4:(j + 1) * 4, :], ps_t[:, 0:16].rearrange("p (s h) -> p s h", s=4), -1.0, None,
                       ALU.mult, None, [R_pst], [R_cneg])
                    for s in range(4):
                        pv, Rpv = ps_v.next()
                        for k in range(8):
                            mm(pv[:, 0:256], xn[:, k, s * 128:(s + 1) * 128], win[:, k, 2052:2308], k == 0, k == 7,
                               [R_win, R_xn], [Rpv])
                        dst = vstage[b][:, s, :].rearrange("p (h c) -> p h c", h=4)[:, :, 0:64]
                        src = pv[:, 0:256].rearrange("p (h c) -> p h c", h=4)
                        cp("dve" if s % 2 == 0 else "act", dst, src, [Rpv], [R_vst[b]])
                    dma("sp", mixin.rearrange("(c p) n -> p c n", p=128)[:, :, 32 + j * 512: 32 + (j + 1) * 512],
                        mstage[b][:], [R_mst[b]], [R_mixin[j + 1]])
                    dma("sp", qT.rearrange("(c p) n -> p c n", p=128)[:, :, cols], qstage[b][:], [R_qst[b]], [R_q[j]])
                    dma("sp", kT.rearrange("(c p) n -> p c n", p=128)[:, :, cols], kstage[b][:], [R_kst[b]], [R_k[j]])
                    dma("sp", c3d[:, :, cols], c3[b][:], [R_c3s[b]], [R_c3[j]])
                    dma("sp", vaug[j * 4:(j + 1) * 4].rearrange("s p c -> p s c"), vstage[b][:], [R_vst[b]], [R_v[j]])
                P.barrier()
                P.flush(final=(stop == "A1"))
            if stop == "A1":
                return nc

            with contextlib.ExitStack() as st:
                dconf = sbt(st, "dconf", [128, 2, 31, 128], BF16)
                dsc = sbt(st, "dsc", [128, 2, 3, 128], BF16)
                dpl = sbt(st, "dpl", [128, 2, 16, 128], BF16)
                R_dg = Res("diag")
                wpw = sbt(st, "wpw", [128, 2, 256], BF16)
                wpbd = sbt(st, "wpbd", [128, 2, 128], BF16)
                R_wA2 = Res("wA2")
                mt = [sbt(st, "mt%d" % i, [128, 8, 544], BF16) for i in range(2)]
                R_mt = [Res("mt0"), Res("mt1")]
                xc = sbt(st, "xc", [128, 2, 512], F32)
                sq2 = sbt(st, "sq2", [128, 2, 512], F32)
                R_xc, R_sq2 = Res("xc"), Res("sq2")
                mean = sbt(st, "mean", [128, 512], F32)
                msq = sbt(st, "msq", [128, 512], F32)
                var = sbt(st, "var", [128, 512], F32)
                sd = sbt(st, "sd", [128, 512], F32)
                rstd2 = sbt(st, "rstd2", [128, 512], F32)
                R_mean, R_msq, R_var, R_sd, R_rstd2 = Res("mean"), Res("msq"), Res("var"), Res("sd"), Res("rstd2")
                xm = [sbt(st, "xm%d" % i, [128, 512], F32) for i in range(2)]
                R_xm = [Res("xm0"), Res("xm1")]
                xnn = [sbt(st, "xnn%d" % i, [128, 512], F32) for i in range(2)]
                R_xnn = [Res("xnn0"), Res("xnn1")]
                sconf = sbt(st, "sconf", [128, 2, 512], BF16)
                R_sconf = Res("sconf")
                dpool = sbt(st, "dpool", [128, 2, 512], BF16)
                R_dpool = Res("dpool")
                t16 = sbt(st, "t16", [128, 2, 16], F32)
                R_t16 = Res("t16")
                ystage = [sbt(st, "ystage%d" % i, [128, 6, 512], BF16) for i in range(2)]
                R_yst = [Res("yst0"), Res("yst1")]
                psc = [pst(st, "psc%d" % i) for i in range(2)]
                R_psc = [PR("psc0"), PR("psc1")]
                ps_s1 = pst(st, "ps_s1")
                ps_s2 = pst(st, "ps_s2")
                R_s1, R_s2 = PR("s1"), PR("s2")
                ps_ring = Ring([(pst(st, "psq%d" % i), PR("psq%d" % i)) for i in range(3)])

                dma("pool", wpw[:], w_pw[l * 256:(l + 1) * 256, :].rearrange("(c p) n -> p c n", p=128), [], [R_wA2])
                dma("pool", wpbd[:], w_pbd[l * 256:(l + 1) * 256, :].rearrange("(c p) n -> p c n", p=128), [], [R_wA2])
                for blk in range(2):
                    for k in range(31):
                        P.op("pool", (lambda o, s: (lambda e: e.tensor_scalar_mul(o, ident_f, s)))(
                            dconf[:, blk, k, :], scol(l, C_DW + blk * 31 + k)), reads=[R_consts, R_small], writes=[R_dg])
                    for k in range(3):
                        P.op("pool", (lambda o, s: (lambda e: e.tensor_scalar_mul(o, ident_f, s)))(
                            dsc[:, blk, k, :], scol(l, C_SC + blk * 3 + k)), reads=[R_consts, R_small], writes=[R_dg])
                    for k in range(16):
                        P.op("pool", (lambda o, s: (lambda e: e.tensor_scalar_mul(o, ident_f, s)))(
                            dpl[:, blk, k, :], consts_sb[:, K_PCOEF + blk * 16 + k: K_PCOEF + blk * 16 + k + 1]),
                            reads=[R_consts], writes=[R_dg])

                def load_mt(j):
                    dma("sp", mt[j % 2][:], mixin.rearrange("(c p) n -> p c n", p=128)[:, :, j * 512: j * 512 + 544],
                        [R_mixin[j], R_mixin[j + 1]], [R_mt[j % 2]])

                load_mt(0)
                sect = stop.split(":")[1] if (stop and ":" in stop) else "all"

                def on(*names):
                    return sect == "all" or sect in names

                for j in range(NT):
                    b = j % 2
                    if j + 1 < NT:
                        load_mt(j + 1)
                    m_ = mt[b]
                    cols = slice(j * 512, (j + 1) * 512)
                    if sect == "conv1":
                        for blk in range(2):
                            for k in range(31):
                                mm(psc[blk][:], dconf[:, blk, k, :], m_[:, blk, 2 + k: 2 + k + 512], k == 0, k == 30,
                                   [R_dg, R_mt[b]], [R_psc[blk]])
                            cp("dve", xc[:, blk, :], psc[blk][:], [R_psc[blk]], [R_xc])
                    if sect == "conv2":
                        for blk in range(2):
                            for k in range(3):
                                mm(psc[blk][:], dconf[:, blk, k, :], m_[:, blk, 2 + k: 2 + k + 512], k == 0, k == 2,
                                   [R_dg, R_mt[b]], [R_psc[blk]])
                            act(sq2[:, blk, :], psc[blk][:], AF.Square, [R_psc[blk]], [R_sq2])
                    if sect == "pool1":
                        for blk in range(2):
                            pp, Rpp = ps_ring.next()
                            for k in range(16):
                                mm(pp[:], dpl[:, blk, k, :], m_[:, 4 + blk, 32 - k: 32 - k + 512], k == 0, k == 15,
                                   [R_dg, R_mt[b]], [Rpp])
                            cp("dve", dpool[:, blk, :], pp[:], [Rpp], [R_dpool])
                    if on("conv", "ln", "silu", "conf"):
                        for blk in range(2):
                            for k in range(31):
                                mm(psc[blk][:], dconf[:, blk, k, :], m_[:, blk, 2 + k: 2 + k + 512], k == 0, k == 30,
                                   [R_dg, R_mt[b]], [R_psc[blk]])
                            cp("dve", xc[:, blk, :], psc[blk][:], [R_psc[blk]], [R_xc])
                            act(sq2[:, blk, :], xc[:, blk, :], AF.Square, [R_xc], [R_sq2])
                    if on("sc"):
                        for blk in range(2):
                            pp, Rpp = ps_ring.next()
                            for k in range(3):
                                mm(pp[:], dsc[:, blk, k, :], m_[:, 2 + blk, 30 + k: 30 + k + 512], k == 0, k == 2,
                                   [R_dg, R_mt[b]], [Rpp])
                            tt("dve", ystage[b][:, 2 + blk, :], pp[:], m_[:, 6 + blk, 32:544], ALU.mult, [Rpp, R_mt[b]], [R_yst[b]])
                    if on("pool"):
                        for blk in range(2):
                            pp, Rpp = ps_ring.next()
                            for k in range(16):
                                mm(pp[:], dpl[:, blk, k, :], m_[:, 4 + blk, 32 - k: 32 - k + 512], k == 0, k == 15,
                                   [R_dg, R_mt[b]], [Rpp])
                            cp("act", dpool[:, blk, :], pp[:], [Rpp], [R_dpool])
                            if j == 0:
                                tt("dve", t16[:, blk, :], pp[:, 0:16], m_[:, 4 + blk, 32:48], ALU.add, [Rpp, R_mt[b]], [R_t16])
                                tt("pool", t16[:, blk, :], t16[:, blk, :],
                                   consts_sb[:, K_PRATIO + blk * 16: K_PRATIO + blk * 16 + 16], ALU.mult, [R_t16, R_consts], [R_t16])
                                tt("dve", dpool[:, blk, 0:16], t16[:, blk, :], m_[:, 4 + blk, 32:48], ALU.subtract,
                                   [R_t16, R_mt[b]], [R_dpool])
                        for blk in range(2):
                            pp, Rpp = ps_ring.next()
                            mm(pp[:], wpbd[:, blk, :], dpool[:, blk, :], True, True, [R_wA2, R_dpool], [Rpp])
                            act(ystage[b][:, 4 + blk, :], pp[:], AF.Copy, [Rpp, R_small], [R_yst[b]], scale=scol(l, C_PSC + blk))
                    if on("ln", "silu", "conf"):
                        for blk in range(2):
                            mm(ps_s1[:], ones_f[:], xc[:, blk, :], blk == 0, blk == 1, [R_xc, R_cst2], [R_s1])
                        for blk in range(2):
                            mm(ps_s2[:], ones_f[:], sq2[:, blk, :], blk == 0, blk == 1, [R_sq2, R_cst2], [R_s2])
                        act(mean[:], ps_s1[:], AF.Copy, [R_s1], [R_mean], scale=1.0 / 256.0)
                        tt("pool", msq[:], mean[:], mean[:], ALU.mult, [R_mean], [R_msq])
                        stt("dve", var[:], ps_s2[:], 1.0 / 256.0, msq[:], ALU.mult, ALU.subtract, [R_s2, R_msq], [R_var])
                        act(sd[:], var[:], AF.Sqrt, [R_var], [R_sd], bias=EPS, scale=1.0)
                        recip(rstd2[:], sd[:], [R_sd], [R_rstd2])
                        for blk in range(2):
                            tt("dve", xm[blk][:], xc[:, blk, :], mean[:], ALU.subtract, [R_xc, R_mean], [R_xm[blk]])
                            tt("pool", xnn[blk][:], xm[blk][:], rstd2[:], ALU.mult, [R_xm[blk], R_rstd2], [R_xnn[blk]])
                    if on("silu", "conf"):
                        for blk in range(2):
                            act(sconf[:, blk, :], xnn[blk][:], AF.Silu, [R_xnn[blk], R_small], [R_sconf],
                                bias=scol(l, C_LNB + blk), scale=scol(l, C_LNG + blk))
                    if on("conf"):
                        for oc in range(2):
                            pp, Rpp = ps_ring.next()
                            for kb in range(2):
                                mm(pp[:], wpw[:, kb, oc * 128:(oc + 1) * 128], sconf[:, kb, :], kb == 0, kb == 1,
                                   [R_wA2, R_sconf], [Rpp])
                            cp("dve", ystage[b][:, oc, :], pp[:], [Rpp], [R_yst[b]])
                    ym_v = ymix.rearrange("(c p) n -> p c n", p=128)
                    dma("sp", ym_v[:, 0:2, cols], ystage[b][:, 0:2, :], [R_yst[b]], [R_ymA[j]])
                    dma("sp", ym_v[:, 4:8, cols], ystage[b][:, 2:6, :], [R_yst[b]], [R_ymA[j]])
                P.barrier()
                P.flush(final=(stop is not None and stop.startswith("A2")))
            if stop is not None and stop.startswith("A2"):
                return nc

            with contextlib.ExitStack() as st:
                kaug = sbt(st, "kaug", [128, 4, T], BF16)
                R_kaug = Res("kaug")
                vsb = sbt(st, "vsb", [128, NB, 512], BF16)
                R_vsb = [Res("vsb%d" % j) for j in range(NT)]
                qa = [sbt(st, "qa%d" % i, [128, 4, 512], BF16) for i in range(2)]
                R_qa = [Res("qa0"), Res("qa1")]
                pts = Ring([(sbt(st, "pt%d" % i, [128, 512], BF16), Res("pt%d" % i)) for i in range(4)])
                rec = sbt(st, "rec", [128, 512], F32)
                R_rec = Res("rec")
                bcs = sbt(st, "bcs", [64, 512], F32)
                R_bcs = Res("bcs")
                yst = Ring([(sbt(st, "ysb%d" % i, [64, 512], BF16), Res("ysb%d" % i)) for i in range(2)])
                ps_s = Ring([(pst(st, "pss%d" % i), PR("pss%d" % i)) for i in range(3)])
                ps_o = Ring([(pst(st, "pso%d" % i), PR("pso%d" % i)) for i in range(2)])
                ps_bc = pst(st, "ps_bc")
                R_bc = PR("ps_bc")

                memset("pool", kaug[64:96, :, :], 0.0, [R_kaug])
                memset("pool", kaug[64:67, :, :], 1.0, [R_kaug])
                for i in range(2):
                    memset("pool", qa[i][64:96, :, :], 0.0, [R_qa[i]])
                for j in range(NT):
                    cols = slice(j * 512, (j + 1) * 512)
                    dma("sp", kaug[0:64, :, cols], kT.rearrange("(h r) n -> r h n", r=64)[:, :, cols], [R_k[j]], [R_kaug])
                    dma("sp", vsb[:, j * 4:(j + 1) * 4, :], vaug[j * 4:(j + 1) * 4].rearrange("s p c -> p s c"),
                        [R_v[j]], [R_vsb[j]])

                def load_q(j):
                    cols = slice(j * 512, (j + 1) * 512)
                    dma("sp", qa[j % 2][0:64, :, :], qT.rearrange("(h r) n -> r h n", r=64)[:, :, cols], [R_q[j]], [R_qa[j % 2]])
                    dma("sp", qa[j % 2][64:67, :, :], c3d.rearrange("h j n -> j h n")[:, :, cols], [R_c3[j]], [R_qa[j % 2]])

                load_q(0)
                for j in range(NT):
                    b = j % 2
                    if j + 1 < NT:
                        load_q(j + 1)
                    cols = slice(j * 512, (j + 1) * 512)
                    for h in range(4):
                        nblk = 4 * j + 4
                        po, Rpo = ps_o.next()

                        def qk(i):
                            dj = i - 4 * j
                            c0 = 128 * dj if dj >= 0 else 0
                            pS, RpS = ps_s.next()
                            mm(pS[:, c0:512], kaug[0:96, h, i * 128:(i + 1) * 128], qa[b][0:96, h, c0:512],
                               True, dj < 0, [R_kaug, R_qa[b]], [RpS])
                            if dj >= 0:
                                mm(pS[:, c0:c0 + 128], ident_bf[:], mask_bf[:], False, True, [R_cst2], [RpS])
                            return pS, RpS, c0

                        nxt = qk(0)
                        for i in range(nblk):
                            pS, RpS, c0 = nxt
                            if i + 1 < nblk:
                                nxt = qk(i + 1)
                            pt, Rpt = pts.next()
                            act(pt[:, c0:512], pS[:, c0:512], AF.Exp, [RpS, R_cneg], [Rpt], bias=cneg[:, i, h:h + 1], scale=1.0)
                            mm(po[:, c0:512], vsb[:, i, h * 128:(h + 1) * 128], pt[:, c0:512],
                               i == 0, i == nblk - 1, [R_vsb[i // 4], Rpt], [Rpo])
                        recip(rec[64:128, :], po[64:128, :], [Rpo], [R_rec])
                        ys, Rys = yst.next()
                        tt("dve", ys[:], po[0:64, :], rec[64:128, :], ALU.mult, [Rpo, R_rec], [Rys])
                        dma("sp", ymix[256 + 64 * h: 256 + 64 * h + 64, cols], ys[:], [Rys], [R_ymB[j]])
                P.barrier()
                P.flush(final=(stop == "B"))
            if stop == "B":
                return nc

            with contextlib.ExitStack() as st:
                wout = sbt(st, "wout", [128, 8, 1024], BF16)
                wgate = sbt(st, "wgate", [128, 8, 1024], BF16)
                wproj = sbt(st, "wproj", [128, 2, 1024], BF16)
                R_wC = Res("wC")
                wu = [sbt(st, "wu%d" % i, [128, 8, 512], BF16) for i in range(2)]
                wd = [sbt(st, "wd%d" % i, [128, 4, 1024], BF16) for i in range(2)]
                R_wu = [Res("wus%d" % i) for i in range(2)]
                R_wd = [Res("wds%d" % i) for i in range(2)]
                ymt = sbt(st, "ymt", [128, 8, 512], BF16)
                R_ymt = Res("ymt")
                ht = [sbt(st, "hc%d" % i, [128, 8, 512], F32) for i in range(2)]
                R_ht = [Res("hc0"), Res("hc1")]
                ptl = [sbt(st, "ptl%d" % i, [128, 2, 512], BF16) for i in range(2)]
                R_ptl = [Res("ptl0"), Res("ptl1")]
                mb = [sbt(st, "mb%d" % i, [128, 8, 512], F32) for i in range(2)]
                R_mb = [Res("mb0"), Res("mb1")]
                hnD = [sbt(st, "hnD%d" % i, [128, 8, 512], BF16) for i in range(2)]
                R_hnD = [Res("hnD0"), Res("hnD1")]
                hnE = sbt(st, "hnE", [128, 8, 512], BF16)
                R_hnE = Res("hnE")
                sqg = sbt(st, "sqg", [128, 8, 512], BF16)
                R_sqg = Res("sqg")
                ag = sbt(st, "ag", [128, 8, 512], BF16)
                R_ag = [Res("ag0"), Res("ag1")]
                rs = sbt(st, "rs_c", [128, 512], F32)
                rstd = sbt(st, "rstd_c", [128, 512], F32)
                R_rs, R_rstd = Res("rs"), Res("rstd")
                tmpf = Ring([(sbt(st, "tmpc%d" % i, [128, 512], F32), Res("tmpc%d" % i)) for i in range(2)])
                rl = Ring([(sbt(st, "rl%d" % i, [128, 512], BF16), Res("rl%d" % i)) for i in range(2)])
                ps_stat = pst(st, "ps_stat_c")
                R_pstat = PR("ps_stat_c")
                ps_ring = Ring([(pst(st, "psm%d" % i), PR("psm%d" % i)) for i in range(3)])
                ps_dn = Ring([(pst(st, "psd%d" % i), PR("psd%d" % i)) for i in range(4)])

                for k in range(8):
                    dma("pool", wout[:, k, :], w_out[l * 1024 + k * 128: l * 1024 + (k + 1) * 128, :], [], [R_wC])
                for k in range(8):
                    dma("pool", wgate[:, k, :], w_gate[l * 1024 + k * 128: l * 1024 + (k + 1) * 128, :], [], [R_wC])
                for k in range(2):
                    dma("pool", wproj[:, k, :], w_proj[l * 256 + k * 128: l * 256 + (k + 1) * 128, :], [], [R_wC])

                def load_wu(n):
                    if n >= NT * 8:
                        return
                    g = n % 8
                    r0 = l * 1024 + g * 128
                    dma("sp", wu[n % 2][:], wup_bf[r0:r0 + 128, :].rearrange("p (k c) -> p k c", k=8), [R_wup[l][g]], [R_wu[n % 2]])

                def load_wd(n):
                    if n >= NT * 8:
                        return
                    g = n % 8
                    r0 = l * 1024 + g * 128
                    dma("sp", wd[n % 2][:], wdn_bf[r0:r0 + 128, :].rearrange("p (k c) -> p k c", k=4), [R_wdn[l][g]], [R_wd[n % 2]])

                def sp(n):
                    for _ in range(n):
                        yield

                def gen_post(m, R_m, h, R_h_, gname):
                    act(sqg[:], m[:], AF.Square, [R_m], [R_sqg])
                    yield from sp(4)
                    rms_stats(sqg, [R_sqg], ps_stat, R_pstat, rs, rstd, R_rs, R_rstd)
                    yield from sp(2)
                    for c in range(8):
                        tb, Rtb = tmpf.next()
                        stt("dve", tb[:], m[:, c, :], scol(l, C_G[gname] + c), rstd[:], ALU.mult, ALU.mult,
                            [R_m, R_rstd, R_small], [Rtb])
                        tt("dve", h[:, c, :], h[:, c, :], tb[:], ALU.add, [R_h_, Rtb], [R_h_])
                        if c % 2 == 1:
                            yield
                    yield from sp(1)

                def gen_pre(h, R_h_, hn, R_hn, gname):
                    act(sqg[:], h[:], AF.Square, [R_h_], [R_sqg])
                    yield from sp(4)
                    rms_stats(sqg, [R_sqg], ps_stat, R_pstat, rs, rstd, R_rs, R_rstd)
                    yield from sp(2)
                    for c in range(8):
                        stt("dve", hn[:, c, :], h[:, c, :], scol(l, C_G[gname] + c), rstd[:], ALU.mult, ALU.mult,
                            [R_h_, R_rstd, R_small], [R_hn])
                        if c % 4 == 3:
                            yield
                    yield from sp(2)

                def H1(j):
                    p = j % 2
                    cols = slice(j * 512, (j + 1) * 512)
                    dma("sp", ht[p][:], h_src_v[:, :, cols], [R_h[j]], [R_ht[p]])
                    dma("pool", ptl[p][:], pT[l * 256:(l + 1) * 256, :].rearrange("(c p) n -> p c n", p=128)[:, :, cols],
                        [], [R_ptl[p]])
                    yield from sp(2)
                    for oc in range(8):
                        pp, Rpp = ps_ring.next()
                        for k in range(8):
                            mm(pp[:], wout[:, k, oc * 128:(oc + 1) * 128], ymt[:, k, :], k == 0, k == 7, [R_wC, R_ymt], [Rpp])
                        cp("dve" if oc % 2 == 0 else "act", mb[p][:, oc, :], pp[:], [Rpp], [R_mb[p]])
                        yield
                    yield from sp(2)
                    yield from gen_post(mb[p], R_mb[p], ht[p], R_ht[p], "mix_post")
                    yield from gen_pre(ht[p], R_ht[p], hnD[p], R_hnD[p], "mlp_pre")

                def H1_loads(j):
                    p = j % 2
                    cols = slice(j * 512, (j + 1) * 512)
                    dma("sp", ymt[:], ymix.rearrange("(c p) n -> p c n", p=128)[:, :, cols], [R_ymA[j], R_ymB[j]], [R_ymt])

                def H3(j):
                    p = j % 2
                    cols = slice(j * 512, (j + 1) * 512)
                    yield from gen_post(mb[p], R_mb[p], ht[p], R_ht[p], "mlp_post")
                    yield from gen_pre(ht[p], R_ht[p], hnE, R_hnE, "ple_pre")
                    for oc in range(8):
                        pp, Rpp = ps_ring.next()
                        for k in range(8):
                            mm(pp[:], wgate[:, k, oc * 128:(oc + 1) * 128], hnE[:, k, :], k == 0, k == 7, [R_wC, R_hnE], [Rpp])
                        act(sqg[:, oc, :], pp[:], AF.Sigmoid, [Rpp], [R_sqg])
                        yield
                    yield from sp(2)
                    for oc in range(8):
                        pp, Rpp = ps_ring.next()
                        for k in range(2):
                            mm(pp[:], wproj[:, k, oc * 128:(oc + 1) * 128], ptl[p][:, k, :], k == 0, k == 1, [R_wC, R_ptl[p]], [Rpp])
                        tt("dve", mb[p][:, oc, :], pp[:], sqg[:, oc, :], ALU.mult, [Rpp, R_sqg], [R_mb[p]])
                        yield
                    yield from sp(3)
                    yield from gen_post(mb[p], R_mb[p], ht[p], R_ht[p], "ple_post")
                    dma("sp", h_dst_v[:, :, cols], ht[p][:], [R_ht[p]], [R_h[j]])
                    yield

                def step(side, n):
                    for _ in range(n):
                        try:
                            next(side)
                        except StopIteration:
                            return

                def up(j, g, side):
                    n = j * 8 + g
                    p = j % 2
                    for c4 in range(4):
                        pp, Rpp = ps_ring.next()
                        for k in range(8):
                            mm(pp[:], wu[n % 2][:, k, c4 * 128:(c4 + 1) * 128], hnD[p][:, k, :], k == 0, k == 7,
                               [R_wu[n % 2], R_hnD[p]], [Rpp])
                        rr, Rrr = rl.next()
                        act(rr[:], pp[:], AF.Relu, [Rpp], [Rrr])
                        tt("dve", ag[:, (g % 2) * 4 + c4, :], rr[:], rr[:], ALU.mult, [Rrr], [R_ag[g % 2]])
                        step(side, 1)

                def down(j, g, side):
                    n = j * 8 + g
                    p = j % 2
                    for oc in range(8):
                        pd, Rpd = ps_dn.next()
                        for k4 in range(4):
                            mm(pd[:], wd[n % 2][:, k4, oc * 128:(oc + 1) * 128], ag[:, (g % 2) * 4 + k4, :], k4 == 0, k4 == 3,
                               [R_wd[n % 2], R_ag[g % 2]], [Rpd])
                        if g == 0:
                            cp("dve", mb[p][:, oc, :], pd[:], [Rpd], [R_mb[p]])
                        else:
                            tt("dve", mb[p][:, oc, :], pd[:], mb[p][:, oc, :], ALU.add, [Rpd, R_mb[p]], [R_mb[p]])
                        step(side, 1)

                def chain(*gens):
                    for gq in gens:
                        if gq is not None:
                            yield from gq

                load_wu(0)
                load_wu(1)
                load_wd(0)
                H1_loads(0)
                for _ in H1(0):
                    pass
                for j in range(NT):
                    if j + 1 < NT:
                        H1_loads(j + 1)
                    side = chain(H3(j - 1) if j > 0 else None, H1(j + 1) if j + 1 < NT else None)
                    for g in range(9):
                        n = j * 8 + g
                        if g < 8:
                            up(j, g, side)
                        if g > 0:
                            down(j, g - 1, side)
                        if g < 8:
                            load_wu(n + 2)
                            load_wd(n + 1)
                    for _ in side:
                        pass
                for _ in H3(NT - 1):
                    pass
                P.barrier()
                P.flush(final=(l == L - 1))
    return nc


POOL_WINDOWS = (2, 4, 8, 16)


def _chunkcols(v):
    return np.ascontiguousarray(v.reshape(-1, 128).T)


def prep_shared(inp, L):
    f32 = np.float32
    w_in = np.asarray(inp["w_in"], f32)[:L]
    perm = np.concatenate([np.arange(0, 1024), np.arange(1284, 2308), np.arange(1280, 1284), np.arange(1024, 1280)])
    w_in_r = np.ascontiguousarray(w_in[:, :, perm]).reshape(L * 1024, 2308)
    w_pw = np.ascontiguousarray(np.asarray(inp["w_conf_pw"], f32)[:L]).reshape(L * 256, 256)
    wp = np.asarray(inp["w_pool"], f32)[:L]
    w_pbd = np.zeros((L, 2, 128, 128), f32)
    for blk in range(2):
        for gg in range(2):
            w_pbd[:, blk, gg * 64:(gg + 1) * 64, gg * 64:(gg + 1) * 64] = wp[:, blk * 2 + gg]
    w_pbd = w_pbd.reshape(L * 256, 128)
    w_out = np.ascontiguousarray(np.asarray(inp["w_out"], f32)[:L]).reshape(L * 1024, 1024)
    wu = np.asarray(inp["w_up"], f32)[:L]
    wu = wu.reshape(L, 8, 128, 8, 512).transpose(0, 3, 2, 1, 4)
    w_up = np.ascontiguousarray(wu).reshape(L * 1024, 4096)
    wdn = np.asarray(inp["w_down"], f32)[:L]
    wdn = wdn.reshape(L, 8, 4, 128, 1024).transpose(0, 1, 3, 2, 4)
    w_dn = np.ascontiguousarray(wdn).reshape(L * 1024, 4096)
    w_gate = np.ascontiguousarray(np.asarray(inp["w_ple_gate"], f32)[:L]).reshape(L * 1024, 1024)
    w_proj = np.ascontiguousarray(np.asarray(inp["w_ple_proj"], f32)[:L]).reshape(L * 256, 1024)
    small = np.zeros((128, L * 128), f32)
    for l in range(L):
        o = l * 128
        for name, key in (("mix_pre", "g_mix_pre"), ("mix_post", "g_mix_post"), ("mlp_pre", "g_mlp_pre"),
                          ("mlp_post", "g_mlp_post"), ("ple_pre", "g_ple_pre"), ("ple_post", "g_ple_post")):
            small[:, o + C_G[name]: o + C_G[name] + 8] = _chunkcols(np.asarray(inp[key], f32)[l])
        small[:, o + C_LNG: o + C_LNG + 2] = _chunkcols(np.asarray(inp["conf_ln_g"], f32)[l])
        small[:, o + C_LNB: o + C_LNB + 2] = _chunkcols(np.asarray(inp["conf_ln_b"], f32)[l])
        small[:, o + C_PSC: o + C_PSC + 2] = _chunkcols(np.asarray(inp["pool_scale"], f32)[l])
        dw = np.asarray(inp["w_conf_dw"], f32)[l]
        for blk in range(2):
            small[:, o + C_DW + blk * 31: o + C_DW + blk * 31 + 31] = dw[:, blk * 128:(blk + 1) * 128].T
        sc = np.asarray(inp["w_sc"], f32)[l]
        for blk in range(2):
            small[:, o + C_SC + blk * 3: o + C_SC + blk * 3 + 3] = sc[:, blk * 128:(blk + 1) * 128].T
        small[0:4, o + C_BF] = np.asarray(inp["b_forget"], f32)[l]
    consts = np.zeros((128, 320), f32)
    consts[:, K_ID:K_ID + 128] = np.eye(128, dtype=f32)
    kk, qq = np.meshgrid(np.arange(128), np.arange(128), indexing="ij")
    consts[:, K_MASK:K_MASK + 128] = np.where(kk > qq, NEG, 0.0)
    for blk in range(2):
        for p in range(128):
            w = POOL_WINDOWS[(blk * 128 + p) // 64]
            for t in range(16):
                consts[p, K_PCOEF + blk * 16 + t] = (1.0 / w if t < w else 0.0) - (1.0 if t == 0 else 0.0)
                consts[p, K_PRATIO + blk * 16 + t] = w / min(t + 1, w)
    return dict(w_in=w_in_r, w_pw=w_pw, w_pbd=w_pbd, w_out=w_out, w_up=w_up, w_dn=w_dn, w_gate=w_gate,
                w_proj=w_proj, small=small, consts=consts)


_NC_CACHE = {}
ACTIVE_CORES = [0, 1, 4, 5]


def run(inp, T, L, n_seq, stop=None, dbg=False):
    f32 = np.float32
    shared = prep_shared(inp, L)
    x = np.asarray(inp["x"], f32)
    p = np.asarray(inp["p"], f32)
    active = ACTIVE_CORES[:n_seq]
    zero_map = None
    in_maps = []
    for core in range(8):
        if core in active:
            bi = active.index(core)
            m = dict(shared)
            m["xT"] = np.ascontiguousarray(x[bi].T)
            m["pT"] = np.ascontiguousarray(p[:L, bi].transpose(0, 2, 1)).reshape(L * 256, T)
        else:
            if zero_map is None:
                zero_map = {k: np.zeros_like(v) for k, v in shared.items()}
                zero_map["xT"] = np.zeros((1024, T), f32)
                zero_map["pT"] = np.zeros((L * 256, T), f32)
            m = zero_map
        in_maps.append(m)
    key = (T, L, stop, dbg)
    if key not in _NC_CACHE:
        _NC_CACHE[key] = build(T, L, stop, dbg)
    nc = _NC_CACHE[key]
    res = run_bass_kernel_spmd(nc, in_maps, core_ids=list(range(8)))
    if dbg:
        return res.results[active[0]]
    out = np.stack([np.ascontiguousarray(res.results[active[bi]]["outT"].T) for bi in range(n_seq)], axis=0)
    return out.astype(f32)


def kernel(**inputs):
    return run(inputs, 8192, 4, 4)
```

```python
import contextlib
import numpy as np
import concourse.bass as bass
import concourse.mybir as mybir
from concourse.bass_utils import run_bass_kernel_spmd

F32 = mybir.dt.float32
BF16 = mybir.dt.bfloat16
ALU = mybir.AluOpType
AF = mybir.ActivationFunctionType
ENGS = ("sp", "act", "pe", "dve", "pool")
EPS = 1e-6
NEG = -30000.0


class Res:
    __slots__ = ("name", "w", "r", "excl")

    def __init__(self, name="", excl=False):
        self.name = name
        self.w = {}
        self.r = {}
        self.excl = excl


def PR(name):
    return Res(name, excl=True)


def _merge(d, s, v):
    if d.get(s, 0) < v:
        d[s] = v


class Prog:
    def __init__(self, nc, stack, n_dma_sems=32):
        self.nc = nc
        self.q = {e: [] for e in ENGS}
        self.sem_names = []
        self.sem_count = []
        self.seen = {e: {} for e in ENGS}
        self.pending_reads = {e: [] for e in ENGS}
        self.pending_writes = {e: [] for e in ENGS}
        self.eng_sem = {}
        for e in ("act", "pe", "dve", "pool"):
            self.eng_sem[e] = self._new_sem("c_" + e)
        self.dma_sems = {"sp": [self._new_sem("d%d" % i) for i in range(n_dma_sems)],
                         "pool": [self._new_sem("w%d" % i) for i in range(16)]}
        self.dma_rr = {"sp": 0, "pool": 0}
        self.n_ops = 0
        self.sems = [stack.enter_context(nc.semaphore(n)) for n in self.sem_names]

    def _new_sem(self, name):
        self.sem_names.append(name)
        self.sem_count.append(0)
        return len(self.sem_names) - 1

    def op(self, eng, fn, reads=(), writes=(), inc=True, dma=False):
        waits = {}
        for r in reads:
            for s, v in r.w.items():
                _merge(waits, s, v)
            if r.excl:
                for s, v in r.r.items():
                    if not (eng in self.eng_sem and s == self.eng_sem[eng]):
                        _merge(waits, s, v)
        for w in writes:
            for s, v in w.w.items():
                _merge(waits, s, v)
            for s, v in w.r.items():
                _merge(waits, s, v)
        tok = None
        if dma:
            pool_ = self.dma_sems[eng]
            s = pool_[self.dma_rr[eng] % len(pool_)]
            self.dma_rr[eng] += 1
            if self.sem_count[s] > 0:
                _merge(waits, s, self.sem_count[s])
            self.sem_count[s] += 16
            tok = (s, self.sem_count[s], 16)
        elif inc:
            s = self.eng_sem[eng]
            self.sem_count[s] += 1
            tok = (s, self.sem_count[s], 1)
        seen = self.seen[eng]
        wl = []
        for s, v in waits.items():
            if seen.get(s, 0) < v:
                if eng == "pe" and s == self.eng_sem["pe"]:
                    continue
                seen[s] = v
                wl.append((s, v))
        self.q[eng].append((fn, wl, tok))
        self.n_ops += 1
        if tok is None:
            self.pending_reads[eng].extend(reads)
            self.pending_writes[eng].extend(writes)
        else:
            s, v, _ = tok
            rl = list(reads)
            wr = list(writes)
            if not dma:
                rl += self.pending_reads[eng]
                wr += self.pending_writes[eng]
                self.pending_reads[eng] = []
                self.pending_writes[eng] = []
            for r in rl:
                _merge(r.r, s, v)
            for w in wr:
                _merge(w.w, s, v)
        return tok

    def barrier(self):
        for e in ENGS:
            wl = []
            for s, c in enumerate(self.sem_count):
                if c > 0 and self.seen[e].get(s, 0) < c:
                    self.seen[e][s] = c
                    wl.append((s, c))
            if wl:
                self.q[e].append((None, wl, None))

    def flush(self, final=False):
        nc = self.nc
        sems = self.sems
        if final:
            wl = [(s, c) for s, c in enumerate(self.sem_count) if c > 0]
            self.q["sp"].append((None, wl, None))
        with nc.Block() as block:
            def run(eng_name):
                ops = self.q[eng_name]

                def f(e):
                    for fn, wl, tok in ops:
                        for s, v in wl:
                            e.wait_ge(sems[s], v)
                        if fn is None:
                            continue
                        ins = fn(e)
                        if tok is not None:
                            ins.then_inc(sems[tok[0]], tok[2])
                return f
            block.sync(run("sp"))
            block.scalar(run("act"))
            block.tensor(run("pe"))
            block.vector(run("dve"))
            block.gpsimd(run("pool"))
        self.q = {e: [] for e in ENGS}


class Ring:
    def __init__(self, items):
        self.items = items
        self.i = 0

    def next(self):
        it = self.items[self.i % len(self.items)]
        self.i += 1
        return it


C_G = {"mix_pre": 0, "mix_post": 8, "mlp_pre": 16, "mlp_post": 24, "ple_pre": 32, "ple_post": 40}
C_LNG, C_LNB, C_PSC, C_DW, C_SC, C_BF = 48, 50, 52, 54, 116, 122
K_ID, K_MASK, K_PCOEF, K_PRATIO = 0, 128, 256, 288


def build(T, L, stop=None, dbg=False):
    NT = T // 512
    NB = T // 128
    nc = bass.Bass("TRN2", target_bir_lowering=False)

    def din(name, shape, dt=F32):
        return nc.dram_tensor(name, shape, dt, kind="ExternalInput").ap()

    def dscr(name, shape, dt):
        return nc.dram_tensor(name, shape, dt, kind="ExternalOutput" if dbg else "Internal").ap()

    xT = din("xT", [1024, T])
    pT = din("pT", [L * 256, T])
    w_in = din("w_in", [L * 1024, 2308])
    w_pw = din("w_pw", [L * 256, 256])
    w_pbd = din("w_pbd", [L * 256, 128])
    w_out = din("w_out", [L * 1024, 1024])
    w_up = din("w_up", [L * 1024, 4096])
    w_dn = din("w_dn", [L * 1024, 4096])
    w_gate = din("w_gate", [L * 1024, 1024])
    w_proj = din("w_proj", [L * 256, 1024])
    small = din("small", [128, L * 128])
    consts = din("consts", [128, 320])
    outT = nc.dram_tensor("outT", [1024, T], F32, kind="ExternalOutput").ap()

    hT = dscr("hT", [1024, T], F32)
    wup_bf = dscr("wup_bf", [L * 1024, 4096], BF16)
    wdn_bf = dscr("wdn_bf", [L * 1024, 4096], BF16)
    mixin = dscr("mixin", [1024, 32 + T], BF16)
    qT = dscr("qT", [256, T], BF16)
    kT = dscr("kT", [256, T], BF16)
    c3d = dscr("c3d", [4, 3, T], BF16)
    vaug = dscr("vaug", [NB, 128, 512], BF16)
    ymix = dscr("ymix", [1024, T], BF16)

    R_h = [Res("h%d" % j) for j in range(NT)]
    R_mixin = [Res("mi%d" % j) for j in range(NT + 1)]
    R_q = [Res("q%d" % j) for j in range(NT)]
    R_k = [Res("k%d" % j) for j in range(NT)]
    R_c3 = [Res("c3%d" % j) for j in range(NT)]
    R_v = [Res("v%d" % j) for j in range(NT)]
    R_ymA = [Res("ymA%d" % j) for j in range(NT)]
    R_ymB = [Res("ymB%d" % j) for j in range(NT)]
    R_wup = [[Res("wu%d_%d" % (l, g)) for g in range(8)] for l in range(L)]
    R_wdn = [[Res("wd%d_%d" % (l, g)) for g in range(8)] for l in range(L)]

    top = contextlib.ExitStack()
    with top:
        P = Prog(nc, top)

        uid = [0]

        def sbt(st, name, shape, dt):
            uid[0] += 1
            return st.enter_context(nc.sbuf_tensor("%s_u%d" % (name, uid[0]), shape, dt))

        def pst(st, name, shape=(128, 512), dt=F32):
            uid[0] += 1
            return st.enter_context(nc.psum_tensor("%s_u%d" % (name, uid[0]), list(shape), dt))

        small_sb = sbt(top, "small_sb", [128, L * 128], F32)
        consts_sb = sbt(top, "consts_sb", [128, 320], F32)
        ident_bf = sbt(top, "ident_bf", [128, 128], BF16)
        mask_bf = sbt(top, "mask_bf", [128, 128], BF16)
        ones_bf = sbt(top, "ones_bf", [128, 128], BF16)
        ones_f = sbt(top, "ones_f", [128, 128], F32)
        cneg = sbt(top, "cneg", [128, NB, 4], F32)
        R_small, R_consts, R_cst2, R_cneg = Res("small"), Res("consts"), Res("cst2"), Res("cneg")
        ident_f = consts_sb[:, K_ID:K_ID + 128]

        def scol(l, c, rows=slice(0, 128)):
            return small_sb[rows, l * 128 + c: l * 128 + c + 1]

        def dma(eng, out, in_, reads, writes):
            P.op(eng, lambda e: e.dma_start(out=out, in_=in_), reads=reads, writes=writes, dma=True)

        def mm(out, lhsT, rhs, start, stop, reads, writes):
            P.op("pe", lambda e: e.matmul(out, lhsT=lhsT, rhs=rhs, start=start, stop=stop),
                 reads=reads, writes=writes, inc=stop)

        def act(out, in_, func, reads, writes, bias=None, scale=None):
            kw = {}
            if bias is not None:
                kw["bias"] = bias
            if scale is not None:
                kw["scale"] = scale
            P.op("act", lambda e: e.activation(out, in_, func, **kw), reads=reads, writes=writes)

        def tt(eng, out, in0, in1, op, reads, writes):
            P.op(eng, lambda e: e.tensor_tensor(out, in0, in1, op), reads=reads, writes=writes)

        def stt(eng, out, in0, scalar, in1, op0, op1, reads, writes):
            P.op(eng, lambda e: e.scalar_tensor_tensor(out, in0, scalar, in1, op0, op1), reads=reads, writes=writes)

        def norm_scale(eng, out, in0, gcol, rstd_ap, reads, writes, ptmp_ring):
            if eng == "dve":
                stt("dve", out, in0, gcol, rstd_ap, ALU.mult, ALU.mult, reads, writes)
            else:
                tb, Rtb = ptmp_ring.next()
                tt("pool", tb[:], in0, rstd_ap, ALU.mult, reads, [Rtb])
                P.op("pool", lambda e: e.tensor_scalar_mul(out, tb[:], gcol), reads=[Rtb] + list(reads), writes=writes)

        def ts(eng, out, in0, s1, s2, op0, op1, reads, writes):
            if s2 is None:
                P.op(eng, lambda e: e.tensor_single_scalar(out, in0, s1, op0), reads=reads, writes=writes)
            else:
                P.op(eng, lambda e: e.tensor_scalar(out, in0, s1, s2, op0, op1), reads=reads, writes=writes)

        def cp(eng, out, in_, reads, writes):
            if eng == "act":
                P.op("act", lambda e: e.copy(out, in_), reads=reads, writes=writes)
            else:
                P.op(eng, lambda e: e.tensor_copy(out, in_), reads=reads, writes=writes)

        def memset(eng, ap, val, writes):
            P.op(eng, lambda e: e.memset(ap, val), writes=writes)

        def recip(out, in_, reads, writes):
            P.op("dve", lambda e: e.reciprocal(out, in_), reads=reads, writes=writes)

        def rms_stats(sq, R_sq, ps_stat, R_ps, rs, rstd, R_rs, R_rstd):
            for c in range(8):
                mm(ps_stat[:], ones_bf[:], sq[:, c, :], c == 0, c == 7, list(R_sq) + [R_cst2], [R_ps])
            act(rs[:], ps_stat[:], AF.Sqrt, [R_ps], [R_rs], bias=EPS, scale=1.0 / 1024.0)
            recip(rstd[:], rs[:], [R_rs], [R_rstd])

        with contextlib.ExitStack() as st:
            zt = sbt(st, "zt", [128, 8, 32], BF16)
            R_zt = Res("zt")
            dma("sp", small_sb[:], small, [], [R_small])
            dma("sp", consts_sb[:], consts, [], [R_consts])
            cp("dve", ident_bf[:], consts_sb[:, K_ID:K_ID + 128], [R_consts], [R_cst2])
            cp("dve", mask_bf[:], consts_sb[:, K_MASK:K_MASK + 128], [R_consts], [R_cst2])
            memset("pool", ones_bf[:], 1.0, [R_cst2])
            memset("pool", ones_f[:], 1.0, [R_cst2])
            memset("pool", zt[:], 0.0, [R_zt])
            dma("sp", mixin.rearrange("(c p) n -> p c n", p=128)[:, :, 0:32], zt[:], [R_zt], [R_mixin[0]])
            P.barrier()
            P.flush(final=(stop == "pro"))
        if stop == "pro":
            return nc

        def cast_mlp_weights(l):
            for g in range(8):
                for (src, dst, RR) in ((w_up, wup_bf, R_wup), (w_dn, wdn_bf, R_wdn)):
                    r0 = l * 1024 + g * 128
                    for hh in range(2):
                        dma("pool", dst[r0:r0 + 128, hh * 2048:(hh + 1) * 2048],
                            src[r0:r0 + 128, hh * 2048:(hh + 1) * 2048], [], [RR[l][g]])

        for l in range(L):
            h_src = xT if l == 0 else hT
            h_dst = outT if l == L - 1 else hT
            h_src_v = h_src.rearrange("(c p) n -> p c n", p=128)
            h_dst_v = h_dst.rearrange("(c p) n -> p c n", p=128)

            with contextlib.ExitStack() as st:
                win = sbt(st, "win", [128, 8, 2308], BF16)
                R_win = Res("win")
                ht = [sbt(st, "ht%d" % i, [128, 8, 512], F32) for i in range(2)]
                R_ht = [Res("ht0"), Res("ht1")]
                sq = sbt(st, "sq", [128, 8, 512], BF16)
                R_sq = Res("sq")
                xn2 = [sbt(st, "xn%d" % i, [128, 8, 512], BF16) for i in range(2)]
                R_xn2 = [Res("xn0"), Res("xn1")]
                rs = sbt(st, "rs", [128, 512], F32)
                rstd = sbt(st, "rstd", [128, 512], F32)
                R_rs, R_rstd = Res("rs"), Res("rstd")
                tmpf = [sbt(st, "tmpf%d" % i, [128, 512], F32) for i in range(2)]
                tmp_ring = Ring([(tmpf[i], Res("tmpf%d" % i)) for i in range(2)])
                ptmp_ring = Ring([(sbt(st, "ptmp%d" % i, [128, 512], F32), Res("ptmp%d" % i)) for i in range(2)])
                mstage = [sbt(st, "mstage%d" % i, [128, 8, 512], BF16) for i in range(2)]
                R_mst = [Res("mst0"), Res("mst1")]
                qstage = [sbt(st, "qstage%d" % i, [128, 2, 512], BF16) for i in range(2)]
                R_qst = [Res("qst0"), Res("qst1")]
                kstage = [sbt(st, "kstage%d" % i, [128, 2, 512], BF16) for i in range(2)]
                R_kst = [Res("kst0"), Res("kst1")]
                vstage = [sbt(st, "vstage%d" % i, [128, 4, 512], BF16) for i in range(2)]
                R_vst = [Res("vst0"), Res("vst1")]
                xb = sbt(st, "xb", [4, 512], F32)
                ef = sbt(st, "ef", [4, 512], F32)
                lf = sbt(st, "lf", [4, 512], F32)
                ones4 = sbt(st, "ones4", [4, 512], F32)
                cc = [sbt(st, "cc%d" % i, [4, 512], F32) for i in range(2)]
                r1 = sbt(st, "r1", [4, 512], F32)
                r2 = sbt(st, "r2", [4, 512], F32)
                c3 = [sbt(st, "c3_%d" % i, [4, 3, 512], BF16) for i in range(2)]
                R_f = Res("fmisc")
                R_cc = [Res("cc0"), Res("cc1")]
                R_c3s = [Res("c3s0"), Res("c3s1")]
                ps_stat = pst(st, "ps_stat")
                R_pstat = PR("ps_stat")
                ps_ring = Ring([(pst(st, "psr%d" % i), PR("psr%d" % i)) for i in range(4)])
                ps_v = Ring([(pst(st, "psv%d" % i), PR("psv%d" % i)) for i in range(2)])
                ps_t = pst(st, "ps_t")
                R_pst = PR("ps_t")

                for k in range(8):
                    for hh in range(2):
                        dma("pool", win[:, k, hh * 1154:(hh + 1) * 1154],
                            w_in[l * 1024 + k * 128: l * 1024 + (k + 1) * 128, hh * 1154:(hh + 1) * 1154],
                            [], [R_win])
                cast_mlp_weights(l)
                memset("pool", ones4[:], 1.0, [R_f])
                for i in range(2):
                    memset("pool", vstage[i][:], 1.0, [R_vst[i]])

                def load_h(j):
                    dma("sp", ht[j % 2][:], h_src_v[:, :, j * 512:(j + 1) * 512], [R_h[j]], [R_ht[j % 2]])

                cur = {}

                def proj(ps, R_ps, col0, M):
                    xn, R_xn = cur["xn"], cur["R_xn"]
                    for k in range(8):
                        mm(ps[0:M, :], win[:, k, col0:col0 + M], xn[:, k, :], k == 0, k == 7, [R_win, R_xn], [R_ps])

                def norm_a(j):
                    bb = j % 2
                    act(sq[:], ht[bb][:], AF.Square, [R_ht[bb]], [R_sq])

                def norm_b(j):
                    bb = j % 2
                    rms_stats(sq, [R_sq], ps_stat, R_pstat, rs, rstd, R_rs, R_rstd)
                    for c in range(8):
                        norm_scale("dve", xn2[bb][:, c, :], ht[bb][:, c, :], scol(l, C_G["mix_pre"] + c),
                                   rstd[:], [R_ht[bb], R_rstd, R_small], [R_xn2[bb]], ptmp_ring)

                load_h(0)
                if NT > 1:
                    load_h(1)
                norm_a(0)
                norm_b(0)
                for j in range(NT):
                    b = j % 2
                    h = ht[b]
                    cols = slice(j * 512, (j + 1) * 512)
                    xnb, R_xnb = xn2[b], R_xn2[b]
                    xn, R_xn = xnb, R_xnb
                    cur["xn"], cur["R_xn"] = xnb, R_xnb
                    pf, Rpf = ps_ring.next()
                    proj(pf, Rpf, 2048, 4)
                    ts("dve", xb[:], pf[0:4, :], scol(l, C_BF, slice(0, 4)), None, ALU.add, None, [Rpf, R_small], [R_f])
                    act(ef[:], xb[:], AF.Exp, [R_f], [R_f], scale=-1.0)
                    act(lf[:], ef[:], AF.Ln, [R_f], [R_f], bias=1.0, scale=1.0)
                    init = 0.0 if j == 0 else cc[1 - b][:, 511:512]
                    P.op("dve", (lambda o, d0, d1, ini: (lambda e: e.tensor_tensor_scan(o, d0, d1, ini, ALU.mult, ALU.subtract)))(
                        cc[b][:], ones4[:], lf[:], init), reads=[R_f, R_cc[1 - b]], writes=[R_cc[b]])
                    if j + 1 < NT:
                        norm_a(j + 1)
                    for blk in range(2):
                        pa, Rpa = ps_ring.next()
                        pb, Rpb = ps_ring.next()
                        proj(pa, Rpa, (0 + blk) * 128, 128)
                        proj(pb, Rpb, (2 + blk) * 128, 128)
                        tb, Rtb = tmp_ring.next()
                        act(tb[:], pb[:], AF.Sigmoid, [Rpb], [Rtb])
                        tt("dve", mstage[b][:, 0 + blk, :], pa[:], tb[:], ALU.mult, [Rpa, Rtb], [R_mst[b]])
                    cp("dve", c3[b][:, 0, :], cc[b][:], [R_cc[b]], [R_c3s[b]])
                    tt("dve", r1[:], cc[b][:], c3[b][:, 0, :], ALU.subtract, [R_cc[b], R_c3s[b]], [R_f])
                    cp("dve", c3[b][:, 1, :], r1[:], [R_f], [R_c3s[b]])
                    tt("dve", r2[:], r1[:], c3[b][:, 1, :], ALU.subtract, [R_f, R_c3s[b]], [R_f])
                    cp("dve", c3[b][:, 2, :], r2[:], [R_f], [R_c3s[b]])
                    for blk in range(2):
                        pa, Rpa = ps_ring.next()
                        pb, Rpb = ps_ring.next()
                        proj(pa, Rpa, (8 + blk) * 128, 128)
                        proj(pb, Rpb, (12 + blk) * 128, 128)
                        tb, Rtb = tmp_ring.next()
                        cp("act", tb[:], pb[:], [Rpb], [Rtb])
                        tt("dve", mstage[b][:, 2 + blk, :], pa[:], tb[:], ALU.mult, [Rpa, Rtb], [R_mst[b]])
                    if j + 1 < NT:
                        norm_b(j + 1)
                        if j + 2 < NT:
                            load_h(j + 2)
                    for blk in range(2):
                        pa, Rpa = ps_ring.next()
                        proj(pa, Rpa, (14 + blk) * 128, 128)
                        cp("dve", mstage[b][:, 4 + blk, :], pa[:], [Rpa], [R_mst[b]])
                        pb, Rpb = ps_ring.next()
                        proj(pb, Rpb, (10 + blk) * 128, 128)
                        cp("act", mstage[b][:, 6 + blk, :], pb[:], [Rpb], [R_mst[b]])
                    for blk in range(2):
                        pa, Rpa = ps_ring.next()
                        proj(pa, Rpa, (4 + blk) * 128, 128)
                        act(qstage[b][:, blk, :], pa[:], AF.Copy, [Rpa], [R_qst[b]], scale=0.125)
                        pb, Rpb = ps_ring.next()
                        proj(pb, Rpb, (6 + blk) * 128, 128)
                        cp("dve", kstage[b][:, blk, :], pb[:], [Rpb], [R_kst[b]])
                    for s in range(4):
                        pv, Rpv = ps_v.next()
                        for k in range(8):
                            mm(pv[:, 0:256], xn[:, k, s * 128:(s + 1) * 128], win[:, k, 2052:2308], k == 0, k == 7,
                               [R_win, R_xn], [Rpv])
                        dst = vstage[b][:, s, :].rearrange("p (h c) -> p h c", h=4)[:, :, 0:64]
                        src = pv[:, 0:256].rearrange("p (h c) -> p h c", h=4)
                        cp("dve" if s % 2 == 0 else "act", dst, src, [Rpv], [R_vst[b]])
                    for s in range(4):
                        P.op("pe", (lambda o, a, bb: (lambda e: e.matmul(o, lhsT=a, rhs=bb, start=True, stop=True)))(
                            ps_t[:, s * 4:s * 4 + 4], cc[b][0:4, s * 128:(s + 1) * 128], consts_sb[0:4, K_ID:K_ID + 4]),
                            reads=[R_cc[b], R_consts], writes=[R_pst], inc=True)
                    ts("dve", cneg[:, j * 4:(j + 1) * 4, :], ps_t[:, 0:16].rearrange("p (s h) -> p s h", s=4), -1.0, None,
                       ALU.mult, None, [R_pst], [R_cneg])
                    dma("sp", mixin.rearrange("(c p) n -> p c n", p=128)[:, :, 32 + j * 512: 32 + (j + 1) * 512],
                        mstage[b][:], [R_mst[b]], [R_mixin[j + 1]])
                    dma("sp", qT.rearrange("(c p) n -> p c n", p=128)[:, :, cols], qstage[b][:], [R_qst[b]], [R_q[j]])
                    dma("sp", kT.rearrange("(c p) n -> p c n", p=128)[:, :, cols], kstage[b][:], [R_kst[b]], [R_k[j]])
                    dma("sp", c3d[:, :, cols], c3[b][:], [R_c3s[b]], [R_c3[j]])
                    dma("sp", vaug[j * 4:(j + 1) * 4].rearrange("s p c -> p s c"), vstage[b][:], [R_vst[b]], [R_v[j]])
                P.barrier()
                P.flush(final=(stop == "A1"))
            if stop == "A1":
                return nc

            with contextlib.ExitStack() as st:
                dconf = sbt(st, "dconf", [128, 2, 31, 128], BF16)
                dsc = sbt(st, "dsc", [128, 2, 3, 128], BF16)
                dpl = sbt(st, "dpl", [128, 2, 16, 128], BF16)
                R_dg = Res("diag")
                wpw = sbt(st, "wpw", [128, 2, 256], BF16)
                wpbd = sbt(st, "wpbd", [128, 2, 128], BF16)
                R_wA2 = Res("wA2")
                mt = [sbt(st, "mt%d" % i, [128, 8, 544], BF16) for i in range(2)]
                R_mt = [Res("mt0"), Res("mt1")]
                xc = sbt(st, "xc", [128, 2, 512], F32)
                sq2 = sbt(st, "sq2", [128, 2, 512], F32)
                R_xc, R_sq2 = Res("xc"), Res("sq2")
                mean = sbt(st, "mean", [128, 512], F32)
                msq = sbt(st, "msq", [128, 512], F32)
                var = sbt(st, "var", [128, 512], F32)
                sd = sbt(st, "sd", [128, 512], F32)
                rstd2 = sbt(st, "rstd2", [128, 512], F32)
                R_mean, R_msq, R_var, R_sd, R_rstd2 = Res("mean"), Res("msq"), Res("var"), Res("sd"), Res("rstd2")
                xm = [sbt(st, "xm%d" % i, [128, 512], F32) for i in range(2)]
                R_xm = [Res("xm0"), Res("xm1")]
                xnn = [sbt(st, "xnn%d" % i, [128, 512], F32) for i in range(2)]
                R_xnn = [Res("xnn0"), Res("xnn1")]
                sconf = sbt(st, "sconf", [128, 2, 512], BF16)
                R_sconf = Res("sconf")
                dpool = sbt(st, "dpool", [128, 2, 512], BF16)
                R_dpool = Res("dpool")
                t16 = sbt(st, "t16", [128, 2, 16], F32)
                R_t16 = Res("t16")
                ystage = [sbt(st, "ystage%d" % i, [128, 6, 512], BF16) for i in range(2)]
                R_yst = [Res("yst0"), Res("yst1")]
                psc = [pst(st, "psc%d" % i) for i in range(2)]
                R_psc = [PR("psc0"), PR("psc1")]
                ps_s1 = pst(st, "ps_s1")
                ps_s2 = pst(st, "ps_s2")
                R_s1, R_s2 = PR("s1"), PR("s2")
                ps_ring = Ring([(pst(st, "psq%d" % i), PR("psq%d" % i)) for i in range(3)])

                dma("pool", wpw[:], w_pw[l * 256:(l + 1) * 256, :].rearrange("(c p) n -> p c n", p=128), [], [R_wA2])
                dma("pool", wpbd[:], w_pbd[l * 256:(l + 1) * 256, :].rearrange("(c p) n -> p c n", p=128), [], [R_wA2])
                for blk in range(2):
                    for k in range(31):
                        P.op("pool", (lambda o, s: (lambda e: e.tensor_scalar_mul(o, ident_f, s)))(
                            dconf[:, blk, k, :], scol(l, C_DW + blk * 31 + k)), reads=[R_consts, R_small], writes=[R_dg])
                    for k in range(3):
                        P.op("pool", (lambda o, s: (lambda e: e.tensor_scalar_mul(o, ident_f, s)))(
                            dsc[:, blk, k, :], scol(l, C_SC + blk * 3 + k)), reads=[R_consts, R_small], writes=[R_dg])
                    for k in range(16):
                        P.op("pool", (lambda o, s: (lambda e: e.tensor_scalar_mul(o, ident_f, s)))(
                            dpl[:, blk, k, :], consts_sb[:, K_PCOEF + blk * 16 + k: K_PCOEF + blk * 16 + k + 1]),
                            reads=[R_consts], writes=[R_dg])

                def load_mt(j):
                    dma("sp", mt[j % 2][:], mixin.rearrange("(c p) n -> p c n", p=128)[:, :, j * 512: j * 512 + 544],
                        [R_mixin[j], R_mixin[j + 1]], [R_mt[j % 2]])

                load_mt(0)
                sect = stop.split(":")[1] if (stop and ":" in stop) else "all"

                def on(*names):
                    return sect == "all" or sect in names

                for j in range(NT):
                    b = j % 2
                    if j + 1 < NT:
                        load_mt(j + 1)
                    m_ = mt[b]
                    cols = slice(j * 512, (j + 1) * 512)
                    if sect == "conv1":
                        for blk in range(2):
                            for k in range(31):
                                mm(psc[blk][:], dconf[:, blk, k, :], m_[:, blk, 2 + k: 2 + k + 512], k == 0, k == 30,
                                   [R_dg, R_mt[b]], [R_psc[blk]])
                            cp("dve", xc[:, blk, :], psc[blk][:], [R_psc[blk]], [R_xc])
                    if sect == "conv2":
                        for blk in range(2):
                            for k in range(3):
                                mm(psc[blk][:], dconf[:, blk, k, :], m_[:, blk, 2 + k: 2 + k + 512], k == 0, k == 2,
                                   [R_dg, R_mt[b]], [R_psc[blk]])
                            act(sq2[:, blk, :], psc[blk][:], AF.Square, [R_psc[blk]], [R_sq2])
                    if sect == "pool1":
                        for blk in range(2):
                            pp, Rpp = ps_ring.next()
                            for k in range(16):
                                mm(pp[:], dpl[:, blk, k, :], m_[:, 4 + blk, 32 - k: 32 - k + 512], k == 0, k == 15,
                                   [R_dg, R_mt[b]], [Rpp])
                            cp("dve", dpool[:, blk, :], pp[:], [Rpp], [R_dpool])
                    if on("conv", "ln", "silu", "conf"):
                        for blk in range(2):
                            for k in range(31):
                                mm(psc[blk][:], dconf[:, blk, k, :], m_[:, blk, 2 + k: 2 + k + 512], k == 0, k == 30,
                                   [R_dg, R_mt[b]], [R_psc[blk]])
                            cp("dve", xc[:, blk, :], psc[blk][:], [R_psc[blk]], [R_xc])
                            act(sq2[:, blk, :], xc[:, blk, :], AF.Square, [R_xc], [R_sq2])
                    if on("sc"):
                        for blk in range(2):
                            pp, Rpp = ps_ring.next()
                            for k in range(3):
                                mm(pp[:], dsc[:, blk, k, :], m_[:, 2 + blk, 30 + k: 30 + k + 512], k == 0, k == 2,
                                   [R_dg, R_mt[b]], [Rpp])
                            tt("dve", ystage[b][:, 2 + blk, :], pp[:], m_[:, 6 + blk, 32:544], ALU.mult, [Rpp, R_mt[b]], [R_yst[b]])
                    if on("pool"):
                        for blk in range(2):
                            pp, Rpp = ps_ring.next()
                            for k in range(16):
                                mm(pp[:], dpl[:, blk, k, :], m_[:, 4 + blk, 32 - k: 32 - k + 512], k == 0, k == 15,
                                   [R_dg, R_mt[b]], [Rpp])
                            cp("act", dpool[:, blk, :], pp[:], [Rpp], [R_dpool])
                            if j == 0:
                                tt("dve", t16[:, blk, :], pp[:, 0:16], m_[:, 4 + blk, 32:48], ALU.add, [Rpp, R_mt[b]], [R_t16])
                                tt("pool", t16[:, blk, :], t16[:, blk, :],
                                   consts_sb[:, K_PRATIO + blk * 16: K_PRATIO + blk * 16 + 16], ALU.mult, [R_t16, R_consts], [R_t16])
                                tt("dve", dpool[:, blk, 0:16], t16[:, blk, :], m_[:, 4 + blk, 32:48], ALU.subtract,
                                   [R_t16, R_mt[b]], [R_dpool])
                        for blk in range(2):
                            pp, Rpp = ps_ring.next()
                            mm(pp[:], wpbd[:, blk, :], dpool[:, blk, :], True, True, [R_wA2, R_dpool], [Rpp])
                            act(ystage[b][:, 4 + blk, :], pp[:], AF.Copy, [Rpp, R_small], [R_yst[b]], scale=scol(l, C_PSC + blk))
                    if on("ln", "silu", "conf"):
                        for blk in range(2):
                            mm(ps_s1[:], ones_f[:], xc[:, blk, :], blk == 0, blk == 1, [R_xc, R_cst2], [R_s1])
                        for blk in range(2):
                            mm(ps_s2[:], ones_f[:], sq2[:, blk, :], blk == 0, blk == 1, [R_sq2, R_cst2], [R_s2])
                        act(mean[:], ps_s1[:], AF.Copy, [R_s1], [R_mean], scale=1.0 / 256.0)
                        tt("pool", msq[:], mean[:], mean[:], ALU.mult, [R_mean], [R_msq])
                        stt("dve", var[:], ps_s2[:], 1.0 / 256.0, msq[:], ALU.mult, ALU.subtract, [R_s2, R_msq], [R_var])
                        act(sd[:], var[:], AF.Sqrt, [R_var], [R_sd], bias=EPS, scale=1.0)
                        recip(rstd2[:], sd[:], [R_sd], [R_rstd2])
                        for blk in range(2):
                            tt("dve", xm[blk][:], xc[:, blk, :], mean[:], ALU.subtract, [R_xc, R_mean], [R_xm[blk]])
                            tt("pool", xnn[blk][:], xm[blk][:], rstd2[:], ALU.mult, [R_xm[blk], R_rstd2], [R_xnn[blk]])
                    if on("silu", "conf"):
                        for blk in range(2):
                            act(sconf[:, blk, :], xnn[blk][:], AF.Silu, [R_xnn[blk], R_small], [R_sconf],
                                bias=scol(l, C_LNB + blk), scale=scol(l, C_LNG + blk))
                    if on("conf"):
                        for oc in range(2):
                            pp, Rpp = ps_ring.next()
                            for kb in range(2):
                                mm(pp[:], wpw[:, kb, oc * 128:(oc + 1) * 128], sconf[:, kb, :], kb == 0, kb == 1,
                                   [R_wA2, R_sconf], [Rpp])
                            cp("dve", ystage[b][:, oc, :], pp[:], [Rpp], [R_yst[b]])
                    ym_v = ymix.rearrange("(c p) n -> p c n", p=128)
                    dma("sp", ym_v[:, 0:2, cols], ystage[b][:, 0:2, :], [R_yst[b]], [R_ymA[j]])
                    dma("sp", ym_v[:, 4:8, cols], ystage[b][:, 2:6, :], [R_yst[b]], [R_ymA[j]])
                P.barrier()
                P.flush(final=(stop is not None and stop.startswith("A2")))
            if stop is not None and stop.startswith("A2"):
                return nc

            with contextlib.ExitStack() as st:
                kaug = sbt(st, "kaug", [128, 4, T], BF16)
                R_kaug = Res("kaug")
                vsb = sbt(st, "vsb", [128, NB, 512], BF16)
                R_vsb = [Res("vsb%d" % j) for j in range(NT)]
                qa = [sbt(st, "qa%d" % i, [128, 4, 512], BF16) for i in range(2)]
                R_qa = [Res("qa0"), Res("qa1")]
                pts = Ring([(sbt(st, "pt%d" % i, [128, 512], BF16), Res("pt%d" % i)) for i in range(4)])
                rec = sbt(st, "rec", [128, 512], F32)
                R_rec = Res("rec")
                bcs = sbt(st, "bcs", [64, 512], F32)
                R_bcs = Res("bcs")
                yst = Ring([(sbt(st, "ysb%d" % i, [64, 512], BF16), Res("ysb%d" % i)) for i in range(2)])
                ps_s = Ring([(pst(st, "pss%d" % i), PR("pss%d" % i)) for i in range(3)])
                ps_o = Ring([(pst(st, "pso%d" % i), PR("pso%d" % i)) for i in range(2)])
                ps_bc = pst(st, "ps_bc")
                R_bc = PR("ps_bc")

                memset("pool", kaug[64:96, :, :], 0.0, [R_kaug])
                memset("pool", kaug[64:67, :, :], 1.0, [R_kaug])
                for i in range(2):
                    memset("pool", qa[i][64:96, :, :], 0.0, [R_qa[i]])
                for j in range(NT):
                    cols = slice(j * 512, (j + 1) * 512)
                    dma("sp", kaug[0:64, :, cols], kT.rearrange("(h r) n -> r h n", r=64)[:, :, cols], [R_k[j]], [R_kaug])
                    dma("sp", vsb[:, j * 4:(j + 1) * 4, :], vaug[j * 4:(j + 1) * 4].rearrange("s p c -> p s c"),
                        [R_v[j]], [R_vsb[j]])

                def load_q(j):
                    cols = slice(j * 512, (j + 1) * 512)
                    dma("sp", qa[j % 2][0:64, :, :], qT.rearrange("(h r) n -> r h n", r=64)[:, :, cols], [R_q[j]], [R_qa[j % 2]])
                    dma("sp", qa[j % 2][64:67, :, :], c3d.rearrange("h j n -> j h n")[:, :, cols], [R_c3[j]], [R_qa[j % 2]])

                load_q(0)
                for j in range(NT):
                    b = j % 2
                    if j + 1 < NT:
                        load_q(j + 1)
                    cols = slice(j * 512, (j + 1) * 512)
                    for h in range(4):
                        nblk = 4 * j + 4
                        po, Rpo = ps_o.next()

                        def qk(i):
                            dj = i - 4 * j
                            c0 = 128 * dj if dj >= 0 else 0
                            pS, RpS = ps_s.next()
                            mm(pS[:, c0:512], kaug[0:96, h, i * 128:(i + 1) * 128], qa[b][0:96, h, c0:512],
                               True, dj < 0, [R_kaug, R_qa[b]], [RpS])
                            if dj >= 0:
                                mm(pS[:, c0:c0 + 128], ident_bf[:], mask_bf[:], False, True, [R_cst2], [RpS])
                            return pS, RpS, c0

                        nxt = qk(0)
                        for i in range(nblk):
                            pS, RpS, c0 = nxt
                            if i + 1 < nblk:
                                nxt = qk(i + 1)
                            pt, Rpt = pts.next()
                            act(pt[:, c0:512], pS[:, c0:512], AF.Exp, [RpS, R_cneg], [Rpt], bias=cneg[:, i, h:h + 1], scale=1.0)
                            mm(po[:, c0:512], vsb[:, i, h * 128:(h + 1) * 128], pt[:, c0:512],
                               i == 0, i == nblk - 1, [R_vsb[i // 4], Rpt], [Rpo])
                        recip(rec[64:128, :], po[64:128, :], [Rpo], [R_rec])
                        ys, Rys = yst.next()
                        tt("dve", ys[:], po[0:64, :], rec[64:128, :], ALU.mult, [Rpo, R_rec], [Rys])
                        dma("sp", ymix[256 + 64 * h: 256 + 64 * h + 64, cols], ys[:], [Rys], [R_ymB[j]])
                P.barrier()
                P.flush(final=(stop == "B"))
            if stop == "B":
                return nc

            with contextlib.ExitStack() as st:
                wout = sbt(st, "wout", [128, 8, 1024], BF16)
                wgate = sbt(st, "wgate", [128, 8, 1024], BF16)
                wproj = sbt(st, "wproj", [128, 2, 1024], BF16)
                R_wC = Res("wC")
                wu = [sbt(st, "wu%d" % i, [128, 8, 512], BF16) for i in range(2)]
                wd = [sbt(st, "wd%d" % i, [128, 4, 1024], BF16) for i in range(2)]
                R_wu = [Res("wus%d" % i) for i in range(2)]
                R_wd = [Res("wds%d" % i) for i in range(2)]
                ymt = sbt(st, "ymt", [128, 8, 512], BF16)
                R_ymt = Res("ymt")
                ht = [sbt(st, "hc%d" % i, [128, 8, 512], F32) for i in range(2)]
                R_ht = [Res("hc0"), Res("hc1")]
                ptl = [sbt(st, "ptl%d" % i, [128, 2, 512], BF16) for i in range(2)]
                R_ptl = [Res("ptl0"), Res("ptl1")]
                mb = [sbt(st, "mb%d" % i, [128, 8, 512], F32) for i in range(2)]
                R_mb = [Res("mb0"), Res("mb1")]
                hnD = [sbt(st, "hnD%d" % i, [128, 8, 512], BF16) for i in range(2)]
                R_hnD = [Res("hnD0"), Res("hnD1")]
                hnE = sbt(st, "hnE", [128, 8, 512], BF16)
                R_hnE = Res("hnE")
                sqg = sbt(st, "sqg", [128, 8, 512], BF16)
                R_sqg = Res("sqg")
                ag = sbt(st, "ag", [128, 8, 512], BF16)
                R_ag = [Res("ag0"), Res("ag1")]
                rs = sbt(st, "rs_c", [128, 512], F32)
                rstd = sbt(st, "rstd_c", [128, 512], F32)
                R_rs, R_rstd = Res("rs"), Res("rstd")
                tmpf = Ring([(sbt(st, "tmpc%d" % i, [128, 512], F32), Res("tmpc%d" % i)) for i in range(2)])
                rl = Ring([(sbt(st, "rl%d" % i, [128, 512], BF16), Res("rl%d" % i)) for i in range(2)])
                ps_stat = pst(st, "ps_stat_c")
                R_pstat = PR("ps_stat_c")
                ps_ring = Ring([(pst(st, "psm%d" % i), PR("psm%d" % i)) for i in range(3)])
                ps_dn = Ring([(pst(st, "psd%d" % i), PR("psd%d" % i)) for i in range(4)])

                for k in range(8):
                    dma("pool", wout[:, k, :], w_out[l * 1024 + k * 128: l * 1024 + (k + 1) * 128, :], [], [R_wC])
                for k in range(8):
                    dma("pool", wgate[:, k, :], w_gate[l * 1024 + k * 128: l * 1024 + (k + 1) * 128, :], [], [R_wC])
                for k in range(2):
                    dma("pool", wproj[:, k, :], w_proj[l * 256 + k * 128: l * 256 + (k + 1) * 128, :], [], [R_wC])

                def load_wu(n):
                    if n >= NT * 8:
                        return
                    g = n % 8
                    r0 = l * 1024 + g * 128
                    dma("sp", wu[n % 2][:], wup_bf[r0:r0 + 128, :].rearrange("p (k c) -> p k c", k=8), [R_wup[l][g]], [R_wu[n % 2]])

                def load_wd(n):
                    if n >= NT * 8:
                        return
                    g = n % 8
                    r0 = l * 1024 + g * 128
                    dma("sp", wd[n % 2][:], wdn_bf[r0:r0 + 128, :].rearrange("p (k c) -> p k c", k=4), [R_wdn[l][g]], [R_wd[n % 2]])

                def sp(n):
                    for _ in range(n):
                        yield

                def gen_post(m, R_m, h, R_h_, gname):
                    act(sqg[:], m[:], AF.Square, [R_m], [R_sqg])
                    yield from sp(4)
                    rms_stats(sqg, [R_sqg], ps_stat, R_pstat, rs, rstd, R_rs, R_rstd)
                    yield from sp(2)
                    for c in range(8):
                        tb, Rtb = tmpf.next()
                        stt("dve", tb[:], m[:, c, :], scol(l, C_G[gname] + c), rstd[:], ALU.mult, ALU.mult,
                            [R_m, R_rstd, R_small], [Rtb])
                        tt("dve", h[:, c, :], h[:, c, :], tb[:], ALU.add, [R_h_, Rtb], [R_h_])
                        if c % 2 == 1:
                            yield
                    yield from sp(1)

                def gen_pre(h, R_h_, hn, R_hn, gname):
                    act(sqg[:], h[:], AF.Square, [R_h_], [R_sqg])
                    yield from sp(4)
                    rms_stats(sqg, [R_sqg], ps_stat, R_pstat, rs, rstd, R_rs, R_rstd)
                    yield from sp(2)
                    for c in range(8):
                        stt("dve", hn[:, c, :], h[:, c, :], scol(l, C_G[gname] + c), rstd[:], ALU.mult, ALU.mult,
                            [R_h_, R_rstd, R_small], [R_hn])
                        if c % 4 == 3:
                            yield
                    yield from sp(2)

                def H1(j):
                    p = j % 2
                    cols = slice(j * 512, (j + 1) * 512)
                    dma("sp", ht[p][:], h_src_v[:, :, cols], [R_h[j]], [R_ht[p]])
                    dma("pool", ptl[p][:], pT[l * 256:(l + 1) * 256, :].rearrange("(c p) n -> p c n", p=128)[:, :, cols],
                        [], [R_ptl[p]])
                    yield from sp(2)
                    for oc in range(8):
                        pp, Rpp = ps_ring.next()
                        for k in range(8):
                            mm(pp[:], wout[:, k, oc * 128:(oc + 1) * 128], ymt[:, k, :], k == 0, k == 7, [R_wC, R_ymt], [Rpp])
                        cp("dve" if oc % 2 == 0 else "act", mb[p][:, oc, :], pp[:], [Rpp], [R_mb[p]])
                        yield
                    yield from sp(2)
                    yield from gen_post(mb[p], R_mb[p], ht[p], R_ht[p], "mix_post")
                    yield from gen_pre(ht[p], R_ht[p], hnD[p], R_hnD[p], "mlp_pre")

                def H1_loads(j):
                    p = j % 2
                    cols = slice(j * 512, (j + 1) * 512)
                    dma("sp", ymt[:], ymix.rearrange("(c p) n -> p c n", p=128)[:, :, cols], [R_ymA[j], R_ymB[j]], [R_ymt])

                def H3(j):
                    p = j % 2
                    cols = slice(j * 512, (j + 1) * 512)
                    yield from gen_post(mb[p], R_mb[p], ht[p], R_ht[p], "mlp_post")
                    yield from gen_pre(ht[p], R_ht[p], hnE, R_hnE, "ple_pre")
                    for oc in range(8):
                        pp, Rpp = ps_ring.next()
                        for k in range(8):
                            mm(pp[:], wgate[:, k, oc * 128:(oc + 1) * 128], hnE[:, k, :], k == 0, k == 7, [R_wC, R_hnE], [Rpp])
                        act(sqg[:, oc, :], pp[:], AF.Sigmoid, [Rpp], [R_sqg])
                        yield
                    yield from sp(2)
                    for oc in range(8):
                        pp, Rpp = ps_ring.next()
                        for k in range(2):
                            mm(pp[:], wproj[:, k, oc * 128:(oc + 1) * 128], ptl[p][:, k, :], k == 0, k == 1, [R_wC, R_ptl[p]], [Rpp])
                        tt("dve", mb[p][:, oc, :], pp[:], sqg[:, oc, :], ALU.mult, [Rpp, R_sqg], [R_mb[p]])
                        yield
                    yield from sp(3)
                    yield from gen_post(mb[p], R_mb[p], ht[p], R_ht[p], "ple_post")
                    dma("sp", h_dst_v[:, :, cols], ht[p][:], [R_ht[p]], [R_h[j]])
                    yield

                def step(side, n):
                    for _ in range(n):
                        try:
                            next(side)
                        except StopIteration:
                            return

                def up(j, g, side):
                    n = j * 8 + g
                    p = j % 2
                    for c4 in range(4):
                        pp, Rpp = ps_ring.next()
                        for k in range(8):
                            mm(pp[:], wu[n % 2][:, k, c4 * 128:(c4 + 1) * 128], hnD[p][:, k, :], k == 0, k == 7,
                               [R_wu[n % 2], R_hnD[p]], [Rpp])
                        rr, Rrr = rl.next()
                        act(rr[:], pp[:], AF.Relu, [Rpp], [Rrr])
                        tt("dve", ag[:, (g % 2) * 4 + c4, :], rr[:], rr[:], ALU.mult, [Rrr], [R_ag[g % 2]])
                        step(side, 1)

                def down(j, g, side):
                    n = j * 8 + g
                    p = j % 2
                    for oc in range(8):
                        pd, Rpd = ps_dn.next()
                        for k4 in range(4):
                            mm(pd[:], wd[n % 2][:, k4, oc * 128:(oc + 1) * 128], ag[:, (g % 2) * 4 + k4, :], k4 == 0, k4 == 3,
                               [R_wd[n % 2], R_ag[g % 2]], [Rpd])
                        if g == 0:
                            cp("dve", mb[p][:, oc, :], pd[:], [Rpd], [R_mb[p]])
                        else:
                            tt("dve", mb[p][:, oc, :], pd[:], mb[p][:, oc, :], ALU.add, [Rpd, R_mb[p]], [R_mb[p]])
                        step(side, 1)

                def chain(*gens):
                    for gq in gens:
                        if gq is not None:
                            yield from gq

                load_wu(0)
                load_wu(1)
                load_wd(0)
                H1_loads(0)
                for _ in H1(0):
                    pass
                for j in range(NT):
                    if j + 1 < NT:
                        H1_loads(j + 1)
                    side = chain(H3(j - 1) if j > 0 else None, H1(j + 1) if j + 1 < NT else None)
                    for g in range(9):
                        n = j * 8 + g
                        if g < 8:
                            up(j, g, side)
                        if g > 0:
                            down(j, g - 1, side)
                        if g < 8:
                            load_wu(n + 2)
                            load_wd(n + 1)
                    for _ in side:
                        pass
                for _ in H3(NT - 1):
                    pass
                P.barrier()
                P.flush(final=(l == L - 1))
    return nc


POOL_WINDOWS = (2, 4, 8, 16)


def _chunkcols(v):
    return np.ascontiguousarray(v.reshape(-1, 128).T)


def prep_shared(inp, L):
    f32 = np.float32
    w_in = np.asarray(inp["w_in"], f32)[:L]
    perm = np.concatenate([np.arange(0, 1024), np.arange(1284, 2308), np.arange(1280, 1284), np.arange(1024, 1280)])
    w_in_r = np.ascontiguousarray(w_in[:, :, perm]).reshape(L * 1024, 2308)
    w_pw = np.ascontiguousarray(np.asarray(inp["w_conf_pw"], f32)[:L]).reshape(L * 256, 256)
    wp = np.asarray(inp["w_pool"], f32)[:L]
    w_pbd = np.zeros((L, 2, 128, 128), f32)
    for blk in range(2):
        for gg in range(2):
            w_pbd[:, blk, gg * 64:(gg + 1) * 64, gg * 64:(gg + 1) * 64] = wp[:, blk * 2 + gg]
    w_pbd = w_pbd.reshape(L * 256, 128)
    w_out = np.ascontiguousarray(np.asarray(inp["w_out"], f32)[:L]).reshape(L * 1024, 1024)
    wu = np.asarray(inp["w_up"], f32)[:L]
    wu = wu.reshape(L, 8, 128, 8, 512).transpose(0, 3, 2, 1, 4)
    w_up = np.ascontiguousarray(wu).reshape(L * 1024, 4096)
    wdn = np.asarray(inp["w_down"], f32)[:L]
    wdn = wdn.reshape(L, 8, 4, 128, 1024).transpose(0, 1, 3, 2, 4)
    w_dn = np.ascontiguousarray(wdn).reshape(L * 1024, 4096)
    w_gate = np.ascontiguousarray(np.asarray(inp["w_ple_gate"], f32)[:L]).reshape(L * 1024, 1024)
    w_proj = np.ascontiguousarray(np.asarray(inp["w_ple_proj"], f32)[:L]).reshape(L * 256, 1024)
    small = np.zeros((128, L * 128), f32)
    for l in range(L):
        o = l * 128
        for name, key in (("mix_pre", "g_mix_pre"), ("mix_post", "g_mix_post"), ("mlp_pre", "g_mlp_pre"),
                          ("mlp_post", "g_mlp_post"), ("ple_pre", "g_ple_pre"), ("ple_post", "g_ple_post")):
            small[:, o + C_G[name]: o + C_G[name] + 8] = _chunkcols(np.asarray(inp[key], f32)[l])
        small[:, o + C_LNG: o + C_LNG + 2] = _chunkcols(np.asarray(inp["conf_ln_g"], f32)[l])
        small[:, o + C_LNB: o + C_LNB + 2] = _chunkcols(np.asarray(inp["conf_ln_b"], f32)[l])
        small[:, o + C_PSC: o + C_PSC + 2] = _chunkcols(np.asarray(inp["pool_scale"], f32)[l])
        dw = np.asarray(inp["w_conf_dw"], f32)[l]
        for blk in range(2):
            small[:, o + C_DW + blk * 31: o + C_DW + blk * 31 + 31] = dw[:, blk * 128:(blk + 1) * 128].T
        sc = np.asarray(inp["w_sc"], f32)[l]
        for blk in range(2):
            small[:, o + C_SC + blk * 3: o + C_SC + blk * 3 + 3] = sc[:, blk * 128:(blk + 1) * 128].T
        small[0:4, o + C_BF] = np.asarray(inp["b_forget"], f32)[l]
    consts = np.zeros((128, 320), f32)
    consts[:, K_ID:K_ID + 128] = np.eye(128, dtype=f32)
    kk, qq = np.meshgrid(np.arange(128), np.arange(128), indexing="ij")
    consts[:, K_MASK:K_MASK + 128] = np.where(kk > qq, NEG, 0.0)
    for blk in range(2):
        for p in range(128):
            w = POOL_WINDOWS[(blk * 128 + p) // 64]
            for t in range(16):
                consts[p, K_PCOEF + blk * 16 + t] = (1.0 / w if t < w else 0.0) - (1.0 if t == 0 else 0.0)
                consts[p, K_PRATIO + blk * 16 + t] = w / min(t + 1, w)
    return dict(w_in=w_in_r, w_pw=w_pw, w_pbd=w_pbd, w_out=w_out, w_up=w_up, w_dn=w_dn, w_gate=w_gate,
                w_proj=w_proj, small=small, consts=consts)


_NC_CACHE = {}
ACTIVE_CORES = [0, 1, 4, 5]


def run(inp, T, L, n_seq, stop=None, dbg=False):
    f32 = np.float32
    shared = prep_shared(inp, L)
    x = np.asarray(inp["x"], f32)
    p = np.asarray(inp["p"], f32)
    active = ACTIVE_CORES[:n_seq]
    zero_map = None
    in_maps = []
    for core in range(8):
        if core in active:
            bi = active.index(core)
            m = dict(shared)
            m["xT"] = np.ascontiguousarray(x[bi].T)
            m["pT"] = np.ascontiguousarray(p[:L, bi].transpose(0, 2, 1)).reshape(L * 256, T)
        else:
            if zero_map is None:
                zero_map = {k: np.zeros_like(v) for k, v in shared.items()}
                zero_map["xT"] = np.zeros((1024, T), f32)
                zero_map["pT"] = np.zeros((L * 256, T), f32)
            m = zero_map
        in_maps.append(m)
    key = (T, L, stop, dbg)
    if key not in _NC_CACHE:
        _NC_CACHE[key] = build(T, L, stop, dbg)
    nc = _NC_CACHE[key]
    res = run_bass_kernel_spmd(nc, in_maps, core_ids=list(range(8)))
    if dbg:
        return res.results[active[0]]
    out = np.stack([np.ascontiguousarray(res.results[active[bi]]["outT"].T) for bi in range(n_seq)], axis=0)
    return out.astype(f32)


def kernel(**inputs):
    return run(inputs, 8192, 4, 4)
```

```python
import contextlib
import numpy as np
import concourse.bass as bass
import concourse.mybir as mybir
from concourse.bass_utils import run_bass_kernel_spmd

F32 = mybir.dt.float32
BF16 = mybir.dt.bfloat16
ALU = mybir.AluOpType
AF = mybir.ActivationFunctionType
ENGS = ("sp", "act", "pe", "dve", "pool")
EPS = 1e-6
NEG = -30000.0


class Res:
    __slots__ = ("name", "w", "r", "excl")

    def __init__(self, name="", excl=False):
        self.name = name
        self.w = {}
        self.r = {}
        self.excl = excl


def PR(name):
    return Res(name, excl=True)


def _merge(d, s, v):
    if d.get(s, 0) < v:
        d[s] = v


class Prog:
    def __init__(self, nc, stack, n_dma_sems=32):
        self.nc = nc
        self.q = {e: [] for e in ENGS}
        self.sem_names = []
        self.sem_count = []
        self.seen = {e: {} for e in ENGS}
        self.pending_reads = {e: [] for e in ENGS}
        self.pending_writes = {e: [] for e in ENGS}
        self.eng_sem = {}
        for e in ("act", "pe", "dve", "pool"):
            self.eng_sem[e] = self._new_sem("c_" + e)
        self.dma_sems = {"sp": [self._new_sem("d%d" % i) for i in range(n_dma_sems)],
                         "pool": [self._new_sem("w%d" % i) for i in range(16)]}
        self.dma_rr = {"sp": 0, "pool": 0}
        self.n_ops = 0
        self.sems = [stack.enter_context(nc.semaphore(n)) for n in self.sem_names]

    def _new_sem(self, name):
        self.sem_names.append(name)
        self.sem_count.append(0)
        return len(self.sem_names) - 1

    def op(self, eng, fn, reads=(), writes=(), inc=True, dma=False):
        waits = {}
        for r in reads:
            for s, v in r.w.items():
                _merge(waits, s, v)
            if r.excl:
                for s, v in r.r.items():
                    if not (eng in self.eng_sem and s == self.eng_sem[eng]):
                        _merge(waits, s, v)
        for w in writes:
            for s, v in w.w.items():
                _merge(waits, s, v)
            for s, v in w.r.items():
                _merge(waits, s, v)
        tok = None
        if dma:
            pool_ = self.dma_sems[eng]
            s = pool_[self.dma_rr[eng] % len(pool_)]
            self.dma_rr[eng] += 1
            if self.sem_count[s] > 0:
                _merge(waits, s, self.sem_count[s])
            self.sem_count[s] += 16
            tok = (s, self.sem_count[s], 16)
        elif inc:
            s = self.eng_sem[eng]
            self.sem_count[s] += 1
            tok = (s, self.sem_count[s], 1)
        seen = self.seen[eng]
        wl = []
        for s, v in waits.items():
            if seen.get(s, 0) < v:
                if eng == "pe" and s == self.eng_sem["pe"]:
                    continue
                seen[s] = v
                wl.append((s, v))
        self.q[eng].append((fn, wl, tok))
        self.n_ops += 1
        if tok is None:
            self.pending_reads[eng].extend(reads)
            self.pending_writes[eng].extend(writes)
        else:
            s, v, _ = tok
            rl = list(reads)
            wr = list(writes)
            if not dma:
                rl += self.pending_reads[eng]
                wr += self.pending_writes[eng]
                self.pending_reads[eng] = []
                self.pending_writes[eng] = []
            for r in rl:
                _merge(r.r, s, v)
            for w in wr:
                _merge(w.w, s, v)
        return tok

    def barrier(self):
        for e in ENGS:
            wl = []
            for s, c in enumerate(self.sem_count):
                if c > 0 and self.seen[e].get(s, 0) < c:
                    self.seen[e][s] = c
                    wl.append((s, c))
            if wl:
                self.q[e].append((None, wl, None))

    def flush(self, final=False):
        nc = self.nc
        sems = self.sems
        if final:
            wl = [(s, c) for s, c in enumerate(self.sem_count) if c > 0]
            self.q["sp"].append((None, wl, None))
        with nc.Block() as block:
            def run(eng_name):
                ops = self.q[eng_name]

                def f(e):
                    for fn, wl, tok in ops:
                        for s, v in wl:
                            e.wait_ge(sems[s], v)
                        if fn is None:
                            continue
                        ins = fn(e)
                        if tok is not None:
                            ins.then_inc(sems[tok[0]], tok[2])
                return f
            block.sync(run("sp"))
            block.scalar(run("act"))
            block.tensor(run("pe"))
            block.vector(run("dve"))
            block.gpsimd(run("pool"))
        self.q = {e: [] for e in ENGS}


class Ring:
    def __init__(self, items):
        self.items = items
        self.i = 0

    def next(self):
        it = self.items[self.i % len(self.items)]
        self.i += 1
        return it


C_G = {"mix_pre": 0, "mix_post": 8, "mlp_pre": 16, "mlp_post": 24, "ple_pre": 32, "ple_post": 40}
C_LNG, C_LNB, C_PSC, C_DW, C_SC, C_BF = 48, 50, 52, 54, 116, 122
K_ID, K_MASK, K_PCOEF, K_PRATIO = 0, 128, 256, 288


def build(T, L, stop=None, dbg=False):
    NT = T // 512
    NB = T // 128
    nc = bass.Bass("TRN2", target_bir_lowering=False)

    def din(name, shape, dt=F32):
        return nc.dram_tensor(name, shape, dt, kind="ExternalInput").ap()

    def dscr(name, shape, dt):
        return nc.dram_tensor(name, shape, dt, kind="ExternalOutput" if dbg else "Internal").ap()

    xT = din("xT", [1024, T])
    pT = din("pT", [L * 256, T])
    w_in = din("w_in", [L * 1024, 2308])
    w_pw = din("w_pw", [L * 256, 256])
    w_pbd = din("w_pbd", [L * 256, 128])
    w_out = din("w_out", [L * 1024, 1024])
    w_up = din("w_up", [L * 1024, 4096])
    w_dn = din("w_dn", [L * 1024, 4096])
    w_gate = din("w_gate", [L * 1024, 1024])
    w_proj = din("w_proj", [L * 256, 1024])
    small = din("small", [128, L * 128])
    consts = din("consts", [128, 320])
    outT = nc.dram_tensor("outT", [1024, T], F32, kind="ExternalOutput").ap()

    hT = dscr("hT", [1024, T], F32)
    wup_bf = dscr("wup_bf", [L * 1024, 4096], BF16)
    wdn_bf = dscr("wdn_bf", [L * 1024, 4096], BF16)
    mixin = dscr("mixin", [1024, 32 + T], BF16)
    qT = dscr("qT", [256, T], BF16)
    kT = dscr("kT", [256, T], BF16)
    c3d = dscr("c3d", [4, 3, T], BF16)
    vaug = dscr("vaug", [NB, 128, 512], BF16)
    ymix = dscr("ymix", [1024, T], BF16)

    R_h = [Res("h%d" % j) for j in range(NT)]
    R_mixin = [Res("mi%d" % j) for j in range(NT + 1)]
    R_q = [Res("q%d" % j) for j in range(NT)]
    R_k = [Res("k%d" % j) for j in range(NT)]
    R_c3 = [Res("c3%d" % j) for j in range(NT)]
    R_v = [Res("v%d" % j) for j in range(NT)]
    R_ymA = [Res("ymA%d" % j) for j in range(NT)]
    R_ymB = [Res("ymB%d" % j) for j in range(NT)]
    R_wup = [[Res("wu%d_%d" % (l, g)) for g in range(8)] for l in range(L)]
    R_wdn = [[Res("wd%d_%d" % (l, g)) for g in range(8)] for l in range(L)]

    top = contextlib.ExitStack()
    with top:
        P = Prog(nc, top)

        uid = [0]

        def sbt(st, name, shape, dt):
            uid[0] += 1
            return st.enter_context(nc.sbuf_tensor("%s_u%d" % (name, uid[0]), shape, dt))

        def pst(st, name, shape=(128, 512), dt=F32):
            uid[0] += 1
            return st.enter_context(nc.psum_tensor("%s_u%d" % (name, uid[0]), list(shape), dt))

        small_sb = sbt(top, "small_sb", [128, L * 128], F32)
        consts_sb = sbt(top, "consts_sb", [128, 320], F32)
        ident_bf = sbt(top, "ident_bf", [128, 128], BF16)
        mask_bf = sbt(top, "mask_bf", [128, 128], BF16)
        ones_bf = sbt(top, "ones_bf", [128, 128], BF16)
        ones_f = sbt(top, "ones_f", [128, 128], F32)
        cneg = sbt(top, "cneg", [128, NB, 4], F32)
        R_small, R_consts, R_cst2, R_cneg = Res("small"), Res("consts"), Res("cst2"), Res("cneg")
        ident_f = consts_sb[:, K_ID:K_ID + 128]

        def scol(l, c, rows=slice(0, 128)):
            return small_sb[rows, l * 128 + c: l * 128 + c + 1]

        def dma(eng, out, in_, reads, writes):
            P.op(eng, lambda e: e.dma_start(out=out, in_=in_), reads=reads, writes=writes, dma=True)

        def mm(out, lhsT, rhs, start, stop, reads, writes):
            P.op("pe", lambda e: e.matmul(out, lhsT=lhsT, rhs=rhs, start=start, stop=stop),
                 reads=reads, writes=writes, inc=stop)

        def act(out, in_, func, reads, writes, bias=None, scale=None):
            kw = {}
            if bias is not None:
                kw["bias"] = bias
            if scale is not None:
                kw["scale"] = scale
            P.op("act", lambda e: e.activation(out, in_, func, **kw), reads=reads, writes=writes)

        def tt(eng, out, in0, in1, op, reads, writes):
            P.op(eng, lambda e: e.tensor_tensor(out, in0, in1, op), reads=reads, writes=writes)

        def stt(eng, out, in0, scalar, in1, op0, op1, reads, writes):
            P.op(eng, lambda e: e.scalar_tensor_tensor(out, in0, scalar, in1, op0, op1), reads=reads, writes=writes)

        def norm_scale(eng, out, in0, gcol, rstd_ap, reads, writes, ptmp_ring):
            if eng == "dve":
                stt("dve", out, in0, gcol, rstd_ap, ALU.mult, ALU.mult, reads, writes)
            else:
                tb, Rtb = ptmp_ring.next()
                tt("pool", tb[:], in0, rstd_ap, ALU.mult, reads, [Rtb])
                P.op("pool", lambda e: e.tensor_scalar_mul(out, tb[:], gcol), reads=[Rtb] + list(reads), writes=writes)

        def ts(eng, out, in0, s1, s2, op0, op1, reads, writes):
            if s2 is None:
                P.op(eng, lambda e: e.tensor_single_scalar(out, in0, s1, op0), reads=reads, writes=writes)
            else:
                P.op(eng, lambda e: e.tensor_scalar(out, in0, s1, s2, op0, op1), reads=reads, writes=writes)

        def cp(eng, out, in_, reads, writes):
            if eng == "act":
                P.op("act", lambda e: e.copy(out, in_), reads=reads, writes=writes)
            else:
                P.op(eng, lambda e: e.tensor_copy(out, in_), reads=reads, writes=writes)

        def memset(eng, ap, val, writes):
            P.op(eng, lambda e: e.memset(ap, val), writes=writes)

        def recip(out, in_, reads, writes):
            P.op("dve", lambda e: e.reciprocal(out, in_), reads=reads, writes=writes)

        def rms_stats(sq, R_sq, ps_stat, R_ps, rs, rstd, R_rs, R_rstd):
            for c in range(8):
                mm(ps_stat[:], ones_bf[:], sq[:, c, :], c == 0, c == 7, list(R_sq) + [R_cst2], [R_ps])
            act(rs[:], ps_stat[:], AF.Sqrt, [R_ps], [R_rs], bias=EPS, scale=1.0 / 1024.0)
            recip(rstd[:], rs[:], [R_rs], [R_rstd])

        with contextlib.ExitStack() as st:
            zt = sbt(st, "zt", [128, 8, 32], BF16)
            R_zt = Res("zt")
            dma("sp", small_sb[:], small, [], [R_small])
            dma("sp", consts_sb[:], consts, [], [R_consts])
            cp("dve", ident_bf[:], consts_sb[:, K_ID:K_ID + 128], [R_consts], [R_cst2])
            cp("dve", mask_bf[:], consts_sb[:, K_MASK:K_MASK + 128], [R_consts], [R_cst2])
            memset("pool", ones_bf[:], 1.0, [R_cst2])
            memset("pool", ones_f[:], 1.0, [R_cst2])
            memset("pool", zt[:], 0.0, [R_zt])
            dma("sp", mixin.rearrange("(c p) n -> p c n", p=128)[:, :, 0:32], zt[:], [R_zt], [R_mixin[0]])
            P.barrier()
            P.flush(final=(stop == "pro"))
        if stop == "pro":
            return nc

        def cast_mlp_weights(l):
            for g in range(8):
                for (src, dst, RR) in ((w_up, wup_bf, R_wup), (w_dn, wdn_bf, R_wdn)):
                    r0 = l * 1024 + g * 128
                    for hh in range(2):
                        dma("pool", dst[r0:r0 + 128, hh * 2048:(hh + 1) * 2048],
                            src[r0:r0 + 128, hh * 2048:(hh + 1) * 2048], [], [RR[l][g]])

        for l in range(L):
            h_src = xT if l == 0 else hT
            h_dst = outT if l == L - 1 else hT
            h_src_v = h_src.rearrange("(c p) n -> p c n", p=128)
            h_dst_v = h_dst.rearrange("(c p) n -> p c n", p=128)

            with contextlib.ExitStack() as st:
                win = sbt(st, "win", [128, 8, 2308], BF16)
                R_win = Res("win")
                ht = [sbt(st, "ht%d" % i, [128, 8, 512], F32) for i in range(2)]
                R_ht = [Res("ht0"), Res("ht1")]
                sq = sbt(st, "sq", [128, 8, 512], BF16)
                R_sq = Res("sq")
                xn2 = [sbt(st, "xn%d" % i, [128, 8, 512], BF16) for i in range(2)]
                R_xn2 = [Res("xn0"), Res("xn1")]
                rs = sbt(st, "rs", [128, 512], F32)
                rstd = sbt(st, "rstd", [128, 512], F32)
                R_rs, R_rstd = Res("rs"), Res("rstd")
                tmpf = [sbt(st, "tmpf%d" % i, [128, 512], F32) for i in range(2)]
                tmp_ring = Ring([(tmpf[i], Res("tmpf%d" % i)) for i in range(2)])
                ptmp_ring = Ring([(sbt(st, "ptmp%d" % i, [128, 512], F32), Res("ptmp%d" % i)) for i in range(2)])
                mstage = [sbt(st, "mstage%d" % i, [128, 8, 512], BF16) for i in range(2)]
                R_mst = [Res("mst0"), Res("mst1")]
                qstage = [sbt(st, "qstage%d" % i, [128, 2, 512], BF16) for i in range(2)]
                R_qst = [Res("qst0"), Res("qst1")]
                kstage = [sbt(st, "kstage%d" % i, [128, 2, 512], BF16) for i in range(2)]
                R_kst = [Res("kst0"), Res("kst1")]
                vstage = [sbt(st, "vstage%d" % i, [128, 4, 512], BF16) for i in range(2)]
                R_vst = [Res("vst0"), Res("vst1")]
                xb = sbt(st, "xb", [4, 512], F32)
                ef = sbt(st, "ef", [4, 512], F32)
                lf = sbt(st, "lf", [4, 512], F32)
                ones4 = sbt(st, "ones4", [4, 512], F32)
                cc = [sbt(st, "cc%d" % i, [4, 512], F32) for i in range(2)]
                r1 = sbt(st, "r1", [4, 512], F32)
                r2 = sbt(st, "r2", [4, 512], F32)
                c3 = [sbt(st, "c3_%d" % i, [4, 3, 512], BF16) for i in range(2)]
                R_f = Res("fmisc")
                R_cc = [Res("cc0"), Res("cc1")]
                R_c3s = [Res("c3s0"), Res("c3s1")]
                ps_stat = pst(st, "ps_stat")
                R_pstat = PR("ps_stat")
                ps_ring = Ring([(pst(st, "psr%d" % i), PR("psr%d" % i)) for i in range(4)])
                ps_v = Ring([(pst(st, "psv%d" % i), PR("psv%d" % i)) for i in range(2)])
                ps_t = pst(st, "ps_t")
                R_pst = PR("ps_t")

                for k in range(8):
                    for hh in range(2):
                        dma("pool", win[:, k, hh * 1154:(hh + 1) * 1154],
                            w_in[l * 1024 + k * 128: l * 1024 + (k + 1) * 128, hh * 1154:(hh + 1) * 1154],
                            [], [R_win])
                cast_mlp_weights(l)
                memset("pool", ones4[:], 1.0, [R_f])
                for i in range(2):
                    memset("pool", vstage[i][:], 1.0, [R_vst[i]])

                def load_h(j):
                    dma("sp", ht[j % 2][:], h_src_v[:, :, j * 512:(j + 1) * 512], [R_h[j]], [R_ht[j % 2]])

                cur = {}

                def proj(ps, R_ps, col0, M):
                    xn, R_xn = cur["xn"], cur["R_xn"]
                    for k in range(8):
                        mm(ps[0:M, :], win[:, k, col0:col0 + M], xn[:, k, :], k == 0, k == 7, [R_win, R_xn], [R_ps])

                def norm_a(j):
                    bb = j % 2
                    act(sq[:], ht[bb][:], AF.Square, [R_ht[bb]], [R_sq])

                def norm_b(j):
                    bb = j % 2
                    rms_stats(sq, [R_sq], ps_stat, R_pstat, rs, rstd, R_rs, R_rstd)
                    for c in range(8):
                        norm_scale("dve", xn2[bb][:, c, :], ht[bb][:, c, :], scol(l, C_G["mix_pre"] + c),
                                   rstd[:], [R_ht[bb], R_rstd, R_small], [R_xn2[bb]], ptmp_ring)

                load_h(0)
                if NT > 1:
                    load_h(1)
                norm_a(0)
                norm_b(0)
                for j in range(NT):
                    b = j % 2
                    h = ht[b]
                    cols = slice(j * 512, (j + 1) * 512)
                    xnb, R_xnb = xn2[b], R_xn2[b]
                    xn, R_xn = xnb, R_xnb
                    cur["xn"], cur["R_xn"] = xnb, R_xnb
                    pf, Rpf = ps_ring.next()
                    proj(pf, Rpf, 2048, 4)
                    ts("dve", xb[:], pf[0:4, :], scol(l, C_BF, slice(0, 4)), None, ALU.add, None, [Rpf, R_small], [R_f])
                    act(ef[:], xb[:], AF.Exp, [R_f], [R_f], scale=-1.0)
                    act(lf[:], ef[:], AF.Ln, [R_f], [R_f], bias=1.0, scale=1.0)
                    init = 0.0 if j == 0 else cc[1 - b][:, 511:512]
                    P.op("dve", (lambda o, d0, d1, ini: (lambda e: e.tensor_tensor_scan(o, d0, d1, ini, ALU.mult, ALU.subtract)))(
                        cc[b][:], ones4[:], lf[:], init), reads=[R_f, R_cc[1 - b]], writes=[R_cc[b]])
                    if j + 1 < NT:
                        norm_a(j + 1)
                    for blk in range(2):
                        pa, Rpa = ps_ring.next()
                        pb, Rpb = ps_ring.next()
                        proj(pa, Rpa, (0 + blk) * 128, 128)
                        proj(pb, Rpb, (2 + blk) * 128, 128)
                        tb, Rtb = tmp_ring.next()
                        act(tb[:], pb[:], AF.Sigmoid, [Rpb], [Rtb])
                        tt("dve", mstage[b][:, 0 + blk, :], pa[:], tb[:], ALU.mult, [Rpa, Rtb], [R_mst[b]])
                    cp("dve", c3[b][:, 0, :], cc[b][:], [R_cc[b]], [R_c3s[b]])
                    tt("dve", r1[:], cc[b][:], c3[b][:, 0, :], ALU.subtract, [R_cc[b], R_c3s[b]], [R_f])
                    cp("dve", c3[b][:, 1, :], r1[:], [R_f], [R_c3s[b]])
                    tt("dve", r2[:], r1[:], c3[b][:, 1, :], ALU.subtract, [R_f, R_c3s[b]], [R_f])
                    cp("dve", c3[b][:, 2, :], r2[:], [R_f], [R_c3s[b]])
                    for blk in range(2):
                        pa, Rpa = ps_ring.next()
                        pb, Rpb = ps_ring.next()
                        proj(pa, Rpa, (8 + blk) * 128, 128)
                        proj(pb, Rpb, (12 + blk) * 128, 128)
                        tb, Rtb = tmp_ring.next()
                        cp("act", tb[:], pb[:], [Rpb], [Rtb])
                        tt("dve", mstage[b][:, 2 + blk, :], pa[:], tb[:], ALU.mult, [Rpa, Rtb], [R_mst[b]])
                    if j + 1 < NT:
                        norm_b(j + 1)
                        if j + 2 < NT:
                            load_h(j + 2)
                    for blk in range(2):
                        pa, Rpa = ps_ring.next()
                        proj(pa, Rpa, (14 + blk) * 128, 128)
                        cp("dve", mstage[b][:, 4 + blk, :], pa[:], [Rpa], [R_mst[b]])
                        pb, Rpb = ps_ring.next()
                        proj(pb, Rpb, (10 + blk) * 128, 128)
                        cp("act", mstage[b][:, 6 + blk, :], pb[:], [Rpb], [R_mst[b]])
                    for blk in range(2):
                        pa, Rpa = ps_ring.next()
                        proj(pa, Rpa, (4 + blk) * 128, 128)
                        act(qstage[b][:, blk, :], pa[:], AF.Copy, [Rpa], [R_qst[b]], scale=0.125)
                        pb, Rpb = ps_ring.next()
                        proj(pb, Rpb, (6 + blk) * 128, 128)
                        cp("dve", kstage[b][:, blk, :], pb[:], [Rpb], [R_kst[b]])
                    for s in range(4):
                        pv, Rpv = ps_v.next()
                        for k in range(8):
                            mm(pv[:, 0:256], xn[:, k, s * 128:(s + 1) * 128], win[:, k, 2052:2308], k == 0, k == 7,
                               [R_win, R_xn], [Rpv])
                        dst = vstage[b][:, s, :].rearrange("p (h c) -> p h c", h=4)[:, :, 0:64]
                        src = pv[:, 0:256].rearrange("p (h c) -> p h c", h=4)
                        cp("dve" if s % 2 == 0 else "act", dst, src, [Rpv], [R_vst[b]])
                    dma("sp", mixin.rearrange("(c p) n -> p c n", p=128)[:, :, 32 + j * 512: 32 + (j + 1) * 512],
                        mstage[b][:], [R_mst[b]], [R_mixin[j + 1]])
                    dma("sp", qT.rearrange("(c p) n -> p c n", p=128)[:, :, cols], qstage[b][:], [R_qst[b]], [R_q[j]])
                    dma("sp", kT.rearrange("(c p) n -> p c n", p=128)[:, :, cols], kstage[b][:], [R_kst[b]], [R_k[j]])
                    dma("sp", c3d[:, :, cols], c3[b][:], [R_c3s[b]], [R_c3[j]])
                    dma("sp", vaug[j * 4:(j + 1) * 4].rearrange("s p c -> p s c"), vstage[b][:], [R_vst[b]], [R_v[j]])
                P.barrier()
                P.flush(final=(stop == "A1"))
            if stop == "A1":
                return nc

            with contextlib.ExitStack() as st:
                dconf = sbt(st, "dconf", [128, 2, 31, 128], BF16)
                dsc = sbt(st, "dsc", [128, 2, 3, 128], BF16)
                dpl = sbt(st, "dpl", [128, 2, 16, 128], BF16)
                R_dg = Res("diag")
                wpw = sbt(st, "wpw", [128, 2, 256], BF16)
                wpbd = sbt(st, "wpbd", [128, 2, 128], BF16)
                R_wA2 = Res("wA2")
                mt = [sbt(st, "mt%d" % i, [128, 8, 544], BF16) for i in range(2)]
                R_mt = [Res("mt0"), Res("mt1")]
                xc = sbt(st, "xc", [128, 2, 512], F32)
                sq2 = sbt(st, "sq2", [128, 2, 512], F32)
                R_xc, R_sq2 = Res("xc"), Res("sq2")
                mean = sbt(st, "mean", [128, 512], F32)
                msq = sbt(st, "msq", [128, 512], F32)
                var = sbt(st, "var", [128, 512], F32)
                sd = sbt(st, "sd", [128, 512], F32)
                rstd2 = sbt(st, "rstd2", [128, 512], F32)
                R_mean, R_msq, R_var, R_sd, R_rstd2 = Res("mean"), Res("msq"), Res("var"), Res("sd"), Res("rstd2")
                xm = [sbt(st, "xm%d" % i, [128, 512], F32) for i in range(2)]
                R_xm = [Res("xm0"), Res("xm1")]
                xnn = [sbt(st, "xnn%d" % i, [128, 512], F32) for i in range(2)]
                R_xnn = [Res("xnn0"), Res("xnn1")]
                sconf = sbt(st, "sconf", [128, 2, 512], BF16)
                R_sconf = Res("sconf")
                dpool = sbt(st, "dpool", [128, 2, 512], BF16)
                R_dpool = Res("dpool")
                t16 = sbt(st, "t16", [128, 2, 16], F32)
                R_t16 = Res("t16")
                ystage = [sbt(st, "ystage%d" % i, [128, 6, 512], BF16) for i in range(2)]
                R_yst = [Res("yst0"), Res("yst1")]
                psc = [pst(st, "psc%d" % i) for i in range(2)]
                R_psc = [PR("psc0"), PR("psc1")]
                ps_s1 = pst(st, "ps_s1")
                ps_s2 = pst(st, "ps_s2")
                R_s1, R_s2 = PR("s1"), PR("s2")
                ps_ring = Ring([(pst(st, "psq%d" % i), PR("psq%d" % i)) for i in range(3)])

                dma("pool", wpw[:], w_pw[l * 256:(l + 1) * 256, :].rearrange("(c p) n -> p c n", p=128), [], [R_wA2])
                dma("pool", wpbd[:], w_pbd[l * 256:(l + 1) * 256, :].rearrange("(c p) n -> p c n", p=128), [], [R_wA2])
                for blk in range(2):
                    for k in range(31):
                        P.op("pool", (lambda o, s: (lambda e: e.tensor_scalar_mul(o, ident_f, s)))(
                            dconf[:, blk, k, :], scol(l, C_DW + blk * 31 + k)), reads=[R_consts, R_small], writes=[R_dg])
                    for k in range(3):
                        P.op("pool", (lambda o, s: (lambda e: e.tensor_scalar_mul(o, ident_f, s)))(
                            dsc[:, blk, k, :], scol(l, C_SC + blk * 3 + k)), reads=[R_consts, R_small], writes=[R_dg])
                    for k in range(16):
                        P.op("pool", (lambda o, s: (lambda e: e.tensor_scalar_mul(o, ident_f, s)))(
                            dpl[:, blk, k, :], consts_sb[:, K_PCOEF + blk * 16 + k: K_PCOEF + blk * 16 + k + 1]),
                            reads=[R_consts], writes=[R_dg])

                def load_mt(j):
                    dma("sp", mt[j % 2][:], mixin.rearrange("(c p) n -> p c n", p=128)[:, :, j * 512: j * 512 + 544],
                        [R_mixin[j], R_mixin[j + 1]], [R_mt[j % 2]])

                load_mt(0)
                sect = stop.split(":")[1] if (stop and ":" in stop) else "all"

                def on(*names):
                    return sect == "all" or sect in names

                for j in range(NT):
                    b = j % 2
                    if j + 1 < NT:
                        load_mt(j + 1)
                    m_ = mt[b]
                    cols = slice(j * 512, (j + 1) * 512)
                    if sect == "conv1":
                        for blk in range(2):
                            for k in range(31):
                                mm(psc[blk][:], dconf[:, blk, k, :], m_[:, blk, 2 + k: 2 + k + 512], k == 0, k == 30,
                                   [R_dg, R_mt[b]], [R_psc[blk]])
                            cp("dve", xc[:, blk, :], psc[blk][:], [R_psc[blk]], [R_xc])
                    if sect == "conv2":
                        for blk in range(2):
                            for k in range(3):
                                mm(psc[blk][:], dconf[:, blk, k, :], m_[:, blk, 2 + k: 2 + k + 512], k == 0, k == 2,
                                   [R_dg, R_mt[b]], [R_psc[blk]])
                            act(sq2[:, blk, :], psc[blk][:], AF.Square, [R_psc[blk]], [R_sq2])
                    if sect == "pool1":
                        for blk in range(2):
                            pp, Rpp = ps_ring.next()
                            for k in range(16):
                                mm(pp[:], dpl[:, blk, k, :], m_[:, 4 + blk, 32 - k: 32 - k + 512], k == 0, k == 15,
                                   [R_dg, R_mt[b]], [Rpp])
                            cp("dve", dpool[:, blk, :], pp[:], [Rpp], [R_dpool])
                    if on("conv", "ln", "silu", "conf"):
                        for blk in range(2):
                            for k in range(31):
                                mm(psc[blk][:], dconf[:, blk, k, :], m_[:, blk, 2 + k: 2 + k + 512], k == 0, k == 30,
                                   [R_dg, R_mt[b]], [R_psc[blk]])
                            cp("dve", xc[:, blk, :], psc[blk][:], [R_psc[blk]], [R_xc])
                            act(sq2[:, blk, :], xc[:, blk, :], AF.Square, [R_xc], [R_sq2])
                    if on("sc"):
                        for blk in range(2):
                            pp, Rpp = ps_ring.next()
                            for k in range(3):
                                mm(pp[:], dsc[:, blk, k, :], m_[:, 2 + blk, 30 + k: 30 + k + 512], k == 0, k == 2,
                                   [R_dg, R_mt[b]], [Rpp])
                            tt("dve", ystage[b][:, 2 + blk, :], pp[:], m_[:, 6 + blk, 32:544], ALU.mult, [Rpp, R_mt[b]], [R_yst[b]])
                    if on("pool"):
                        for blk in range(2):
                            pp, Rpp = ps_ring.next()
                            for k in range(16):
                                mm(pp[:], dpl[:, blk, k, :], m_[:, 4 + blk, 32 - k: 32 - k + 512], k == 0, k == 15,
                                   [R_dg, R_mt[b]], [Rpp])
                            cp("act", dpool[:, blk, :], pp[:], [Rpp], [R_dpool])
                            if j == 0:
                                tt("dve", t16[:, blk, :], pp[:, 0:16], m_[:, 4 + blk, 32:48], ALU.add, [Rpp, R_mt[b]], [R_t16])
                                tt("pool", t16[:, blk, :], t16[:, blk, :],
                                   consts_sb[:, K_PRATIO + blk * 16: K_PRATIO + blk * 16 + 16], ALU.mult, [R_t16, R_consts], [R_t16])
                                tt("dve", dpool[:, blk, 0:16], t16[:, blk, :], m_[:, 4 + blk, 32:48], ALU.subtract,
                                   [R_t16, R_mt[b]], [R_dpool])
                        for blk in range(2):
                            pp, Rpp = ps_ring.next()
                            mm(pp[:], wpbd[:, blk, :], dpool[:, blk, :], True, True, [R_wA2, R_dpool], [Rpp])
                            act(ystage[b][:, 4 + blk, :], pp[:], AF.Copy, [Rpp, R_small], [R_yst[b]], scale=scol(l, C_PSC + blk))
                    if on("ln", "silu", "conf"):
                        for blk in range(2):
                            mm(ps_s1[:], ones_f[:], xc[:, blk, :], blk == 0, blk == 1, [R_xc, R_cst2], [R_s1])
                        for blk in range(2):
                            mm(ps_s2[:], ones_f[:], sq2[:, blk, :], blk == 0, blk == 1, [R_sq2, R_cst2], [R_s2])
                        act(mean[:], ps_s1[:], AF.Copy, [R_s1], [R_mean], scale=1.0 / 256.0)
                        tt("pool", msq[:], mean[:], mean[:], ALU.mult, [R_mean], [R_msq])
                        stt("dve", var[:], ps_s2[:], 1.0 / 256.0, msq[:], ALU.mult, ALU.subtract, [R_s2, R_msq], [R_var])
                        act(sd[:], var[:], AF.Sqrt, [R_var], [R_sd], bias=EPS, scale=1.0)
                        recip(rstd2[:], sd[:], [R_sd], [R_rstd2])
                        for blk in range(2):
                            tt("dve", xm[blk][:], xc[:, blk, :], mean[:], ALU.subtract, [R_xc, R_mean], [R_xm[blk]])
                            tt("pool", xnn[blk][:], xm[blk][:], rstd2[:], ALU.mult, [R_xm[blk], R_rstd2], [R_xnn[blk]])
                    if on("silu", "conf"):
                        for blk in range(2):
                            act(sconf[:, blk, :], xnn[blk][:], AF.Silu, [R_xnn[blk], R_small], [R_sconf],
                                bias=scol(l, C_LNB + blk), scale=scol(l, C_LNG + blk))
                    if on("conf"):
                        for oc in range(2):
                            pp, Rpp = ps_ring.next()
                            for kb in range(2):
                                mm(pp[:], wpw[:, kb, oc * 128:(oc + 1) * 128], sconf[:, kb, :], kb == 0, kb == 1,
                                   [R_wA2, R_sconf], [Rpp])
                            cp("dve", ystage[b][:, oc, :], pp[:], [Rpp], [R_yst[b]])
                    ym_v = ymix.rearrange("(c p) n -> p c n", p=128)
                    dma("sp", ym_v[:, 0:2, cols], ystage[b][:, 0:2, :], [R_yst[b]], [R_ymA[j]])
                    dma("sp", ym_v[:, 4:8, cols], ystage[b][:, 2:6, :], [R_yst[b]], [R_ymA[j]])
                P.barrier()
                P.flush(final=(stop is not None and stop.startswith("A2")))
            if stop is not None and stop.startswith("A2"):
                return nc

            with contextlib.ExitStack() as st:
                kaug = sbt(st, "kaug", [128, 4, T], BF16)
                R_kaug = [Res("kaug%d" % j) for j in range(NT)]
                R_kaugc = Res("kaugc")
                vsb = sbt(st, "vsb", [128, NB, 512], BF16)
                R_vsb = [Res("vsb%d" % j) for j in range(NT)]
                qa = [sbt(st, "qa%d" % i, [128, 4, 512], BF16) for i in range(2)]
                R_qa = [Res("qa0"), Res("qa1")]
                pts = Ring([(sbt(st, "pt%d" % i, [128, 1024], BF16), Res("pt%d" % i)) for i in range(3)])
                rec = sbt(st, "rec", [128, 512], F32)
                R_rec = Res("rec")
                yst = Ring([(sbt(st, "ysb%d" % i, [64, 512], BF16), Res("ysb%d" % i)) for i in range(2)])
                ps_s = Ring([(pst(st, "pss%d" % i, (128, 1024)), PR("pss%d" % i)) for i in range(3)])
                ps_o = Ring([(pst(st, "pso%d" % i), PR("pso%d" % i)) for i in range(2)])

                for i in range(2):
                    memset("dve", qa[i][64:96, :, :], -1.0, [R_qa[i]])
                def kv_load(j):
                    cols = slice(j * 512, (j + 1) * 512)
                    memset("dve" if j % 2 == 0 else "pool", kaug[64:96, :, cols], 0.0, [R_kaug[j]])
                    memset("dve" if j % 2 == 0 else "pool", kaug[64:67, :, cols], 1.0, [R_kaug[j]])
                    dma("sp", kaug[0:64, :, cols], kT.rearrange("(h r) n -> r h n", r=64)[:, :, cols], [R_k[j]], [R_kaug[j]])
                    dma("sp", kaug[67:70, :, cols], c3d.rearrange("h j n -> j h n")[:, :, cols], [R_c3[j]], [R_kaug[j]])
                    dma("sp", vsb[:, j * 4:(j + 1) * 4, :], vaug[j * 4:(j + 1) * 4].rearrange("s p c -> p s c"),
                        [R_v[j]], [R_vsb[j]])

                def load_q(j):
                    cols = slice(j * 512, (j + 1) * 512)
                    dma("sp", qa[j % 2][0:64, :, :], qT.rearrange("(h r) n -> r h n", r=64)[:, :, cols], [R_q[j]], [R_qa[j % 2]])
                    dma("sp", qa[j % 2][64:67, :, :], c3d.rearrange("h j n -> j h n")[:, :, cols], [R_c3[j]], [R_qa[j % 2]])

                kv_load(0)
                load_q(0)
                for j in range(1, NT):
                    kv_load(j)
                for j in range(NT):
                    b = j % 2
                    if j + 1 < NT:
                        load_q(j + 1)
                    cols = slice(j * 512, (j + 1) * 512)
                    for h in range(4):
                        po, Rpo = ps_o.next()
                        units = [("pair", i0) for i0 in range(0, 4 * j, 2)] + [("diag", dj) for dj in range(4)]

                        def qk(u):
                            kind, v = u
                            pS, RpS = ps_s.next()
                            if kind == "pair":
                                for half in range(2):
                                    i = v + half
                                    mm(pS[:, half * 512:(half + 1) * 512], kaug[0:96, h, i * 128:(i + 1) * 128],
                                       qa[b][0:96, h, :], True, True, [R_kaug[i // 4], R_kaugc, R_qa[b]], [RpS])
                            else:
                                i = 4 * j + v
                                c0 = 128 * v
                                mm(pS[:, c0:512], kaug[0:96, h, i * 128:(i + 1) * 128], qa[b][0:96, h, c0:512],
                                   True, False, [R_kaug[i // 4], R_kaugc, R_qa[b]], [RpS])
                                mm(pS[:, c0:c0 + 128], ident_bf[:], mask_bf[:], False, True, [R_cst2], [RpS])
                            return pS, RpS

                        LA = 2
                        pend = [qk(units[x]) for x in range(min(LA, len(units)))]
                        for ui, u in enumerate(units):
                            pS, RpS = pend.pop(0)
                            if ui + LA < len(units):
                                pend.append(qk(units[ui + LA]))
                            pt, Rpt = pts.next()
                            kind, v = u
                            first = (ui == 0)
                            last = (ui == len(units) - 1)
                            if kind == "pair":
                                act(pt[:, :], pS[:, :], AF.Exp, [RpS], [Rpt])
                                for half in range(2):
                                    i = v + half
                                    mm(po[:, :], vsb[:, i, h * 128:(h + 1) * 128], pt[:, half * 512:(half + 1) * 512],
                                       first and half == 0, False, [R_vsb[i // 4], Rpt], [Rpo])
                            else:
                                i = 4 * j + v
                                c0 = 128 * v
                                act(pt[:, c0:512], pS[:, c0:512], AF.Exp, [RpS], [Rpt])
                                mm(po[:, c0:512], vsb[:, i, h * 128:(h + 1) * 128], pt[:, c0:512], first, last,
                                   [R_vsb[i // 4], Rpt], [Rpo])
                        recip(rec[64:128, :], po[64:128, :], [Rpo], [R_rec])
                        ys, Rys = yst.next()
                        tt("dve", ys[:], po[0:64, :], rec[64:128, :], ALU.mult, [Rpo, R_rec], [Rys])
                        dma("sp", ymix[256 + 64 * h: 256 + 64 * h + 64, cols], ys[:], [Rys], [R_ymB[j]])
                P.barrier()
                P.flush(final=(stop == "B"))
            if stop == "B":
                return nc

            with contextlib.ExitStack() as st:
                wout = sbt(st, "wout", [128, 8, 1024], BF16)
                wgate = sbt(st, "wgate", [128, 8, 1024], BF16)
                wproj = sbt(st, "wproj", [128, 2, 1024], BF16)
                R_wC = Res("wC")
                wu = [sbt(st, "wu%d" % i, [128, 8, 512], BF16) for i in range(2)]
                wd = [sbt(st, "wd%d" % i, [128, 4, 1024], BF16) for i in range(2)]
                R_wu = [Res("wus%d" % i) for i in range(2)]
                R_wd = [Res("wds%d" % i) for i in range(2)]
                ymt = sbt(st, "ymt", [128, 8, 512], BF16)
                R_ymt = Res("ymt")
                ht = [sbt(st, "hc%d" % i, [128, 8, 512], F32) for i in range(2)]
                R_ht = [Res("hc0"), Res("hc1")]
                ptl = [sbt(st, "ptl%d" % i, [128, 2, 512], BF16) for i in range(2)]
                R_ptl = [Res("ptl0"), Res("ptl1")]
                mb = [sbt(st, "mb%d" % i, [128, 8, 512], F32) for i in range(2)]
                R_mb = [Res("mb0"), Res("mb1")]
                hnD = [sbt(st, "hnD%d" % i, [128, 8, 512], BF16) for i in range(2)]
                R_hnD = [Res("hnD0"), Res("hnD1")]
                hnE = sbt(st, "hnE", [128, 8, 512], BF16)
                R_hnE = Res("hnE")
                sqg = sbt(st, "sqg", [128, 8, 512], BF16)
                R_sqg = Res("sqg")
                ag = sbt(st, "ag", [128, 8, 512], BF16)
                R_ag = [Res("ag0"), Res("ag1")]
                rs = sbt(st, "rs_c", [128, 512], F32)
                rstd = sbt(st, "rstd_c", [128, 512], F32)
                R_rs, R_rstd = Res("rs"), Res("rstd")
                tmpf = Ring([(sbt(st, "tmpc%d" % i, [128, 512], F32), Res("tmpc%d" % i)) for i in range(2)])
                rl = Ring([(sbt(st, "rl%d" % i, [128, 512], BF16), Res("rl%d" % i)) for i in range(2)])
                ps_stat = pst(st, "ps_stat_c")
                R_pstat = PR("ps_stat_c")
                ps_ring = Ring([(pst(st, "psm%d" % i), PR("psm%d" % i)) for i in range(3)])
                ps_dn = Ring([(pst(st, "psd%d" % i), PR("psd%d" % i)) for i in range(4)])

                for k in range(8):
                    dma("pool", wout[:, k, :], w_out[l * 1024 + k * 128: l * 1024 + (k + 1) * 128, :], [], [R_wC])
                for k in range(8):
                    dma("pool", wgate[:, k, :], w_gate[l * 1024 + k * 128: l * 1024 + (k + 1) * 128, :], [], [R_wC])
                for k in range(2):
                    dma("pool", wproj[:, k, :], w_proj[l * 256 + k * 128: l * 256 + (k + 1) * 128, :], [], [R_wC])

                def load_wu(n):
                    if n >= NT * 8:
                        return
                    g = n % 8
                    r0 = l * 1024 + g * 128
                    dma("sp", wu[n % 2][:], wup_bf[r0:r0 + 128, :].rearrange("p (k c) -> p k c", k=8), [R_wup[l][g]], [R_wu[n % 2]])

                def load_wd(n):
                    if n >= NT * 8:
                        return
                    g = n % 8
                    r0 = l * 1024 + g * 128
                    dma("sp", wd[n % 2][:], wdn_bf[r0:r0 + 128, :].rearrange("p (k c) -> p k c", k=4), [R_wdn[l][g]], [R_wd[n % 2]])

                def sp(n):
                    for _ in range(n):
                        yield

                def gen_post(m, R_m, h, R_h_, gname):
                    act(sqg[:], m[:], AF.Square, [R_m], [R_sqg])
                    yield from sp(4)
                    rms_stats(sqg, [R_sqg], ps_stat, R_pstat, rs, rstd, R_rs, R_rstd)
                    yield from sp(2)
                    for c in range(8):
                        tb, Rtb = tmpf.next()
                        stt("dve", tb[:], m[:, c, :], scol(l, C_G[gname] + c), rstd[:], ALU.mult, ALU.mult,
                            [R_m, R_rstd, R_small], [Rtb])
                        tt("dve", h[:, c, :], h[:, c, :], tb[:], ALU.add, [R_h_, Rtb], [R_h_])
                        if c % 2 == 1:
                            yield
                    yield from sp(1)

                def gen_pre(h, R_h_, hn, R_hn, gname):
                    act(sqg[:], h[:], AF.Square, [R_h_], [R_sqg])
                    yield from sp(4)
                    rms_stats(sqg, [R_sqg], ps_stat, R_pstat, rs, rstd, R_rs, R_rstd)
                    yield from sp(2)
                    for c in range(8):
                        stt("dve", hn[:, c, :], h[:, c, :], scol(l, C_G[gname] + c), rstd[:], ALU.mult, ALU.mult,
                            [R_h_, R_rstd, R_small], [R_hn])
                        if c % 4 == 3:
                            yield
                    yield from sp(2)

                def H1(j):
                    p = j % 2
                    cols = slice(j * 512, (j + 1) * 512)
                    dma("sp", ht[p][:], h_src_v[:, :, cols], [R_h[j]], [R_ht[p]])
                    dma("pool", ptl[p][:], pT[l * 256:(l + 1) * 256, :].rearrange("(c p) n -> p c n", p=128)[:, :, cols],
                        [], [R_ptl[p]])
                    yield from sp(2)
                    for oc in range(8):
                        pp, Rpp = ps_ring.next()
                        for k in range(8):
                            mm(pp[:], wout[:, k, oc * 128:(oc + 1) * 128], ymt[:, k, :], k == 0, k == 7, [R_wC, R_ymt], [Rpp])
                        cp("dve" if oc % 2 == 0 else "act", mb[p][:, oc, :], pp[:], [Rpp], [R_mb[p]])
                        yield
                    yield from sp(2)
                    yield from gen_post(mb[p], R_mb[p], ht[p], R_ht[p], "mix_post")
                    yield from gen_pre(ht[p], R_ht[p], hnD[p], R_hnD[p], "mlp_pre")

                def H1_loads(j):
                    p = j % 2
                    cols = slice(j * 512, (j + 1) * 512)
                    dma("sp", ymt[:], ymix.rearrange("(c p) n -> p c n", p=128)[:, :, cols], [R_ymA[j], R_ymB[j]], [R_ymt])

                def H3(j):
                    p = j % 2
                    cols = slice(j * 512, (j + 1) * 512)
                    yield from gen_post(mb[p], R_mb[p], ht[p], R_ht[p], "mlp_post")
                    yield from gen_pre(ht[p], R_ht[p], hnE, R_hnE, "ple_pre")
                    for oc in range(8):
                        pp, Rpp = ps_ring.next()
                        for k in range(8):
                            mm(pp[:], wgate[:, k, oc * 128:(oc + 1) * 128], hnE[:, k, :], k == 0, k == 7, [R_wC, R_hnE], [Rpp])
                        act(sqg[:, oc, :], pp[:], AF.Sigmoid, [Rpp], [R_sqg])
                        yield
                    yield from sp(2)
                    for oc in range(8):
                        pp, Rpp = ps_ring.next()
                        for k in range(2):
                            mm(pp[:], wproj[:, k, oc * 128:(oc + 1) * 128], ptl[p][:, k, :], k == 0, k == 1, [R_wC, R_ptl[p]], [Rpp])
                        tt("dve", mb[p][:, oc, :], pp[:], sqg[:, oc, :], ALU.mult, [Rpp, R_sqg], [R_mb[p]])
                        yield
                    yield from sp(3)
                    yield from gen_post(mb[p], R_mb[p], ht[p], R_ht[p], "ple_post")
                    dma("sp", h_dst_v[:, :, cols], ht[p][:], [R_ht[p]], [R_h[j]])
                    yield

                def step(side, n):
                    for _ in range(n):
                        try:
                            next(side)
                        except StopIteration:
                            return

                def up(j, g, side):
                    n = j * 8 + g
                    p = j % 2
                    for c4 in range(4):
                        pp, Rpp = ps_ring.next()
                        for k in range(8):
                            mm(pp[:], wu[n % 2][:, k, c4 * 128:(c4 + 1) * 128], hnD[p][:, k, :], k == 0, k == 7,
                               [R_wu[n % 2], R_hnD[p]], [Rpp])
                        rr, Rrr = rl.next()
                        act(rr[:], pp[:], AF.Relu, [Rpp], [Rrr])
                        tt("dve", ag[:, (g % 2) * 4 + c4, :], rr[:], rr[:], ALU.mult, [Rrr], [R_ag[g % 2]])
                        step(side, 1)

                def down(j, g, side):
                    n = j * 8 + g
                    p = j % 2
                    for oc in range(8):
                        pd, Rpd = ps_dn.next()
                        for k4 in range(4):
                            mm(pd[:], wd[n % 2][:, k4, oc * 128:(oc + 1) * 128], ag[:, (g % 2) * 4 + k4, :], k4 == 0, k4 == 3,
                               [R_wd[n % 2], R_ag[g % 2]], [Rpd])
                        if g == 0:
                            cp("dve", mb[p][:, oc, :], pd[:], [Rpd], [R_mb[p]])
                        else:
                            tt("dve", mb[p][:, oc, :], pd[:], mb[p][:, oc, :], ALU.add, [Rpd, R_mb[p]], [R_mb[p]])
                        step(side, 1)

                def chain(*gens):
                    for gq in gens:
                        if gq is not None:
                            yield from gq

                load_wu(0)
                load_wu(1)
                load_wd(0)
                H1_loads(0)
                for _ in H1(0):
                    pass
                for j in range(NT):
                    if j + 1 < NT:
                        H1_loads(j + 1)
                    side = chain(H3(j - 1) if j > 0 else None, H1(j + 1) if j + 1 < NT else None)
                    for g in range(9):
                        n = j * 8 + g
                        if g < 8:
                            up(j, g, side)
                        if g > 0:
                            down(j, g - 1, side)
                        if g < 8:
                            load_wu(n + 2)
                            load_wd(n + 1)
                    for _ in side:
                        pass
                for _ in H3(NT - 1):
                    pass
                P.barrier()
                P.flush(final=(l == L - 1))
    return nc


POOL_WINDOWS = (2, 4, 8, 16)


def _chunkcols(v):
    return np.ascontiguousarray(v.reshape(-1, 128).T)


def prep_shared(inp, L):
    f32 = np.float32
    w_in = np.asarray(inp["w_in"], f32)[:L]
    perm = np.concatenate([np.arange(0, 1024), np.arange(1284, 2308), np.arange(1280, 1284), np.arange(1024, 1280)])
    w_in_r = np.ascontiguousarray(w_in[:, :, perm]).reshape(L * 1024, 2308)
    w_pw = np.ascontiguousarray(np.asarray(inp["w_conf_pw"], f32)[:L]).reshape(L * 256, 256)
    wp = np.asarray(inp["w_pool"], f32)[:L]
    w_pbd = np.zeros((L, 2, 128, 128), f32)
    for blk in range(2):
        for gg in range(2):
            w_pbd[:, blk, gg * 64:(gg + 1) * 64, gg * 64:(gg + 1) * 64] = wp[:, blk * 2 + gg]
    w_pbd = w_pbd.reshape(L * 256, 128)
    w_out = np.ascontiguousarray(np.asarray(inp["w_out"], f32)[:L]).reshape(L * 1024, 1024)
    wu = np.asarray(inp["w_up"], f32)[:L]
    wu = wu.reshape(L, 8, 128, 8, 512).transpose(0, 3, 2, 1, 4)
    w_up = np.ascontiguousarray(wu).reshape(L * 1024, 4096)
    wdn = np.asarray(inp["w_down"], f32)[:L]
    wdn = wdn.reshape(L, 8, 4, 128, 1024).transpose(0, 1, 3, 2, 4)
    w_dn = np.ascontiguousarray(wdn).reshape(L * 1024, 4096)
    w_gate = np.ascontiguousarray(np.asarray(inp["w_ple_gate"], f32)[:L]).reshape(L * 1024, 1024)
    w_proj = np.ascontiguousarray(np.asarray(inp["w_ple_proj"], f32)[:L]).reshape(L * 256, 1024)
    small = np.zeros((128, L * 128), f32)
    for l in range(L):
        o = l * 128
        for name, key in (("mix_pre", "g_mix_pre"), ("mix_post", "g_mix_post"), ("mlp_pre", "g_mlp_pre"),
                          ("mlp_post", "g_mlp_post"), ("ple_pre", "g_ple_pre"), ("ple_post", "g_ple_post")):
            small[:, o + C_G[name]: o + C_G[name] + 8] = _chunkcols(np.asarray(inp[key], f32)[l])
        small[:, o + C_LNG: o + C_LNG + 2] = _chunkcols(np.asarray(inp["conf_ln_g"], f32)[l])
        small[:, o + C_LNB: o + C_LNB + 2] = _chunkcols(np.asarray(inp["conf_ln_b"], f32)[l])
        small[:, o + C_PSC: o + C_PSC + 2] = _chunkcols(np.asarray(inp["pool_scale"], f32)[l])
        dw = np.asarray(inp["w_conf_dw"], f32)[l]
        for blk in range(2):
            small[:, o + C_DW + blk * 31: o + C_DW + blk * 31 + 31] = dw[:, blk * 128:(blk + 1) * 128].T
        sc = np.asarray(inp["w_sc"], f32)[l]
        for blk in range(2):
            small[:, o + C_SC + blk * 3: o + C_SC + blk * 3 + 3] = sc[:, blk * 128:(blk + 1) * 128].T
        small[0:4, o + C_BF] = np.asarray(inp["b_forget"], f32)[l]
    consts = np.zeros((128, 320), f32)
    consts[:, K_ID:K_ID + 128] = np.eye(128, dtype=f32)
    kk, qq = np.meshgrid(np.arange(128), np.arange(128), indexing="ij")
    consts[:, K_MASK:K_MASK + 128] = np.where(kk > qq, NEG, 0.0)
    for blk in range(2):
        for p in range(128):
            w = POOL_WINDOWS[(blk * 128 + p) // 64]
            for t in range(16):
                consts[p, K_PCOEF + blk * 16 + t] = (1.0 / w if t < w else 0.0) - (1.0 if t == 0 else 0.0)
                consts[p, K_PRATIO + blk * 16 + t] = w / min(t + 1, w)
    return dict(w_in=w_in_r, w_pw=w_pw, w_pbd=w_pbd, w_out=w_out, w_up=w_up, w_dn=w_dn, w_gate=w_gate,
                w_proj=w_proj, small=small, consts=consts)


_NC_CACHE = {}
ACTIVE_CORES = [0, 1, 4, 5]


def run(inp, T, L, n_seq, stop=None, dbg=False):
    f32 = np.float32
    shared = prep_shared(inp, L)
    x = np.asarray(inp["x"], f32)
    p = np.asarray(inp["p"], f32)
    active = ACTIVE_CORES[:n_seq]
    zero_map = None
    in_maps = []
    for core in range(8):
        if core in active:
            bi = active.index(core)
            m = dict(shared)
            m["xT"] = np.ascontiguousarray(x[bi].T)
            m["pT"] = np.ascontiguousarray(p[:L, bi].transpose(0, 2, 1)).reshape(L * 256, T)
        else:
            if zero_map is None:
                zero_map = {k: np.zeros_like(v) for k, v in shared.items()}
                zero_map["xT"] = np.zeros((1024, T), f32)
                zero_map["pT"] = np.zeros((L * 256, T), f32)
            m = zero_map
        in_maps.append(m)
    key = (T, L, stop, dbg)
    if key not in _NC_CACHE:
        _NC_CACHE[key] = build(T, L, stop, dbg)
    nc = _NC_CACHE[key]
    res = run_bass_kernel_spmd(nc, in_maps, core_ids=list(range(8)))
    if dbg:
        return res.results[active[0]]
    out = np.stack([np.ascontiguousarray(res.results[active[bi]]["outT"].T) for bi in range(n_seq)], axis=0)
    return out.astype(f32)


def kernel(**inputs):
    return run(inputs, 8192, 4, 4)
```

```python
import contextlib
import numpy as np
import concourse.bass as bass
import concourse.mybir as mybir
from concourse.bass_utils import run_bass_kernel_spmd

F32 = mybir.dt.float32
BF16 = mybir.dt.bfloat16
ALU = mybir.AluOpType
AF = mybir.ActivationFunctionType
ENGS = ("sp", "act", "pe", "dve", "pool")
EPS = 1e-6
NEG = -30000.0


class Res:
    __slots__ = ("name", "w", "r", "excl")

    def __init__(self, name="", excl=False):
        self.name = name
        self.w = {}
        self.r = {}
        self.excl = excl


def PR(name):
    return Res(name, excl=True)


def _merge(d, s, v):
    if d.get(s, 0) < v:
        d[s] = v


class Prog:
    def __init__(self, nc, stack, n_dma_sems=32):
        self.nc = nc
        self.q = {e: [] for e in ENGS}
        self.sem_names = []
        self.sem_count = []
        self.seen = {e: {} for e in ENGS}
        self.pending_reads = {e: [] for e in ENGS}
        self.pending_writes = {e: [] for e in ENGS}
        self.eng_sem = {}
        for e in ("act", "pe", "dve", "pool"):
            self.eng_sem[e] = self._new_sem("c_" + e)
        self.dma_sems = {"sp": [self._new_sem("d%d" % i) for i in range(n_dma_sems)],
                         "pool": [self._new_sem("w%d" % i) for i in range(16)]}
        self.dma_rr = {"sp": 0, "pool": 0}
        self.n_ops = 0
        self.sems = [stack.enter_context(nc.semaphore(n)) for n in self.sem_names]

    def _new_sem(self, name):
        self.sem_names.append(name)
        self.sem_count.append(0)
        return len(self.sem_names) - 1

    def op(self, eng, fn, reads=(), writes=(), inc=True, dma=False):
        waits = {}
        for r in reads:
            for s, v in r.w.items():
                _merge(waits, s, v)
            if r.excl:
                for s, v in r.r.items():
                    if not (eng in self.eng_sem and s == self.eng_sem[eng]):
                        _merge(waits, s, v)
        for w in writes:
            for s, v in w.w.items():
                _merge(waits, s, v)
            for s, v in w.r.items():
                _merge(waits, s, v)
        tok = None
        if dma:
            pool_ = self.dma_sems[eng]
            s = pool_[self.dma_rr[eng] % len(pool_)]
            self.dma_rr[eng] += 1
            if self.sem_count[s] > 0:
                _merge(waits, s, self.sem_count[s])
            self.sem_count[s] += 16
            tok = (s, self.sem_count[s], 16)
        elif inc:
            s = self.eng_sem[eng]
            self.sem_count[s] += 1
            tok = (s, self.sem_count[s], 1)
        seen = self.seen[eng]
        wl = []
        for s, v in waits.items():
            if seen.get(s, 0) < v:
                if eng == "pe" and s == self.eng_sem["pe"]:
                    continue
                seen[s] = v
                wl.append((s, v))
        self.q[eng].append((fn, wl, tok))
        self.n_ops += 1
        if tok is None:
            self.pending_reads[eng].extend(reads)
            self.pending_writes[eng].extend(writes)
        else:
            s, v, _ = tok
            rl = list(reads)
            wr = list(writes)
            if not dma:
                rl += self.pending_reads[eng]
                wr += self.pending_writes[eng]
                self.pending_reads[eng] = []
                self.pending_writes[eng] = []
            for r in rl:
                _merge(r.r, s, v)
            for w in wr:
                _merge(w.w, s, v)
        return tok

    def barrier(self):
        for e in ENGS:
            wl = []
            for s, c in enumerate(self.sem_count):
                if c > 0 and self.seen[e].get(s, 0) < c:
                    self.seen[e][s] = c
                    wl.append((s, c))
            if wl:
                self.q[e].append((None, wl, None))

    def flush(self, final=False):
        nc = self.nc
        sems = self.sems
        if final:
            wl = [(s, c) for s, c in enumerate(self.sem_count) if c > 0]
            self.q["sp"].append((None, wl, None))
        with nc.Block() as block:
            def run(eng_name):
                ops = self.q[eng_name]

                def f(e):
                    for fn, wl, tok in ops:
                        for s, v in wl:
                            e.wait_ge(sems[s], v)
                        if fn is None:
                            continue
                        ins = fn(e)
                        if tok is not None:
                            ins.then_inc(sems[tok[0]], tok[2])
                return f
            block.sync(run("sp"))
            block.scalar(run("act"))
            block.tensor(run("pe"))
            block.vector(run("dve"))
            block.gpsimd(run("pool"))
        self.q = {e: [] for e in ENGS}


class Ring:
    def __init__(self, items):
        self.items = items
        self.i = 0

    def next(self):
        it = self.items[self.i % len(self.items)]
        self.i += 1
        return it


C_G = {"mix_pre": 0, "mix_post": 8, "mlp_pre": 16, "mlp_post": 24, "ple_pre": 32, "ple_post": 40}
C_LNG, C_LNB, C_PSC, C_DW, C_SC, C_BF = 48, 50, 52, 54, 116, 122
K_ID, K_MASK, K_PCOEF, K_PRATIO = 0, 128, 256, 288


def build(T, L, stop=None, dbg=False):
    NT = T // 512
    NB = T // 128
    nc = bass.Bass("TRN2", target_bir_lowering=False)

    def din(name, shape, dt=F32):
        return nc.dram_tensor(name, shape, dt, kind="ExternalInput").ap()

    def dscr(name, shape, dt):
        return nc.dram_tensor(name, shape, dt, kind="ExternalOutput" if dbg else "Internal").ap()

    xT = din("xT", [1024, T])
    pT = din("pT", [L * 256, T])
    w_in = din("w_in", [L * 1024, 2308])
    w_pw = din("w_pw", [L * 256, 256])
    w_pbd = din("w_pbd", [L * 256, 128])
    w_out = din("w_out", [L * 1024, 1024])
    w_up = din("w_up", [L * 1024, 4096])
    w_dn = din("w_dn", [L * 1024, 4096])
    w_gate = din("w_gate", [L * 1024, 1024])
    w_proj = din("w_proj", [L * 256, 1024])
    small = din("small", [128, L * 128])
    consts = din("consts", [128, 320])
    outT = nc.dram_tensor("outT", [1024, T], F32, kind="ExternalOutput").ap()

    hT = dscr("hT", [1024, T], F32)
    wup_bf = dscr("wup_bf", [L * 1024, 4096], BF16)
    wdn_bf = dscr("wdn_bf", [L * 1024, 4096], BF16)
    win_bf = dscr("win_bf", [L * 1024, 2308], BF16)
    wout_bf = dscr("wout_bf", [L * 1024, 1024], BF16)
    wgate_bf = dscr("wgate_bf", [L * 1024, 1024], BF16)
    wproj_bf = dscr("wproj_bf", [L * 256, 1024], BF16)
    mixin = dscr("mixin", [1024, 32 + T], BF16)
    qT = dscr("qT", [256, T], BF16)
    kT = dscr("kT", [256, T], BF16)
    c3d = dscr("c3d", [4, 3, T], BF16)
    vaug = dscr("vaug", [NB, 128, 512], BF16)
    ymix = dscr("ymix", [1024, T], BF16)

    R_h = [Res("h%d" % j) for j in range(NT)]
    R_mixin = [Res("mi%d" % j) for j in range(NT + 1)]
    R_q = [Res("q%d" % j) for j in range(NT)]
    R_k = [Res("k%d" % j) for j in range(NT)]
    R_c3 = [Res("c3%d" % j) for j in range(NT)]
    R_v = [Res("v%d" % j) for j in range(NT)]
    R_ymA = [Res("ymA%d" % j) for j in range(NT)]
    R_ymB = [Res("ymB%d" % j) for j in range(NT)]
    R_winbf = [Res("winbf%d" % l) for l in range(L)]
    R_wCbf = [Res("wCbf%d" % l) for l in range(L)]
    R_wup = [[Res("wu%d_%d" % (l, g)) for g in range(8)] for l in range(L)]
    R_wdn = [[Res("wd%d_%d" % (l, g)) for g in range(8)] for l in range(L)]

    top = contextlib.ExitStack()
    with top:
        P = Prog(nc, top)

        uid = [0]

        def sbt(st, name, shape, dt):
            uid[0] += 1
            return st.enter_context(nc.sbuf_tensor("%s_u%d" % (name, uid[0]), shape, dt))

        def pst(st, name, shape=(128, 512), dt=F32):
            uid[0] += 1
            return st.enter_context(nc.psum_tensor("%s_u%d" % (name, uid[0]), list(shape), dt))

        small_sb = sbt(top, "small_sb", [128, L * 128], F32)
        consts_sb = sbt(top, "consts_sb", [128, 320], F32)
        ident_bf = sbt(top, "ident_bf", [128, 128], BF16)
        mask_bf = sbt(top, "mask_bf", [128, 128], BF16)
        ones_bf = sbt(top, "ones_bf", [128, 128], BF16)
        ones_f = sbt(top, "ones_f", [128, 128], F32)
        cneg = sbt(top, "cneg", [128, NB, 4], F32)
        R_small, R_consts, R_cst2, R_cneg = Res("small"), Res("consts"), Res("cst2"), Res("cneg")
        ident_f = consts_sb[:, K_ID:K_ID + 128]

        def scol(l, c, rows=slice(0, 128)):
            return small_sb[rows, l * 128 + c: l * 128 + c + 1]

        def dma(eng, out, in_, reads, writes):
            P.op(eng, lambda e: e.dma_start(out=out, in_=in_), reads=reads, writes=writes, dma=True)

        def mm(out, lhsT, rhs, start, stop, reads, writes):
            P.op("pe", lambda e: e.matmul(out, lhsT=lhsT, rhs=rhs, start=start, stop=stop),
                 reads=reads, writes=writes, inc=stop)

        def act(out, in_, func, reads, writes, bias=None, scale=None):
            kw = {}
            if bias is not None:
                kw["bias"] = bias
            if scale is not None:
                kw["scale"] = scale
            P.op("act", lambda e: e.activation(out, in_, func, **kw), reads=reads, writes=writes)

        def tt(eng, out, in0, in1, op, reads, writes):
            P.op(eng, lambda e: e.tensor_tensor(out, in0, in1, op), reads=reads, writes=writes)

        def stt(eng, out, in0, scalar, in1, op0, op1, reads, writes):
            P.op(eng, lambda e: e.scalar_tensor_tensor(out, in0, scalar, in1, op0, op1), reads=reads, writes=writes)

        def norm_scale(eng, out, in0, gcol, rstd_ap, reads, writes, ptmp_ring):
            if eng == "dve":
                stt("dve", out, in0, gcol, rstd_ap, ALU.mult, ALU.mult, reads, writes)
            else:
                tb, Rtb = ptmp_ring.next()
                tt("pool", tb[:], in0, rstd_ap, ALU.mult, reads, [Rtb])
                P.op("pool", lambda e: e.tensor_scalar_mul(out, tb[:], gcol), reads=[Rtb] + list(reads), writes=writes)

        def ts(eng, out, in0, s1, s2, op0, op1, reads, writes):
            if s2 is None:
                P.op(eng, lambda e: e.tensor_single_scalar(out, in0, s1, op0), reads=reads, writes=writes)
            else:
                P.op(eng, lambda e: e.tensor_scalar(out, in0, s1, s2, op0, op1), reads=reads, writes=writes)

        def cp(eng, out, in_, reads, writes):
            if eng == "act":
                P.op("act", lambda e: e.copy(out, in_), reads=reads, writes=writes)
            else:
                P.op(eng, lambda e: e.tensor_copy(out, in_), reads=reads, writes=writes)

        def memset(eng, ap, val, writes):
            P.op(eng, lambda e: e.memset(ap, val), writes=writes)

        def recip(out, in_, reads, writes):
            P.op("dve", lambda e: e.reciprocal(out, in_), reads=reads, writes=writes)

        def rms_stats(sq, R_sq, ps_stat, R_ps, rs, rstd, R_rs, R_rstd):
            for c in range(8):
                mm(ps_stat[:], ones_bf[:], sq[:, c, :], c == 0, c == 7, list(R_sq) + [R_cst2], [R_ps])
            act(rs[:], ps_stat[:], AF.Sqrt, [R_ps], [R_rs], bias=EPS, scale=1.0 / 1024.0)
            recip(rstd[:], rs[:], [R_rs], [R_rstd])

        with contextlib.ExitStack() as st:
            zt = sbt(st, "zt", [128, 8, 32], BF16)
            R_zt = Res("zt")
            dma("sp", small_sb[:], small, [], [R_small])
            dma("sp", consts_sb[:], consts, [], [R_consts])
            cp("dve", ident_bf[:], consts_sb[:, K_ID:K_ID + 128], [R_consts], [R_cst2])
            cp("dve", mask_bf[:], consts_sb[:, K_MASK:K_MASK + 128], [R_consts], [R_cst2])
            memset("pool", ones_bf[:], 1.0, [R_cst2])
            memset("pool", ones_f[:], 1.0, [R_cst2])
            memset("pool", zt[:], 0.0, [R_zt])
            dma("sp", mixin.rearrange("(c p) n -> p c n", p=128)[:, :, 0:32], zt[:], [R_zt], [R_mixin[0]])
            P.barrier()
            P.flush(final=(stop == "pro"))
        if stop == "pro":
            return nc

        def cast_small_weights(l):
            for k in range(8):
                r0 = l * 1024 + k * 128
                for hh in range(2):
                    dma("pool", win_bf[r0:r0 + 128, hh * 1154:(hh + 1) * 1154], w_in[r0:r0 + 128, hh * 1154:(hh + 1) * 1154],
                        [], [R_winbf[l]])
            for k in range(8):
                r0 = l * 1024 + k * 128
                dma("pool", wout_bf[r0:r0 + 128, :], w_out[r0:r0 + 128, :], [], [R_wCbf[l]])
                dma("pool", wgate_bf[r0:r0 + 128, :], w_gate[r0:r0 + 128, :], [], [R_wCbf[l]])
            for k in range(2):
                r0 = l * 256 + k * 128
                dma("pool", wproj_bf[r0:r0 + 128, :], w_proj[r0:r0 + 128, :], [], [R_wCbf[l]])

        def cast_mlp_weights(l):
            for g in range(8):
                for (src, dst, RR) in ((w_up, wup_bf, R_wup), (w_dn, wdn_bf, R_wdn)):
                    r0 = l * 1024 + g * 128
                    for hh in range(2):
                        dma("pool", dst[r0:r0 + 128, hh * 2048:(hh + 1) * 2048],
                            src[r0:r0 + 128, hh * 2048:(hh + 1) * 2048], [], [RR[l][g]])

        cast_small_weights(0)
        for l in range(L):
            h_src = xT if l == 0 else hT
            h_dst = outT if l == L - 1 else hT
            h_src_v = h_src.rearrange("(c p) n -> p c n", p=128)
            h_dst_v = h_dst.rearrange("(c p) n -> p c n", p=128)

            with contextlib.ExitStack() as st:
                win = sbt(st, "win", [128, 8, 2308], BF16)
                R_win = Res("win")
                ht = [sbt(st, "ht%d" % i, [128, 8, 512], F32) for i in range(2)]
                R_ht = [Res("ht0"), Res("ht1")]
                sq = sbt(st, "sq", [128, 8, 512], BF16)
                R_sq = Res("sq")
                xn2 = [sbt(st, "xn%d" % i, [128, 8, 512], BF16) for i in range(2)]
                R_xn2 = [Res("xn0"), Res("xn1")]
                rs = sbt(st, "rs", [128, 512], F32)
                rstd = sbt(st, "rstd", [128, 512], F32)
                R_rs, R_rstd = Res("rs"), Res("rstd")
                tmpf = [sbt(st, "tmpf%d" % i, [128, 512], F32) for i in range(2)]
                tmp_ring = Ring([(tmpf[i], Res("tmpf%d" % i)) for i in range(2)])
                ptmp_ring = Ring([(sbt(st, "ptmp%d" % i, [128, 512], F32), Res("ptmp%d" % i)) for i in range(2)])
                mstage = [sbt(st, "mstage%d" % i, [128, 8, 512], BF16) for i in range(2)]
                R_mst = [Res("mst0"), Res("mst1")]
                qstage = [sbt(st, "qstage%d" % i, [128, 2, 512], BF16) for i in range(2)]
                R_qst = [Res("qst0"), Res("qst1")]
                kstage = [sbt(st, "kstage%d" % i, [128, 2, 512], BF16) for i in range(2)]
                R_kst = [Res("kst0"), Res("kst1")]
                vstage = [sbt(st, "vstage%d" % i, [128, 4, 512], BF16) for i in range(2)]
                R_vst = [Res("vst0"), Res("vst1")]
                xb = sbt(st, "xb", [4, 512], F32)
                ef = sbt(st, "ef", [4, 512], F32)
                lf = sbt(st, "lf", [4, 512], F32)
                ones4 = sbt(st, "ones4", [4, 512], F32)
                cc = [sbt(st, "cc%d" % i, [4, 512], F32) for i in range(2)]
                r1 = sbt(st, "r1", [4, 512], F32)
                r2 = sbt(st, "r2", [4, 512], F32)
                c3 = [sbt(st, "c3_%d" % i, [4, 3, 512], BF16) for i in range(2)]
                R_f = Res("fmisc")
                R_cc = [Res("cc0"), Res("cc1")]
                R_c3s = [Res("c3s0"), Res("c3s1")]
                ps_stat = pst(st, "ps_stat")
                R_pstat = PR("ps_stat")
                ps_ring = Ring([(pst(st, "psr%d" % i), PR("psr%d" % i)) for i in range(4)])
                ps_v = Ring([(pst(st, "psv%d" % i), PR("psv%d" % i)) for i in range(2)])
                ps_t = pst(st, "ps_t")
                R_pst = PR("ps_t")

                for k in range(8):
                    dma("sp", win[:, k, :], win_bf[l * 1024 + k * 128: l * 1024 + (k + 1) * 128, :], [R_winbf[l]], [R_win])
                cast_mlp_weights(l)
                if l + 1 < L:
                    cast_small_weights(l + 1)
                memset("pool", ones4[:], 1.0, [R_f])
                for i in range(2):
                    memset("pool", vstage[i][:], 1.0, [R_vst[i]])

                def load_h(j):
                    dma("sp", ht[j % 2][:], h_src_v[:, :, j * 512:(j + 1) * 512], [R_h[j]], [R_ht[j % 2]])

                cur = {}

                def proj(ps, R_ps, col0, M):
                    xn, R_xn = cur["xn"], cur["R_xn"]
                    for k in range(8):
                        mm(ps[0:M, :], win[:, k, col0:col0 + M], xn[:, k, :], k == 0, k == 7, [R_win, R_xn], [R_ps])

                def norm_a(j):
                    bb = j % 2
                    act(sq[:], ht[bb][:], AF.Square, [R_ht[bb]], [R_sq])

                def norm_b(j):
                    bb = j % 2
                    rms_stats(sq, [R_sq], ps_stat, R_pstat, rs, rstd, R_rs, R_rstd)
                    for c in range(8):
                        norm_scale("dve", xn2[bb][:, c, :], ht[bb][:, c, :], scol(l, C_G["mix_pre"] + c),
                                   rstd[:], [R_ht[bb], R_rstd, R_small], [R_xn2[bb]], ptmp_ring)

                load_h(0)
                if NT > 1:
                    load_h(1)
                norm_a(0)
                norm_b(0)
                for j in range(NT):
                    b = j % 2
                    h = ht[b]
                    cols = slice(j * 512, (j + 1) * 512)
                    xnb, R_xnb = xn2[b], R_xn2[b]
                    xn, R_xn = xnb, R_xnb
                    cur["xn"], cur["R_xn"] = xnb, R_xnb
                    pf, Rpf = ps_ring.next()
                    proj(pf, Rpf, 2048, 4)
                    ts("dve", xb[:], pf[0:4, :], scol(l, C_BF, slice(0, 4)), None, ALU.add, None, [Rpf, R_small], [R_f])
                    act(ef[:], xb[:], AF.Exp, [R_f], [R_f], scale=-1.0)
                    act(lf[:], ef[:], AF.Ln, [R_f], [R_f], bias=1.0, scale=1.0)
                    init = 0.0 if j == 0 else cc[1 - b][:, 511:512]
                    P.op("dve", (lambda o, d0, d1, ini: (lambda e: e.tensor_tensor_scan(o, d0, d1, ini, ALU.mult, ALU.subtract)))(
                        cc[b][:], ones4[:], lf[:], init), reads=[R_f, R_cc[1 - b]], writes=[R_cc[b]])
                    if j + 1 < NT:
                        norm_a(j + 1)
                    for blk in range(2):
                        pa, Rpa = ps_ring.next()
                        pb, Rpb = ps_ring.next()
                        proj(pa, Rpa, (0 + blk) * 128, 128)
                        proj(pb, Rpb, (2 + blk) * 128, 128)
                        tb, Rtb = tmp_ring.next()
                        act(tb[:], pb[:], AF.Sigmoid, [Rpb], [Rtb])
                        tt("dve", mstage[b][:, 0 + blk, :], pa[:], tb[:], ALU.mult, [Rpa, Rtb], [R_mst[b]])
                    cp("dve", c3[b][:, 0, :], cc[b][:], [R_cc[b]], [R_c3s[b]])
                    tt("dve", r1[:], cc[b][:], c3[b][:, 0, :], ALU.subtract, [R_cc[b], R_c3s[b]], [R_f])
                    cp("dve", c3[b][:, 1, :], r1[:], [R_f], [R_c3s[b]])
                    tt("dve", r2[:], r1[:], c3[b][:, 1, :], ALU.subtract, [R_f, R_c3s[b]], [R_f])
                    cp("dve", c3[b][:, 2, :], r2[:], [R_f], [R_c3s[b]])
                    for blk in range(2):
                        pa, Rpa = ps_ring.next()
                        pb, Rpb = ps_ring.next()
                        proj(pa, Rpa, (8 + blk) * 128, 128)
                        proj(pb, Rpb, (12 + blk) * 128, 128)
                        tb, Rtb = tmp_ring.next()
                        cp("act", tb[:], pb[:], [Rpb], [Rtb])
                        tt("dve", mstage[b][:, 2 + blk, :], pa[:], tb[:], ALU.mult, [Rpa, Rtb], [R_mst[b]])
                    if j + 1 < NT:
                        norm_b(j + 1)
                        if j + 2 < NT:
                            load_h(j + 2)
                    for blk in range(2):
                        pa, Rpa = ps_ring.next()
                        proj(pa, Rpa, (14 + blk) * 128, 128)
                        cp("dve", mstage[b][:, 4 + blk, :], pa[:], [Rpa], [R_mst[b]])
                        pb, Rpb = ps_ring.next()
                        proj(pb, Rpb, (10 + blk) * 128, 128)
                        cp("act", mstage[b][:, 6 + blk, :], pb[:], [Rpb], [R_mst[b]])
                    for blk in range(2):
                        pa, Rpa = ps_ring.next()
                        proj(pa, Rpa, (4 + blk) * 128, 128)
                        act(qstage[b][:, blk, :], pa[:], AF.Copy, [Rpa], [R_qst[b]], scale=0.125)
                        pb, Rpb = ps_ring.next()
                        proj(pb, Rpb, (6 + blk) * 128, 128)
                        cp("dve", kstage[b][:, blk, :], pb[:], [Rpb], [R_kst[b]])
                    for s in range(4):
                        pv, Rpv = ps_v.next()
                        for k in range(8):
                            mm(pv[:, 0:256], xn[:, k, s * 128:(s + 1) * 128], win[:, k, 2052:2308], k == 0, k == 7,
                               [R_win, R_xn], [Rpv])
                        dst = vstage[b][:, s, :].rearrange("p (h c) -> p h c", h=4)[:, :, 0:64]
                        src = pv[:, 0:256].rearrange("p (h c) -> p h c", h=4)
                        cp("dve" if s % 2 == 0 else "act", dst, src, [Rpv], [R_vst[b]])
                    dma("sp", mixin.rearrange("(c p) n -> p c n", p=128)[:, :, 32 + j * 512: 32 + (j + 1) * 512],
                        mstage[b][:], [R_mst[b]], [R_mixin[j + 1]])
                    dma("sp", qT.rearrange("(c p) n -> p c n", p=128)[:, :, cols], qstage[b][:], [R_qst[b]], [R_q[j]])
                    dma("sp", kT.rearrange("(c p) n -> p c n", p=128)[:, :, cols], kstage[b][:], [R_kst[b]], [R_k[j]])
                    dma("sp", c3d[:, :, cols], c3[b][:], [R_c3s[b]], [R_c3[j]])
                    dma("sp", vaug[j * 4:(j + 1) * 4].rearrange("s p c -> p s c"), vstage[b][:], [R_vst[b]], [R_v[j]])
                P.barrier()
                P.flush(final=(stop == "A1"))
            if stop == "A1":
                return nc

            with contextlib.ExitStack() as st:
                dconf = sbt(st, "dconf", [128, 2, 31, 128], BF16)
                dsc = sbt(st, "dsc", [128, 2, 3, 128], BF16)
                dpl = sbt(st, "dpl", [128, 2, 16, 128], BF16)
                R_dg = Res("diag")
                wpw = sbt(st, "wpw", [128, 2, 256], BF16)
                wpbd = sbt(st, "wpbd", [128, 2, 128], BF16)
                R_wA2 = Res("wA2")
                mt = [sbt(st, "mt%d" % i, [128, 8, 544], BF16) for i in range(2)]
                R_mt = [Res("mt0"), Res("mt1")]
                xc = sbt(st, "xc", [128, 2, 512], F32)
                sq2 = sbt(st, "sq2", [128, 2, 512], F32)
                R_xc, R_sq2 = Res("xc"), Res("sq2")
                mean = sbt(st, "mean", [128, 512], F32)
                msq = sbt(st, "msq", [128, 512], F32)
                var = sbt(st, "var", [128, 512], F32)
                sd = sbt(st, "sd", [128, 512], F32)
                rstd2 = sbt(st, "rstd2", [128, 512], F32)
                R_mean, R_msq, R_var, R_sd, R_rstd2 = Res("mean"), Res("msq"), Res("var"), Res("sd"), Res("rstd2")
                xm = [sbt(st, "xm%d" % i, [128, 512], F32) for i in range(2)]
                R_xm = [Res("xm0"), Res("xm1")]
                xnn = [sbt(st, "xnn%d" % i, [128, 512], F32) for i in range(2)]
                R_xnn = [Res("xnn0"), Res("xnn1")]
                sconf = sbt(st, "sconf", [128, 2, 512], BF16)
                R_sconf = Res("sconf")
                dpool = sbt(st, "dpool", [128, 2, 512], BF16)
                R_dpool = Res("dpool")
                t16 = sbt(st, "t16", [128, 2, 16], F32)
                R_t16 = Res("t16")
                ystage = [sbt(st, "ystage%d" % i, [128, 6, 512], BF16) for i in range(2)]
                R_yst = [Res("yst0"), Res("yst1")]
                psc = [pst(st, "psc%d" % i) for i in range(2)]
                R_psc = [PR("psc0"), PR("psc1")]
                ps_s1 = pst(st, "ps_s1")
                ps_s2 = pst(st, "ps_s2")
                R_s1, R_s2 = PR("s1"), PR("s2")
                ps_ring = Ring([(pst(st, "psq%d" % i), PR("psq%d" % i)) for i in range(3)])

                dma("pool", wpw[:], w_pw[l * 256:(l + 1) * 256, :].rearrange("(c p) n -> p c n", p=128), [], [R_wA2])
                dma("pool", wpbd[:], w_pbd[l * 256:(l + 1) * 256, :].rearrange("(c p) n -> p c n", p=128), [], [R_wA2])
                for blk in range(2):
                    for k in range(31):
                        P.op("pool", (lambda o, s: (lambda e: e.tensor_scalar_mul(o, ident_f, s)))(
                            dconf[:, blk, k, :], scol(l, C_DW + blk * 31 + k)), reads=[R_consts, R_small], writes=[R_dg])
                    for k in range(3):
                        P.op("pool", (lambda o, s: (lambda e: e.tensor_scalar_mul(o, ident_f, s)))(
                            dsc[:, blk, k, :], scol(l, C_SC + blk * 3 + k)), reads=[R_consts, R_small], writes=[R_dg])
                    for k in range(16):
                        P.op("pool", (lambda o, s: (lambda e: e.tensor_scalar_mul(o, ident_f, s)))(
                            dpl[:, blk, k, :], consts_sb[:, K_PCOEF + blk * 16 + k: K_PCOEF + blk * 16 + k + 1]),
                            reads=[R_consts], writes=[R_dg])

                def load_mt(j):
                    dma("sp", mt[j % 2][:], mixin.rearrange("(c p) n -> p c n", p=128)[:, :, j * 512: j * 512 + 544],
                        [R_mixin[j], R_mixin[j + 1]], [R_mt[j % 2]])

                def pw_store(jj):
                    bb = jj % 2
                    cols_ = slice(jj * 512, (jj + 1) * 512)
                    for oc in range(2):
                        pp, Rpp = ps_ring.next()
                        for kb in range(2):
                            mm(pp[:], wpw[:, kb, oc * 128:(oc + 1) * 128], sconf[:, kb, :], kb == 0, kb == 1,
                               [R_wA2, R_sconf], [Rpp])
                        cp("dve", ystage[bb][:, oc, :], pp[:], [Rpp], [R_yst[bb]])
                    ym_v = ymix.rearrange("(c p) n -> p c n", p=128)
                    dma("sp", ym_v[:, 0:2, cols_], ystage[bb][:, 0:2, :], [R_yst[bb]], [R_ymA[jj]])
                    dma("sp", ym_v[:, 4:8, cols_], ystage[bb][:, 2:6, :], [R_yst[bb]], [R_ymA[jj]])

                load_mt(0)
                sect = stop.split(":")[1] if (stop and ":" in stop) else "all"

                def on(*names):
                    return sect == "all" or sect in names

                for j in range(NT):
                    b = j % 2
                    if j + 1 < NT:
                        load_mt(j + 1)
                    m_ = mt[b]
                    cols = slice(j * 512, (j + 1) * 512)
                    if sect == "conv1":
                        for blk in range(2):
                            for k in range(31):
                                mm(psc[blk][:], dconf[:, blk, k, :], m_[:, blk, 2 + k: 2 + k + 512], k == 0, k == 30,
                                   [R_dg, R_mt[b]], [R_psc[blk]])
                            cp("dve", xc[:, blk, :], psc[blk][:], [R_psc[blk]], [R_xc])
                    if sect == "conv2":
                        for blk in range(2):
                            for k in range(3):
                                mm(psc[blk][:], dconf[:, blk, k, :], m_[:, blk, 2 + k: 2 + k + 512], k == 0, k == 2,
                                   [R_dg, R_mt[b]], [R_psc[blk]])
                            act(sq2[:, blk, :], psc[blk][:], AF.Square, [R_psc[blk]], [R_sq2])
                    if sect == "pool1":
                        for blk in range(2):
                            pp, Rpp = ps_ring.next()
                            for k in range(16):
                                mm(pp[:], dpl[:, blk, k, :], m_[:, 4 + blk, 32 - k: 32 - k + 512], k == 0, k == 15,
                                   [R_dg, R_mt[b]], [Rpp])
                            cp("dve", dpool[:, blk, :], pp[:], [Rpp], [R_dpool])
                    if on("conv", "ln", "silu", "conf"):
                        for blk in range(2):
                            for k in range(31):
                                mm(psc[blk][:], dconf[:, blk, k, :], m_[:, blk, 2 + k: 2 + k + 512], k == 0, k == 30,
                                   [R_dg, R_mt[b]], [R_psc[blk]])
                            cp("dve", xc[:, blk, :], psc[blk][:], [R_psc[blk]], [R_xc])
                            act(sq2[:, blk, :], xc[:, blk, :], AF.Square, [R_xc], [R_sq2])
                    if on("sc"):
                        for blk in range(2):
                            pp, Rpp = ps_ring.next()
                            for k in range(3):
                                mm(pp[:], dsc[:, blk, k, :], m_[:, 2 + blk, 30 + k: 30 + k + 512], k == 0, k == 2,
                                   [R_dg, R_mt[b]], [Rpp])
                            tt("dve", ystage[b][:, 2 + blk, :], pp[:], m_[:, 6 + blk, 32:544], ALU.mult, [Rpp, R_mt[b]], [R_yst[b]])
                    if on("pool"):
                        for blk in range(2):
                            pp, Rpp = ps_ring.next()
                            for k in range(16):
                                mm(pp[:], dpl[:, blk, k, :], m_[:, 4 + blk, 32 - k: 32 - k + 512], k == 0, k == 15,
                                   [R_dg, R_mt[b]], [Rpp])
                            cp("act", dpool[:, blk, :], pp[:], [Rpp], [R_dpool])
                            if j == 0:
                                tt("dve", t16[:, blk, :], pp[:, 0:16], m_[:, 4 + blk, 32:48], ALU.add, [Rpp, R_mt[b]], [R_t16])
                                tt("pool", t16[:, blk, :], t16[:, blk, :],
                                   consts_sb[:, K_PRATIO + blk * 16: K_PRATIO + blk * 16 + 16], ALU.mult, [R_t16, R_consts], [R_t16])
                                tt("dve", dpool[:, blk, 0:16], t16[:, blk, :], m_[:, 4 + blk, 32:48], ALU.subtract,
                                   [R_t16, R_mt[b]], [R_dpool])
                        for blk in range(2):
                            pp, Rpp = ps_ring.next()
                            mm(pp[:], wpbd[:, blk, :], dpool[:, blk, :], True, True, [R_wA2, R_dpool], [Rpp])
                            act(ystage[b][:, 4 + blk, :], pp[:], AF.Copy, [Rpp, R_small], [R_yst[b]], scale=scol(l, C_PSC + blk))
                    if j > 0:
                        pw_store(j - 1)
                    if on("ln", "silu", "conf"):
                        for blk in range(2):
                            mm(ps_s1[:], ones_f[:], xc[:, blk, :], blk == 0, blk == 1, [R_xc, R_cst2], [R_s1])
                        for blk in range(2):
                            mm(ps_s2[:], ones_f[:], sq2[:, blk, :], blk == 0, blk == 1, [R_sq2, R_cst2], [R_s2])
                        act(mean[:], ps_s1[:], AF.Copy, [R_s1], [R_mean], scale=1.0 / 256.0)
                        tt("dve", msq[:], mean[:], mean[:], ALU.mult, [R_mean], [R_msq])
                        stt("dve", var[:], ps_s2[:], 1.0 / 256.0, msq[:], ALU.mult, ALU.subtract, [R_s2, R_msq], [R_var])
                        act(sd[:], var[:], AF.Sqrt, [R_var], [R_sd], bias=EPS, scale=1.0)
                        recip(rstd2[:], sd[:], [R_sd], [R_rstd2])
                        for blk in range(2):
                            tt("dve", xm[blk][:], xc[:, blk, :], mean[:], ALU.subtract, [R_xc, R_mean], [R_xm[blk]])
                            tt("dve", xnn[blk][:], xm[blk][:], rstd2[:], ALU.mult, [R_xm[blk], R_rstd2], [R_xnn[blk]])
                    if on("silu", "conf"):
                        for blk in range(2):
                            act(sconf[:, blk, :], xnn[blk][:], AF.Silu, [R_xnn[blk], R_small], [R_sconf],
                                bias=scol(l, C_LNB + blk), scale=scol(l, C_LNG + blk))
                pw_store(NT - 1)
                P.barrier()
                P.flush(final=(stop is not None and stop.startswith("A2")))
            if stop is not None and stop.startswith("A2"):
                return nc

            with contextlib.ExitStack() as st:
                kaug = sbt(st, "kaug", [128, 4, T], BF16)
                R_kaug = [Res("kaug%d" % j) for j in range(NT)]
                R_kaugc = Res("kaugc")
                vsb = sbt(st, "vsb", [128, NB, 512], BF16)
                R_vsb = [Res("vsb%d" % j) for j in range(NT)]
                qa = [sbt(st, "qa%d" % i, [128, 4, 512], BF16) for i in range(2)]
                R_qa = [Res("qa0"), Res("qa1")]
                pts = Ring([(sbt(st, "pt%d" % i, [128, 1024], BF16), Res("pt%d" % i)) for i in range(3)])
                rec = sbt(st, "rec", [128, 512], F32)
                R_rec = Res("rec")
                yst = Ring([(sbt(st, "ysb%d" % i, [64, 512], BF16), Res("ysb%d" % i)) for i in range(2)])
                ps_s = Ring([(pst(st, "pss%d" % i, (128, 1024)), PR("pss%d" % i)) for i in range(3)])
                ps_o = Ring([(pst(st, "pso%d" % i), PR("pso%d" % i)) for i in range(2)])

                for i in range(2):
                    memset("dve", qa[i][64:96, :, :], -1.0, [R_qa[i]])
                def kv_load(j):
                    cols = slice(j * 512, (j + 1) * 512)
                    memset("dve" if j % 2 == 0 else "pool", kaug[64:96, :, cols], 0.0, [R_kaug[j]])
                    memset("dve" if j % 2 == 0 else "pool", kaug[64:67, :, cols], 1.0, [R_kaug[j]])
                    dma("sp", kaug[0:64, :, cols], kT.rearrange("(h r) n -> r h n", r=64)[:, :, cols], [R_k[j]], [R_kaug[j]])
                    dma("sp", kaug[67:70, :, cols], c3d.rearrange("h j n -> j h n")[:, :, cols], [R_c3[j]], [R_kaug[j]])
                    dma("sp", vsb[:, j * 4:(j + 1) * 4, :], vaug[j * 4:(j + 1) * 4].rearrange("s p c -> p s c"),
                        [R_v[j]], [R_vsb[j]])

                def load_q(j):
                    cols = slice(j * 512, (j + 1) * 512)
                    dma("sp", qa[j % 2][0:64, :, :], qT.rearrange("(h r) n -> r h n", r=64)[:, :, cols], [R_q[j]], [R_qa[j % 2]])
                    dma("sp", qa[j % 2][64:67, :, :], c3d.rearrange("h j n -> j h n")[:, :, cols], [R_c3[j]], [R_qa[j % 2]])

                kv_load(0)
                load_q(0)
                for j in range(1, NT):
                    kv_load(j)
                for j in range(NT):
                    b = j % 2
                    if j + 1 < NT:
                        load_q(j + 1)
                    cols = slice(j * 512, (j + 1) * 512)
                    for h in range(4):
                        po, Rpo = ps_o.next()
                        units = [("pair", i0) for i0 in range(0, 4 * j, 2)] + [("diag", dj) for dj in range(4)]

                        def qk(u):
                            kind, v = u
                            pS, RpS = ps_s.next()
                            if kind == "pair":
                                for half in range(2):
                                    i = v + half
                                    mm(pS[:, half * 512:(half + 1) * 512], kaug[0:96, h, i * 128:(i + 1) * 128],
                                       qa[b][0:96, h, :], True, True, [R_kaug[i // 4], R_kaugc, R_qa[b]], [RpS])
                            else:
                                i = 4 * j + v
                                c0 = 128 * v
                                mm(pS[:, c0:512], kaug[0:96, h, i * 128:(i + 1) * 128], qa[b][0:96, h, c0:512],
                                   True, False, [R_kaug[i // 4], R_kaugc, R_qa[b]], [RpS])
                                mm(pS[:, c0:c0 + 128], ident_bf[:], mask_bf[:], False, True, [R_cst2], [RpS])
                            return pS, RpS

                        LA = 2
                        pend = [qk(units[x]) for x in range(min(LA, len(units)))]
                        for ui, u in enumerate(units):
                            pS, RpS = pend.pop(0)
                            if ui + LA < len(units):
                                pend.append(qk(units[ui + LA]))
                            pt, Rpt = pts.next()
                            kind, v = u
                            first = (ui == 0)
                            last = (ui == len(units) - 1)
                            if kind == "pair":
                                act(pt[:, :], pS[:, :], AF.Exp, [RpS], [Rpt])
                                for half in range(2):
                                    i = v + half
                                    mm(po[:, :], vsb[:, i, h * 128:(h + 1) * 128], pt[:, half * 512:(half + 1) * 512],
                                       first and half == 0, False, [R_vsb[i // 4], Rpt], [Rpo])
                            else:
                                i = 4 * j + v
                                c0 = 128 * v
                                act(pt[:, c0:512], pS[:, c0:512], AF.Exp, [RpS], [Rpt])
                                mm(po[:, c0:512], vsb[:, i, h * 128:(h + 1) * 128], pt[:, c0:512], first, last,
                                   [R_vsb[i // 4], Rpt], [Rpo])
                        recip(rec[64:128, :], po[64:128, :], [Rpo], [R_rec])
                        ys, Rys = yst.next()
                        tt("dve", ys[:], po[0:64, :], rec[64:128, :], ALU.mult, [Rpo, R_rec], [Rys])
                        dma("sp", ymix[256 + 64 * h: 256 + 64 * h + 64, cols], ys[:], [Rys], [R_ymB[j]])
                P.barrier()
                P.flush(final=(stop == "B"))
            if stop == "B":
                return nc

            with contextlib.ExitStack() as st:
                wout = sbt(st, "wout", [128, 8, 1024], BF16)
                wgate = sbt(st, "wgate", [128, 8, 1024], BF16)
                wproj = sbt(st, "wproj", [128, 2, 1024], BF16)
                R_wC = Res("wC")
                wu = [sbt(st, "wu%d" % i, [128, 8, 512], BF16) for i in range(2)]
                wd = [sbt(st, "wd%d" % i, [128, 4, 1024], BF16) for i in range(2)]
                R_wu = [Res("wus%d" % i) for i in range(2)]
                R_wd = [Res("wds%d" % i) for i in range(2)]
                ymt = sbt(st, "ymt", [128, 8, 512], BF16)
                R_ymt = Res("ymt")
                ht = [sbt(st, "hc%d" % i, [128, 8, 512], F32) for i in range(2)]
                R_ht = [Res("hc0"), Res("hc1")]
                ptl = [sbt(st, "ptl%d" % i, [128, 2, 512], BF16) for i in range(2)]
                R_ptl = [Res("ptl0"), Res("ptl1")]
                mb = [sbt(st, "mb%d" % i, [128, 8, 512], F32) for i in range(2)]
                R_mb = [Res("mb0"), Res("mb1")]
                hnD = [sbt(st, "hnD%d" % i, [128, 8, 512], BF16) for i in range(2)]
                R_hnD = [Res("hnD0"), Res("hnD1")]
                hnE = sbt(st, "hnE", [128, 8, 512], BF16)
                R_hnE = Res("hnE")
                sqg = sbt(st, "sqg", [128, 8, 512], BF16)
                R_sqg = Res("sqg")
                ag = sbt(st, "ag", [128, 8, 512], BF16)
                R_ag = [Res("ag0"), Res("ag1")]
                rs = sbt(st, "rs_c", [128, 512], F32)
                rstd = sbt(st, "rstd_c", [128, 512], F32)
                R_rs, R_rstd = Res("rs"), Res("rstd")
                tmpf = Ring([(sbt(st, "tmpc%d" % i, [128, 512], F32), Res("tmpc%d" % i)) for i in range(2)])
                rl = Ring([(sbt(st, "rl%d" % i, [128, 512], BF16), Res("rl%d" % i)) for i in range(2)])
                ps_stat = pst(st, "ps_stat_c")
                R_pstat = PR("ps_stat_c")
                ps_ring = Ring([(pst(st, "psm%d" % i), PR("psm%d" % i)) for i in range(3)])
                ps_dn = Ring([(pst(st, "psd%d" % i), PR("psd%d" % i)) for i in range(4)])

                for k in range(8):
                    dma("sp", wout[:, k, :], wout_bf[l * 1024 + k * 128: l * 1024 + (k + 1) * 128, :], [R_wCbf[l]], [R_wC])
                for k in range(8):
                    dma("sp", wgate[:, k, :], wgate_bf[l * 1024 + k * 128: l * 1024 + (k + 1) * 128, :], [R_wCbf[l]], [R_wC])
                for k in range(2):
                    dma("sp", wproj[:, k, :], wproj_bf[l * 256 + k * 128: l * 256 + (k + 1) * 128, :], [R_wCbf[l]], [R_wC])

                def load_wu(n):
                    if n >= NT * 8:
                        return
                    g = n % 8
                    r0 = l * 1024 + g * 128
                    dma("sp", wu[n % 2][:], wup_bf[r0:r0 + 128, :].rearrange("p (k c) -> p k c", k=8), [R_wup[l][g]], [R_wu[n % 2]])

                def load_wd(n):
                    if n >= NT * 8:
                        return
                    g = n % 8
                    r0 = l * 1024 + g * 128
                    dma("sp", wd[n % 2][:], wdn_bf[r0:r0 + 128, :].rearrange("p (k c) -> p k c", k=4), [R_wdn[l][g]], [R_wd[n % 2]])

                def sp(n):
                    for _ in range(n):
                        yield

                def gen_post(m, R_m, h, R_h_, gname):
                    act(sqg[:], m[:], AF.Square, [R_m], [R_sqg])
                    yield from sp(4)
                    rms_stats(sqg, [R_sqg], ps_stat, R_pstat, rs, rstd, R_rs, R_rstd)
                    yield from sp(2)
                    for c in range(8):
                        tb, Rtb = tmpf.next()
                        stt("dve", tb[:], m[:, c, :], scol(l, C_G[gname] + c), rstd[:], ALU.mult, ALU.mult,
                            [R_m, R_rstd, R_small], [Rtb])
                        tt("dve", h[:, c, :], h[:, c, :], tb[:], ALU.add, [R_h_, Rtb], [R_h_])
                        if c % 2 == 1:
                            yield
                    yield from sp(1)

                def gen_pre(h, R_h_, hn, R_hn, gname):
                    act(sqg[:], h[:], AF.Square, [R_h_], [R_sqg])
                    yield from sp(4)
                    rms_stats(sqg, [R_sqg], ps_stat, R_pstat, rs, rstd, R_rs, R_rstd)
                    yield from sp(2)
                    for c in range(8):
                        stt("dve", hn[:, c, :], h[:, c, :], scol(l, C_G[gname] + c), rstd[:], ALU.mult, ALU.mult,
                            [R_h_, R_rstd, R_small], [R_hn])
                        if c % 4 == 3:
                            yield
                    yield from sp(2)

                def H1(j):
                    p = j % 2
                    cols = slice(j * 512, (j + 1) * 512)
                    dma("sp", ht[p][:], h_src_v[:, :, cols], [R_h[j]], [R_ht[p]])
                    dma("pool", ptl[p][:], pT[l * 256:(l + 1) * 256, :].rearrange("(c p) n -> p c n", p=128)[:, :, cols],
                        [], [R_ptl[p]])
                    yield from sp(2)
                    for oc in range(8):
                        pp, Rpp = ps_ring.next()
                        for k in range(8):
                            mm(pp[:], wout[:, k, oc * 128:(oc + 1) * 128], ymt[:, k, :], k == 0, k == 7, [R_wC, R_ymt], [Rpp])
                        cp("dve" if oc % 2 == 0 else "act", mb[p][:, oc, :], pp[:], [Rpp], [R_mb[p]])
                        yield
                    yield from sp(2)
                    yield from gen_post(mb[p], R_mb[p], ht[p], R_ht[p], "mix_post")
                    yield from gen_pre(ht[p], R_ht[p], hnD[p], R_hnD[p], "mlp_pre")

                def H1_loads(j):
                    p = j % 2
                    cols = slice(j * 512, (j + 1) * 512)
                    dma("sp", ymt[:], ymix.rearrange("(c p) n -> p c n", p=128)[:, :, cols], [R_ymA[j], R_ymB[j]], [R_ymt])

                def H3(j):
                    p = j % 2
                    cols = slice(j * 512, (j + 1) * 512)
                    yield from gen_post(mb[p], R_mb[p], ht[p], R_ht[p], "mlp_post")
                    yield from gen_pre(ht[p], R_ht[p], hnE, R_hnE, "ple_pre")
                    for oc in range(8):
                        pp, Rpp = ps_ring.next()
                        for k in range(8):
                            mm(pp[:], wgate[:, k, oc * 128:(oc + 1) * 128], hnE[:, k, :], k == 0, k == 7, [R_wC, R_hnE], [Rpp])
                        act(sqg[:, oc, :], pp[:], AF.Sigmoid, [Rpp], [R_sqg])
                        yield
                    yield from sp(2)
                    for oc in range(8):
                        pp, Rpp = ps_ring.next()
                        for k in range(2):
                            mm(pp[:], wproj[:, k, oc * 128:(oc + 1) * 128], ptl[p][:, k, :], k == 0, k == 1, [R_wC, R_ptl[p]], [Rpp])
                        tt("dve", mb[p][:, oc, :], pp[:], sqg[:, oc, :], ALU.mult, [Rpp, R_sqg], [R_mb[p]])
                        yield
                    yield from sp(3)
                    yield from gen_post(mb[p], R_mb[p], ht[p], R_ht[p], "ple_post")
                    dma("sp", h_dst_v[:, :, cols], ht[p][:], [R_ht[p]], [R_h[j]])
                    yield

                def step(side, n):
                    for _ in range(n):
                        try:
                            next(side)
                        except StopIteration:
                            return

                def up(j, g, side):
                    n = j * 8 + g
                    p = j % 2
                    for c4 in range(4):
                        pp, Rpp = ps_ring.next()
                        for k in range(8):
                            mm(pp[:], wu[n % 2][:, k, c4 * 128:(c4 + 1) * 128], hnD[p][:, k, :], k == 0, k == 7,
                               [R_wu[n % 2], R_hnD[p]], [Rpp])
                        rr, Rrr = rl.next()
                        act(rr[:], pp[:], AF.Relu, [Rpp], [Rrr])
                        tt("dve", ag[:, (g % 2) * 4 + c4, :], rr[:], rr[:], ALU.mult, [Rrr], [R_ag[g % 2]])
                        step(side, 1)

                def down(j, g, side):
                    n = j * 8 + g
                    p = j % 2
                    for oc in range(8):
                        pd, Rpd = ps_dn.next()
                        for k4 in range(4):
                            mm(pd[:], wd[n % 2][:, k4, oc * 128:(oc + 1) * 128], ag[:, (g % 2) * 4 + k4, :], k4 == 0, k4 == 3,
                               [R_wd[n % 2], R_ag[g % 2]], [Rpd])
                        if g == 0:
                            cp("dve", mb[p][:, oc, :], pd[:], [Rpd], [R_mb[p]])
                        else:
                            tt("dve", mb[p][:, oc, :], pd[:], mb[p][:, oc, :], ALU.add, [Rpd, R_mb[p]], [R_mb[p]])
                        step(side, 1)

                def chain(*gens):
                    for gq in gens:
                        if gq is not None:
                            yield from gq

                load_wu(0)
                load_wu(1)
                load_wd(0)
                H1_loads(0)
                for _ in H1(0):
                    pass
                for j in range(NT):
                    if j + 1 < NT:
                        H1_loads(j + 1)
                    side = chain(H3(j - 1) if j > 0 else None, H1(j + 1) if j + 1 < NT else None)
                    for g in range(9):
                        n = j * 8 + g
                        if g < 8:
                            up(j, g, side)
                        if g > 0:
                            down(j, g - 1, side)
                        if g < 8:
                            load_wu(n + 2)
                            load_wd(n + 1)
                    for _ in side:
                        pass
                for _ in H3(NT - 1):
                    pass
                P.barrier()
                P.flush(final=(l == L - 1))
    return nc


POOL_WINDOWS = (2, 4, 8, 16)


def _chunkcols(v):
    return np.ascontiguousarray(v.reshape(-1, 128).T)


def prep_shared(inp, L):
    f32 = np.float32
    w_in = np.asarray(inp["w_in"], f32)[:L]
    perm = np.concatenate([np.arange(0, 1024), np.arange(1284, 2308), np.arange(1280, 1284), np.arange(1024, 1280)])
    w_in_r = np.ascontiguousarray(w_in[:, :, perm]).reshape(L * 1024, 2308)
    w_pw = np.ascontiguousarray(np.asarray(inp["w_conf_pw"], f32)[:L]).reshape(L * 256, 256)
    wp = np.asarray(inp["w_pool"], f32)[:L]
    w_pbd = np.zeros((L, 2, 128, 128), f32)
    for blk in range(2):
        for gg in range(2):
            w_pbd[:, blk, gg * 64:(gg + 1) * 64, gg * 64:(gg + 1) * 64] = wp[:, blk * 2 + gg]
    w_pbd = w_pbd.reshape(L * 256, 128)
    w_out = np.ascontiguousarray(np.asarray(inp["w_out"], f32)[:L]).reshape(L * 1024, 1024)
    wu = np.asarray(inp["w_up"], f32)[:L]
    wu = wu.reshape(L, 8, 128, 8, 512).transpose(0, 3, 2, 1, 4)
    w_up = np.ascontiguousarray(wu).reshape(L * 1024, 4096)
    wdn = np.asarray(inp["w_down"], f32)[:L]
    wdn = wdn.reshape(L, 8, 4, 128, 1024).transpose(0, 1, 3, 2, 4)
    w_dn = np.ascontiguousarray(wdn).reshape(L * 1024, 4096)
    w_gate = np.ascontiguousarray(np.asarray(inp["w_ple_gate"], f32)[:L]).reshape(L * 1024, 1024)
    w_proj = np.ascontiguousarray(np.asarray(inp["w_ple_proj"], f32)[:L]).reshape(L * 256, 1024)
    small = np.zeros((128, L * 128), f32)
    for l in range(L):
        o = l * 128
        for name, key in (("mix_pre", "g_mix_pre"), ("mix_post", "g_mix_post"), ("mlp_pre", "g_mlp_pre"),
                          ("mlp_post", "g_mlp_post"), ("ple_pre", "g_ple_pre"), ("ple_post", "g_ple_post")):
            small[:, o + C_G[name]: o + C_G[name] + 8] = _chunkcols(np.asarray(inp[key], f32)[l])
        small[:, o + C_LNG: o + C_LNG + 2] = _chunkcols(np.asarray(inp["conf_ln_g"], f32)[l])
        small[:, o + C_LNB: o + C_LNB + 2] = _chunkcols(np.asarray(inp["conf_ln_b"], f32)[l])
        small[:, o + C_PSC: o + C_PSC + 2] = _chunkcols(np.asarray(inp["pool_scale"], f32)[l])
        dw = np.asarray(inp["w_conf_dw"], f32)[l]
        for blk in range(2):
            small[:, o + C_DW + blk * 31: o + C_DW + blk * 31 + 31] = dw[:, blk * 128:(blk + 1) * 128].T
        sc = np.asarray(inp["w_sc"], f32)[l]
        for blk in range(2):
            small[:, o + C_SC + blk * 3: o + C_SC + blk * 3 + 3] = sc[:, blk * 128:(blk + 1) * 128].T
        small[0:4, o + C_BF] = np.asarray(inp["b_forget"], f32)[l]
    consts = np.zeros((128, 320), f32)
    consts[:, K_ID:K_ID + 128] = np.eye(128, dtype=f32)
    kk, qq = np.meshgrid(np.arange(128), np.arange(128), indexing="ij")
    consts[:, K_MASK:K_MASK + 128] = np.where(kk > qq, NEG, 0.0)
    for blk in range(2):
        for p in range(128):
            w = POOL_WINDOWS[(blk * 128 + p) // 64]
            for t in range(16):
                consts[p, K_PCOEF + blk * 16 + t] = (1.0 / w if t < w else 0.0) - (1.0 if t == 0 else 0.0)
                consts[p, K_PRATIO + blk * 16 + t] = w / min(t + 1, w)
    return dict(w_in=w_in_r, w_pw=w_pw, w_pbd=w_pbd, w_out=w_out, w_up=w_up, w_dn=w_dn, w_gate=w_gate,
                w_proj=w_proj, small=small, consts=consts)


_NC_CACHE = {}
ACTIVE_CORES = [0, 1, 4, 5]


def run(inp, T, L, n_seq, stop=None, dbg=False):
    f32 = np.float32
    shared = prep_shared(inp, L)
    x = np.asarray(inp["x"], f32)
    p = np.asarray(inp["p"], f32)
    active = ACTIVE_CORES[:n_seq]
    zero_map = None
    in_maps = []
    for core in range(8):
        if core in active:
            bi = active.index(core)
            m = dict(shared)
            m["xT"] = np.ascontiguousarray(x[bi].T)
            m["pT"] = np.ascontiguousarray(p[:L, bi].transpose(0, 2, 1)).reshape(L * 256, T)
        else:
            if zero_map is None:
                zero_map = {k: np.zeros_like(v) for k, v in shared.items()}
                zero_map["xT"] = np.zeros((1024, T), f32)
                zero_map["pT"] = np.zeros((L * 256, T), f32)
            m = zero_map
        in_maps.append(m)
    key = (T, L, stop, dbg)
    if key not in _NC_CACHE:
        _NC_CACHE[key] = build(T, L, stop, dbg)
    nc = _NC_CACHE[key]
    res = run_bass_kernel_spmd(nc, in_maps, core_ids=list(range(8)))
    if dbg:
        return res.results[active[0]]
    out = np.stack([np.ascontiguousarray(res.results[active[bi]]["outT"].T) for bi in range(n_seq)], axis=0)
    return out.astype(f32)


def kernel(**inputs):
    return run(inputs, 8192, 4, 4)
```

```python
import contextlib
import numpy as np
import concourse.bass as bass
import concourse.mybir as mybir
from concourse.bass_utils import run_bass_kernel_spmd

F32 = mybir.dt.float32
BF16 = mybir.dt.bfloat16
ALU = mybir.AluOpType
AF = mybir.ActivationFunctionType
ENGS = ("sp", "act", "pe", "dve", "pool")
EPS = 1e-6
NEG = -30000.0


class Res:
    __slots__ = ("name", "w", "r", "excl")

    def __init__(self, name="", excl=False):
        self.name = name
        self.w = {}
        self.r = {}
        self.excl = excl


def PR(name):
    return Res(name, excl=True)


def _merge(d, s, v):
    if d.get(s, 0) < v:
        d[s] = v


class Prog:
    def __init__(self, nc, stack, n_dma_sems=32):
        self.nc = nc
        self.q = {e: [] for e in ENGS}
        self.sem_names = []
        self.sem_count = []
        self.seen = {e: {} for e in ENGS}
        self.pending_reads = {e: [] for e in ENGS}
        self.pending_writes = {e: [] for e in ENGS}
        self.eng_sem = {}
        for e in ("act", "pe", "dve", "pool"):
            self.eng_sem[e] = self._new_sem("c_" + e)
        self.dma_sems = {"sp": [self._new_sem("d%d" % i) for i in range(n_dma_sems)],
                         "pool": [self._new_sem("w%d" % i) for i in range(16)]}
        self.dma_rr = {"sp": 0, "pool": 0}
        self.n_ops = 0
        self.sems = [stack.enter_context(nc.semaphore(n)) for n in self.sem_names]

    def _new_sem(self, name):
        self.sem_names.append(name)
        self.sem_count.append(0)
        return len(self.sem_names) - 1

    def op(self, eng, fn, reads=(), writes=(), inc=True, dma=False):
        waits = {}
        for r in reads:
            for s, v in r.w.items():
                _merge(waits, s, v)
            if r.excl:
                for s, v in r.r.items():
                    if not (eng in self.eng_sem and s == self.eng_sem[eng]):
                        _merge(waits, s, v)
        for w in writes:
            for s, v in w.w.items():
                _merge(waits, s, v)
            for s, v in w.r.items():
                _merge(waits, s, v)
        tok = None
        if dma:
            pool_ = self.dma_sems[eng]
            s = pool_[self.dma_rr[eng] % len(pool_)]
            self.dma_rr[eng] += 1
            if self.sem_count[s] > 0:
                _merge(waits, s, self.sem_count[s])
            self.sem_count[s] += 16
            tok = (s, self.sem_count[s], 16)
        elif inc:
            s = self.eng_sem[eng]
            self.sem_count[s] += 1
            tok = (s, self.sem_count[s], 1)
        seen = self.seen[eng]
        wl = []
        for s, v in waits.items():
            if seen.get(s, 0) < v:
                if eng == "pe" and s == self.eng_sem["pe"]:
                    continue
                seen[s] = v
                wl.append((s, v))
        self.q[eng].append((fn, wl, tok))
        self.n_ops += 1
        if tok is None:
            self.pending_reads[eng].extend(reads)
            self.pending_writes[eng].extend(writes)
        else:
            s, v, _ = tok
            rl = list(reads)
            wr = list(writes)
            if not dma:
                rl += self.pending_reads[eng]
                wr += self.pending_writes[eng]
                self.pending_reads[eng] = []
                self.pending_writes[eng] = []
            for r in rl:
                _merge(r.r, s, v)
            for w in wr:
                _merge(w.w, s, v)
        return tok

    def barrier(self):
        for e in ENGS:
            wl = []
            for s, c in enumerate(self.sem_count):
                if c > 0 and self.seen[e].get(s, 0) < c:
                    self.seen[e][s] = c
                    wl.append((s, c))
            if wl:
                self.q[e].append((None, wl, None))

    def flush(self, final=False):
        nc = self.nc
        sems = self.sems
        if final:
            wl = [(s, c) for s, c in enumerate(self.sem_count) if c > 0]
            self.q["sp"].append((None, wl, None))
        with nc.Block() as block:
            def run(eng_name):
                ops = self.q[eng_name]

                def f(e):
                    for fn, wl, tok in ops:
                        for s, v in wl:
                            e.wait_ge(sems[s], v)
                        if fn is None:
                            continue
                        ins = fn(e)
                        if tok is not None:
                            ins.then_inc(sems[tok[0]], tok[2])
                return f
            block.sync(run("sp"))
            block.scalar(run("act"))
            block.tensor(run("pe"))
            block.vector(run("dve"))
            block.gpsimd(run("pool"))
        self.q = {e: [] for e in ENGS}


class Ring:
    def __init__(self, items):
        self.items = items
        self.i = 0

    def next(self):
        it = self.items[self.i % len(self.items)]
        self.i += 1
        return it


C_G = {"mix_pre": 0, "mix_post": 8, "mlp_pre": 16, "mlp_post": 24, "ple_pre": 32, "ple_post": 40}
C_LNG, C_LNB, C_PSC, C_DW, C_SC, C_BF = 48, 50, 52, 54, 116, 122
K_ID, K_MASK, K_PCOEF, K_PRATIO = 0, 128, 256, 288


def build(T, L, stop=None, dbg=False):
    NT = T // 512
    NB = T // 128
    nc = bass.Bass("TRN2", target_bir_lowering=False)

    def din(name, shape, dt=F32):
        return nc.dram_tensor(name, shape, dt, kind="ExternalInput").ap()

    def dscr(name, shape, dt):
        return nc.dram_tensor(name, shape, dt, kind="ExternalOutput" if dbg else "Internal").ap()

    xT = din("xT", [1024, T])
    pT = din("pT", [L * 256, T])
    w_in = din("w_in", [L * 1024, 2308])
    w_pw = din("w_pw", [L * 256, 256])
    w_pbd = din("w_pbd", [L * 256, 128])
    w_out = din("w_out", [L * 1024, 1024])
    w_up = din("w_up", [L * 1024, 4096])
    w_dn = din("w_dn", [L * 1024, 4096])
    w_gate = din("w_gate", [L * 1024, 1024])
    w_proj = din("w_proj", [L * 256, 1024])
    small = din("small", [128, L * 128])
    consts = din("consts", [128, 320])
    outT = nc.dram_tensor("outT", [1024, T], F32, kind="ExternalOutput").ap()

    hT = dscr("hT", [1024, T], F32)
    wup_bf = dscr("wup_bf", [L * 1024, 4096], BF16)
    wdn_bf = dscr("wdn_bf", [L * 1024, 4096], BF16)
    win_bf = dscr("win_bf", [L * 1024, 2308], BF16)
    wout_bf = dscr("wout_bf", [L * 1024, 1024], BF16)
    wgate_bf = dscr("wgate_bf", [L * 1024, 1024], BF16)
    wproj_bf = dscr("wproj_bf", [L * 256, 1024], BF16)
    mixin = dscr("mixin", [1024, 32 + T], BF16)
    qT = dscr("qT", [256, T], BF16)
    kT = dscr("kT", [256, T], BF16)
    c3d = dscr("c3d", [4, 3, T], BF16)
    vaug = dscr("vaug", [NB, 128, 512], BF16)
    ymix = dscr("ymix", [1024, T], BF16)

    R_h = [Res("h%d" % j) for j in range(NT)]
    R_mixin = [Res("mi%d" % j) for j in range(NT + 1)]
    R_q = [Res("q%d" % j) for j in range(NT)]
    R_k = [Res("k%d" % j) for j in range(NT)]
    R_c3 = [Res("c3%d" % j) for j in range(NT)]
    R_v = [Res("v%d" % j) for j in range(NT)]
    R_ymA = [Res("ymA%d" % j) for j in range(NT)]
    R_ymB = [Res("ymB%d" % j) for j in range(NT)]
    R_winbf = [Res("winbf%d" % l) for l in range(L)]
    R_wCbf = [Res("wCbf%d" % l) for l in range(L)]
    R_wup = [[Res("wu%d_%d" % (l, g)) for g in range(8)] for l in range(L)]
    R_wdn = [[Res("wd%d_%d" % (l, g)) for g in range(8)] for l in range(L)]

    top = contextlib.ExitStack()
    with top:
        P = Prog(nc, top)

        uid = [0]

        def sbt(st, name, shape, dt):
            uid[0] += 1
            return st.enter_context(nc.sbuf_tensor("%s_u%d" % (name, uid[0]), shape, dt))

        def pst(st, name, shape=(128, 512), dt=F32):
            uid[0] += 1
            return st.enter_context(nc.psum_tensor("%s_u%d" % (name, uid[0]), list(shape), dt))

        small_sb = sbt(top, "small_sb", [128, L * 128], F32)
        consts_sb = sbt(top, "consts_sb", [128, 320], F32)
        ident_bf = sbt(top, "ident_bf", [128, 128], BF16)
        mask_bf = sbt(top, "mask_bf", [128, 128], BF16)
        ones_bf = sbt(top, "ones_bf", [128, 128], BF16)
        ones_f = sbt(top, "ones_f", [128, 128], F32)
        cneg = sbt(top, "cneg", [128, NB, 4], F32)
        R_small, R_consts, R_cst2, R_cneg = Res("small"), Res("consts"), Res("cst2"), Res("cneg")
        ident_f = consts_sb[:, K_ID:K_ID + 128]

        def scol(l, c, rows=slice(0, 128)):
            return small_sb[rows, l * 128 + c: l * 128 + c + 1]

        def dma(eng, out, in_, reads, writes):
            P.op(eng, lambda e: e.dma_start(out=out, in_=in_), reads=reads, writes=writes, dma=True)

        def mm(out, lhsT, rhs, start, stop, reads, writes):
            P.op("pe", lambda e: e.matmul(out, lhsT=lhsT, rhs=rhs, start=start, stop=stop),
                 reads=reads, writes=writes, inc=stop)

        def act(out, in_, func, reads, writes, bias=None, scale=None):
            kw = {}
            if bias is not None:
                kw["bias"] = bias
            if scale is not None:
                kw["scale"] = scale
            P.op("act", lambda e: e.activation(out, in_, func, **kw), reads=reads, writes=writes)

        def tt(eng, out, in0, in1, op, reads, writes):
            P.op(eng, lambda e: e.tensor_tensor(out, in0, in1, op), reads=reads, writes=writes)

        def stt(eng, out, in0, scalar, in1, op0, op1, reads, writes):
            P.op(eng, lambda e: e.scalar_tensor_tensor(out, in0, scalar, in1, op0, op1), reads=reads, writes=writes)

        def norm_scale(eng, out, in0, gcol, rstd_ap, reads, writes, ptmp_ring):
            if eng == "dve":
                stt("dve", out, in0, gcol, rstd_ap, ALU.mult, ALU.mult, reads, writes)
            else:
                tb, Rtb = ptmp_ring.next()
                tt("pool", tb[:], in0, rstd_ap, ALU.mult, reads, [Rtb])
                P.op("pool", lambda e: e.tensor_scalar_mul(out, tb[:], gcol), reads=[Rtb] + list(reads), writes=writes)

        def ts(eng, out, in0, s1, s2, op0, op1, reads, writes):
            if s2 is None:
                P.op(eng, lambda e: e.tensor_single_scalar(out, in0, s1, op0), reads=reads, writes=writes)
            else:
                P.op(eng, lambda e: e.tensor_scalar(out, in0, s1, s2, op0, op1), reads=reads, writes=writes)

        def cp(eng, out, in_, reads, writes):
            if eng == "act":
                P.op("act", lambda e: e.copy(out, in_), reads=reads, writes=writes)
            else:
                P.op(eng, lambda e: e.tensor_copy(out, in_), reads=reads, writes=writes)

        def memset(eng, ap, val, writes):
            P.op(eng, lambda e: e.memset(ap, val), writes=writes)

        def recip(out, in_, reads, writes):
            P.op("dve", lambda e: e.reciprocal(out, in_), reads=reads, writes=writes)

        def rms_stats(sq, R_sq, ps_stat, R_ps, rs, rstd, R_rs, R_rstd):
            for c in range(8):
                mm(ps_stat[:], ones_bf[:], sq[:, c, :], c == 0, c == 7, list(R_sq) + [R_cst2], [R_ps])
            act(rs[:], ps_stat[:], AF.Sqrt, [R_ps], [R_rs], bias=EPS, scale=1.0 / 1024.0)
            recip(rstd[:], rs[:], [R_rs], [R_rstd])

        with contextlib.ExitStack() as st:
            zt = sbt(st, "zt", [128, 8, 32], BF16)
            R_zt = Res("zt")
            dma("sp", small_sb[:], small, [], [R_small])
            dma("sp", consts_sb[:], consts, [], [R_consts])
            cp("dve", ident_bf[:], consts_sb[:, K_ID:K_ID + 128], [R_consts], [R_cst2])
            cp("dve", mask_bf[:], consts_sb[:, K_MASK:K_MASK + 128], [R_consts], [R_cst2])
            memset("pool", ones_bf[:], 1.0, [R_cst2])
            memset("pool", ones_f[:], 1.0, [R_cst2])
            memset("pool", zt[:], 0.0, [R_zt])
            dma("sp", mixin.rearrange("(c p) n -> p c n", p=128)[:, :, 0:32], zt[:], [R_zt], [R_mixin[0]])
            P.barrier()
            P.flush(final=(stop == "pro"))
        if stop == "pro":
            return nc

        def cast_small_weights(l):
            for k in range(8):
                r0 = l * 1024 + k * 128
                for hh in range(2):
                    dma("pool", win_bf[r0:r0 + 128, hh * 1154:(hh + 1) * 1154], w_in[r0:r0 + 128, hh * 1154:(hh + 1) * 1154],
                        [], [R_winbf[l]])
            for k in range(8):
                r0 = l * 1024 + k * 128
                dma("pool", wout_bf[r0:r0 + 128, :], w_out[r0:r0 + 128, :], [], [R_wCbf[l]])
                dma("pool", wgate_bf[r0:r0 + 128, :], w_gate[r0:r0 + 128, :], [], [R_wCbf[l]])
            for k in range(2):
                r0 = l * 256 + k * 128
                dma("pool", wproj_bf[r0:r0 + 128, :], w_proj[r0:r0 + 128, :], [], [R_wCbf[l]])

        def cast_mlp_weights(l):
            for g in range(8):
                for (src, dst, RR) in ((w_up, wup_bf, R_wup), (w_dn, wdn_bf, R_wdn)):
                    r0 = l * 1024 + g * 128
                    for hh in range(2):
                        dma("pool", dst[r0:r0 + 128, hh * 2048:(hh + 1) * 2048],
                            src[r0:r0 + 128, hh * 2048:(hh + 1) * 2048], [], [RR[l][g]])

        cast_small_weights(0)
        for l in range(L):
            h_src = xT if l == 0 else hT
            h_dst = outT if l == L - 1 else hT
            h_src_v = h_src.rearrange("(c p) n -> p c n", p=128)
            h_dst_v = h_dst.rearrange("(c p) n -> p c n", p=128)

            with contextlib.ExitStack() as st:
                win = sbt(st, "win", [128, 8, 2308], BF16)
                R_win = Res("win")
                ht = [sbt(st, "ht%d" % i, [128, 8, 512], F32) for i in range(2)]
                R_ht = [Res("ht0"), Res("ht1")]
                sq = sbt(st, "sq", [128, 8, 512], BF16)
                R_sq = Res("sq")
                xn2 = [sbt(st, "xn%d" % i, [128, 8, 512], BF16) for i in range(2)]
                R_xn2 = [Res("xn0"), Res("xn1")]
                rs = sbt(st, "rs", [128, 512], F32)
                rstd = sbt(st, "rstd", [128, 512], F32)
                R_rs, R_rstd = Res("rs"), Res("rstd")
                tmpf = [sbt(st, "tmpf%d" % i, [128, 512], F32) for i in range(2)]
                tmp_ring = Ring([(tmpf[i], Res("tmpf%d" % i)) for i in range(2)])
                ptmp_ring = Ring([(sbt(st, "ptmp%d" % i, [128, 512], F32), Res("ptmp%d" % i)) for i in range(2)])
                mstage = [sbt(st, "mstage%d" % i, [128, 8, 512], BF16) for i in range(2)]
                R_mst = [Res("mst0"), Res("mst1")]
                qstage = [sbt(st, "qstage%d" % i, [128, 2, 512], BF16) for i in range(2)]
                R_qst = [Res("qst0"), Res("qst1")]
                kstage = [sbt(st, "kstage%d" % i, [128, 2, 512], BF16) for i in range(2)]
                R_kst = [Res("kst0"), Res("kst1")]
                vstage = [sbt(st, "vstage%d" % i, [128, 4, 512], BF16) for i in range(2)]
                R_vst = [Res("vst0"), Res("vst1")]
                xb = sbt(st, "xb", [4, 512], F32)
                ef = sbt(st, "ef", [4, 512], F32)
                lf = sbt(st, "lf", [4, 512], F32)
                ones4 = sbt(st, "ones4", [4, 512], F32)
                cc = [sbt(st, "cc%d" % i, [4, 512], F32) for i in range(2)]
                r1 = sbt(st, "r1", [4, 512], F32)
                r2 = sbt(st, "r2", [4, 512], F32)
                c3 = [sbt(st, "c3_%d" % i, [4, 3, 512], BF16) for i in range(2)]
                R_f = Res("fmisc")
                R_cc = [Res("cc0"), Res("cc1")]
                R_c3s = [Res("c3s0"), Res("c3s1")]
                ps_stat = pst(st, "ps_stat")
                R_pstat = PR("ps_stat")
                ps_ring = Ring([(pst(st, "psr%d" % i), PR("psr%d" % i)) for i in range(5)])
                ps_v = Ring([(pst(st, "psv%d" % i), PR("psv%d" % i)) for i in range(2)])

                for k in range(8):
                    dma("sp", win[:, k, :], win_bf[l * 1024 + k * 128: l * 1024 + (k + 1) * 128, :], [R_winbf[l]], [R_win])
                cast_mlp_weights(l)
                if l + 1 < L:
                    cast_small_weights(l + 1)
                memset("pool", ones4[:], 1.0, [R_f])
                for i in range(2):
                    memset("pool", vstage[i][:], 1.0, [R_vst[i]])

                def load_h(j):
                    dma("sp", ht[j % 2][:], h_src_v[:, :, j * 512:(j + 1) * 512], [R_h[j]], [R_ht[j % 2]])

                cur = {}

                def proj(ps, R_ps, col0, M):
                    xn, R_xn = cur["xn"], cur["R_xn"]
                    for k in range(8):
                        mm(ps[0:M, :], win[:, k, col0:col0 + M], xn[:, k, :], k == 0, k == 7, [R_win, R_xn], [R_ps])

                def norm_a(j):
                    bb = j % 2
                    act(sq[:], ht[bb][:], AF.Square, [R_ht[bb]], [R_sq])

                def norm_b(j):
                    bb = j % 2
                    rms_stats(sq, [R_sq], ps_stat, R_pstat, rs, rstd, R_rs, R_rstd)
                    for c in range(8):
                        norm_scale("dve", xn2[bb][:, c, :], ht[bb][:, c, :], scol(l, C_G["mix_pre"] + c),
                                   rstd[:], [R_ht[bb], R_rstd, R_small], [R_xn2[bb]], ptmp_ring)

                load_h(0)
                if NT > 1:
                    load_h(1)
                norm_a(0)
                norm_b(0)
                for j in range(NT):
                    b = j % 2
                    h = ht[b]
                    cols = slice(j * 512, (j + 1) * 512)
                    xnb, R_xnb = xn2[b], R_xn2[b]
                    xn, R_xn = xnb, R_xnb
                    cur["xn"], cur["R_xn"] = xnb, R_xnb
                    pf, Rpf = ps_ring.next()
                    proj(pf, Rpf, 2048, 4)
                    ts("dve", xb[:], pf[0:4, :], scol(l, C_BF, slice(0, 4)), None, ALU.add, None, [Rpf, R_small], [R_f])
                    act(ef[:], xb[:], AF.Exp, [R_f], [R_f], scale=-1.0)
                    act(lf[:], ef[:], AF.Ln, [R_f], [R_f], bias=1.0, scale=1.0)
                    init = 0.0 if j == 0 else cc[1 - b][:, 511:512]
                    P.op("dve", (lambda o, d0, d1, ini: (lambda e: e.tensor_tensor_scan(o, d0, d1, ini, ALU.mult, ALU.subtract)))(
                        cc[b][:], ones4[:], lf[:], init), reads=[R_f, R_cc[1 - b]], writes=[R_cc[b]])
                    if j + 1 < NT:
                        norm_a(j + 1)
                    for blk in range(2):
                        pa, Rpa = ps_ring.next()
                        pb, Rpb = ps_ring.next()
                        proj(pa, Rpa, (0 + blk) * 128, 128)
                        proj(pb, Rpb, (2 + blk) * 128, 128)
                        tb, Rtb = tmp_ring.next()
                        act(tb[:], pb[:], AF.Sigmoid, [Rpb], [Rtb])
                        tt("dve", mstage[b][:, 0 + blk, :], pa[:], tb[:], ALU.mult, [Rpa, Rtb], [R_mst[b]])
                    cp("dve", c3[b][:, 0, :], cc[b][:], [R_cc[b]], [R_c3s[b]])
                    tt("dve", r1[:], cc[b][:], c3[b][:, 0, :], ALU.subtract, [R_cc[b], R_c3s[b]], [R_f])
                    cp("dve", c3[b][:, 1, :], r1[:], [R_f], [R_c3s[b]])
                    tt("dve", r2[:], r1[:], c3[b][:, 1, :], ALU.subtract, [R_f, R_c3s[b]], [R_f])
                    cp("dve", c3[b][:, 2, :], r2[:], [R_f], [R_c3s[b]])
                    for blk in range(2):
                        pa, Rpa = ps_ring.next()
                        pb, Rpb = ps_ring.next()
                        proj(pa, Rpa, (8 + blk) * 128, 128)
                        proj(pb, Rpb, (12 + blk) * 128, 128)
                        tb, Rtb = tmp_ring.next()
                        cp("act", tb[:], pb[:], [Rpb], [Rtb])
                        tt("dve", mstage[b][:, 2 + blk, :], pa[:], tb[:], ALU.mult, [Rpa, Rtb], [R_mst[b]])
                    if j + 1 < NT:
                        norm_b(j + 1)
                        if j + 2 < NT:
                            load_h(j + 2)
                    for blk in range(2):
                        pa, Rpa = ps_ring.next()
                        proj(pa, Rpa, (14 + blk) * 128, 128)
                        cp("dve", mstage[b][:, 4 + blk, :], pa[:], [Rpa], [R_mst[b]])
                        pb, Rpb = ps_ring.next()
                        proj(pb, Rpb, (10 + blk) * 128, 128)
                        cp("act", mstage[b][:, 6 + blk, :], pb[:], [Rpb], [R_mst[b]])
                    for blk in range(2):
                        pa, Rpa = ps_ring.next()
                        proj(pa, Rpa, (4 + blk) * 128, 128)
                        act(qstage[b][:, blk, :], pa[:], AF.Copy, [Rpa], [R_qst[b]], scale=0.125)
                        pb, Rpb = ps_ring.next()
                        proj(pb, Rpb, (6 + blk) * 128, 128)
                        cp("dve", kstage[b][:, blk, :], pb[:], [Rpb], [R_kst[b]])
                    for s in range(4):
                        pv, Rpv = ps_v.next()
                        for k in range(8):
                            mm(pv[:, 0:256], xn[:, k, s * 128:(s + 1) * 128], win[:, k, 2052:2308], k == 0, k == 7,
                               [R_win, R_xn], [Rpv])
                        dst = vstage[b][:, s, :].rearrange("p (h c) -> p h c", h=4)[:, :, 0:64]
                        src = pv[:, 0:256].rearrange("p (h c) -> p h c", h=4)
                        cp("dve" if s % 2 == 0 else "act", dst, src, [Rpv], [R_vst[b]])
                    dma("sp", mixin.rearrange("(c p) n -> p c n", p=128)[:, :, 32 + j * 512: 32 + (j + 1) * 512],
                        mstage[b][:], [R_mst[b]], [R_mixin[j + 1]])
                    dma("sp", qT.rearrange("(c p) n -> p c n", p=128)[:, :, cols], qstage[b][:], [R_qst[b]], [R_q[j]])
                    dma("sp", kT.rearrange("(c p) n -> p c n", p=128)[:, :, cols], kstage[b][:], [R_kst[b]], [R_k[j]])
                    dma("sp", c3d[:, :, cols], c3[b][:], [R_c3s[b]], [R_c3[j]])
                    dma("sp", vaug[j * 4:(j + 1) * 4].rearrange("s p c -> p s c"), vstage[b][:], [R_vst[b]], [R_v[j]])
                P.barrier()
                P.flush(final=(stop == "A1"))
            if stop == "A1":
                return nc

            with contextlib.ExitStack() as st:
                dconf = sbt(st, "dconf", [128, 2, 31, 128], BF16)
                dsc = sbt(st, "dsc", [128, 2, 3, 128], BF16)
                dpl = sbt(st, "dpl", [128, 2, 16, 128], BF16)
                R_dg = Res("diag")
                wpw = sbt(st, "wpw", [128, 2, 256], BF16)
                wpbd = sbt(st, "wpbd", [128, 2, 128], BF16)
                R_wA2 = Res("wA2")
                mt = [sbt(st, "mt%d" % i, [128, 8, 544], BF16) for i in range(2)]
                R_mt = [Res("mt0"), Res("mt1")]
                xc = sbt(st, "xc", [128, 2, 512], F32)
                sq2 = sbt(st, "sq2", [128, 2, 512], F32)
                R_xc, R_sq2 = Res("xc"), Res("sq2")
                mean = sbt(st, "mean", [128, 512], F32)
                msq = sbt(st, "msq", [128, 512], F32)
                var = sbt(st, "var", [128, 512], F32)
                sd = sbt(st, "sd", [128, 512], F32)
                rstd2 = sbt(st, "rstd2", [128, 512], F32)
                R_mean, R_msq, R_var, R_sd, R_rstd2 = Res("mean"), Res("msq"), Res("var"), Res("sd"), Res("rstd2")
                xm = [sbt(st, "xm%d" % i, [128, 512], F32) for i in range(2)]
                R_xm = [Res("xm0"), Res("xm1")]
                xnn = [sbt(st, "xnn%d" % i, [128, 512], F32) for i in range(2)]
                R_xnn = [Res("xnn0"), Res("xnn1")]
                sconf = sbt(st, "sconf", [128, 2, 512], BF16)
                R_sconf = Res("sconf")
                dpool = sbt(st, "dpool", [128, 2, 512], BF16)
                R_dpool = Res("dpool")
                t16 = sbt(st, "t16", [128, 2, 16], F32)
                R_t16 = Res("t16")
                ystage = [sbt(st, "ystage%d" % i, [128, 6, 512], BF16) for i in range(2)]
                R_yst = [Res("yst0"), Res("yst1")]
                psc = [pst(st, "psc%d" % i) for i in range(2)]
                R_psc = [PR("psc0"), PR("psc1")]
                ps_s1 = pst(st, "ps_s1")
                ps_s2 = pst(st, "ps_s2")
                R_s1, R_s2 = PR("s1"), PR("s2")
                ps_ring = Ring([(pst(st, "psq%d" % i), PR("psq%d" % i)) for i in range(3)])

                dma("pool", wpw[:], w_pw[l * 256:(l + 1) * 256, :].rearrange("(c p) n -> p c n", p=128), [], [R_wA2])
                dma("pool", wpbd[:], w_pbd[l * 256:(l + 1) * 256, :].rearrange("(c p) n -> p c n", p=128), [], [R_wA2])
                for blk in range(2):
                    for k in range(31):
                        P.op("pool", (lambda o, s: (lambda e: e.tensor_scalar_mul(o, ident_f, s)))(
                            dconf[:, blk, k, :], scol(l, C_DW + blk * 31 + k)), reads=[R_consts, R_small], writes=[R_dg])
                    for k in range(3):
                        P.op("pool", (lambda o, s: (lambda e: e.tensor_scalar_mul(o, ident_f, s)))(
                            dsc[:, blk, k, :], scol(l, C_SC + blk * 3 + k)), reads=[R_consts, R_small], writes=[R_dg])
                    for k in range(16):
                        P.op("pool", (lambda o, s: (lambda e: e.tensor_scalar_mul(o, ident_f, s)))(
                            dpl[:, blk, k, :], consts_sb[:, K_PCOEF + blk * 16 + k: K_PCOEF + blk * 16 + k + 1]),
                            reads=[R_consts], writes=[R_dg])

                def load_mt(j):
                    dma("sp", mt[j % 2][:], mixin.rearrange("(c p) n -> p c n", p=128)[:, :, j * 512: j * 512 + 544],
                        [R_mixin[j], R_mixin[j + 1]], [R_mt[j % 2]])

                def pw_store(jj):
                    bb = jj % 2
                    cols_ = slice(jj * 512, (jj + 1) * 512)
                    for oc in range(2):
                        pp, Rpp = ps_ring.next()
                        for kb in range(2):
                            mm(pp[:], wpw[:, kb, oc * 128:(oc + 1) * 128], sconf[:, kb, :], kb == 0, kb == 1,
                               [R_wA2, R_sconf], [Rpp])
                        cp("dve", ystage[bb][:, oc, :], pp[:], [Rpp], [R_yst[bb]])
                    ym_v = ymix.rearrange("(c p) n -> p c n", p=128)
                    dma("sp", ym_v[:, 0:2, cols_], ystage[bb][:, 0:2, :], [R_yst[bb]], [R_ymA[jj]])
                    dma("sp", ym_v[:, 4:8, cols_], ystage[bb][:, 2:6, :], [R_yst[bb]], [R_ymA[jj]])

                load_mt(0)
                sect = stop.split(":")[1] if (stop and ":" in stop) else "all"

                def on(*names):
                    return sect == "all" or sect in names

                for j in range(NT):
                    b = j % 2
                    if j + 1 < NT:
                        load_mt(j + 1)
                    m_ = mt[b]
                    cols = slice(j * 512, (j + 1) * 512)
                    if sect == "conv1":
                        for blk in range(2):
                            for k in range(31):
                                mm(psc[blk][:], dconf[:, blk, k, :], m_[:, blk, 2 + k: 2 + k + 512], k == 0, k == 30,
                                   [R_dg, R_mt[b]], [R_psc[blk]])
                            cp("dve", xc[:, blk, :], psc[blk][:], [R_psc[blk]], [R_xc])
                    if sect == "conv2":
                        for blk in range(2):
                            for k in range(3):
                                mm(psc[blk][:], dconf[:, blk, k, :], m_[:, blk, 2 + k: 2 + k + 512], k == 0, k == 2,
                                   [R_dg, R_mt[b]], [R_psc[blk]])
                            act(sq2[:, blk, :], psc[blk][:], AF.Square, [R_psc[blk]], [R_sq2])
                    if sect == "pool1":
                        for blk in range(2):
                            pp, Rpp = ps_ring.next()
                            for k in range(16):
                                mm(pp[:], dpl[:, blk, k, :], m_[:, 4 + blk, 32 - k: 32 - k + 512], k == 0, k == 15,
                                   [R_dg, R_mt[b]], [Rpp])
                            cp("dve", dpool[:, blk, :], pp[:], [Rpp], [R_dpool])
                    if on("conv", "ln", "silu", "conf"):
                        for blk in range(2):
                            for k in range(31):
                                mm(psc[blk][:], dconf[:, blk, k, :], m_[:, blk, 2 + k: 2 + k + 512], k == 0, k == 30,
                                   [R_dg, R_mt[b]], [R_psc[blk]])
                            cp("dve", xc[:, blk, :], psc[blk][:], [R_psc[blk]], [R_xc])
                            act(sq2[:, blk, :], xc[:, blk, :], AF.Square, [R_xc], [R_sq2])
                    if on("sc"):
                        for blk in range(2):
                            pp, Rpp = ps_ring.next()
                            for k in range(3):
                                mm(pp[:], dsc[:, blk, k, :], m_[:, 2 + blk, 30 + k: 30 + k + 512], k == 0, k == 2,
                                   [R_dg, R_mt[b]], [Rpp])
                            tt("dve", ystage[b][:, 2 + blk, :], pp[:], m_[:, 6 + blk, 32:544], ALU.mult, [Rpp, R_mt[b]], [R_yst[b]])
                    if on("pool"):
                        for blk in range(2):
                            pp, Rpp = ps_ring.next()
                            for k in range(16):
                                mm(pp[:], dpl[:, blk, k, :], m_[:, 4 + blk, 32 - k: 32 - k + 512], k == 0, k == 15,
                                   [R_dg, R_mt[b]], [Rpp])
                            cp("act", dpool[:, blk, :], pp[:], [Rpp], [R_dpool])
                            if j == 0:
                                tt("dve", t16[:, blk, :], pp[:, 0:16], m_[:, 4 + blk, 32:48], ALU.add, [Rpp, R_mt[b]], [R_t16])
                                tt("pool", t16[:, blk, :], t16[:, blk, :],
                                   consts_sb[:, K_PRATIO + blk * 16: K_PRATIO + blk * 16 + 16], ALU.mult, [R_t16, R_consts], [R_t16])
                                tt("dve", dpool[:, blk, 0:16], t16[:, blk, :], m_[:, 4 + blk, 32:48], ALU.subtract,
                                   [R_t16, R_mt[b]], [R_dpool])
                        for blk in range(2):
                            pp, Rpp = ps_ring.next()
                            mm(pp[:], wpbd[:, blk, :], dpool[:, blk, :], True, True, [R_wA2, R_dpool], [Rpp])
                            act(ystage[b][:, 4 + blk, :], pp[:], AF.Copy, [Rpp, R_small], [R_yst[b]], scale=scol(l, C_PSC + blk))
                    if j > 0:
                        pw_store(j - 1)
                    if on("ln", "silu", "conf"):
                        for blk in range(2):
                            mm(ps_s1[:], ones_f[:], xc[:, blk, :], blk == 0, blk == 1, [R_xc, R_cst2], [R_s1])
                        for blk in range(2):
                            mm(ps_s2[:], ones_f[:], sq2[:, blk, :], blk == 0, blk == 1, [R_sq2, R_cst2], [R_s2])
                        act(mean[:], ps_s1[:], AF.Copy, [R_s1], [R_mean], scale=1.0 / 256.0)
                        tt("dve", msq[:], mean[:], mean[:], ALU.mult, [R_mean], [R_msq])
                        stt("dve", var[:], ps_s2[:], 1.0 / 256.0, msq[:], ALU.mult, ALU.subtract, [R_s2, R_msq], [R_var])
                        act(sd[:], var[:], AF.Sqrt, [R_var], [R_sd], bias=EPS, scale=1.0)
                        recip(rstd2[:], sd[:], [R_sd], [R_rstd2])
                        for blk in range(2):
                            tt("dve", xm[blk][:], xc[:, blk, :], mean[:], ALU.subtract, [R_xc, R_mean], [R_xm[blk]])
                            tt("dve", xnn[blk][:], xm[blk][:], rstd2[:], ALU.mult, [R_xm[blk], R_rstd2], [R_xnn[blk]])
                    if on("silu", "conf"):
                        for blk in range(2):
                            act(sconf[:, blk, :], xnn[blk][:], AF.Silu, [R_xnn[blk], R_small], [R_sconf],
                                bias=scol(l, C_LNB + blk), scale=scol(l, C_LNG + blk))
                pw_store(NT - 1)
                P.barrier()
                P.flush(final=(stop is not None and stop.startswith("A2")))
            if stop is not None and stop.startswith("A2"):
                return nc

            with contextlib.ExitStack() as st:
                kaug = sbt(st, "kaug", [128, 4, T], BF16)
                R_kaug = [Res("kaug%d" % j) for j in range(NT)]
                R_kaugc = Res("kaugc")
                vsb = sbt(st, "vsb", [128, NB, 512], BF16)
                R_vsb = [Res("vsb%d" % j) for j in range(NT)]
                qa = [sbt(st, "qa%d" % i, [128, 4, 512], BF16) for i in range(2)]
                R_qa = [Res("qa0"), Res("qa1")]
                pts = Ring([(sbt(st, "pt%d" % i, [128, 1024], BF16), Res("pt%d" % i)) for i in range(3)])
                rec = sbt(st, "rec", [128, 512], F32)
                R_rec = Res("rec")
                yst = Ring([(sbt(st, "ysb%d" % i, [64, 512], BF16), Res("ysb%d" % i)) for i in range(2)])
                ps_s = Ring([(pst(st, "pss%d" % i, (128, 1024)), PR("pss%d" % i)) for i in range(3)])
                ps_o = Ring([(pst(st, "pso%d" % i), PR("pso%d" % i)) for i in range(2)])

                for i in range(2):
                    memset("dve", qa[i][64:96, :, :], -1.0, [R_qa[i]])
                def kv_load(j):
                    cols = slice(j * 512, (j + 1) * 512)
                    memset("dve" if j % 2 == 0 else "pool", kaug[64:96, :, cols], 0.0, [R_kaug[j]])
                    memset("dve" if j % 2 == 0 else "pool", kaug[64:67, :, cols], 1.0, [R_kaug[j]])
                    dma("sp", kaug[0:64, :, cols], kT.rearrange("(h r) n -> r h n", r=64)[:, :, cols], [R_k[j]], [R_kaug[j]])
                    dma("sp", kaug[67:70, :, cols], c3d.rearrange("h j n -> j h n")[:, :, cols], [R_c3[j]], [R_kaug[j]])
                    dma("sp", vsb[:, j * 4:(j + 1) * 4, :], vaug[j * 4:(j + 1) * 4].rearrange("s p c -> p s c"),
                        [R_v[j]], [R_vsb[j]])

                def load_q(j):
                    cols = slice(j * 512, (j + 1) * 512)
                    dma("sp", qa[j % 2][0:64, :, :], qT.rearrange("(h r) n -> r h n", r=64)[:, :, cols], [R_q[j]], [R_qa[j % 2]])
                    dma("sp", qa[j % 2][64:67, :, :], c3d.rearrange("h j n -> j h n")[:, :, cols], [R_c3[j]], [R_qa[j % 2]])

                kv_load(0)
                load_q(0)
                for j in range(1, NT):
                    kv_load(j)
                for j in range(NT):
                    b = j % 2
                    if j + 1 < NT:
                        load_q(j + 1)
                    cols = slice(j * 512, (j + 1) * 512)
                    for h in range(4):
                        po, Rpo = ps_o.next()
                        units = [("pair", i0) for i0 in range(0, 4 * j, 2)] + [("diag", dj) for dj in range(4)]

                        def qk(u):
                            kind, v = u
                            pS, RpS = ps_s.next()
                            if kind == "pair":
                                for half in range(2):
                                    i = v + half
                                    mm(pS[:, half * 512:(half + 1) * 512], kaug[0:96, h, i * 128:(i + 1) * 128],
                                       qa[b][0:96, h, :], True, True, [R_kaug[i // 4], R_kaugc, R_qa[b]], [RpS])
                            else:
                                i = 4 * j + v
                                c0 = 128 * v
                                mm(pS[:, c0:512], kaug[0:96, h, i * 128:(i + 1) * 128], qa[b][0:96, h, c0:512],
                                   True, False, [R_kaug[i // 4], R_kaugc, R_qa[b]], [RpS])
                                mm(pS[:, c0:c0 + 128], ident_bf[:], mask_bf[:], False, True, [R_cst2], [RpS])
                            return pS, RpS

                        LA = 2
                        pend = [qk(units[x]) for x in range(min(LA, len(units)))]
                        for ui, u in enumerate(units):
                            pS, RpS = pend.pop(0)
                            if ui + LA < len(units):
                                pend.append(qk(units[ui + LA]))
                            pt, Rpt = pts.next()
                            kind, v = u
                            first = (ui == 0)
                            last = (ui == len(units) - 1)
                            if kind == "pair":
                                act(pt[:, :], pS[:, :], AF.Exp, [RpS], [Rpt])
                                for half in range(2):
                                    i = v + half
                                    mm(po[:, :], vsb[:, i, h * 128:(h + 1) * 128], pt[:, half * 512:(half + 1) * 512],
                                       first and half == 0, False, [R_vsb[i // 4], Rpt], [Rpo])
                            else:
                                i = 4 * j + v
                                c0 = 128 * v
                                act(pt[:, c0:512], pS[:, c0:512], AF.Exp, [RpS], [Rpt])
                                mm(po[:, c0:512], vsb[:, i, h * 128:(h + 1) * 128], pt[:, c0:512], first, last,
                                   [R_vsb[i // 4], Rpt], [Rpo])
                        recip(rec[64:128, :], po[64:128, :], [Rpo], [R_rec])
                        ys, Rys = yst.next()
                        tt("dve", ys[:], po[0:64, :], rec[64:128, :], ALU.mult, [Rpo, R_rec], [Rys])
                        dma("sp", ymix[256 + 64 * h: 256 + 64 * h + 64, cols], ys[:], [Rys], [R_ymB[j]])
                P.barrier()
                P.flush(final=(stop == "B"))
            if stop == "B":
                return nc

            with contextlib.ExitStack() as st:
                wout = sbt(st, "wout", [128, 8, 1024], BF16)
                wgate = sbt(st, "wgate", [128, 8, 1024], BF16)
                wproj = sbt(st, "wproj", [128, 2, 1024], BF16)
                R_wC = Res("wC")
                wu = [sbt(st, "wu%d" % i, [128, 8, 512], BF16) for i in range(2)]
                wd = [sbt(st, "wd%d" % i, [128, 4, 1024], BF16) for i in range(2)]
                R_wu = [Res("wus%d" % i) for i in range(2)]
                R_wd = [Res("wds%d" % i) for i in range(2)]
                ymt = sbt(st, "ymt", [128, 8, 512], BF16)
                R_ymt = Res("ymt")
                ht = [sbt(st, "hc%d" % i, [128, 8, 512], F32) for i in range(2)]
                R_ht = [Res("hc0"), Res("hc1")]
                ptl = [sbt(st, "ptl%d" % i, [128, 2, 512], BF16) for i in range(2)]
                R_ptl = [Res("ptl0"), Res("ptl1")]
                mb = [sbt(st, "mb%d" % i, [128, 8, 512], F32) for i in range(2)]
                R_mb = [Res("mb0"), Res("mb1")]
                hnD = [sbt(st, "hnD%d" % i, [128, 8, 512], BF16) for i in range(2)]
                R_hnD = [Res("hnD0"), Res("hnD1")]
                hnE = sbt(st, "hnE", [128, 8, 512], BF16)
                R_hnE = Res("hnE")
                sqg = sbt(st, "sqg", [128, 8, 512], BF16)
                R_sqg = Res("sqg")
                ag = sbt(st, "ag", [128, 8, 512], BF16)
                R_ag = [Res("ag0"), Res("ag1")]
                rs = sbt(st, "rs_c", [128, 512], F32)
                rstd = sbt(st, "rstd_c", [128, 512], F32)
                R_rs, R_rstd = Res("rs"), Res("rstd")
                tmpf = Ring([(sbt(st, "tmpc%d" % i, [128, 512], F32), Res("tmpc%d" % i)) for i in range(2)])
                rl = Ring([(sbt(st, "rl%d" % i, [128, 512], BF16), Res("rl%d" % i)) for i in range(2)])
                ps_stat = pst(st, "ps_stat_c")
                R_pstat = PR("ps_stat_c")
                ps_ring = Ring([(pst(st, "psm%d" % i), PR("psm%d" % i)) for i in range(3)])
                ps_dn = Ring([(pst(st, "psd%d" % i), PR("psd%d" % i)) for i in range(4)])

                for k in range(8):
                    dma("sp", wout[:, k, :], wout_bf[l * 1024 + k * 128: l * 1024 + (k + 1) * 128, :], [R_wCbf[l]], [R_wC])
                for k in range(8):
                    dma("sp", wgate[:, k, :], wgate_bf[l * 1024 + k * 128: l * 1024 + (k + 1) * 128, :], [R_wCbf[l]], [R_wC])
                for k in range(2):
                    dma("sp", wproj[:, k, :], wproj_bf[l * 256 + k * 128: l * 256 + (k + 1) * 128, :], [R_wCbf[l]], [R_wC])

                def load_wu(n):
                    if n >= NT * 8:
                        return
                    g = n % 8
                    r0 = l * 1024 + g * 128
                    dma("sp", wu[n % 2][:], wup_bf[r0:r0 + 128, :].rearrange("p (k c) -> p k c", k=8), [R_wup[l][g]], [R_wu[n % 2]])

                def load_wd(n):
                    if n >= NT * 8:
                        return
                    g = n % 8
                    r0 = l * 1024 + g * 128
                    dma("sp", wd[n % 2][:], wdn_bf[r0:r0 + 128, :].rearrange("p (k c) -> p k c", k=4), [R_wdn[l][g]], [R_wd[n % 2]])

                def sp(n):
                    for _ in range(n):
                        yield

                def gen_post(m, R_m, h, R_h_, gname):
                    act(sqg[:], m[:], AF.Square, [R_m], [R_sqg])
                    yield from sp(4)
                    rms_stats(sqg, [R_sqg], ps_stat, R_pstat, rs, rstd, R_rs, R_rstd)
                    yield from sp(2)
                    for c in range(8):
                        tb, Rtb = tmpf.next()
                        stt("dve", tb[:], m[:, c, :], scol(l, C_G[gname] + c), rstd[:], ALU.mult, ALU.mult,
                            [R_m, R_rstd, R_small], [Rtb])
                        tt("dve", h[:, c, :], h[:, c, :], tb[:], ALU.add, [R_h_, Rtb], [R_h_])
                        if c % 2 == 1:
                            yield
                    yield from sp(1)

                def gen_pre(h, R_h_, hn, R_hn, gname):
                    act(sqg[:], h[:], AF.Square, [R_h_], [R_sqg])
                    yield from sp(4)
                    rms_stats(sqg, [R_sqg], ps_stat, R_pstat, rs, rstd, R_rs, R_rstd)
                    yield from sp(2)
                    for c in range(8):
                        stt("dve", hn[:, c, :], h[:, c, :], scol(l, C_G[gname] + c), rstd[:], ALU.mult, ALU.mult,
                            [R_h_, R_rstd, R_small], [R_hn])
                        if c % 4 == 3:
                            yield
                    yield from sp(2)

                def H1(j):
                    p = j % 2
                    cols = slice(j * 512, (j + 1) * 512)
                    dma("sp", ht[p][:], h_src_v[:, :, cols], [R_h[j]], [R_ht[p]])
                    dma("pool", ptl[p][:], pT[l * 256:(l + 1) * 256, :].rearrange("(c p) n -> p c n", p=128)[:, :, cols],
                        [], [R_ptl[p]])
                    yield from sp(2)
                    for oc in range(8):
                        pp, Rpp = ps_ring.next()
                        for k in range(8):
                            mm(pp[:], wout[:, k, oc * 128:(oc + 1) * 128], ymt[:, k, :], k == 0, k == 7, [R_wC, R_ymt], [Rpp])
                        cp("dve" if oc % 2 == 0 else "act", mb[p][:, oc, :], pp[:], [Rpp], [R_mb[p]])
                        yield
                    yield from sp(2)
                    yield from gen_post(mb[p], R_mb[p], ht[p], R_ht[p], "mix_post")
                    yield from gen_pre(ht[p], R_ht[p], hnD[p], R_hnD[p], "mlp_pre")

                def H1_loads(j):
                    p = j % 2
                    cols = slice(j * 512, (j + 1) * 512)
                    dma("sp", ymt[:], ymix.rearrange("(c p) n -> p c n", p=128)[:, :, cols], [R_ymA[j], R_ymB[j]], [R_ymt])

                def H3(j):
                    p = j % 2
                    cols = slice(j * 512, (j + 1) * 512)
                    yield from gen_post(mb[p], R_mb[p], ht[p], R_ht[p], "mlp_post")
                    yield from gen_pre(ht[p], R_ht[p], hnE, R_hnE, "ple_pre")
                    for oc in range(8):
                        pp, Rpp = ps_ring.next()
                        for k in range(8):
                            mm(pp[:], wgate[:, k, oc * 128:(oc + 1) * 128], hnE[:, k, :], k == 0, k == 7, [R_wC, R_hnE], [Rpp])
                        act(sqg[:, oc, :], pp[:], AF.Sigmoid, [Rpp], [R_sqg])
                        yield
                    yield from sp(2)
                    for oc in range(8):
                        pp, Rpp = ps_ring.next()
                        for k in range(2):
                            mm(pp[:], wproj[:, k, oc * 128:(oc + 1) * 128], ptl[p][:, k, :], k == 0, k == 1, [R_wC, R_ptl[p]], [Rpp])
                        tt("dve", mb[p][:, oc, :], pp[:], sqg[:, oc, :], ALU.mult, [Rpp, R_sqg], [R_mb[p]])
                        yield
                    yield from sp(3)
                    yield from gen_post(mb[p], R_mb[p], ht[p], R_ht[p], "ple_post")
                    dma("sp", h_dst_v[:, :, cols], ht[p][:], [R_ht[p]], [R_h[j]])
                    yield

                def step(side, n):
                    for _ in range(n):
                        try:
                            next(side)
                        except StopIteration:
                            return

                def up(j, g, side):
                    n = j * 8 + g
                    p = j % 2
                    for c4 in range(4):
                        pp, Rpp = ps_ring.next()
                        for k in range(8):
                            mm(pp[:], wu[n % 2][:, k, c4 * 128:(c4 + 1) * 128], hnD[p][:, k, :], k == 0, k == 7,
                               [R_wu[n % 2], R_hnD[p]], [Rpp])
                        rr, Rrr = rl.next()
                        act(rr[:], pp[:], AF.Relu, [Rpp], [Rrr])
                        tt("dve", ag[:, (g % 2) * 4 + c4, :], rr[:], rr[:], ALU.mult, [Rrr], [R_ag[g % 2]])
                        step(side, 1)

                def down(j, g, side):
                    n = j * 8 + g
                    p = j % 2
                    for oc in range(8):
                        pd, Rpd = ps_dn.next()
                        for k4 in range(4):
                            mm(pd[:], wd[n % 2][:, k4, oc * 128:(oc + 1) * 128], ag[:, (g % 2) * 4 + k4, :], k4 == 0, k4 == 3,
                               [R_wd[n % 2], R_ag[g % 2]], [Rpd])
                        if g == 0:
                            cp("dve", mb[p][:, oc, :], pd[:], [Rpd], [R_mb[p]])
                        else:
                            tt("dve", mb[p][:, oc, :], pd[:], mb[p][:, oc, :], ALU.add, [Rpd, R_mb[p]], [R_mb[p]])
                        step(side, 1)

                def chain(*gens):
                    for gq in gens:
                        if gq is not None:
                            yield from gq

                load_wu(0)
                load_wu(1)
                load_wd(0)
                H1_loads(0)
                for _ in H1(0):
                    pass
                for j in range(NT):
                    if j + 1 < NT:
                        H1_loads(j + 1)
                    side = chain(H3(j - 1) if j > 0 else None, H1(j + 1) if j + 1 < NT else None)
                    for g in range(9):
                        n = j * 8 + g
                        if g < 8:
                            up(j, g, side)
                        if g > 0:
                            down(j, g - 1, side)
                        if g < 8:
                            load_wu(n + 2)
                            load_wd(n + 1)
                    for _ in side:
                        pass
                for _ in H3(NT - 1):
                    pass
                P.barrier()
                P.flush(final=(l == L - 1))
    return nc


POOL_WINDOWS = (2, 4, 8, 16)


def _chunkcols(v):
    return np.ascontiguousarray(v.reshape(-1, 128).T)


def prep_shared(inp, L):
    f32 = np.float32
    w_in = np.asarray(inp["w_in"], f32)[:L]
    perm = np.concatenate([np.arange(0, 1024), np.arange(1284, 2308), np.arange(1280, 1284), np.arange(1024, 1280)])
    w_in_r = np.ascontiguousarray(w_in[:, :, perm]).reshape(L * 1024, 2308)
    w_pw = np.ascontiguousarray(np.asarray(inp["w_conf_pw"], f32)[:L]).reshape(L * 256, 256)
    wp = np.asarray(inp["w_pool"], f32)[:L]
    w_pbd = np.zeros((L, 2, 128, 128), f32)
    for blk in range(2):
        for gg in range(2):
            w_pbd[:, blk, gg * 64:(gg + 1) * 64, gg * 64:(gg + 1) * 64] = wp[:, blk * 2 + gg]
    w_pbd = w_pbd.reshape(L * 256, 128)
    w_out = np.ascontiguousarray(np.asarray(inp["w_out"], f32)[:L]).reshape(L * 1024, 1024)
    wu = np.asarray(inp["w_up"], f32)[:L]
    wu = wu.reshape(L, 8, 128, 8, 512).transpose(0, 3, 2, 1, 4)
    w_up = np.ascontiguousarray(wu).reshape(L * 1024, 4096)
    wdn = np.asarray(inp["w_down"], f32)[:L]
    wdn = wdn.reshape(L, 8, 4, 128, 1024).transpose(0, 1, 3, 2, 4)
    w_dn = np.ascontiguousarray(wdn).reshape(L * 1024, 4096)
    w_gate = np.ascontiguousarray(np.asarray(inp["w_ple_gate"], f32)[:L]).reshape(L * 1024, 1024)
    w_proj = np.ascontiguousarray(np.asarray(inp["w_ple_proj"], f32)[:L]).reshape(L * 256, 1024)
    small = np.zeros((128, L * 128), f32)
    for l in range(L):
        o = l * 128
        for name, key in (("mix_pre", "g_mix_pre"), ("mix_post", "g_mix_post"), ("mlp_pre", "g_mlp_pre"),
                          ("mlp_post", "g_mlp_post"), ("ple_pre", "g_ple_pre"), ("ple_post", "g_ple_post")):
            small[:, o + C_G[name]: o + C_G[name] + 8] = _chunkcols(np.asarray(inp[key], f32)[l])
        small[:, o + C_LNG: o + C_LNG + 2] = _chunkcols(np.asarray(inp["conf_ln_g"], f32)[l])
        small[:, o + C_LNB: o + C_LNB + 2] = _chunkcols(np.asarray(inp["conf_ln_b"], f32)[l])
        small[:, o + C_PSC: o + C_PSC + 2] = _chunkcols(np.asarray(inp["pool_scale"], f32)[l])
        dw = np.asarray(inp["w_conf_dw"], f32)[l]
        for blk in range(2):
            small[:, o + C_DW + blk * 31: o + C_DW + blk * 31 + 31] = dw[:, blk * 128:(blk + 1) * 128].T
        sc = np.asarray(inp["w_sc"], f32)[l]
        for blk in range(2):
            small[:, o + C_SC + blk * 3: o + C_SC + blk * 3 + 3] = sc[:, blk * 128:(blk + 1) * 128].T
        small[0:4, o + C_BF] = np.asarray(inp["b_forget"], f32)[l]
    consts = np.zeros((128, 320), f32)
    consts[:, K_ID:K_ID + 128] = np.eye(128, dtype=f32)
    kk, qq = np.meshgrid(np.arange(128), np.arange(128), indexing="ij")
    consts[:, K_MASK:K_MASK + 128] = np.where(kk > qq, NEG, 0.0)
    for blk in range(2):
        for p in range(128):
            w = POOL_WINDOWS[(blk * 128 + p) // 64]
            for t in range(16):
                consts[p, K_PCOEF + blk * 16 + t] = (1.0 / w if t < w else 0.0) - (1.0 if t == 0 else 0.0)
                consts[p, K_PRATIO + blk * 16 + t] = w / min(t + 1, w)
    return dict(w_in=w_in_r, w_pw=w_pw, w_pbd=w_pbd, w_out=w_out, w_up=w_up, w_dn=w_dn, w_gate=w_gate,
                w_proj=w_proj, small=small, consts=consts)


_NC_CACHE = {}
ACTIVE_CORES = [0, 1, 4, 5]


def run(inp, T, L, n_seq, stop=None, dbg=False):
    f32 = np.float32
    shared = prep_shared(inp, L)
    x = np.asarray(inp["x"], f32)
    p = np.asarray(inp["p"], f32)
    active = ACTIVE_CORES[:n_seq]
    zero_map = None
    in_maps = []
    for core in range(8):
        if core in active:
            bi = active.index(core)
            m = dict(shared)
            m["xT"] = np.ascontiguousarray(x[bi].T)
            m["pT"] = np.ascontiguousarray(p[:L, bi].transpose(0, 2, 1)).reshape(L * 256, T)
        else:
            if zero_map is None:
                zero_map = {k: np.zeros_like(v) for k, v in shared.items()}
                zero_map["xT"] = np.zeros((1024, T), f32)
                zero_map["pT"] = np.zeros((L * 256, T), f32)
            m = zero_map
        in_maps.append(m)
    key = (T, L, stop, dbg)
    if key not in _NC_CACHE:
        _NC_CACHE[key] = build(T, L, stop, dbg)
    nc = _NC_CACHE[key]
    res = run_bass_kernel_spmd(nc, in_maps, core_ids=list(range(8)))
    if dbg:
        return res.results[active[0]]
    out = np.stack([np.ascontiguousarray(res.results[active[bi]]["outT"].T) for bi in range(n_seq)], axis=0)
    return out.astype(f32)


def kernel(**inputs):
    return run(inputs, 8192, 4, 4)
```

```python
import contextlib
import numpy as np
import concourse.bass as bass
import concourse.mybir as mybir
from concourse.bass_utils import run_bass_kernel_spmd

F32 = mybir.dt.float32
BF16 = mybir.dt.bfloat16
ALU = mybir.AluOpType
AF = mybir.ActivationFunctionType
ENGS = ("sp", "act", "pe", "dve", "pool")
EPS = 1e-6
NEG = -30000.0


class Res:
    __slots__ = ("name", "w", "r", "excl")

    def __init__(self, name="", excl=False):
        self.name = name
        self.w = {}
        self.r = {}
        self.excl = excl


def PR(name):
    return Res(name, excl=True)


def _merge(d, s, v):
    if d.get(s, 0) < v:
        d[s] = v


class Prog:
    def __init__(self, nc, stack, n_dma_sems=32):
        self.nc = nc
        self.q = {e: [] for e in ENGS}
        self.sem_names = []
        self.sem_count = []
        self.seen = {e: {} for e in ENGS}
        self.pending_reads = {e: [] for e in ENGS}
        self.pending_writes = {e: [] for e in ENGS}
        self.eng_sem = {}
        for e in ("act", "pe", "dve", "pool"):
            self.eng_sem[e] = self._new_sem("c_" + e)
        self.dma_sems = {"sp": [self._new_sem("d%d" % i) for i in range(n_dma_sems)],
                         "pool": [self._new_sem("w%d" % i) for i in range(16)]}
        self.dma_rr = {"sp": 0, "pool": 0}
        self.n_ops = 0
        self.sems = [stack.enter_context(nc.semaphore(n)) for n in self.sem_names]

    def _new_sem(self, name):
        self.sem_names.append(name)
        self.sem_count.append(0)
        return len(self.sem_names) - 1

    def op(self, eng, fn, reads=(), writes=(), inc=True, dma=False):
        waits = {}
        for r in reads:
            for s, v in r.w.items():
                _merge(waits, s, v)
            if r.excl:
                for s, v in r.r.items():
                    if not (eng in self.eng_sem and s == self.eng_sem[eng]):
                        _merge(waits, s, v)
        for w in writes:
            for s, v in w.w.items():
                _merge(waits, s, v)
            for s, v in w.r.items():
                _merge(waits, s, v)
        tok = None
        if dma:
            pool_ = self.dma_sems[eng]
            s = pool_[self.dma_rr[eng] % len(pool_)]
            self.dma_rr[eng] += 1
            if self.sem_count[s] > 0:
                _merge(waits, s, self.sem_count[s])
            self.sem_count[s] += 16
            tok = (s, self.sem_count[s], 16)
        elif inc:
            s = self.eng_sem[eng]
            self.sem_count[s] += 1
            tok = (s, self.sem_count[s], 1)
        seen = self.seen[eng]
        wl = []
        for s, v in waits.items():
            if seen.get(s, 0) < v:
                if eng == "pe" and s == self.eng_sem["pe"]:
                    continue
                seen[s] = v
                wl.append((s, v))
        self.q[eng].append((fn, wl, tok))
        self.n_ops += 1
        if tok is None:
            self.pending_reads[eng].extend(reads)
            self.pending_writes[eng].extend(writes)
        else:
            s, v, _ = tok
            rl = list(reads)
            wr = list(writes)
            if not dma:
                rl += self.pending_reads[eng]
                wr += self.pending_writes[eng]
                self.pending_reads[eng] = []
                self.pending_writes[eng] = []
            for r in rl:
                _merge(r.r, s, v)
            for w in wr:
                _merge(w.w, s, v)
        return tok

    def barrier(self):
        for e in ENGS:
            wl = []
            for s, c in enumerate(self.sem_count):
                if c > 0 and self.seen[e].get(s, 0) < c:
                    self.seen[e][s] = c
                    wl.append((s, c))
            if wl:
                self.q[e].append((None, wl, None))

    def flush(self, final=False):
        nc = self.nc
        sems = self.sems
        if final:
            wl = [(s, c) for s, c in enumerate(self.sem_count) if c > 0]
            self.q["sp"].append((None, wl, None))
        with nc.Block() as block:
            def run(eng_name):
                ops = self.q[eng_name]

                def f(e):
                    for fn, wl, tok in ops:
                        for s, v in wl:
                            e.wait_ge(sems[s], v)
                        if fn is None:
                            continue
                        ins = fn(e)
                        if tok is not None:
                            ins.then_inc(sems[tok[0]], tok[2])
                return f
            block.sync(run("sp"))
            block.scalar(run("act"))
            block.tensor(run("pe"))
            block.vector(run("dve"))
            block.gpsimd(run("pool"))
        self.q = {e: [] for e in ENGS}


class Ring:
    def __init__(self, items):
        self.items = items
        self.i = 0

    def next(self):
        it = self.items[self.i % len(self.items)]
        self.i += 1
        return it


C_G = {"mix_pre": 0, "mix_post": 8, "mlp_pre": 16, "mlp_post": 24, "ple_pre": 32, "ple_post": 40}
C_LNG, C_LNB, C_PSC, C_DW, C_SC, C_BF = 48, 50, 52, 54, 116, 122
K_ID, K_MASK, K_PCOEF, K_PRATIO = 0, 128, 256, 288


def build(T, L, stop=None, dbg=False):
    NT = T // 512
    NB = T // 128
    nc = bass.Bass("TRN2", target_bir_lowering=False)

    def din(name, shape, dt=F32):
        return nc.dram_tensor(name, shape, dt, kind="ExternalInput").ap()

    def dscr(name, shape, dt):
        return nc.dram_tensor(name, shape, dt, kind="ExternalOutput" if dbg else "Internal").ap()

    xT = din("xT", [1024, T])
    pT = din("pT", [L * 256, T])
    w_in = din("w_in", [L * 1024, 2308])
    w_pw = din("w_pw", [L * 256, 256])
    w_pbd = din("w_pbd", [L * 256, 128])
    w_out = din("w_out", [L * 1024, 1024])
    w_up = din("w_up", [L * 1024, 4096])
    w_dn = din("w_dn", [L * 1024, 4096])
    w_gate = din("w_gate", [L * 1024, 1024])
    w_proj = din("w_proj", [L * 256, 1024])
    small = din("small", [128, L * 128])
    consts = din("consts", [128, 320])
    outT = nc.dram_tensor("outT", [1024, T], F32, kind="ExternalOutput").ap()

    hT = dscr("hT", [1024, T], F32)
    wup_bf = dscr("wup_bf", [L * 1024, 4096], BF16)
    wdn_bf = dscr("wdn_bf", [L * 1024, 4096], BF16)
    win_bf = dscr("win_bf", [L * 1024, 2308], BF16)
    wout_bf = dscr("wout_bf", [L * 1024, 1024], BF16)
    wgate_bf = dscr("wgate_bf", [L * 1024, 1024], BF16)
    wproj_bf = dscr("wproj_bf", [L * 256, 1024], BF16)
    mixin = dscr("mixin", [1024, 32 + T], BF16)
    qT = dscr("qT", [256, T], BF16)
    kT = dscr("kT", [256, T], BF16)
    c3d = dscr("c3d", [4, 3, T], BF16)
    vaug = dscr("vaug", [NB, 128, 512], BF16)
    ymix = dscr("ymix", [1024, T], BF16)

    R_h = [Res("h%d" % j) for j in range(NT)]
    R_mixin = [Res("mi%d" % j) for j in range(NT + 1)]
    R_q = [Res("q%d" % j) for j in range(NT)]
    R_k = [Res("k%d" % j) for j in range(NT)]
    R_c3 = [Res("c3%d" % j) for j in range(NT)]
    R_v = [Res("v%d" % j) for j in range(NT)]
    R_ymA = [Res("ymA%d" % j) for j in range(NT)]
    R_ymB = [Res("ymB%d" % j) for j in range(NT)]
    R_winbf = [Res("winbf%d" % l) for l in range(L)]
    R_wCbf = [Res("wCbf%d" % l) for l in range(L)]
    R_wup = [[Res("wu%d_%d" % (l, g)) for g in range(8)] for l in range(L)]
    R_wdn = [[Res("wd%d_%d" % (l, g)) for g in range(8)] for l in range(L)]

    top = contextlib.ExitStack()
    with top:
        P = Prog(nc, top)

        uid = [0]

        def sbt(st, name, shape, dt):
            uid[0] += 1
            return st.enter_context(nc.sbuf_tensor("%s_u%d" % (name, uid[0]), shape, dt))

        def pst(st, name, shape=(128, 512), dt=F32):
            uid[0] += 1
            return st.enter_context(nc.psum_tensor("%s_u%d" % (name, uid[0]), list(shape), dt))

        small_sb = sbt(top, "small_sb", [128, L * 128], F32)
        consts_sb = sbt(top, "consts_sb", [128, 320], F32)
        ident_bf = sbt(top, "ident_bf", [128, 128], BF16)
        mask_bf = sbt(top, "mask_bf", [128, 128], BF16)
        ones_bf = sbt(top, "ones_bf", [128, 128], BF16)
        ones_f = sbt(top, "ones_f", [128, 128], F32)
        cneg = sbt(top, "cneg", [128, NB, 4], F32)
        R_small, R_consts, R_cst2, R_cneg = Res("small"), Res("consts"), Res("cst2"), Res("cneg")
        ident_f = consts_sb[:, K_ID:K_ID + 128]

        def scol(l, c, rows=slice(0, 128)):
            return small_sb[rows, l * 128 + c: l * 128 + c + 1]

        def dma(eng, out, in_, reads, writes):
            P.op(eng, lambda e: e.dma_start(out=out, in_=in_), reads=reads, writes=writes, dma=True)

        def mm(out, lhsT, rhs, start, stop, reads, writes):
            P.op("pe", lambda e: e.matmul(out, lhsT=lhsT, rhs=rhs, start=start, stop=stop),
                 reads=reads, writes=writes, inc=stop)

        def act(out, in_, func, reads, writes, bias=None, scale=None):
            kw = {}
            if bias is not None:
                kw["bias"] = bias
            if scale is not None:
                kw["scale"] = scale
            P.op("act", lambda e: e.activation(out, in_, func, **kw), reads=reads, writes=writes)

        def tt(eng, out, in0, in1, op, reads, writes):
            P.op(eng, lambda e: e.tensor_tensor(out, in0, in1, op), reads=reads, writes=writes)

        def stt(eng, out, in0, scalar, in1, op0, op1, reads, writes):
            P.op(eng, lambda e: e.scalar_tensor_tensor(out, in0, scalar, in1, op0, op1), reads=reads, writes=writes)

        def norm_scale(eng, out, in0, gcol, rstd_ap, reads, writes, ptmp_ring):
            if eng == "dve":
                stt("dve", out, in0, gcol, rstd_ap, ALU.mult, ALU.mult, reads, writes)
            else:
                tb, Rtb = ptmp_ring.next()
                tt("pool", tb[:], in0, rstd_ap, ALU.mult, reads, [Rtb])
                P.op("pool", lambda e: e.tensor_scalar_mul(out, tb[:], gcol), reads=[Rtb] + list(reads), writes=writes)

        def ts(eng, out, in0, s1, s2, op0, op1, reads, writes):
            if s2 is None:
                P.op(eng, lambda e: e.tensor_single_scalar(out, in0, s1, op0), reads=reads, writes=writes)
            else:
                P.op(eng, lambda e: e.tensor_scalar(out, in0, s1, s2, op0, op1), reads=reads, writes=writes)

        def cp(eng, out, in_, reads, writes):
            if eng == "act":
                P.op("act", lambda e: e.copy(out, in_), reads=reads, writes=writes)
            else:
                P.op(eng, lambda e: e.tensor_copy(out, in_), reads=reads, writes=writes)

        def memset(eng, ap, val, writes):
            P.op(eng, lambda e: e.memset(ap, val), writes=writes)

        def recip(out, in_, reads, writes):
            P.op("dve", lambda e: e.reciprocal(out, in_), reads=reads, writes=writes)

        def rms_stats(sq, R_sq, ps_stat, R_ps, rs, rstd, R_rs, R_rstd):
            for c in range(8):
                mm(ps_stat[:], ones_bf[:], sq[:, c, :], c == 0, c == 7, list(R_sq) + [R_cst2], [R_ps])
            act(rs[:], ps_stat[:], AF.Sqrt, [R_ps], [R_rs], bias=EPS, scale=1.0 / 1024.0)
            recip(rstd[:], rs[:], [R_rs], [R_rstd])

        with contextlib.ExitStack() as st:
            zt = sbt(st, "zt", [128, 8, 32], BF16)
            R_zt = Res("zt")
            dma("sp", small_sb[:], small, [], [R_small])
            dma("sp", consts_sb[:], consts, [], [R_consts])
            cp("dve", ident_bf[:], consts_sb[:, K_ID:K_ID + 128], [R_consts], [R_cst2])
            cp("dve", mask_bf[:], consts_sb[:, K_MASK:K_MASK + 128], [R_consts], [R_cst2])
            memset("pool", ones_bf[:], 1.0, [R_cst2])
            memset("pool", ones_f[:], 1.0, [R_cst2])
            memset("pool", zt[:], 0.0, [R_zt])
            dma("sp", mixin.rearrange("(c p) n -> p c n", p=128)[:, :, 0:32], zt[:], [R_zt], [R_mixin[0]])
            P.barrier()
            P.flush(final=(stop == "pro"))
        if stop == "pro":
            return nc

        def cast_small_weights(l):
            for k in range(8):
                r0 = l * 1024 + k * 128
                for hh in range(2):
                    dma("pool", win_bf[r0:r0 + 128, hh * 1154:(hh + 1) * 1154], w_in[r0:r0 + 128, hh * 1154:(hh + 1) * 1154],
                        [], [R_winbf[l]])
            for k in range(8):
                r0 = l * 1024 + k * 128
                dma("pool", wout_bf[r0:r0 + 128, :], w_out[r0:r0 + 128, :], [], [R_wCbf[l]])
                dma("pool", wgate_bf[r0:r0 + 128, :], w_gate[r0:r0 + 128, :], [], [R_wCbf[l]])
            for k in range(2):
                r0 = l * 256 + k * 128
                dma("pool", wproj_bf[r0:r0 + 128, :], w_proj[r0:r0 + 128, :], [], [R_wCbf[l]])

        def cast_mlp_weights(l):
            for g in range(8):
                for (src, dst, RR) in ((w_up, wup_bf, R_wup), (w_dn, wdn_bf, R_wdn)):
                    r0 = l * 1024 + g * 128
                    for hh in range(2):
                        dma("pool", dst[r0:r0 + 128, hh * 2048:(hh + 1) * 2048],
                            src[r0:r0 + 128, hh * 2048:(hh + 1) * 2048], [], [RR[l][g]])

        cast_small_weights(0)
        for l in range(L):
            h_src = xT if l == 0 else hT
            h_dst = outT if l == L - 1 else hT
            h_src_v = h_src.rearrange("(c p) n -> p c n", p=128)
            h_dst_v = h_dst.rearrange("(c p) n -> p c n", p=128)

            with contextlib.ExitStack() as st:
                win = sbt(st, "win", [128, 8, 2308], BF16)
                R_win = Res("win")
                ht = [sbt(st, "ht%d" % i, [128, 8, 512], F32) for i in range(2)]
                R_ht = [Res("ht0"), Res("ht1")]
                sq = sbt(st, "sq", [128, 8, 512], BF16)
                R_sq = Res("sq")
                xn2 = [sbt(st, "xn%d" % i, [128, 8, 512], BF16) for i in range(2)]
                R_xn2 = [Res("xn0"), Res("xn1")]
                rs = sbt(st, "rs", [128, 512], F32)
                rstd = sbt(st, "rstd", [128, 512], F32)
                R_rs, R_rstd = Res("rs"), Res("rstd")
                tmpf = [sbt(st, "tmpf%d" % i, [128, 512], F32) for i in range(2)]
                tmp_ring = Ring([(tmpf[i], Res("tmpf%d" % i)) for i in range(2)])
                ptmp_ring = Ring([(sbt(st, "ptmp%d" % i, [128, 512], F32), Res("ptmp%d" % i)) for i in range(2)])
                mstage = [sbt(st, "mstage%d" % i, [128, 8, 512], BF16) for i in range(2)]
                R_mst = [Res("mst0"), Res("mst1")]
                qstage = [sbt(st, "qstage%d" % i, [128, 2, 512], BF16) for i in range(2)]
                R_qst = [Res("qst0"), Res("qst1")]
                kstage = [sbt(st, "kstage%d" % i, [128, 2, 512], BF16) for i in range(2)]
                R_kst = [Res("kst0"), Res("kst1")]
                vstage = [sbt(st, "vstage%d" % i, [128, 4, 512], BF16) for i in range(2)]
                R_vst = [Res("vst0"), Res("vst1")]
                xb = sbt(st, "xb", [4, 512], F32)
                ef = sbt(st, "ef", [4, 512], F32)
                lf = sbt(st, "lf", [4, 512], F32)
                ones4 = sbt(st, "ones4", [4, 512], F32)
                cc = [sbt(st, "cc%d" % i, [4, 512], F32) for i in range(2)]
                r1 = sbt(st, "r1", [4, 512], F32)
                r2 = sbt(st, "r2", [4, 512], F32)
                c3 = [sbt(st, "c3_%d" % i, [4, 3, 512], BF16) for i in range(2)]
                R_f = Res("fmisc")
                R_cc = [Res("cc0"), Res("cc1")]
                R_c3s = [Res("c3s0"), Res("c3s1")]
                ps_stat = pst(st, "ps_stat")
                R_pstat = PR("ps_stat")
                ps_ring = Ring([(pst(st, "psr%d" % i), PR("psr%d" % i)) for i in range(5)])
                ps_v = Ring([(pst(st, "psv%d" % i), PR("psv%d" % i)) for i in range(2)])

                for k in range(8):
                    dma("sp", win[:, k, :], win_bf[l * 1024 + k * 128: l * 1024 + (k + 1) * 128, :], [R_winbf[l]], [R_win])
                cast_mlp_weights(l)
                if l + 1 < L:
                    cast_small_weights(l + 1)
                memset("pool", ones4[:], 1.0, [R_f])
                for i in range(2):
                    memset("pool", vstage[i][:], 1.0, [R_vst[i]])

                def load_h(j):
                    dma("sp", ht[j % 2][:], h_src_v[:, :, j * 512:(j + 1) * 512], [R_h[j]], [R_ht[j % 2]])

                cur = {}

                def proj(ps, R_ps, col0, M):
                    xn, R_xn = cur["xn"], cur["R_xn"]
                    for k in range(8):
                        mm(ps[0:M, :], win[:, k, col0:col0 + M], xn[:, k, :], k == 0, k == 7, [R_win, R_xn], [R_ps])

                def norm_a(j):
                    bb = j % 2
                    act(sq[:], ht[bb][:], AF.Square, [R_ht[bb]], [R_sq])

                def norm_b(j):
                    bb = j % 2
                    rms_stats(sq, [R_sq], ps_stat, R_pstat, rs, rstd, R_rs, R_rstd)
                    for c in range(8):
                        norm_scale("dve", xn2[bb][:, c, :], ht[bb][:, c, :], scol(l, C_G["mix_pre"] + c),
                                   rstd[:], [R_ht[bb], R_rstd, R_small], [R_xn2[bb]], ptmp_ring)

                load_h(0)
                if NT > 1:
                    load_h(1)
                norm_a(0)
                norm_b(0)
                for j in range(NT):
                    b = j % 2
                    h = ht[b]
                    cols = slice(j * 512, (j + 1) * 512)
                    xnb, R_xnb = xn2[b], R_xn2[b]
                    xn, R_xn = xnb, R_xnb
                    cur["xn"], cur["R_xn"] = xnb, R_xnb
                    pf, Rpf = ps_ring.next()
                    proj(pf, Rpf, 2048, 4)
                    ts("dve", xb[:], pf[0:4, :], scol(l, C_BF, slice(0, 4)), None, ALU.add, None, [Rpf, R_small], [R_f])
                    act(ef[:], xb[:], AF.Exp, [R_f], [R_f], scale=-1.0)
                    act(lf[:], ef[:], AF.Ln, [R_f], [R_f], bias=1.0, scale=1.0)
                    init = 0.0 if j == 0 else cc[1 - b][:, 511:512]
                    P.op("dve", (lambda o, d0, d1, ini: (lambda e: e.tensor_tensor_scan(o, d0, d1, ini, ALU.mult, ALU.subtract)))(
                        cc[b][:], ones4[:], lf[:], init), reads=[R_f, R_cc[1 - b]], writes=[R_cc[b]])
                    if j + 1 < NT:
                        norm_a(j + 1)
                    for blk in range(2):
                        pa, Rpa = ps_ring.next()
                        pb, Rpb = ps_ring.next()
                        proj(pa, Rpa, (0 + blk) * 128, 128)
                        proj(pb, Rpb, (2 + blk) * 128, 128)
                        tb, Rtb = tmp_ring.next()
                        act(tb[:], pb[:], AF.Sigmoid, [Rpb], [Rtb])
                        tt("dve", mstage[b][:, 0 + blk, :], pa[:], tb[:], ALU.mult, [Rpa, Rtb], [R_mst[b]])
                    cp("dve", c3[b][:, 0, :], cc[b][:], [R_cc[b]], [R_c3s[b]])
                    tt("dve", r1[:], cc[b][:], c3[b][:, 0, :], ALU.subtract, [R_cc[b], R_c3s[b]], [R_f])
                    cp("dve", c3[b][:, 1, :], r1[:], [R_f], [R_c3s[b]])
                    tt("dve", r2[:], r1[:], c3[b][:, 1, :], ALU.subtract, [R_f, R_c3s[b]], [R_f])
                    cp("dve", c3[b][:, 2, :], r2[:], [R_f], [R_c3s[b]])
                    for blk in range(2):
                        pa, Rpa = ps_ring.next()
                        pb, Rpb = ps_ring.next()
                        proj(pa, Rpa, (8 + blk) * 128, 128)
                        proj(pb, Rpb, (12 + blk) * 128, 128)
                        tb, Rtb = tmp_ring.next()
                        cp("act", tb[:], pb[:], [Rpb], [Rtb])
                        tt("dve", mstage[b][:, 2 + blk, :], pa[:], tb[:], ALU.mult, [Rpa, Rtb], [R_mst[b]])
                    if j + 1 < NT:
                        norm_b(j + 1)
                        if j + 2 < NT:
                            load_h(j + 2)
                    for blk in range(2):
                        pa, Rpa = ps_ring.next()
                        proj(pa, Rpa, (14 + blk) * 128, 128)
                        cp("dve", mstage[b][:, 4 + blk, :], pa[:], [Rpa], [R_mst[b]])
                        pb, Rpb = ps_ring.next()
                        proj(pb, Rpb, (10 + blk) * 128, 128)
                        cp("act", mstage[b][:, 6 + blk, :], pb[:], [Rpb], [R_mst[b]])
                    for blk in range(2):
                        pa, Rpa = ps_ring.next()
                        proj(pa, Rpa, (4 + blk) * 128, 128)
                        act(qstage[b][:, blk, :], pa[:], AF.Copy, [Rpa], [R_qst[b]], scale=0.125)
                        pb, Rpb = ps_ring.next()
                        proj(pb, Rpb, (6 + blk) * 128, 128)
                        cp("dve", kstage[b][:, blk, :], pb[:], [Rpb], [R_kst[b]])
                    for s in range(4):
                        pv, Rpv = ps_v.next()
                        for k in range(8):
                            mm(pv[:, 0:256], xn[:, k, s * 128:(s + 1) * 128], win[:, k, 2052:2308], k == 0, k == 7,
                               [R_win, R_xn], [Rpv])
                        dst = vstage[b][:, s, :].rearrange("p (h c) -> p h c", h=4)[:, :, 0:64]
                        src = pv[:, 0:256].rearrange("p (h c) -> p h c", h=4)
                        cp("dve" if s % 2 == 0 else "act", dst, src, [Rpv], [R_vst[b]])
                    dma("sp", mixin.rearrange("(c p) n -> p c n", p=128)[:, :, 32 + j * 512: 32 + (j + 1) * 512],
                        mstage[b][:], [R_mst[b]], [R_mixin[j + 1]])
                    dma("sp", qT.rearrange("(c p) n -> p c n", p=128)[:, :, cols], qstage[b][:], [R_qst[b]], [R_q[j]])
                    dma("sp", kT.rearrange("(c p) n -> p c n", p=128)[:, :, cols], kstage[b][:], [R_kst[b]], [R_k[j]])
                    dma("sp", c3d[:, :, cols], c3[b][:], [R_c3s[b]], [R_c3[j]])
                    dma("sp", vaug[j * 4:(j + 1) * 4].rearrange("s p c -> p s c"), vstage[b][:], [R_vst[b]], [R_v[j]])
                P.barrier()
                P.flush(final=(stop == "A1"))
            if stop == "A1":
                return nc

            with contextlib.ExitStack() as st:
                dconf = sbt(st, "dconf", [128, 2, 31, 128], BF16)
                dsc = sbt(st, "dsc", [128, 2, 3, 128], BF16)
                dpl = sbt(st, "dpl", [128, 2, 16, 128], BF16)
                R_dg = Res("diag")
                wpw = sbt(st, "wpw", [128, 2, 256], BF16)
                wpbd = sbt(st, "wpbd", [128, 2, 128], BF16)
                R_wA2 = Res("wA2")
                mt = [sbt(st, "mt%d" % i, [128, 8, 544], BF16) for i in range(2)]
                R_mt = [Res("mt0"), Res("mt1")]
                xc = sbt(st, "xc", [128, 2, 512], F32)
                sq2 = sbt(st, "sq2", [128, 2, 512], F32)
                R_xc, R_sq2 = Res("xc"), Res("sq2")
                mean = sbt(st, "mean", [128, 512], F32)
                msq = sbt(st, "msq", [128, 512], F32)
                var = sbt(st, "var", [128, 512], F32)
                sd = sbt(st, "sd", [128, 512], F32)
                rstd2 = sbt(st, "rstd2", [128, 512], F32)
                R_mean, R_msq, R_var, R_sd, R_rstd2 = Res("mean"), Res("msq"), Res("var"), Res("sd"), Res("rstd2")
                xm = [sbt(st, "xm%d" % i, [128, 512], F32) for i in range(2)]
                R_xm = [Res("xm0"), Res("xm1")]
                xnn = [sbt(st, "xnn%d" % i, [128, 512], F32) for i in range(2)]
                R_xnn = [Res("xnn0"), Res("xnn1")]
                sconf = sbt(st, "sconf", [128, 2, 512], BF16)
                R_sconf = Res("sconf")
                dpool = sbt(st, "dpool", [128, 2, 512], BF16)
                R_dpool = Res("dpool")
                t16 = sbt(st, "t16", [128, 2, 16], F32)
                R_t16 = Res("t16")
                ystage = [sbt(st, "ystage%d" % i, [128, 6, 512], BF16) for i in range(2)]
                R_yst = [Res("yst0"), Res("yst1")]
                psc = [pst(st, "psc%d" % i) for i in range(2)]
                R_psc = [PR("psc0"), PR("psc1")]
                ps_s1 = pst(st, "ps_s1")
                ps_s2 = pst(st, "ps_s2")
                R_s1, R_s2 = PR("s1"), PR("s2")
                ps_ring = Ring([(pst(st, "psq%d" % i), PR("psq%d" % i)) for i in range(3)])

                dma("pool", wpw[:], w_pw[l * 256:(l + 1) * 256, :].rearrange("(c p) n -> p c n", p=128), [], [R_wA2])
                dma("pool", wpbd[:], w_pbd[l * 256:(l + 1) * 256, :].rearrange("(c p) n -> p c n", p=128), [], [R_wA2])
                for blk in range(2):
                    for k in range(31):
                        P.op("pool", (lambda o, s: (lambda e: e.tensor_scalar_mul(o, ident_f, s)))(
                            dconf[:, blk, k, :], scol(l, C_DW + blk * 31 + k)), reads=[R_consts, R_small], writes=[R_dg])
                    for k in range(3):
                        P.op("pool", (lambda o, s: (lambda e: e.tensor_scalar_mul(o, ident_f, s)))(
                            dsc[:, blk, k, :], scol(l, C_SC + blk * 3 + k)), reads=[R_consts, R_small], writes=[R_dg])
                    for k in range(16):
                        P.op("pool", (lambda o, s: (lambda e: e.tensor_scalar_mul(o, ident_f, s)))(
                            dpl[:, blk, k, :], consts_sb[:, K_PCOEF + blk * 16 + k: K_PCOEF + blk * 16 + k + 1]),
                            reads=[R_consts], writes=[R_dg])

                def load_mt(j):
                    dma("sp", mt[j % 2][:], mixin.rearrange("(c p) n -> p c n", p=128)[:, :, j * 512: j * 512 + 544],
                        [R_mixin[j], R_mixin[j + 1]], [R_mt[j % 2]])

                def pw_store(jj):
                    bb = jj % 2
                    cols_ = slice(jj * 512, (jj + 1) * 512)
                    for oc in range(2):
                        pp, Rpp = ps_ring.next()
                        for kb in range(2):
                            mm(pp[:], wpw[:, kb, oc * 128:(oc + 1) * 128], sconf[:, kb, :], kb == 0, kb == 1,
                               [R_wA2, R_sconf], [Rpp])
                        cp("dve", ystage[bb][:, oc, :], pp[:], [Rpp], [R_yst[bb]])
                    ym_v = ymix.rearrange("(c p) n -> p c n", p=128)
                    dma("sp", ym_v[:, 0:2, cols_], ystage[bb][:, 0:2, :], [R_yst[bb]], [R_ymA[jj]])
                    dma("sp", ym_v[:, 4:8, cols_], ystage[bb][:, 2:6, :], [R_yst[bb]], [R_ymA[jj]])

                load_mt(0)
                sect = stop.split(":")[1] if (stop and ":" in stop) else "all"

                def on(*names):
                    return sect == "all" or sect in names

                for j in range(NT):
                    b = j % 2
                    if j + 1 < NT:
                        load_mt(j + 1)
                    m_ = mt[b]
                    cols = slice(j * 512, (j + 1) * 512)
                    if sect == "conv1":
                        for blk in range(2):
                            for k in range(31):
                                mm(psc[blk][:], dconf[:, blk, k, :], m_[:, blk, 2 + k: 2 + k + 512], k == 0, k == 30,
                                   [R_dg, R_mt[b]], [R_psc[blk]])
                            cp("dve", xc[:, blk, :], psc[blk][:], [R_psc[blk]], [R_xc])
                    if sect == "conv2":
                        for blk in range(2):
                            for k in range(3):
                                mm(psc[blk][:], dconf[:, blk, k, :], m_[:, blk, 2 + k: 2 + k + 512], k == 0, k == 2,
                                   [R_dg, R_mt[b]], [R_psc[blk]])
                            act(sq2[:, blk, :], psc[blk][:], AF.Square, [R_psc[blk]], [R_sq2])
                    if sect == "pool1":
                        for blk in range(2):
                            pp, Rpp = ps_ring.next()
                            for k in range(16):
                                mm(pp[:], dpl[:, blk, k, :], m_[:, 4 + blk, 32 - k: 32 - k + 512], k == 0, k == 15,
                                   [R_dg, R_mt[b]], [Rpp])
                            cp("dve", dpool[:, blk, :], pp[:], [Rpp], [R_dpool])
                    if on("conv", "ln", "silu", "conf"):
                        for blk in range(2):
                            for k in range(31):
                                mm(psc[blk][:], dconf[:, blk, k, :], m_[:, blk, 2 + k: 2 + k + 512], k == 0, k == 30,
                                   [R_dg, R_mt[b]], [R_psc[blk]])
                            cp("dve", xc[:, blk, :], psc[blk][:], [R_psc[blk]], [R_xc])
                            act(sq2[:, blk, :], xc[:, blk, :], AF.Square, [R_xc], [R_sq2])
                    if on("sc"):
                        for blk in range(2):
                            pp, Rpp = ps_ring.next()
                            for k in range(3):
                                mm(pp[:], dsc[:, blk, k, :], m_[:, 2 + blk, 30 + k: 30 + k + 512], k == 0, k == 2,
                                   [R_dg, R_mt[b]], [Rpp])
                            tt("dve", ystage[b][:, 2 + blk, :], pp[:], m_[:, 6 + blk, 32:544], ALU.mult, [Rpp, R_mt[b]], [R_yst[b]])
                    if on("pool"):
                        for blk in range(2):
                            pp, Rpp = ps_ring.next()
                            for k in range(16):
                                mm(pp[:], dpl[:, blk, k, :], m_[:, 4 + blk, 32 - k: 32 - k + 512], k == 0, k == 15,
                                   [R_dg, R_mt[b]], [Rpp])
                            cp("act", dpool[:, blk, :], pp[:], [Rpp], [R_dpool])
                            if j == 0:
                                tt("dve", t16[:, blk, :], pp[:, 0:16], m_[:, 4 + blk, 32:48], ALU.add, [Rpp, R_mt[b]], [R_t16])
                                tt("pool", t16[:, blk, :], t16[:, blk, :],
                                   consts_sb[:, K_PRATIO + blk * 16: K_PRATIO + blk * 16 + 16], ALU.mult, [R_t16, R_consts], [R_t16])
                                tt("dve", dpool[:, blk, 0:16], t16[:, blk, :], m_[:, 4 + blk, 32:48], ALU.subtract,
                                   [R_t16, R_mt[b]], [R_dpool])
                        for blk in range(2):
                            pp, Rpp = ps_ring.next()
                            mm(pp[:], wpbd[:, blk, :], dpool[:, blk, :], True, True, [R_wA2, R_dpool], [Rpp])
                            act(ystage[b][:, 4 + blk, :], pp[:], AF.Copy, [Rpp, R_small], [R_yst[b]], scale=scol(l, C_PSC + blk))
                    if j > 0:
                        pw_store(j - 1)
                    if on("ln", "silu", "conf"):
                        for blk in range(2):
                            mm(ps_s1[:], ones_f[:], xc[:, blk, :], blk == 0, blk == 1, [R_xc, R_cst2], [R_s1])
                        for blk in range(2):
                            mm(ps_s2[:], ones_f[:], sq2[:, blk, :], blk == 0, blk == 1, [R_sq2, R_cst2], [R_s2])
                        act(mean[:], ps_s1[:], AF.Copy, [R_s1], [R_mean], scale=1.0 / 256.0)
                        tt("dve", msq[:], mean[:], mean[:], ALU.mult, [R_mean], [R_msq])
                        stt("dve", var[:], ps_s2[:], 1.0 / 256.0, msq[:], ALU.mult, ALU.subtract, [R_s2, R_msq], [R_var])
                        act(sd[:], var[:], AF.Sqrt, [R_var], [R_sd], bias=EPS, scale=1.0)
                        recip(rstd2[:], sd[:], [R_sd], [R_rstd2])
                        for blk in range(2):
                            tt("dve", xm[blk][:], xc[:, blk, :], mean[:], ALU.subtract, [R_xc, R_mean], [R_xm[blk]])
                            tt("dve", xnn[blk][:], xm[blk][:], rstd2[:], ALU.mult, [R_xm[blk], R_rstd2], [R_xnn[blk]])
                    if on("silu", "conf"):
                        for blk in range(2):
                            act(sconf[:, blk, :], xnn[blk][:], AF.Silu, [R_xnn[blk], R_small], [R_sconf],
                                bias=scol(l, C_LNB + blk), scale=scol(l, C_LNG + blk))
                pw_store(NT - 1)
                P.barrier()
                P.flush(final=(stop is not None and stop.startswith("A2")))
            if stop is not None and stop.startswith("A2"):
                return nc

            with contextlib.ExitStack() as st:
                kaug = sbt(st, "kaug", [128, 4, T], BF16)
                R_kaug = [Res("kaug%d" % j) for j in range(NT)]
                R_kaugc = Res("kaugc")
                vsb = sbt(st, "vsb", [128, NB, 512], BF16)
                R_vsb = [Res("vsb%d" % j) for j in range(NT)]
                qa = [sbt(st, "qa%d" % i, [128, 4, 512], BF16) for i in range(2)]
                R_qa = [Res("qa0"), Res("qa1")]
                pts = Ring([(sbt(st, "pt%d" % i, [128, 1024], BF16), Res("pt%d" % i)) for i in range(3)])
                rec = sbt(st, "rec", [128, 512], F32)
                R_rec = Res("rec")
                yst = Ring([(sbt(st, "ysb%d" % i, [64, 512], BF16), Res("ysb%d" % i)) for i in range(2)])
                ps_s = Ring([(pst(st, "pss%d" % i, (128, 1024)), PR("pss%d" % i)) for i in range(3)])
                ps_o = Ring([(pst(st, "pso%d" % i), PR("pso%d" % i)) for i in range(2)])

                for i in range(2):
                    memset("dve", qa[i][64:96, :, :], -1.0, [R_qa[i]])
                def kv_load(j):
                    cols = slice(j * 512, (j + 1) * 512)
                    memset("dve" if j % 2 == 0 else "pool", kaug[64:96, :, cols], 0.0, [R_kaug[j]])
                    memset("dve" if j % 2 == 0 else "pool", kaug[64:67, :, cols], 1.0, [R_kaug[j]])
                    dma("sp", kaug[0:64, :, cols], kT.rearrange("(h r) n -> r h n", r=64)[:, :, cols], [R_k[j]], [R_kaug[j]])
                    dma("sp", kaug[67:70, :, cols], c3d.rearrange("h j n -> j h n")[:, :, cols], [R_c3[j]], [R_kaug[j]])
                    dma("sp", vsb[:, j * 4:(j + 1) * 4, :], vaug[j * 4:(j + 1) * 4].rearrange("s p c -> p s c"),
                        [R_v[j]], [R_vsb[j]])

                def load_q(j):
                    cols = slice(j * 512, (j + 1) * 512)
                    dma("sp", qa[j % 2][0:64, :, :], qT.rearrange("(h r) n -> r h n", r=64)[:, :, cols], [R_q[j]], [R_qa[j % 2]])
                    dma("sp", qa[j % 2][64:67, :, :], c3d.rearrange("h j n -> j h n")[:, :, cols], [R_c3[j]], [R_qa[j % 2]])

                kv_load(0)
                load_q(0)
                for j in range(1, NT):
                    kv_load(j)
                for j in range(NT):
                    b = j % 2
                    if j + 1 < NT:
                        load_q(j + 1)
                    cols = slice(j * 512, (j + 1) * 512)
                    for h in range(4):
                        po, Rpo = ps_o.next()
                        units = [("pair", i0) for i0 in range(0, 4 * j, 2)] + [("diag", dj) for dj in range(4)]

                        def qk(u):
                            kind, v = u
                            pS, RpS = ps_s.next()
                            if kind == "pair":
                                for half in range(2):
                                    i = v + half
                                    mm(pS[:, half * 512:(half + 1) * 512], kaug[0:96, h, i * 128:(i + 1) * 128],
                                       qa[b][0:96, h, :], True, True, [R_kaug[i // 4], R_kaugc, R_qa[b]], [RpS])
                            else:
                                i = 4 * j + v
                                c0 = 128 * v
                                mm(pS[:, c0:512], kaug[0:96, h, i * 128:(i + 1) * 128], qa[b][0:96, h, c0:512],
                                   True, False, [R_kaug[i // 4], R_kaugc, R_qa[b]], [RpS])
                                mm(pS[:, c0:c0 + 128], ident_bf[:], mask_bf[:], False, True, [R_cst2], [RpS])
                            return pS, RpS

                        LA = 2
                        pend = [qk(units[x]) for x in range(min(LA, len(units)))]
                        for ui, u in enumerate(units):
                            pS, RpS = pend.pop(0)
                            if ui + LA < len(units):
                                pend.append(qk(units[ui + LA]))
                            pt, Rpt = pts.next()
                            kind, v = u
                            first = (ui == 0)
                            last = (ui == len(units) - 1)
                            if kind == "pair":
                                act(pt[:, :], pS[:, :], AF.Exp, [RpS], [Rpt])
                                for half in range(2):
                                    i = v + half
                                    mm(po[:, :], vsb[:, i, h * 128:(h + 1) * 128], pt[:, half * 512:(half + 1) * 512],
                                       first and half == 0, False, [R_vsb[i // 4], Rpt], [Rpo])
                            else:
                                i = 4 * j + v
                                c0 = 128 * v
                                act(pt[:, c0:512], pS[:, c0:512], AF.Exp, [RpS], [Rpt])
                                mm(po[:, c0:512], vsb[:, i, h * 128:(h + 1) * 128], pt[:, c0:512], first, last,
                                   [R_vsb[i // 4], Rpt], [Rpo])
                        recip(rec[64:128, :], po[64:128, :], [Rpo], [R_rec])
                        ys, Rys = yst.next()
                        tt("dve", ys[:], po[0:64, :], rec[64:128, :], ALU.mult, [Rpo, R_rec], [Rys])
                        dma("sp", ymix[256 + 64 * h: 256 + 64 * h + 64, cols], ys[:], [Rys], [R_ymB[j]])
                P.barrier()
                P.flush(final=(stop == "B"))
            if stop == "B":
                return nc

            with contextlib.ExitStack() as st:
                wout = sbt(st, "wout", [128, 8, 1024], BF16)
                wgate = sbt(st, "wgate", [128, 8, 1024], BF16)
                wproj = sbt(st, "wproj", [128, 2, 1024], BF16)
                R_wC = Res("wC")
                wu = [sbt(st, "wu%d" % i, [128, 8, 512], BF16) for i in range(2)]
                wd = [sbt(st, "wd%d" % i, [128, 4, 1024], BF16) for i in range(2)]
                R_wu = [Res("wus%d" % i) for i in range(2)]
                R_wd = [Res("wds%d" % i) for i in range(2)]
                ymt = sbt(st, "ymt", [128, 8, 512], BF16)
                R_ymt = Res("ymt")
                ht = [sbt(st, "hc%d" % i, [128, 8, 512], F32) for i in range(2)]
                R_ht = [Res("hc0"), Res("hc1")]
                ptl = [sbt(st, "ptl%d" % i, [128, 2, 512], BF16) for i in range(2)]
                R_ptl = [Res("ptl0"), Res("ptl1")]
                mb = [sbt(st, "mb%d" % i, [128, 8, 512], F32) for i in range(2)]
                R_mb = [Res("mb0"), Res("mb1")]
                hnD = [sbt(st, "hnD%d" % i, [128, 8, 512], BF16) for i in range(2)]
                R_hnD = [Res("hnD0"), Res("hnD1")]
                hnE = sbt(st, "hnE", [128, 8, 512], BF16)
                R_hnE = Res("hnE")
                sqg = sbt(st, "sqg", [128, 8, 512], BF16)
                R_sqg = Res("sqg")
                ag = sbt(st, "ag", [128, 8, 512], BF16)
                R_ag = [Res("ag0"), Res("ag1")]
                rs = sbt(st, "rs_c", [128, 512], F32)
                rstd = sbt(st, "rstd_c", [128, 512], F32)
                R_rs, R_rstd = Res("rs"), Res("rstd")
                tmpf = Ring([(sbt(st, "tmpc%d" % i, [128, 512], F32), Res("tmpc%d" % i)) for i in range(2)])
                rl = Ring([(sbt(st, "rl%d" % i, [128, 512], BF16), Res("rl%d" % i)) for i in range(2)])
                ps_stat = pst(st, "ps_stat_c")
                R_pstat = PR("ps_stat_c")
                ps_ring = Ring([(pst(st, "psm%d" % i), PR("psm%d" % i)) for i in range(3)])
                ps_dn = Ring([(pst(st, "psd%d" % i), PR("psd%d" % i)) for i in range(4)])

                for k in range(8):
                    dma("sp", wout[:, k, :], wout_bf[l * 1024 + k * 128: l * 1024 + (k + 1) * 128, :], [R_wCbf[l]], [R_wC])
                for k in range(8):
                    dma("sp", wgate[:, k, :], wgate_bf[l * 1024 + k * 128: l * 1024 + (k + 1) * 128, :], [R_wCbf[l]], [R_wC])
                for k in range(2):
                    dma("sp", wproj[:, k, :], wproj_bf[l * 256 + k * 128: l * 256 + (k + 1) * 128, :], [R_wCbf[l]], [R_wC])

                def load_wu(n):
                    if n >= NT * 8:
                        return
                    g = n % 8
                    r0 = l * 1024 + g * 128
                    dma("sp", wu[n % 2][:], wup_bf[r0:r0 + 128, :].rearrange("p (k c) -> p k c", k=8), [R_wup[l][g]], [R_wu[n % 2]])

                def load_wd(n):
                    if n >= NT * 8:
                        return
                    g = n % 8
                    r0 = l * 1024 + g * 128
                    dma("sp", wd[n % 2][:], wdn_bf[r0:r0 + 128, :].rearrange("p (k c) -> p k c", k=4), [R_wdn[l][g]], [R_wd[n % 2]])

                def sp(n):
                    for _ in range(n):
                        yield

                def gen_post(m, R_m, h, R_h_, gname):
                    act(sqg[:], m[:], AF.Square, [R_m], [R_sqg])
                    yield from sp(4)
                    rms_stats(sqg, [R_sqg], ps_stat, R_pstat, rs, rstd, R_rs, R_rstd)
                    yield from sp(2)
                    for c in range(8):
                        tb, Rtb = tmpf.next()
                        stt("dve", tb[:], m[:, c, :], scol(l, C_G[gname] + c), rstd[:], ALU.mult, ALU.mult,
                            [R_m, R_rstd, R_small], [Rtb])
                        tt("dve", h[:, c, :], h[:, c, :], tb[:], ALU.add, [R_h_, Rtb], [R_h_])
                        if c % 2 == 1:
                            yield
                    yield from sp(1)

                def gen_pre(h, R_h_, hn, R_hn, gname):
                    act(sqg[:], h[:], AF.Square, [R_h_], [R_sqg])
                    yield from sp(4)
                    rms_stats(sqg, [R_sqg], ps_stat, R_pstat, rs, rstd, R_rs, R_rstd)
                    yield from sp(2)
                    for c in range(8):
                        stt("dve", hn[:, c, :], h[:, c, :], scol(l, C_G[gname] + c), rstd[:], ALU.mult, ALU.mult,
                            [R_h_, R_rstd, R_small], [R_hn])
                        if c % 4 == 3:
                            yield
                    yield from sp(2)

                def H1(j):
                    p = j % 2
                    cols = slice(j * 512, (j + 1) * 512)
                    dma("sp", ht[p][:], h_src_v[:, :, cols], [R_h[j]], [R_ht[p]])
                    dma("pool", ptl[p][:], pT[l * 256:(l + 1) * 256, :].rearrange("(c p) n -> p c n", p=128)[:, :, cols],
                        [], [R_ptl[p]])
                    yield from sp(2)
                    for oc in range(8):
                        pp, Rpp = ps_ring.next()
                        for k in range(8):
                            mm(pp[:], wout[:, k, oc * 128:(oc + 1) * 128], ymt[:, k, :], k == 0, k == 7, [R_wC, R_ymt], [Rpp])
                        cp("dve" if oc % 2 == 0 else "act", mb[p][:, oc, :], pp[:], [Rpp], [R_mb[p]])
                        yield
                    yield from sp(2)
                    yield from gen_post(mb[p], R_mb[p], ht[p], R_ht[p], "mix_post")
                    yield from gen_pre(ht[p], R_ht[p], hnD[p], R_hnD[p], "mlp_pre")

                def H1_loads(j):
                    p = j % 2
                    cols = slice(j * 512, (j + 1) * 512)
                    dma("sp", ymt[:], ymix.rearrange("(c p) n -> p c n", p=128)[:, :, cols], [R_ymA[j], R_ymB[j]], [R_ymt])

                def H3(j):
                    p = j % 2
                    cols = slice(j * 512, (j + 1) * 512)
                    yield from gen_post(mb[p], R_mb[p], ht[p], R_ht[p], "mlp_post")
                    yield from gen_pre(ht[p], R_ht[p], hnE, R_hnE, "ple_pre")
                    for oc in range(8):
                        pp, Rpp = ps_ring.next()
                        for k in range(8):
                            mm(pp[:], wgate[:, k, oc * 128:(oc + 1) * 128], hnE[:, k, :], k == 0, k == 7, [R_wC, R_hnE], [Rpp])
                        act(sqg[:, oc, :], pp[:], AF.Sigmoid, [Rpp], [R_sqg])
                        yield
                    yield from sp(2)
                    for oc in range(8):
                        pp, Rpp = ps_ring.next()
                        for k in range(2):
                            mm(pp[:], wproj[:, k, oc * 128:(oc + 1) * 128], ptl[p][:, k, :], k == 0, k == 1, [R_wC, R_ptl[p]], [Rpp])
                        tt("dve", mb[p][:, oc, :], pp[:], sqg[:, oc, :], ALU.mult, [Rpp, R_sqg], [R_mb[p]])
                        yield
                    yield from sp(3)
                    yield from gen_post(mb[p], R_mb[p], ht[p], R_ht[p], "ple_post")
                    dma("sp", h_dst_v[:, :, cols], ht[p][:], [R_ht[p]], [R_h[j]])
                    yield

                def step(side, n):
                    for _ in range(n):
                        try:
                            next(side)
                        except StopIteration:
                            return

                def up(j, g, side):
                    n = j * 8 + g
                    p = j % 2
                    for c4 in range(4):
                        pp, Rpp = ps_ring.next()
                        for k in range(8):
                            mm(pp[:], wu[n % 2][:, k, c4 * 128:(c4 + 1) * 128], hnD[p][:, k, :], k == 0, k == 7,
                               [R_wu[n % 2], R_hnD[p]], [Rpp])
                        rr, Rrr = rl.next()
                        act(rr[:], pp[:], AF.Relu, [Rpp], [Rrr])
                        act(ag[:, (g % 2) * 4 + c4, :], rr[:], AF.Square, [Rrr], [R_ag[g % 2]])
                        step(side, 1)

                def down(j, g, side):
                    n = j * 8 + g
                    p = j % 2
                    for oc in range(8):
                        pd, Rpd = ps_dn.next()
                        for k4 in range(4):
                            mm(pd[:], wd[n % 2][:, k4, oc * 128:(oc + 1) * 128], ag[:, (g % 2) * 4 + k4, :], k4 == 0, k4 == 3,
                               [R_wd[n % 2], R_ag[g % 2]], [Rpd])
                        if g == 0:
                            cp("dve", mb[p][:, oc, :], pd[:], [Rpd], [R_mb[p]])
                        else:
                            tt("dve", mb[p][:, oc, :], pd[:], mb[p][:, oc, :], ALU.add, [Rpd, R_mb[p]], [R_mb[p]])
                        step(side, 1)

                def chain(*gens):
                    for gq in gens:
                        if gq is not None:
                            yield from gq

                load_wu(0)
                load_wu(1)
                load_wd(0)
                H1_loads(0)
                for _ in H1(0):
                    pass
                for j in range(NT):
                    if j + 1 < NT:
                        H1_loads(j + 1)
                    side = chain(H3(j - 1) if j > 0 else None, H1(j + 1) if j + 1 < NT else None)
                    for g in range(9):
                        n = j * 8 + g
                        if g < 8:
                            up(j, g, side)
                        if g > 0:
                            down(j, g - 1, side)
                        if g < 8:
                            load_wu(n + 2)
                            load_wd(n + 1)
                    for _ in side:
                        pass
                for _ in H3(NT - 1):
                    pass
                P.barrier()
                P.flush(final=(l == L - 1))
    return nc


POOL_WINDOWS = (2, 4, 8, 16)


def _chunkcols(v):
    return np.ascontiguousarray(v.reshape(-1, 128).T)


def prep_shared(inp, L):
    f32 = np.float32
    w_in = np.asarray(inp["w_in"], f32)[:L]
    perm = np.concatenate([np.arange(0, 1024), np.arange(1284, 2308), np.arange(1280, 1284), np.arange(1024, 1280)])
    w_in_r = np.ascontiguousarray(w_in[:, :, perm]).reshape(L * 1024, 2308)
    w_pw = np.ascontiguousarray(np.asarray(inp["w_conf_pw"], f32)[:L]).reshape(L * 256, 256)
    wp = np.asarray(inp["w_pool"], f32)[:L]
    w_pbd = np.zeros((L, 2, 128, 128), f32)
    for blk in range(2):
        for gg in range(2):
            w_pbd[:, blk, gg * 64:(gg + 1) * 64, gg * 64:(gg + 1) * 64] = wp[:, blk * 2 + gg]
    w_pbd = w_pbd.reshape(L * 256, 128)
    w_out = np.ascontiguousarray(np.asarray(inp["w_out"], f32)[:L]).reshape(L * 1024, 1024)
    wu = np.asarray(inp["w_up"], f32)[:L]
    wu = wu.reshape(L, 8, 128, 8, 512).transpose(0, 3, 2, 1, 4)
    w_up = np.ascontiguousarray(wu).reshape(L * 1024, 4096)
    wdn = np.asarray(inp["w_down"], f32)[:L]
    wdn = wdn.reshape(L, 8, 4, 128, 1024).transpose(0, 1, 3, 2, 4)
    w_dn = np.ascontiguousarray(wdn).reshape(L * 1024, 4096)
    w_gate = np.ascontiguousarray(np.asarray(inp["w_ple_gate"], f32)[:L]).reshape(L * 1024, 1024)
    w_proj = np.ascontiguousarray(np.asarray(inp["w_ple_proj"], f32)[:L]).reshape(L * 256, 1024)
    small = np.zeros((128, L * 128), f32)
    for l in range(L):
        o = l * 128
        for name, key in (("mix_pre", "g_mix_pre"), ("mix_post", "g_mix_post"), ("mlp_pre", "g_mlp_pre"),
                          ("mlp_post", "g_mlp_post"), ("ple_pre", "g_ple_pre"), ("ple_post", "g_ple_post")):
            small[:, o + C_G[name]: o + C_G[name] + 8] = _chunkcols(np.asarray(inp[key], f32)[l])
        small[:, o + C_LNG: o + C_LNG + 2] = _chunkcols(np.asarray(inp["conf_ln_g"], f32)[l])
        small[:, o + C_LNB: o + C_LNB + 2] = _chunkcols(np.asarray(inp["conf_ln_b"], f32)[l])
        small[:, o + C_PSC: o + C_PSC + 2] = _chunkcols(np.asarray(inp["pool_scale"], f32)[l])
        dw = np.asarray(inp["w_conf_dw"], f32)[l]
        for blk in range(2):
            small[:, o + C_DW + blk * 31: o + C_DW + blk * 31 + 31] = dw[:, blk * 128:(blk + 1) * 128].T
        sc = np.asarray(inp["w_sc"], f32)[l]
        for blk in range(2):
            small[:, o + C_SC + blk * 3: o + C_SC + blk * 3 + 3] = sc[:, blk * 128:(blk + 1) * 128].T
        small[0:4, o + C_BF] = np.asarray(inp["b_forget"], f32)[l]
    consts = np.zeros((128, 320), f32)
    consts[:, K_ID:K_ID + 128] = np.eye(128, dtype=f32)
    kk, qq = np.meshgrid(np.arange(128), np.arange(128), indexing="ij")
    consts[:, K_MASK:K_MASK + 128] = np.where(kk > qq, NEG, 0.0)
    for blk in range(2):
        for p in range(128):
            w = POOL_WINDOWS[(blk * 128 + p) // 64]
            for t in range(16):
                consts[p, K_PCOEF + blk * 16 + t] = (1.0 / w if t < w else 0.0) - (1.0 if t == 0 else 0.0)
                consts[p, K_PRATIO + blk * 16 + t] = w / min(t + 1, w)
    return dict(w_in=w_in_r, w_pw=w_pw, w_pbd=w_pbd, w_out=w_out, w_up=w_up, w_dn=w_dn, w_gate=w_gate,
                w_proj=w_proj, small=small, consts=consts)


_NC_CACHE = {}
ACTIVE_CORES = [0, 1, 4, 5]


def run(inp, T, L, n_seq, stop=None, dbg=False):
    f32 = np.float32
    shared = prep_shared(inp, L)
    x = np.asarray(inp["x"], f32)
    p = np.asarray(inp["p"], f32)
    active = ACTIVE_CORES[:n_seq]
    zero_map = None
    in_maps = []
    for core in range(8):
        if core in active:
            bi = active.index(core)
            m = dict(shared)
            m["xT"] = np.ascontiguousarray(x[bi].T)
            m["pT"] = np.ascontiguousarray(p[:L, bi].transpose(0, 2, 1)).reshape(L * 256, T)
        else:
            if zero_map is None:
                zero_map = {k: np.zeros_like(v) for k, v in shared.items()}
                zero_map["xT"] = np.zeros((1024, T), f32)
                zero_map["pT"] = np.zeros((L * 256, T), f32)
            m = zero_map
        in_maps.append(m)
    key = (T, L, stop, dbg)
    if key not in _NC_CACHE:
        _NC_CACHE[key] = build(T, L, stop, dbg)
    nc = _NC_CACHE[key]
    res = run_bass_kernel_spmd(nc, in_maps, core_ids=list(range(8)))
    if dbg:
        return res.results[active[0]]
    out = np.stack([np.ascontiguousarray(res.results[active[bi]]["outT"].T) for bi in range(n_seq)], axis=0)
    return out.astype(f32)


def kernel(**inputs):
    return run(inputs, 8192, 4, 4)
```

```python
import contextlib
import numpy as np
import concourse.bass as bass
import concourse.mybir as mybir
from concourse.bass_utils import run_bass_kernel_spmd

F32 = mybir.dt.float32
BF16 = mybir.dt.bfloat16
ALU = mybir.AluOpType
AF = mybir.ActivationFunctionType
ENGS = ("sp", "act", "pe", "dve", "pool")
EPS = 1e-6
NEG = -30000.0


class Res:
    __slots__ = ("name", "w", "r", "excl")

    def __init__(self, name="", excl=False):
        self.name = name
        self.w = {}
        self.r = {}
        self.excl = excl


def PR(name):
    return Res(name, excl=True)


def _merge(d, s, v):
    if d.get(s, 0) < v:
        d[s] = v


class Prog:
    def __init__(self, nc, stack, n_dma_sems=32):
        self.nc = nc
        self.q = {e: [] for e in ENGS}
        self.sem_names = []
        self.sem_count = []
        self.seen = {e: {} for e in ENGS}
        self.pending_reads = {e: [] for e in ENGS}
        self.pending_writes = {e: [] for e in ENGS}
        self.eng_sem = {}
        for e in ("act", "pe", "dve", "pool"):
            self.eng_sem[e] = self._new_sem("c_" + e)
        self.dma_sems = {"sp": [self._new_sem("d%d" % i) for i in range(n_dma_sems)],
                         "pool": [self._new_sem("w%d" % i) for i in range(16)]}
        self.dma_rr = {"sp": 0, "pool": 0}
        self.n_ops = 0
        self.sems = [stack.enter_context(nc.semaphore(n)) for n in self.sem_names]

    def _new_sem(self, name):
        self.sem_names.append(name)
        self.sem_count.append(0)
        return len(self.sem_names) - 1

    def op(self, eng, fn, reads=(), writes=(), inc=True, dma=False):
        waits = {}
        for r in reads:
            for s, v in r.w.items():
                _merge(waits, s, v)
            if r.excl:
                for s, v in r.r.items():
                    if not (eng in self.eng_sem and s == self.eng_sem[eng]):
                        _merge(waits, s, v)
        for w in writes:
            for s, v in w.w.items():
                _merge(waits, s, v)
            for s, v in w.r.items():
                _merge(waits, s, v)
        tok = None
        if dma:
            pool_ = self.dma_sems[eng]
            s = pool_[self.dma_rr[eng] % len(pool_)]
            self.dma_rr[eng] += 1
            if self.sem_count[s] > 0:
                _merge(waits, s, self.sem_count[s])
            self.sem_count[s] += 16
            tok = (s, self.sem_count[s], 16)
        elif inc:
            s = self.eng_sem[eng]
            self.sem_count[s] += 1
            tok = (s, self.sem_count[s], 1)
        seen = self.seen[eng]
        wl = []
        for s, v in waits.items():
            if seen.get(s, 0) < v:
                if eng == "pe" and s == self.eng_sem["pe"]:
                    continue
                seen[s] = v
                wl.append((s, v))
        self.q[eng].append((fn, wl, tok))
        self.n_ops += 1
        if tok is None:
            self.pending_reads[eng].extend(reads)
            self.pending_writes[eng].extend(writes)
        else:
            s, v, _ = tok
            rl = list(reads)
            wr = list(writes)
            if not dma:
                rl += self.pending_reads[eng]
                wr += self.pending_writes[eng]
                self.pending_reads[eng] = []
                self.pending_writes[eng] = []
            for r in rl:
                _merge(r.r, s, v)
            for w in wr:
                _merge(w.w, s, v)
        return tok

    def barrier(self):
        for e in ENGS:
            wl = []
            for s, c in enumerate(self.sem_count):
                if c > 0 and self.seen[e].get(s, 0) < c:
                    self.seen[e][s] = c
                    wl.append((s, c))
            if wl:
                self.q[e].append((None, wl, None))

    def flush(self, final=False):
        nc = self.nc
        sems = self.sems
        if final:
            wl = [(s, c) for s, c in enumerate(self.sem_count) if c > 0]
            self.q["sp"].append((None, wl, None))
        with nc.Block() as block:
            def run(eng_name):
                ops = self.q[eng_name]

                def f(e):
                    for fn, wl, tok in ops:
                        for s, v in wl:
                            e.wait_ge(sems[s], v)
                        if fn is None:
                            continue
                        ins = fn(e)
                        if tok is not None:
                            ins.then_inc(sems[tok[0]], tok[2])
                return f
            block.sync(run("sp"))
            block.scalar(run("act"))
            block.tensor(run("pe"))
            block.vector(run("dve"))
            block.gpsimd(run("pool"))
        self.q = {e: [] for e in ENGS}


class Ring:
    def __init__(self, items):
        self.items = items
        self.i = 0

    def next(self):
        it = self.items[self.i % len(self.items)]
        self.i += 1
        return it


C_G = {"mix_pre": 0, "mix_post": 8, "mlp_pre": 16, "mlp_post": 24, "ple_pre": 32, "ple_post": 40}
C_LNG, C_LNB, C_PSC, C_DW, C_SC, C_BF = 48, 50, 52, 54, 116, 122
K_ID, K_MASK, K_PCOEF, K_PRATIO = 0, 128, 256, 288


def build(T, L, stop=None, dbg=False):
    NT = T // 512
    NB = T // 128
    nc = bass.Bass("TRN2", target_bir_lowering=False)

    def din(name, shape, dt=F32):
        return nc.dram_tensor(name, shape, dt, kind="ExternalInput").ap()

    def dscr(name, shape, dt):
        return nc.dram_tensor(name, shape, dt, kind="ExternalOutput" if dbg else "Internal").ap()

    xT = din("xT", [1024, T])
    pT = din("pT", [L * 256, T])
    w_in = din("w_in", [L * 1024, 2308])
    w_pw = din("w_pw", [L * 256, 256])
    w_pbd = din("w_pbd", [L * 256, 128])
    w_out = din("w_out", [L * 1024, 1024])
    w_up = din("w_up", [L * 1024, 4096])
    w_dn = din("w_dn", [L * 1024, 4096])
    w_gate = din("w_gate", [L * 1024, 1024])
    w_proj = din("w_proj", [L * 256, 1024])
    small = din("small", [128, L * 128])
    consts = din("consts", [128, 320])
    outT = nc.dram_tensor("outT", [1024, T], F32, kind="ExternalOutput").ap()

    hT = dscr("hT", [1024, T], F32)
    wup_bf = dscr("wup_bf", [L * 1024, 4096], BF16)
    wdn_bf = dscr("wdn_bf", [L * 1024, 4096], BF16)
    win_bf = dscr("win_bf", [L * 1024, 2308], BF16)
    wout_bf = dscr("wout_bf", [L * 1024, 1024], BF16)
    wgate_bf = dscr("wgate_bf", [L * 1024, 1024], BF16)
    wproj_bf = dscr("wproj_bf", [L * 256, 1024], BF16)
    mixin = dscr("mixin", [1024, 32 + T], BF16)
    qT = dscr("qT", [256, T], BF16)
    kT = dscr("kT", [256, T], BF16)
    c3d = dscr("c3d", [4, 3, T], BF16)
    vaug = dscr("vaug", [NB, 128, 512], BF16)
    ymix = dscr("ymix", [1024, T], BF16)

    R_h = [Res("h%d" % j) for j in range(NT)]
    R_mixin = [Res("mi%d" % j) for j in range(NT + 1)]
    R_q = [Res("q%d" % j) for j in range(NT)]
    R_k = [Res("k%d" % j) for j in range(NT)]
    R_c3 = [Res("c3%d" % j) for j in range(NT)]
    R_v = [Res("v%d" % j) for j in range(NT)]
    R_ymA = [Res("ymA%d" % j) for j in range(NT)]
    R_ymB = [Res("ymB%d" % j) for j in range(NT)]
    R_winbf = [Res("winbf%d" % l) for l in range(L)]
    R_wCbf = [Res("wCbf%d" % l) for l in range(L)]
    R_wup = [[Res("wu%d_%d" % (l, g)) for g in range(8)] for l in range(L)]
    R_wdn = [[Res("wd%d_%d" % (l, g)) for g in range(8)] for l in range(L)]

    top = contextlib.ExitStack()
    with top:
        P = Prog(nc, top)

        uid = [0]

        def sbt(st, name, shape, dt):
            uid[0] += 1
            return st.enter_context(nc.sbuf_tensor("%s_u%d" % (name, uid[0]), shape, dt))

        def pst(st, name, shape=(128, 512), dt=F32):
            uid[0] += 1
            return st.enter_context(nc.psum_tensor("%s_u%d" % (name, uid[0]), list(shape), dt))

        small_sb = sbt(top, "small_sb", [128, L * 128], F32)
        consts_sb = sbt(top, "consts_sb", [128, 320], F32)
        ident_bf = sbt(top, "ident_bf", [128, 128], BF16)
        mask_bf = sbt(top, "mask_bf", [128, 128], BF16)
        ones_bf = sbt(top, "ones_bf", [128, 128], BF16)
        ones_f = sbt(top, "ones_f", [128, 128], F32)
        cneg = sbt(top, "cneg", [128, NB, 4], F32)
        R_small, R_consts, R_cst2, R_cneg = Res("small"), Res("consts"), Res("cst2"), Res("cneg")
        ident_f = consts_sb[:, K_ID:K_ID + 128]

        def scol(l, c, rows=slice(0, 128)):
            return small_sb[rows, l * 128 + c: l * 128 + c + 1]

        def dma(eng, out, in_, reads, writes):
            P.op(eng, lambda e: e.dma_start(out=out, in_=in_), reads=reads, writes=writes, dma=True)

        def mm(out, lhsT, rhs, start, stop, reads, writes):
            P.op("pe", lambda e: e.matmul(out, lhsT=lhsT, rhs=rhs, start=start, stop=stop),
                 reads=reads, writes=writes, inc=stop)

        def act(out, in_, func, reads, writes, bias=None, scale=None):
            kw = {}
            if bias is not None:
                kw["bias"] = bias
            if scale is not None:
                kw["scale"] = scale
            P.op("act", lambda e: e.activation(out, in_, func, **kw), reads=reads, writes=writes)

        def tt(eng, out, in0, in1, op, reads, writes):
            P.op(eng, lambda e: e.tensor_tensor(out, in0, in1, op), reads=reads, writes=writes)

        def stt(eng, out, in0, scalar, in1, op0, op1, reads, writes):
            P.op(eng, lambda e: e.scalar_tensor_tensor(out, in0, scalar, in1, op0, op1), reads=reads, writes=writes)

        def norm_scale(eng, out, in0, gcol, rstd_ap, reads, writes, ptmp_ring):
            if eng == "dve":
                stt("dve", out, in0, gcol, rstd_ap, ALU.mult, ALU.mult, reads, writes)
            else:
                tb, Rtb = ptmp_ring.next()
                tt("pool", tb[:], in0, rstd_ap, ALU.mult, reads, [Rtb])
                P.op("pool", lambda e: e.tensor_scalar_mul(out, tb[:], gcol), reads=[Rtb] + list(reads), writes=writes)

        def ts(eng, out, in0, s1, s2, op0, op1, reads, writes):
            if s2 is None:
                P.op(eng, lambda e: e.tensor_single_scalar(out, in0, s1, op0), reads=reads, writes=writes)
            else:
                P.op(eng, lambda e: e.tensor_scalar(out, in0, s1, s2, op0, op1), reads=reads, writes=writes)

        def cp(eng, out, in_, reads, writes):
            if eng == "act":
                P.op("act", lambda e: e.copy(out, in_), reads=reads, writes=writes)
            else:
                P.op(eng, lambda e: e.tensor_copy(out, in_), reads=reads, writes=writes)

        def memset(eng, ap, val, writes):
            P.op(eng, lambda e: e.memset(ap, val), writes=writes)

        def recip(out, in_, reads, writes):
            P.op("dve", lambda e: e.reciprocal(out, in_), reads=reads, writes=writes)

        def rms_stats(sq, R_sq, ps_stat, R_ps, rs, rstd, R_rs, R_rstd):
            for c in range(8):
                mm(ps_stat[:], ones_bf[:], sq[:, c, :], c == 0, c == 7, list(R_sq) + [R_cst2], [R_ps])
            act(rs[:], ps_stat[:], AF.Sqrt, [R_ps], [R_rs], bias=EPS, scale=1.0 / 1024.0)
            recip(rstd[:], rs[:], [R_rs], [R_rstd])

        with contextlib.ExitStack() as st:
            zt = sbt(st, "zt", [128, 8, 32], BF16)
            R_zt = Res("zt")
            dma("sp", small_sb[:], small, [], [R_small])
            dma("sp", consts_sb[:], consts, [], [R_consts])
            cp("dve", ident_bf[:], consts_sb[:, K_ID:K_ID + 128], [R_consts], [R_cst2])
            cp("dve", mask_bf[:], consts_sb[:, K_MASK:K_MASK + 128], [R_consts], [R_cst2])
            memset("pool", ones_bf[:], 1.0, [R_cst2])
            memset("pool", ones_f[:], 1.0, [R_cst2])
            memset("pool", zt[:], 0.0, [R_zt])
            dma("sp", mixin.rearrange("(c p) n -> p c n", p=128)[:, :, 0:32], zt[:], [R_zt], [R_mixin[0]])
            P.barrier()
            P.flush(final=(stop == "pro"))
        if stop == "pro":
            return nc

        def cast_small_weights(l):
            for k in range(8):
                r0 = l * 1024 + k * 128
                for hh in range(2):
                    dma("pool", win_bf[r0:r0 + 128, hh * 1154:(hh + 1) * 1154], w_in[r0:r0 + 128, hh * 1154:(hh + 1) * 1154],
                        [], [R_winbf[l]])
            for k in range(8):
                r0 = l * 1024 + k * 128
                dma("pool", wout_bf[r0:r0 + 128, :], w_out[r0:r0 + 128, :], [], [R_wCbf[l]])
                dma("pool", wgate_bf[r0:r0 + 128, :], w_gate[r0:r0 + 128, :], [], [R_wCbf[l]])
            for k in range(2):
                r0 = l * 256 + k * 128
                dma("pool", wproj_bf[r0:r0 + 128, :], w_proj[r0:r0 + 128, :], [], [R_wCbf[l]])

        def cast_mlp_weights(l):
            for g in range(8):
                for (src, dst, RR) in ((w_up, wup_bf, R_wup), (w_dn, wdn_bf, R_wdn)):
                    r0 = l * 1024 + g * 128
                    for hh in range(2):
                        dma("pool", dst[r0:r0 + 128, hh * 2048:(hh + 1) * 2048],
                            src[r0:r0 + 128, hh * 2048:(hh + 1) * 2048], [], [RR[l][g]])

        cast_small_weights(0)
        for l in range(L):
            h_src = xT if l == 0 else hT
            h_dst = outT if l == L - 1 else hT
            h_src_v = h_src.rearrange("(c p) n -> p c n", p=128)
            h_dst_v = h_dst.rearrange("(c p) n -> p c n", p=128)

            with contextlib.ExitStack() as st:
                win = sbt(st, "win", [128, 8, 2308], BF16)
                R_win = Res("win")
                ht = [sbt(st, "ht%d" % i, [128, 8, 512], F32) for i in range(2)]
                R_ht = [Res("ht0"), Res("ht1")]
                sq = sbt(st, "sq", [128, 8, 512], BF16)
                R_sq = Res("sq")
                xn2 = [sbt(st, "xn%d" % i, [128, 8, 512], BF16) for i in range(2)]
                R_xn2 = [Res("xn0"), Res("xn1")]
                rs = sbt(st, "rs", [128, 512], F32)
                rstd = sbt(st, "rstd", [128, 512], F32)
                R_rs, R_rstd = Res("rs"), Res("rstd")
                tmpf = [sbt(st, "tmpf%d" % i, [128, 512], F32) for i in range(2)]
                tmp_ring = Ring([(tmpf[i], Res("tmpf%d" % i)) for i in range(2)])
                ptmp_ring = Ring([(sbt(st, "ptmp%d" % i, [128, 512], F32), Res("ptmp%d" % i)) for i in range(2)])
                mstage = [sbt(st, "mstage%d" % i, [128, 8, 512], BF16) for i in range(2)]
                R_mst = [Res("mst0"), Res("mst1")]
                qstage = [sbt(st, "qstage%d" % i, [128, 2, 512], BF16) for i in range(2)]
                R_qst = [Res("qst0"), Res("qst1")]
                kstage = [sbt(st, "kstage%d" % i, [128, 2, 512], BF16) for i in range(2)]
                R_kst = [Res("kst0"), Res("kst1")]
                vstage = [sbt(st, "vstage%d" % i, [128, 4, 512], BF16) for i in range(2)]
                R_vst = [Res("vst0"), Res("vst1")]
                xb = sbt(st, "xb", [4, 512], F32)
                ef = sbt(st, "ef", [4, 512], F32)
                lf = sbt(st, "lf", [4, 512], F32)
                ones4 = sbt(st, "ones4", [4, 512], F32)
                cc = [sbt(st, "cc%d" % i, [4, 512], F32) for i in range(2)]
                r1 = sbt(st, "r1", [4, 512], F32)
                r2 = sbt(st, "r2", [4, 512], F32)
                c3 = [sbt(st, "c3_%d" % i, [4, 3, 512], BF16) for i in range(2)]
                R_f = Res("fmisc")
                R_cc = [Res("cc0"), Res("cc1")]
                R_c3s = [Res("c3s0"), Res("c3s1")]
                ps_stat = pst(st, "ps_stat")
                R_pstat = PR("ps_stat")
                ps_ring = Ring([(pst(st, "psr%d" % i), PR("psr%d" % i)) for i in range(5)])
                ps_v = Ring([(pst(st, "psv%d" % i), PR("psv%d" % i)) for i in range(2)])

                for k in range(8):
                    dma("sp", win[:, k, :], win_bf[l * 1024 + k * 128: l * 1024 + (k + 1) * 128, :], [R_winbf[l]], [R_win])
                cast_mlp_weights(l)
                if l + 1 < L:
                    cast_small_weights(l + 1)
                memset("pool", ones4[:], 1.0, [R_f])
                for i in range(2):
                    memset("pool", vstage[i][:], 1.0, [R_vst[i]])

                def load_h(j):
                    dma("sp", ht[j % 2][:], h_src_v[:, :, j * 512:(j + 1) * 512], [R_h[j]], [R_ht[j % 2]])

                cur = {}

                def proj(ps, R_ps, col0, M):
                    xn, R_xn = cur["xn"], cur["R_xn"]
                    for k in range(8):
                        mm(ps[0:M, :], win[:, k, col0:col0 + M], xn[:, k, :], k == 0, k == 7, [R_win, R_xn], [R_ps])

                def norm_a(j):
                    bb = j % 2
                    act(sq[:], ht[bb][:], AF.Square, [R_ht[bb]], [R_sq])

                def norm_b(j):
                    bb = j % 2
                    rms_stats(sq, [R_sq], ps_stat, R_pstat, rs, rstd, R_rs, R_rstd)
                    for c in range(8):
                        norm_scale("dve", xn2[bb][:, c, :], ht[bb][:, c, :], scol(l, C_G["mix_pre"] + c),
                                   rstd[:], [R_ht[bb], R_rstd, R_small], [R_xn2[bb]], ptmp_ring)

                load_h(0)
                if NT > 1:
                    load_h(1)
                norm_a(0)
                norm_b(0)
                for j in range(NT):
                    b = j % 2
                    h = ht[b]
                    cols = slice(j * 512, (j + 1) * 512)
                    xnb, R_xnb = xn2[b], R_xn2[b]
                    xn, R_xn = xnb, R_xnb
                    cur["xn"], cur["R_xn"] = xnb, R_xnb
                    pf, Rpf = ps_ring.next()
                    proj(pf, Rpf, 2048, 4)
                    ts("dve", xb[:], pf[0:4, :], scol(l, C_BF, slice(0, 4)), None, ALU.add, None, [Rpf, R_small], [R_f])
                    act(ef[:], xb[:], AF.Exp, [R_f], [R_f], scale=-1.0)
                    act(lf[:], ef[:], AF.Ln, [R_f], [R_f], bias=1.0, scale=1.0)
                    init = 0.0 if j == 0 else cc[1 - b][:, 511:512]
                    P.op("dve", (lambda o, d0, d1, ini: (lambda e: e.tensor_tensor_scan(o, d0, d1, ini, ALU.mult, ALU.subtract)))(
                        cc[b][:], ones4[:], lf[:], init), reads=[R_f, R_cc[1 - b]], writes=[R_cc[b]])
                    if j + 1 < NT:
                        norm_a(j + 1)
                    for blk in range(2):
                        pa, Rpa = ps_ring.next()
                        pb, Rpb = ps_ring.next()
                        proj(pa, Rpa, (0 + blk) * 128, 128)
                        proj(pb, Rpb, (2 + blk) * 128, 128)
                        tb, Rtb = tmp_ring.next()
                        act(tb[:], pb[:], AF.Sigmoid, [Rpb], [Rtb])
                        tt("dve", mstage[b][:, 0 + blk, :], pa[:], tb[:], ALU.mult, [Rpa, Rtb], [R_mst[b]])
                    cp("dve", c3[b][:, 0, :], cc[b][:], [R_cc[b]], [R_c3s[b]])
                    tt("dve", r1[:], cc[b][:], c3[b][:, 0, :], ALU.subtract, [R_cc[b], R_c3s[b]], [R_f])
                    cp("dve", c3[b][:, 1, :], r1[:], [R_f], [R_c3s[b]])
                    tt("dve", r2[:], r1[:], c3[b][:, 1, :], ALU.subtract, [R_f, R_c3s[b]], [R_f])
                    cp("dve", c3[b][:, 2, :], r2[:], [R_f], [R_c3s[b]])
                    for blk in range(2):
                        pa, Rpa = ps_ring.next()
                        pb, Rpb = ps_ring.next()
                        proj(pa, Rpa, (8 + blk) * 128, 128)
                        proj(pb, Rpb, (12 + blk) * 128, 128)
                        tb, Rtb = tmp_ring.next()
                        cp("act", tb[:], pb[:], [Rpb], [Rtb])
                        tt("dve", mstage[b][:, 2 + blk, :], pa[:], tb[:], ALU.mult, [Rpa, Rtb], [R_mst[b]])
                    if j + 1 < NT:
                        norm_b(j + 1)
                        if j + 2 < NT:
                            load_h(j + 2)
                    for blk in range(2):
                        pa, Rpa = ps_ring.next()
                        proj(pa, Rpa, (14 + blk) * 128, 128)
                        cp("dve", mstage[b][:, 4 + blk, :], pa[:], [Rpa], [R_mst[b]])
                        pb, Rpb = ps_ring.next()
                        proj(pb, Rpb, (10 + blk) * 128, 128)
                        cp("act", mstage[b][:, 6 + blk, :], pb[:], [Rpb], [R_mst[b]])
                    for blk in range(2):
                        pa, Rpa = ps_ring.next()
                        proj(pa, Rpa, (4 + blk) * 128, 128)
                        act(qstage[b][:, blk, :], pa[:], AF.Copy, [Rpa], [R_qst[b]], scale=0.125)
                        pb, Rpb = ps_ring.next()
                        proj(pb, Rpb, (6 + blk) * 128, 128)
                        cp("dve", kstage[b][:, blk, :], pb[:], [Rpb], [R_kst[b]])
                    for s in range(4):
                        pv, Rpv = ps_v.next()
                        for k in range(8):
                            mm(pv[:, 0:256], xn[:, k, s * 128:(s + 1) * 128], win[:, k, 2052:2308], k == 0, k == 7,
                               [R_win, R_xn], [Rpv])
                        dst = vstage[b][:, s, :].rearrange("p (h c) -> p h c", h=4)[:, :, 0:64]
                        src = pv[:, 0:256].rearrange("p (h c) -> p h c", h=4)
                        cp("dve" if s % 2 == 0 else "act", dst, src, [Rpv], [R_vst[b]])
                    dma("sp", mixin.rearrange("(c p) n -> p c n", p=128)[:, :, 32 + j * 512: 32 + (j + 1) * 512],
                        mstage[b][:], [R_mst[b]], [R_mixin[j + 1]])
                    dma("sp", qT.rearrange("(c p) n -> p c n", p=128)[:, :, cols], qstage[b][:], [R_qst[b]], [R_q[j]])
                    dma("sp", kT.rearrange("(c p) n -> p c n", p=128)[:, :, cols], kstage[b][:], [R_kst[b]], [R_k[j]])
                    dma("sp", c3d[:, :, cols], c3[b][:], [R_c3s[b]], [R_c3[j]])
                    dma("sp", vaug[j * 4:(j + 1) * 4].rearrange("s p c -> p s c"), vstage[b][:], [R_vst[b]], [R_v[j]])
                P.barrier()
                P.flush(final=(stop == "A1"))
            if stop == "A1":
                return nc

            with contextlib.ExitStack() as st:
                dconf = sbt(st, "dconf", [128, 2, 31, 128], BF16)
                dsc = sbt(st, "dsc", [128, 2, 3, 128], BF16)
                dpl = sbt(st, "dpl", [128, 2, 16, 128], BF16)
                R_dg = Res("diag")
                wpw = sbt(st, "wpw", [128, 2, 256], BF16)
                wpbd = sbt(st, "wpbd", [128, 2, 128], BF16)
                R_wA2 = Res("wA2")
                mt = [sbt(st, "mt%d" % i, [128, 8, 544], BF16) for i in range(2)]
                R_mt = [Res("mt0"), Res("mt1")]
                xc = sbt(st, "xc", [128, 2, 512], F32)
                sq2 = sbt(st, "sq2", [128, 2, 512], F32)
                R_xc, R_sq2 = Res("xc"), Res("sq2")
                mean = sbt(st, "mean", [128, 512], F32)
                msq = sbt(st, "msq", [128, 512], F32)
                var = sbt(st, "var", [128, 512], F32)
                sd = sbt(st, "sd", [128, 512], F32)
                rstd2 = sbt(st, "rstd2", [128, 512], F32)
                R_mean, R_msq, R_var, R_sd, R_rstd2 = Res("mean"), Res("msq"), Res("var"), Res("sd"), Res("rstd2")
                xm = [sbt(st, "xm%d" % i, [128, 512], F32) for i in range(2)]
                R_xm = [Res("xm0"), Res("xm1")]
                xnn = [sbt(st, "xnn%d" % i, [128, 512], F32) for i in range(2)]
                R_xnn = [Res("xnn0"), Res("xnn1")]
                sconf = sbt(st, "sconf", [128, 2, 512], BF16)
                R_sconf = Res("sconf")
                dpool = sbt(st, "dpool", [128, 2, 512], BF16)
                R_dpool = Res("dpool")
                t16 = sbt(st, "t16", [128, 2, 16], F32)
                R_t16 = Res("t16")
                ystage = [sbt(st, "ystage%d" % i, [128, 6, 512], BF16) for i in range(2)]
                R_yst = [Res("yst0"), Res("yst1")]
                psc = [pst(st, "psc%d" % i) for i in range(2)]
                R_psc = [PR("psc0"), PR("psc1")]
                ps_s1 = pst(st, "ps_s1")
                ps_s2 = pst(st, "ps_s2")
                R_s1, R_s2 = PR("s1"), PR("s2")
                ps_ring = Ring([(pst(st, "psq%d" % i), PR("psq%d" % i)) for i in range(3)])

                dma("pool", wpw[:], w_pw[l * 256:(l + 1) * 256, :].rearrange("(c p) n -> p c n", p=128), [], [R_wA2])
                dma("pool", wpbd[:], w_pbd[l * 256:(l + 1) * 256, :].rearrange("(c p) n -> p c n", p=128), [], [R_wA2])
                for blk in range(2):
                    for k in range(31):
                        P.op("pool", (lambda o, s: (lambda e: e.tensor_scalar_mul(o, ident_f, s)))(
                            dconf[:, blk, k, :], scol(l, C_DW + blk * 31 + k)), reads=[R_consts, R_small], writes=[R_dg])
                    for k in range(3):
                        P.op("pool", (lambda o, s: (lambda e: e.tensor_scalar_mul(o, ident_f, s)))(
                            dsc[:, blk, k, :], scol(l, C_SC + blk * 3 + k)), reads=[R_consts, R_small], writes=[R_dg])
                    for k in range(16):
                        P.op("pool", (lambda o, s: (lambda e: e.tensor_scalar_mul(o, ident_f, s)))(
                            dpl[:, blk, k, :], consts_sb[:, K_PCOEF + blk * 16 + k: K_PCOEF + blk * 16 + k + 1]),
                            reads=[R_consts], writes=[R_dg])

                def load_mt(j):
                    dma("sp", mt[j % 2][:], mixin.rearrange("(c p) n -> p c n", p=128)[:, :, j * 512: j * 512 + 544],
                        [R_mixin[j], R_mixin[j + 1]], [R_mt[j % 2]])

                def pw_store(jj):
                    bb = jj % 2
                    cols_ = slice(jj * 512, (jj + 1) * 512)
                    for oc in range(2):
                        pp, Rpp = ps_ring.next()
                        for kb in range(2):
                            mm(pp[:], wpw[:, kb, oc * 128:(oc + 1) * 128], sconf[:, kb, :], kb == 0, kb == 1,
                               [R_wA2, R_sconf], [Rpp])
                        cp("dve", ystage[bb][:, oc, :], pp[:], [Rpp], [R_yst[bb]])
                    ym_v = ymix.rearrange("(c p) n -> p c n", p=128)
                    dma("sp", ym_v[:, 0:2, cols_], ystage[bb][:, 0:2, :], [R_yst[bb]], [R_ymA[jj]])
                    dma("sp", ym_v[:, 4:8, cols_], ystage[bb][:, 2:6, :], [R_yst[bb]], [R_ymA[jj]])

                load_mt(0)
                sect = stop.split(":")[1] if (stop and ":" in stop) else "all"

                def on(*names):
                    return sect == "all" or sect in names

                for j in range(NT):
                    b = j % 2
                    if j + 1 < NT:
                        load_mt(j + 1)
                    m_ = mt[b]
                    cols = slice(j * 512, (j + 1) * 512)
                    if sect == "conv1":
                        for blk in range(2):
                            for k in range(31):
                                mm(psc[blk][:], dconf[:, blk, k, :], m_[:, blk, 2 + k: 2 + k + 512], k == 0, k == 30,
                                   [R_dg, R_mt[b]], [R_psc[blk]])
                            cp("dve", xc[:, blk, :], psc[blk][:], [R_psc[blk]], [R_xc])
                    if sect == "conv2":
                        for blk in range(2):
                            for k in range(3):
                                mm(psc[blk][:], dconf[:, blk, k, :], m_[:, blk, 2 + k: 2 + k + 512], k == 0, k == 2,
                                   [R_dg, R_mt[b]], [R_psc[blk]])
                            act(sq2[:, blk, :], psc[blk][:], AF.Square, [R_psc[blk]], [R_sq2])
                    if sect == "pool1":
                        for blk in range(2):
                            pp, Rpp = ps_ring.next()
                            for k in range(16):
                                mm(pp[:], dpl[:, blk, k, :], m_[:, 4 + blk, 32 - k: 32 - k + 512], k == 0, k == 15,
                                   [R_dg, R_mt[b]], [Rpp])
                            cp("dve", dpool[:, blk, :], pp[:], [Rpp], [R_dpool])
                    if on("conv", "ln", "silu", "conf"):
                        for blk in range(2):
                            for k in range(31):
                                mm(psc[blk][:], dconf[:, blk, k, :], m_[:, blk, 2 + k: 2 + k + 512], k == 0, k == 30,
                                   [R_dg, R_mt[b]], [R_psc[blk]])
                            cp("dve", xc[:, blk, :], psc[blk][:], [R_psc[blk]], [R_xc])
                            act(sq2[:, blk, :], xc[:, blk, :], AF.Square, [R_xc], [R_sq2])
                    if on("sc"):
                        for blk in range(2):
                            pp, Rpp = ps_ring.next()
                            for k in range(3):
                                mm(pp[:], dsc[:, blk, k, :], m_[:, 2 + blk, 30 + k: 30 + k + 512], k == 0, k == 2,
                                   [R_dg, R_mt[b]], [Rpp])
                            tt("dve", ystage[b][:, 2 + blk, :], pp[:], m_[:, 6 + blk, 32:544], ALU.mult, [Rpp, R_mt[b]], [R_yst[b]])
                    if on("pool"):
                        for blk in range(2):
                            pp, Rpp = ps_ring.next()
                            for k in range(16):
                                mm(pp[:], dpl[:, blk, k, :], m_[:, 4 + blk, 32 - k: 32 - k + 512], k == 0, k == 15,
                                   [R_dg, R_mt[b]], [Rpp])
                            cp("act", dpool[:, blk, :], pp[:], [Rpp], [R_dpool])
                            if j == 0:
                                tt("dve", t16[:, blk, :], pp[:, 0:16], m_[:, 4 + blk, 32:48], ALU.add, [Rpp, R_mt[b]], [R_t16])
                                tt("pool", t16[:, blk, :], t16[:, blk, :],
                                   consts_sb[:, K_PRATIO + blk * 16: K_PRATIO + blk * 16 + 16], ALU.mult, [R_t16, R_consts], [R_t16])
                                tt("dve", dpool[:, blk, 0:16], t16[:, blk, :], m_[:, 4 + blk, 32:48], ALU.subtract,
                                   [R_t16, R_mt[b]], [R_dpool])
                        for blk in range(2):
                            pp, Rpp = ps_ring.next()
                            mm(pp[:], wpbd[:, blk, :], dpool[:, blk, :], True, True, [R_wA2, R_dpool], [Rpp])
                            act(ystage[b][:, 4 + blk, :], pp[:], AF.Copy, [Rpp, R_small], [R_yst[b]], scale=scol(l, C_PSC + blk))
                    if j > 0:
                        pw_store(j - 1)
                    if on("ln", "silu", "conf"):
                        for blk in range(2):
                            mm(ps_s1[:], ones_f[:], xc[:, blk, :], blk == 0, blk == 1, [R_xc, R_cst2], [R_s1])
                        for blk in range(2):
                            mm(ps_s2[:], ones_f[:], sq2[:, blk, :], blk == 0, blk == 1, [R_sq2, R_cst2], [R_s2])
                        act(mean[:], ps_s1[:], AF.Copy, [R_s1], [R_mean], scale=1.0 / 256.0)
                        tt("dve", msq[:], mean[:], mean[:], ALU.mult, [R_mean], [R_msq])
                        stt("dve", var[:], ps_s2[:], 1.0 / 256.0, msq[:], ALU.mult, ALU.subtract, [R_s2, R_msq], [R_var])
                        act(sd[:], var[:], AF.Sqrt, [R_var], [R_sd], bias=EPS, scale=1.0)
                        recip(rstd2[:], sd[:], [R_sd], [R_rstd2])
                        for blk in range(2):
                            tt("dve", xm[blk][:], xc[:, blk, :], mean[:], ALU.subtract, [R_xc, R_mean], [R_xm[blk]])
                            tt("dve", xnn[blk][:], xm[blk][:], rstd2[:], ALU.mult, [R_xm[blk], R_rstd2], [R_xnn[blk]])
                    if on("silu", "conf"):
                        for blk in range(2):
                            act(sconf[:, blk, :], xnn[blk][:], AF.Silu, [R_xnn[blk], R_small], [R_sconf],
                                bias=scol(l, C_LNB + blk), scale=scol(l, C_LNG + blk))
                pw_store(NT - 1)
                P.barrier()
                P.flush(final=(stop is not None and stop.startswith("A2")))
            if stop is not None and stop.startswith("A2"):
                return nc

            with contextlib.ExitStack() as st:
                kaug = sbt(st, "kaug", [128, 4, T], BF16)
                R_kaug = [Res("kaug%d" % j) for j in range(NT)]
                R_kaugc = Res("kaugc")
                vsb = sbt(st, "vsb", [128, NB, 512], BF16)
                R_vsb = [Res("vsb%d" % j) for j in range(NT)]
                qa = [sbt(st, "qa%d" % i, [128, 4, 512], BF16) for i in range(2)]
                R_qa = [Res("qa0"), Res("qa1")]
                pts = Ring([(sbt(st, "pt%d" % i, [128, 1024], BF16), Res("pt%d" % i)) for i in range(3)])
                rec = sbt(st, "rec", [128, 512], F32)
                R_rec = Res("rec")
                yst = Ring([(sbt(st, "ysb%d" % i, [64, 512], BF16), Res("ysb%d" % i)) for i in range(2)])
                ps_s = Ring([(pst(st, "pss%d" % i, (128, 1024)), PR("pss%d" % i)) for i in range(3)])
                ps_o = Ring([(pst(st, "pso%d" % i), PR("pso%d" % i)) for i in range(2)])

                for i in range(2):
                    memset("dve", qa[i][64:96, :, :], -1.0, [R_qa[i]])
                def kv_load(j):
                    cols = slice(j * 512, (j + 1) * 512)
                    memset("dve" if j % 2 == 0 else "pool", kaug[64:96, :, cols], 0.0, [R_kaug[j]])
                    memset("dve" if j % 2 == 0 else "pool", kaug[64:67, :, cols], 1.0, [R_kaug[j]])
                    dma("sp", kaug[0:64, :, cols], kT.rearrange("(h r) n -> r h n", r=64)[:, :, cols], [R_k[j]], [R_kaug[j]])
                    dma("sp", kaug[67:70, :, cols], c3d.rearrange("h j n -> j h n")[:, :, cols], [R_c3[j]], [R_kaug[j]])
                    dma("sp", vsb[:, j * 4:(j + 1) * 4, :], vaug[j * 4:(j + 1) * 4].rearrange("s p c -> p s c"),
                        [R_v[j]], [R_vsb[j]])

                def load_q(j):
                    cols = slice(j * 512, (j + 1) * 512)
                    dma("sp", qa[j % 2][0:64, :, :], qT.rearrange("(h r) n -> r h n", r=64)[:, :, cols], [R_q[j]], [R_qa[j % 2]])
                    dma("sp", qa[j % 2][64:67, :, :], c3d.rearrange("h j n -> j h n")[:, :, cols], [R_c3[j]], [R_qa[j % 2]])

                kv_load(0)
                load_q(0)
                for j in range(1, NT):
                    kv_load(j)
                for j in range(NT):
                    b = j % 2
                    if j + 1 < NT:
                        load_q(j + 1)
                    cols = slice(j * 512, (j + 1) * 512)
                    for h in range(4):
                        po, Rpo = ps_o.next()
                        units = [("pair", i0) for i0 in range(0, 4 * j, 2)] + [("diag", dj) for dj in range(4)]

                        def qk(u):
                            kind, v = u
                            pS, RpS = ps_s.next()
                            if kind == "pair":
                                for half in range(2):
                                    i = v + half
                                    mm(pS[:, half * 512:(half + 1) * 512], kaug[0:96, h, i * 128:(i + 1) * 128],
                                       qa[b][0:96, h, :], True, True, [R_kaug[i // 4], R_kaugc, R_qa[b]], [RpS])
                            else:
                                i = 4 * j + v
                                c0 = 128 * v
                                mm(pS[:, c0:512], kaug[0:96, h, i * 128:(i + 1) * 128], qa[b][0:96, h, c0:512],
                                   True, False, [R_kaug[i // 4], R_kaugc, R_qa[b]], [RpS])
                                mm(pS[:, c0:c0 + 128], ident_bf[:], mask_bf[:], False, True, [R_cst2], [RpS])
                            return pS, RpS

                        LA = 2
                        pend = [qk(units[x]) for x in range(min(LA, len(units)))]
                        for ui, u in enumerate(units):
                            pS, RpS = pend.pop(0)
                            if ui + LA < len(units):
                                pend.append(qk(units[ui + LA]))
                            pt, Rpt = pts.next()
                            kind, v = u
                            first = (ui == 0)
                            last = (ui == len(units) - 1)
                            if kind == "pair":
                                act(pt[:, :], pS[:, :], AF.Exp, [RpS], [Rpt])
                                for half in range(2):
                                    i = v + half
                                    mm(po[:, :], vsb[:, i, h * 128:(h + 1) * 128], pt[:, half * 512:(half + 1) * 512],
                                       first and half == 0, False, [R_vsb[i // 4], Rpt], [Rpo])
                            else:
                                i = 4 * j + v
                                c0 = 128 * v
                                act(pt[:, c0:512], pS[:, c0:512], AF.Exp, [RpS], [Rpt])
                                mm(po[:, c0:512], vsb[:, i, h * 128:(h + 1) * 128], pt[:, c0:512], first, last,
                                   [R_vsb[i // 4], Rpt], [Rpo])
                        recip(rec[64:128, :], po[64:128, :], [Rpo], [R_rec])
                        ys, Rys = yst.next()
                        tt("dve", ys[:], po[0:64, :], rec[64:128, :], ALU.mult, [Rpo, R_rec], [Rys])
                        dma("sp", ymix[256 + 64 * h: 256 + 64 * h + 64, cols], ys[:], [Rys], [R_ymB[j]])
                P.barrier()
                P.flush(final=(stop == "B"))
            if stop == "B":
                return nc

            with contextlib.ExitStack() as st:
                wout = sbt(st, "wout", [128, 8, 1024], BF16)
                wgate = sbt(st, "wgate", [128, 8, 1024], BF16)
                wproj = sbt(st, "wproj", [128, 2, 1024], BF16)
                R_wC = Res("wC")
                wu = [sbt(st, "wu%d" % i, [128, 8, 512], BF16) for i in range(2)]
                wd = [sbt(st, "wd%d" % i, [128, 4, 1024], BF16) for i in range(2)]
                R_wu = [Res("wus%d" % i) for i in range(2)]
                R_wd = [Res("wds%d" % i) for i in range(2)]
                ymt = sbt(st, "ymt", [128, 8, 512], BF16)
                R_ymt = Res("ymt")
                ht = [sbt(st, "hc%d" % i, [128, 8, 512], F32) for i in range(2)]
                R_ht = [Res("hc0"), Res("hc1")]
                ptl = [sbt(st, "ptl%d" % i, [128, 2, 512], BF16) for i in range(2)]
                R_ptl = [Res("ptl0"), Res("ptl1")]
                mb = [sbt(st, "mb%d" % i, [128, 8, 512], F32) for i in range(2)]
                R_mb = [Res("mb0"), Res("mb1")]
                hnD = [sbt(st, "hnD%d" % i, [128, 8, 512], BF16) for i in range(2)]
                R_hnD = [Res("hnD0"), Res("hnD1")]
                hnE = sbt(st, "hnE", [128, 8, 512], BF16)
                R_hnE = Res("hnE")
                sqg = sbt(st, "sqg", [128, 8, 512], BF16)
                R_sqg = Res("sqg")
                ag = sbt(st, "ag", [128, 8, 512], BF16)
                R_ag = [Res("ag0"), Res("ag1")]
                rs = sbt(st, "rs_c", [128, 512], F32)
                rstd = sbt(st, "rstd_c", [128, 512], F32)
                R_rs, R_rstd = Res("rs"), Res("rstd")
                tmpf = Ring([(sbt(st, "tmpc%d" % i, [128, 512], F32), Res("tmpc%d" % i)) for i in range(2)])
                rl = Ring([(sbt(st, "rl%d" % i, [128, 512], BF16), Res("rl%d" % i)) for i in range(2)])
                ps_stat = pst(st, "ps_stat_c")
                R_pstat = PR("ps_stat_c")
                ps_ring = Ring([(pst(st, "psm%d" % i), PR("psm%d" % i)) for i in range(3)])
                ps_dn = Ring([(pst(st, "psd%d" % i), PR("psd%d" % i)) for i in range(4)])


                def load_wu(n):
                    if n >= NT * 8:
                        return
                    g = n % 8
                    r0 = l * 1024 + g * 128
                    dma("sp", wu[n % 2][:], wup_bf[r0:r0 + 128, :].rearrange("p (k c) -> p k c", k=8), [R_wup[l][g]], [R_wu[n % 2]])

                def load_wd(n):
                    if n >= NT * 8:
                        return
                    g = n % 8
                    r0 = l * 1024 + g * 128
                    dma("sp", wd[n % 2][:], wdn_bf[r0:r0 + 128, :].rearrange("p (k c) -> p k c", k=4), [R_wdn[l][g]], [R_wd[n % 2]])

                def sp(n):
                    for _ in range(n):
                        yield

                def gen_post(m, R_m, h, R_h_, gname):
                    act(sqg[:], m[:], AF.Square, [R_m], [R_sqg])
                    yield from sp(4)
                    rms_stats(sqg, [R_sqg], ps_stat, R_pstat, rs, rstd, R_rs, R_rstd)
                    yield from sp(2)
                    for c in range(8):
                        tb, Rtb = tmpf.next()
                        stt("dve", tb[:], m[:, c, :], scol(l, C_G[gname] + c), rstd[:], ALU.mult, ALU.mult,
                            [R_m, R_rstd, R_small], [Rtb])
                        tt("dve", h[:, c, :], h[:, c, :], tb[:], ALU.add, [R_h_, Rtb], [R_h_])
                        if c % 2 == 1:
                            yield
                    yield from sp(1)

                def gen_pre(h, R_h_, hn, R_hn, gname):
                    act(sqg[:], h[:], AF.Square, [R_h_], [R_sqg])
                    yield from sp(4)
                    rms_stats(sqg, [R_sqg], ps_stat, R_pstat, rs, rstd, R_rs, R_rstd)
                    yield from sp(2)
                    for c in range(8):
                        stt("dve", hn[:, c, :], h[:, c, :], scol(l, C_G[gname] + c), rstd[:], ALU.mult, ALU.mult,
                            [R_h_, R_rstd, R_small], [R_hn])
                        if c % 4 == 3:
                            yield
                    yield from sp(2)

                def H1_hload(j):
                    p = j % 2
                    cols = slice(j * 512, (j + 1) * 512)
                    dma("sp", ht[p][:], h_src_v[:, :, cols], [R_h[j]], [R_ht[p]])
                    dma("pool", ptl[p][:], pT[l * 256:(l + 1) * 256, :].rearrange("(c p) n -> p c n", p=128)[:, :, cols],
                        [], [R_ptl[p]])

                def H1(j):
                    p = j % 2
                    cols = slice(j * 512, (j + 1) * 512)
                    if j > 0:
                        H1_hload(j)
                    yield from sp(2)
                    for oc in range(8):
                        pp, Rpp = ps_ring.next()
                        for k in range(8):
                            mm(pp[:], wout[:, k, oc * 128:(oc + 1) * 128], ymt[:, k, :], k == 0, k == 7, [R_wC, R_ymt], [Rpp])
                        cp("dve" if oc % 2 == 0 else "act", mb[p][:, oc, :], pp[:], [Rpp], [R_mb[p]])
                        yield
                    yield from sp(2)
                    yield from gen_post(mb[p], R_mb[p], ht[p], R_ht[p], "mix_post")
                    yield from gen_pre(ht[p], R_ht[p], hnD[p], R_hnD[p], "mlp_pre")

                def H1_loads(j):
                    p = j % 2
                    cols = slice(j * 512, (j + 1) * 512)
                    dma("sp", ymt[:], ymix.rearrange("(c p) n -> p c n", p=128)[:, :, cols], [R_ymA[j], R_ymB[j]], [R_ymt])

                def H3(j):
                    p = j % 2
                    cols = slice(j * 512, (j + 1) * 512)
                    yield from gen_post(mb[p], R_mb[p], ht[p], R_ht[p], "mlp_post")
                    yield from gen_pre(ht[p], R_ht[p], hnE, R_hnE, "ple_pre")
                    for oc in range(8):
                        pp, Rpp = ps_ring.next()
                        for k in range(8):
                            mm(pp[:], wgate[:, k, oc * 128:(oc + 1) * 128], hnE[:, k, :], k == 0, k == 7, [R_wG, R_hnE], [Rpp])
                        act(sqg[:, oc, :], pp[:], AF.Sigmoid, [Rpp], [R_sqg])
                        yield
                    yield from sp(2)
                    for oc in range(8):
                        pp, Rpp = ps_ring.next()
                        for k in range(2):
                            mm(pp[:], wproj[:, k, oc * 128:(oc + 1) * 128], ptl[p][:, k, :], k == 0, k == 1, [R_wG, R_ptl[p]], [Rpp])
                        tt("dve", mb[p][:, oc, :], pp[:], sqg[:, oc, :], ALU.mult, [Rpp, R_sqg], [R_mb[p]])
                        yield
                    yield from sp(3)
                    yield from gen_post(mb[p], R_mb[p], ht[p], R_ht[p], "ple_post")
                    dma("sp", h_dst_v[:, :, cols], ht[p][:], [R_ht[p]], [R_h[j]])
                    yield

                def step(side, n):
                    for _ in range(n):
                        try:
                            next(side)
                        except StopIteration:
                            return

                def up(j, g, side):
                    n = j * 8 + g
                    p = j % 2
                    for c4 in range(4):
                        pp, Rpp = ps_ring.next()
                        for k in range(8):
                            mm(pp[:], wu[n % 2][:, k, c4 * 128:(c4 + 1) * 128], hnD[p][:, k, :], k == 0, k == 7,
                               [R_wu[n % 2], R_hnD[p]], [Rpp])
                        rr, Rrr = rl.next()
                        act(rr[:], pp[:], AF.Relu, [Rpp], [Rrr])
                        act(ag[:, (g % 2) * 4 + c4, :], rr[:], AF.Square, [Rrr], [R_ag[g % 2]])
                        step(side, 1)

                def down(j, g, side):
                    n = j * 8 + g
                    p = j % 2
                    for oc in range(8):
                        pd, Rpd = ps_dn.next()
                        for k4 in range(4):
                            mm(pd[:], wd[n % 2][:, k4, oc * 128:(oc + 1) * 128], ag[:, (g % 2) * 4 + k4, :], k4 == 0, k4 == 3,
                               [R_wd[n % 2], R_ag[g % 2]], [Rpd])
                        if g == 0:
                            cp("dve", mb[p][:, oc, :], pd[:], [Rpd], [R_mb[p]])
                        else:
                            tt("dve", mb[p][:, oc, :], pd[:], mb[p][:, oc, :], ALU.add, [Rpd, R_mb[p]], [R_mb[p]])
                        step(side, 1)

                def chain(*gens):
                    for gq in gens:
                        if gq is not None:
                            yield from gq

                R_wG = Res("wG")
                H1_loads(0)
                H1_hload(0)
                for k in range(8):
                    dma("sp", wout[:, k, :], wout_bf[l * 1024 + k * 128: l * 1024 + (k + 1) * 128, :], [R_wCbf[l]], [R_wC])
                load_wu(0)
                load_wd(0)
                load_wu(1)
                for k in range(8):
                    dma("sp", wgate[:, k, :], wgate_bf[l * 1024 + k * 128: l * 1024 + (k + 1) * 128, :], [R_wCbf[l]], [R_wG])
                for k in range(2):
                    dma("sp", wproj[:, k, :], wproj_bf[l * 256 + k * 128: l * 256 + (k + 1) * 128, :], [R_wCbf[l]], [R_wG])
                for _ in H1(0):
                    pass
                for j in range(NT):
                    if j + 1 < NT:
                        H1_loads(j + 1)
                    side = chain(H3(j - 1) if j > 0 else None, H1(j + 1) if j + 1 < NT else None)
                    for g in range(9):
                        n = j * 8 + g
                        if g < 8:
                            up(j, g, side)
                        if g > 0:
                            down(j, g - 1, side)
                        if g < 8:
                            load_wu(n + 2)
                            load_wd(n + 1)
                    for _ in side:
                        pass
                for _ in H3(NT - 1):
                    pass
                P.barrier()
                P.flush(final=(l == L - 1))
    return nc


POOL_WINDOWS = (2, 4, 8, 16)


def _chunkcols(v):
    return np.ascontiguousarray(v.reshape(-1, 128).T)


def prep_shared(inp, L):
    f32 = np.float32
    w_in = np.asarray(inp["w_in"], f32)[:L]
    perm = np.concatenate([np.arange(0, 1024), np.arange(1284, 2308), np.arange(1280, 1284), np.arange(1024, 1280)])
    w_in_r = np.ascontiguousarray(w_in[:, :, perm]).reshape(L * 1024, 2308)
    w_pw = np.ascontiguousarray(np.asarray(inp["w_conf_pw"], f32)[:L]).reshape(L * 256, 256)
    wp = np.asarray(inp["w_pool"], f32)[:L]
    w_pbd = np.zeros((L, 2, 128, 128), f32)
    for blk in range(2):
        for gg in range(2):
            w_pbd[:, blk, gg * 64:(gg + 1) * 64, gg * 64:(gg + 1) * 64] = wp[:, blk * 2 + gg]
    w_pbd = w_pbd.reshape(L * 256, 128)
    w_out = np.ascontiguousarray(np.asarray(inp["w_out"], f32)[:L]).reshape(L * 1024, 1024)
    wu = np.asarray(inp["w_up"], f32)[:L]
    wu = wu.reshape(L, 8, 128, 8, 512).transpose(0, 3, 2, 1, 4)
    w_up = np.ascontiguousarray(wu).reshape(L * 1024, 4096)
    wdn = np.asarray(inp["w_down"], f32)[:L]
    wdn = wdn.reshape(L, 8, 4, 128, 1024).transpose(0, 1, 3, 2, 4)
    w_dn = np.ascontiguousarray(wdn).reshape(L * 1024, 4096)
    w_gate = np.ascontiguousarray(np.asarray(inp["w_ple_gate"], f32)[:L]).reshape(L * 1024, 1024)
    w_proj = np.ascontiguousarray(np.asarray(inp["w_ple_proj"], f32)[:L]).reshape(L * 256, 1024)
    small = np.zeros((128, L * 128), f32)
    for l in range(L):
        o = l * 128
        for name, key in (("mix_pre", "g_mix_pre"), ("mix_post", "g_mix_post"), ("mlp_pre", "g_mlp_pre"),
                          ("mlp_post", "g_mlp_post"), ("ple_pre", "g_ple_pre"), ("ple_post", "g_ple_post")):
            small[:, o + C_G[name]: o + C_G[name] + 8] = _chunkcols(np.asarray(inp[key], f32)[l])
        small[:, o + C_LNG: o + C_LNG + 2] = _chunkcols(np.asarray(inp["conf_ln_g"], f32)[l])
        small[:, o + C_LNB: o + C_LNB + 2] = _chunkcols(np.asarray(inp["conf_ln_b"], f32)[l])
        small[:, o + C_PSC: o + C_PSC + 2] = _chunkcols(np.asarray(inp["pool_scale"], f32)[l])
        dw = np.asarray(inp["w_conf_dw"], f32)[l]
        for blk in range(2):
            small[:, o + C_DW + blk * 31: o + C_DW + blk * 31 + 31] = dw[:, blk * 128:(blk + 1) * 128].T
        sc = np.asarray(inp["w_sc"], f32)[l]
        for blk in range(2):
            small[:, o + C_SC + blk * 3: o + C_SC + blk * 3 + 3] = sc[:, blk * 128:(blk + 1) * 128].T
        small[0:4, o + C_BF] = np.asarray(inp["b_forget"], f32)[l]
    consts = np.zeros((128, 320), f32)
    consts[:, K_ID:K_ID + 128] = np.eye(128, dtype=f32)
    kk, qq = np.meshgrid(np.arange(128), np.arange(128), indexing="ij")
    consts[:, K_MASK:K_MASK + 128] = np.where(kk > qq, NEG, 0.0)
    for blk in range(2):
        for p in range(128):
            w = POOL_WINDOWS[(blk * 128 + p) // 64]
            for t in range(16):
                consts[p, K_PCOEF + blk * 16 + t] = (1.0 / w if t < w else 0.0) - (1.0 if t == 0 else 0.0)
                consts[p, K_PRATIO + blk * 16 + t] = w / min(t + 1, w)
    return dict(w_in=w_in_r, w_pw=w_pw, w_pbd=w_pbd, w_out=w_out, w_up=w_up, w_dn=w_dn, w_gate=w_gate,
                w_proj=w_proj, small=small, consts=consts)


_NC_CACHE = {}
ACTIVE_CORES = [0, 1, 4, 5]


def run(inp, T, L, n_seq, stop=None, dbg=False):
    f32 = np.float32
    shared = prep_shared(inp, L)
    x = np.asarray(inp["x"], f32)
    p = np.asarray(inp["p"], f32)
    active = ACTIVE_CORES[:n_seq]
    zero_map = None
    in_maps = []
    for core in range(8):
        if core in active:
            bi = active.index(core)
            m = dict(shared)
            m["xT"] = np.ascontiguousarray(x[bi].T)
            m["pT"] = np.ascontiguousarray(p[:L, bi].transpose(0, 2, 1)).reshape(L * 256, T)
        else:
            if zero_map is None:
                zero_map = {k: np.zeros_like(v) for k, v in shared.items()}
                zero_map["xT"] = np.zeros((1024, T), f32)
                zero_map["pT"] = np.zeros((L * 256, T), f32)
            m = zero_map
        in_maps.append(m)
    key = (T, L, stop, dbg)
    if key not in _NC_CACHE:
        _NC_CACHE[key] = build(T, L, stop, dbg)
    nc = _NC_CACHE[key]
    res = run_bass_kernel_spmd(nc, in_maps, core_ids=list(range(8)))
    if dbg:
        return res.results[active[0]]
    out = np.stack([np.ascontiguousarray(res.results[active[bi]]["outT"].T) for bi in range(n_seq)], axis=0)
    return out.astype(f32)


def kernel(**inputs):
    return run(inputs, 8192, 4, 4)
```
